# Optimizing a Trainium2 kernel written in Bass

```python
import math
import jax, jax.numpy as jnp
from jax import lax

D_MODEL = 1024
BATCH = 16
SEQ = 2048
DEPTH = 2
DEC_BATCH = 128
DEC_SEQ = 4
PAST_LEN = 16384
PAGE_SIZE = 128

N_MIXERS = 4
GROUP_W = D_MODEL // N_MIXERS
CONV_K = 4
FFN_DIM = 2816
EPS = 1e-6
F32 = jnp.float32

GDN_HEADS = 4
GDN_DK = GROUP_W // GDN_HEADS
GDN_DV = GROUP_W // GDN_HEADS
GDN_CHUNK = 64
MLP_CHUNK = 128
MLP_GROUPS = 4
MLP_GW = GROUP_W // MLP_GROUPS
SSM_HEADS = 4
SSM_HEAD_DIM = GROUP_W // SSM_HEADS
SSM_GROUPS = 2
SSM_STATE = 128
SSM_CHUNK = 128
SSM_CONV_W = GROUP_W + 2 * SSM_GROUPS * SSM_STATE
MLA_HEADS = 4
MLA_NOPE = 64
MLA_ROPE = 32
MLA_V_DIM = GROUP_W // MLA_HEADS
MLA_Q_RANK = 256
MLA_KV_RANK = 128
MLA_SCALE = (MLA_NOPE + MLA_ROPE) ** -0.5
ROPE_THETA = 10000.0
Q_BLOCK = 128

A_COLS = 4 * GROUP_W + 2 * GDN_HEADS
B_COLS = 2 * GROUP_W
C_COLS = GROUP_W + SSM_CONV_W + SSM_HEADS
D_COLS = MLA_Q_RANK + MLA_KV_RANK + MLA_ROPE
IN_COLS = A_COLS + B_COLS + C_COLS + D_COLS

kernel_name = 'hymba_gdn_gmlp_ssd_mla_decode_step'


def rms_norm(x, g):
    xf = x.astype(F32)
    return (xf * lax.rsqrt(jnp.mean(xf * xf, axis=-1, keepdims=True) + EPS) * g.astype(F32)).astype(x.dtype)


def layer_norm(x, g, b):
    xf = x.astype(F32)
    mu = jnp.mean(xf, axis=-1, keepdims=True)
    var = jnp.mean(jnp.square(xf - mu), axis=-1, keepdims=True)
    return ((xf - mu) * lax.rsqrt(var + EPS) * g.astype(F32) + b.astype(F32)).astype(x.dtype)


def l2_norm(x):
    xf = x.astype(F32)
    return (xf * lax.rsqrt(jnp.sum(xf * xf, axis=-1, keepdims=True) + EPS)).astype(x.dtype)


def swiglu(x, w_in, w_out):
    gate, up = jnp.split(x @ w_in, 2, axis=-1)
    return (jax.nn.silu(gate) * up) @ w_out


def causal_conv(x, buf, w):
    seq_len = x.shape[1]
    xp = jnp.concatenate([buf.astype(x.dtype), x], axis=1)
    y = sum(xp[:, i:i + seq_len] * w[i] for i in range(CONV_K))
    return y, xp[:, -(CONV_K - 1):]


def rope(x, pos):
    half = MLA_ROPE // 2
    inv_freq = ROPE_THETA ** (-jnp.arange(half, dtype=F32) / half)
    ang = pos.astype(F32)[:, None] * inv_freq[None, :]
    cos, sin = jnp.cos(ang)[None, :, None, :], jnp.sin(ang)[None, :, None, :]
    xf = x.astype(F32)
    x1, x2 = xf[..., :half], xf[..., half:]
    return jnp.concatenate([x1 * cos - x2 * sin, x1 * sin + x2 * cos], axis=-1).astype(x.dtype)


def gated_delta_chunked(q, k, v, log_alpha, beta, s0, chunk):
    out_dtype = v.dtype
    bsz, nh, seq_len, dk = q.shape
    n = seq_len // chunk

    def blk(t):
        return t.astype(F32).reshape(bsz, nh, n, chunk, *t.shape[3:])

    q, k, v, beta = blk(q), blk(k), blk(v), blk(beta)
    gcum = jnp.cumsum(blk(log_alpha), axis=-1)
    causal = jnp.tril(jnp.ones((chunk, chunk), bool))
    decay = jnp.exp(jnp.where(causal, gcum[..., :, None] - gcum[..., None, :], -jnp.inf))
    kb = k * beta[..., None]
    a_strict = jnp.tril(jnp.einsum('bhnid,bhnjd->bhnij', kb, k) * decay, -1)
    eye = jnp.eye(chunk, dtype=F32)
    rhs = jnp.concatenate([kb * jnp.exp(gcum)[..., None], v * beta[..., None]], axis=-1)
    sol = lax.linalg.triangular_solve(eye + a_strict, rhs, left_side=True, lower=True, unit_diagonal=True)
    w_c, u_c = sol[..., :dk], sol[..., dk:]
    qk = jnp.einsum('bhnid,bhnjd->bhnij', q, k) * decay
    q_dec = q * jnp.exp(gcum)[..., None]
    k_dec = k * jnp.exp(gcum[..., -1:] - gcum)[..., None]
    a_last = jnp.exp(gcum[..., -1])

    def step(s, xs):
        w_i, u_i, qk_i, qd_i, kd_i, al_i = xs
        u_new = u_i - jnp.einsum('bhcd,bhde->bhce', w_i, s)
        o = jnp.einsum('bhcd,bhde->bhce', qd_i, s) + jnp.einsum('bhcj,bhje->bhce', qk_i, u_new)
        s = s * al_i[..., None, None] + jnp.einsum('bhcd,bhce->bhde', kd_i, u_new)
        return s, o

    xs = tuple(jnp.moveaxis(t, 2, 0) for t in (w_c, u_c, qk, q_dec, k_dec, a_last))
    s_fin, o = lax.scan(step, s0.astype(F32), xs)
    o = jnp.moveaxis(o, 0, 2).reshape(bsz, nh, seq_len, -1)
    return o.astype(out_dtype), s_fin.astype(s0.dtype)


def ssd_chunked(x, dt, a_neg, bm, cm, h0, chunk):
    out_dtype = x.dtype
    bsz, nh, seq_len, _ = x.shape
    n = seq_len // chunk

    def blk(t):
        return t.astype(F32).reshape(bsz, nh, n, chunk, *t.shape[3:])

    x, dt, bm, cm = blk(x), blk(dt), blk(bm), blk(cm)
    acum = jnp.cumsum(dt * a_neg.astype(F32)[:, None, None], axis=-1)
    causal = jnp.tril(jnp.ones((chunk, chunk), bool))
    decay = jnp.exp(jnp.where(causal, acum[..., :, None] - acum[..., None, :], -jnp.inf))
    xdt = x * dt[..., None]
    y_intra = jnp.einsum('bhnij,bhnjp->bhnip', jnp.einsum('bhnid,bhnjd->bhnij', cm, bm) * decay, xdt)
    c_dec = cm * jnp.exp(acum)[..., None]
    chunk_state = jnp.einsum('bhnjd,bhnjp->bhnpd', bm * jnp.exp(acum[..., -1:] - acum)[..., None], xdt)
    a_last = jnp.exp(acum[..., -1])

    def step(h, xs):
        cd_i, st_i, al_i = xs
        y = jnp.einsum('bhid,bhpd->bhip', cd_i, h)
        return h * al_i[..., None, None] + st_i, y

    xs = tuple(jnp.moveaxis(t, 2, 0) for t in (c_dec, chunk_state, a_last))
    h_fin, y_inter = lax.scan(step, h0.astype(F32), xs)
    y = y_intra + jnp.moveaxis(y_inter, 0, 2)
    return y.reshape(bsz, nh, seq_len, -1).astype(out_dtype), h_fin.astype(h0.dtype)


def mla_attend(q_lat, q_pe, q_pos, ckv, kpe, k_pos):
    s = jnp.einsum('bqhr,bkr->bhqk', q_lat, ckv) + jnp.einsum('bqhe,bke->bhqk', q_pe, kpe)
    s = jnp.where(k_pos[None, :] <= q_pos[:, None], s.astype(F32) * MLA_SCALE, -jnp.inf)
    p = jax.nn.softmax(s, axis=-1).astype(ckv.dtype)
    return jnp.einsum('bhqk,bkr->bqhr', p, ckv)


def token_mixing(h, w, gdn_conv, gdn_s, ssm_conv, ssm_h, past_ckv, past_kpe):
    bsz, seq_len, _ = h.shape
    prompt = past_ckv is None
    proj = h @ w['w_in']
    p_a, p_b, p_c, p_d = jnp.split(proj, [A_COLS, A_COLS + B_COLS, A_COLS + B_COLS + C_COLS], axis=-1)

    qkv_raw, z_a, a_a, b_a = jnp.split(p_a, [3 * GROUP_W, 4 * GROUP_W, 4 * GROUP_W + GDN_HEADS], axis=-1)
    qkv, gdn_conv_new = causal_conv(qkv_raw, gdn_conv, w['gdn_conv_w'])
    q, k, v = jnp.split(jax.nn.silu(qkv), 3, axis=-1)

    def heads(t):
        return t.reshape(bsz, seq_len, GDN_HEADS, -1).transpose(0, 2, 1, 3)

    q = l2_norm(heads(q)) * GDN_DK ** -0.5
    k = l2_norm(heads(k))
    log_alpha = -jnp.exp(w['gdn_a_log'].astype(F32)) * jax.nn.softplus(a_a.astype(F32) + w['gdn_dt_bias'].astype(F32))
    beta = jax.nn.sigmoid(b_a.astype(F32))
    o_a, gdn_s_new = gated_delta_chunked(q, k, heads(v), log_alpha.transpose(0, 2, 1), beta.transpose(0, 2, 1),
                                         gdn_s, GDN_CHUNK if prompt else seq_len)
    o_a = rms_norm(o_a.transpose(0, 2, 1, 3), w['gdn_norm_g']) * jax.nn.silu(z_a.reshape(bsz, seq_len, GDN_HEADS, GDN_DV))
    out_a = o_a.reshape(bsz, seq_len, GROUP_W)

    u, v_b = jnp.split(jax.nn.gelu(p_b), 2, axis=-1)
    v_b = layer_norm(v_b, w['mlp_ln_g'], w['mlp_ln_b'])
    cm = MLP_CHUNK if prompt else seq_len
    ws = jnp.where(jnp.tril(jnp.ones((MLP_CHUNK, MLP_CHUNK), bool)), w['mlp_ws'], 0)[:, :cm, :cm]
    vr = v_b.reshape(bsz, seq_len // cm, cm, MLP_GROUPS, MLP_GW)
    mixed = jnp.einsum('gpq,bnqgc->bnpgc', ws, vr) + w['mlp_bs'][:, :cm].T[:, :, None]
    out_b = u * mixed.reshape(bsz, seq_len, GROUP_W)

    z_c, xbc_raw, dt_raw = jnp.split(p_c, [GROUP_W, GROUP_W + SSM_CONV_W], axis=-1)
    xbc, ssm_conv_new = causal_conv(xbc_raw, ssm_conv, w['ssm_conv_w'])
    xbc = jax.nn.silu(xbc + w['ssm_conv_b'])
    xs_c, b_c, c_c = jnp.split(xbc, [GROUP_W, GROUP_W + SSM_GROUPS * SSM_STATE], axis=-1)

    def to_heads(t):
        t = t.reshape(bsz, seq_len, SSM_GROUPS, SSM_STATE)
        return jnp.repeat(t, SSM_HEADS // SSM_GROUPS, axis=2).transpose(0, 2, 1, 3)

    xs_h = xs_c.reshape(bsz, seq_len, SSM_HEADS, SSM_HEAD_DIM).transpose(0, 2, 1, 3)
    dt = jax.nn.softplus(dt_raw.astype(F32) + w['ssm_dt_bias'].astype(F32)).transpose(0, 2, 1)
    y_c, ssm_h_new = ssd_chunked(xs_h, dt, -jnp.exp(w['ssm_a_log'].astype(F32)), to_heads(b_c), to_heads(c_c),
                                 ssm_h, SSM_CHUNK if prompt else seq_len)
    y_c = (y_c + w['ssm_d'][:, None, None] * xs_h).transpose(0, 2, 1, 3).reshape(bsz, seq_len, GROUP_W)
    out_c = rms_norm(y_c * jax.nn.silu(z_c), w['ssm_norm_g'])

    c_q, c_kv, k_pe = jnp.split(p_d, [MLA_Q_RANK, MLA_Q_RANK + MLA_KV_RANK], axis=-1)
    c_q = rms_norm(c_q, w['mla_q_norm_g'])
    c_kv = rms_norm(c_kv, w['mla_kv_norm_g'])
    pos0 = 0 if prompt else past_ckv.shape[1]
    q_pos = pos0 + jnp.arange(seq_len)
    q_full = (c_q @ w['mla_w_uq']).reshape(bsz, seq_len, MLA_HEADS, MLA_NOPE + MLA_ROPE)
    q_nope, q_pe = jnp.split(q_full, [MLA_NOPE], axis=-1)
    q_pe = rope(q_pe, q_pos)
    k_pe = rope(k_pe[:, :, None, :], q_pos)[:, :, 0, :]
    q_lat = jnp.einsum('blhd,hrd->blhr', q_nope, w['mla_w_uk'])
    if prompt:
        keys_lat, keys_pe, k_pos = c_kv, k_pe, q_pos
    else:
        keys_lat = jnp.concatenate([past_ckv.astype(c_kv.dtype), c_kv], axis=1)
        keys_pe = jnp.concatenate([past_kpe.astype(k_pe.dtype), k_pe], axis=1)
        k_pos = jnp.arange(pos0 + seq_len)
    blk = Q_BLOCK if prompt else seq_len
    nb = seq_len // blk

    def qblocks(t):
        return t.reshape(bsz, nb, blk, *t.shape[2:]).swapaxes(0, 1)

    o_lat = lax.map(lambda t: mla_attend(t[0], t[1], t[2], keys_lat, keys_pe, k_pos),
                    (qblocks(q_lat), qblocks(q_pe), q_pos.reshape(nb, blk)))
    o_lat = o_lat.swapaxes(0, 1).reshape(bsz, seq_len, MLA_HEADS, MLA_KV_RANK)
    out_d = jnp.einsum('blhr,hrd->blhd', o_lat, w['mla_w_uv']).reshape(bsz, seq_len, GROUP_W)

    mix = jnp.concatenate([out_a, out_b, out_c, out_d], axis=-1) @ w['w_out']
    return mix, (c_kv, k_pe, gdn_s_new, gdn_conv_new, ssm_h_new, ssm_conv_new, v_b)


def trunk_layer(x, w, gdn_conv, gdn_s, ssm_conv, ssm_h, past_ckv, past_kpe):
    g = w['norm_g']
    x = x + 0.5 * rms_norm(swiglu(rms_norm(x, g[0]), w['ffn_w_in'][0], w['ffn_w_out'][0]), g[1])
    mix, new_state = token_mixing(rms_norm(x, g[2]), w, gdn_conv, gdn_s, ssm_conv, ssm_h, past_ckv, past_kpe)
    x = x + rms_norm(mix, g[3])
    x = x + 0.5 * rms_norm(swiglu(rms_norm(x, g[4]), w['ffn_w_in'][1], w['ffn_w_out'][1]), g[5])
    return x, new_state


def setup_inputs(seed: int = 0) -> dict:
    key = jax.random.key(seed)
    keys = jax.random.split(key, 48)
    count = [0]

    def nk():
        count[0] += 1
        return keys[count[0] - 1]

    def normal(shape, scale):
        return scale * jax.random.normal(nk(), shape, F32)

    def gain(shape):
        return 1.0 + normal(shape, 0.02)

    def a_log(shape):
        return jnp.log(jax.random.uniform(nk(), shape, F32, 1.0, 16.0))

    def dt_bias(shape):
        dt = jnp.exp(jax.random.uniform(nk(), shape, F32, math.log(1e-3), math.log(1e-1)))
        return dt + jnp.log(-jnp.expm1(-dt))

    n_pages = PAST_LEN // PAGE_SIZE
    n_phys = (DEC_BATCH * n_pages * 5) // 4
    page_table = jax.random.permutation(nk(), n_phys)[:DEC_BATCH * n_pages].reshape(DEC_BATCH, n_pages).astype(jnp.int32)
    return {
        'x_prompt': normal((BATCH, SEQ, D_MODEL), 1.0),
        'x_sample': normal((DEC_BATCH, DEC_SEQ, D_MODEL), 1.0),
        'cache_ckv': normal((DEPTH, n_phys, PAGE_SIZE, MLA_KV_RANK), 1.0),
        'cache_kpe': normal((DEPTH, n_phys, PAGE_SIZE, MLA_ROPE), 1.0),
        'page_table': page_table,
        'state_gdn_s': normal((DEPTH, DEC_BATCH, GDN_HEADS, GDN_DK, GDN_DV), 0.1),
        'state_gdn_conv': normal((DEPTH, DEC_BATCH, CONV_K - 1, 3 * GROUP_W), 1.0),
        'state_ssm_h': normal((DEPTH, DEC_BATCH, SSM_HEADS, SSM_HEAD_DIM, SSM_STATE), 0.1),
        'state_ssm_conv': normal((DEPTH, DEC_BATCH, CONV_K - 1, SSM_CONV_W), 1.0),
        'norm_g': gain((DEPTH, 6, D_MODEL)),
        'ffn_w_in': normal((DEPTH, 2, D_MODEL, 2 * FFN_DIM), D_MODEL ** -0.5),
        'ffn_w_out': normal((DEPTH, 2, FFN_DIM, D_MODEL), FFN_DIM ** -0.5),
        'w_in': normal((DEPTH, D_MODEL, IN_COLS), D_MODEL ** -0.5),
        'w_out': normal((DEPTH, N_MIXERS * GROUP_W, D_MODEL), (N_MIXERS * GROUP_W) ** -0.5),
        'gdn_conv_w': normal((DEPTH, CONV_K, 3 * GROUP_W), CONV_K ** -0.5),
        'gdn_a_log': a_log((DEPTH, GDN_HEADS)),
        'gdn_dt_bias': dt_bias((DEPTH, GDN_HEADS)),
        'gdn_norm_g': gain((DEPTH, GDN_DV)),
        'mlp_ln_g': gain((DEPTH, GROUP_W)),
        'mlp_ln_b': normal((DEPTH, GROUP_W), 0.02),
        'mlp_ws': normal((DEPTH, MLP_GROUPS, MLP_CHUNK, MLP_CHUNK), MLP_CHUNK ** -0.5),
        'mlp_bs': normal((DEPTH, MLP_GROUPS, MLP_CHUNK), 0.02),
        'ssm_conv_w': normal((DEPTH, CONV_K, SSM_CONV_W), CONV_K ** -0.5),
        'ssm_conv_b': normal((DEPTH, SSM_CONV_W), 0.02),
        'ssm_a_log': a_log((DEPTH, SSM_HEADS)),
        'ssm_dt_bias': dt_bias((DEPTH, SSM_HEADS)),
        'ssm_d': gain((DEPTH, SSM_HEADS)),
        'ssm_norm_g': gain((DEPTH, GROUP_W)),
        'mla_q_norm_g': gain((DEPTH, MLA_Q_RANK)),
        'mla_w_uq': normal((DEPTH, MLA_Q_RANK, MLA_HEADS * (MLA_NOPE + MLA_ROPE)), MLA_Q_RANK ** -0.5),
        'mla_kv_norm_g': gain((DEPTH, MLA_KV_RANK)),
        'mla_w_uk': normal((DEPTH, MLA_HEADS, MLA_KV_RANK, MLA_NOPE), MLA_KV_RANK ** -0.5),
        'mla_w_uv': normal((DEPTH, MLA_HEADS, MLA_KV_RANK, MLA_V_DIM), MLA_KV_RANK ** -0.5),
    }


def reference(x_prompt, x_sample, cache_ckv, cache_kpe, page_table, state_gdn_s, state_gdn_conv,
              state_ssm_h, state_ssm_conv, norm_g, ffn_w_in, ffn_w_out, w_in, w_out,
              gdn_conv_w, gdn_a_log, gdn_dt_bias, gdn_norm_g, mlp_ln_g, mlp_ln_b, mlp_ws, mlp_bs,
              ssm_conv_w, ssm_conv_b, ssm_a_log, ssm_dt_bias, ssm_d, ssm_norm_g,
              mla_q_norm_g, mla_w_uq, mla_kv_norm_g, mla_w_uk, mla_w_uv):
    weights = dict(norm_g=norm_g, ffn_w_in=ffn_w_in, ffn_w_out=ffn_w_out, w_in=w_in, w_out=w_out,
                   gdn_conv_w=gdn_conv_w, gdn_a_log=gdn_a_log, gdn_dt_bias=gdn_dt_bias, gdn_norm_g=gdn_norm_g,
                   mlp_ln_g=mlp_ln_g, mlp_ln_b=mlp_ln_b, mlp_ws=mlp_ws, mlp_bs=mlp_bs,
                   ssm_conv_w=ssm_conv_w, ssm_conv_b=ssm_conv_b, ssm_a_log=ssm_a_log,
                   ssm_dt_bias=ssm_dt_bias, ssm_d=ssm_d, ssm_norm_g=ssm_norm_g,
                   mla_q_norm_g=mla_q_norm_g, mla_w_uq=mla_w_uq, mla_kv_norm_g=mla_kv_norm_g,
                   mla_w_uk=mla_w_uk, mla_w_uv=mla_w_uv)
    bp = x_prompt.shape[0]
    n_seq, n_pages = page_table.shape
    dtype = x_prompt.dtype
    xp, xs = x_prompt, x_sample
    st_p, st_s = [], []
    for l in range(DEPTH):
        wl = {name: arr[l] for name, arr in weights.items()}
        xp, st = trunk_layer(xp, wl,
                             jnp.zeros((bp, CONV_K - 1, 3 * GROUP_W), dtype),
                             jnp.zeros((bp, GDN_HEADS, GDN_DK, GDN_DV), dtype),
                             jnp.zeros((bp, CONV_K - 1, SSM_CONV_W), dtype),
                             jnp.zeros((bp, SSM_HEADS, SSM_HEAD_DIM, SSM_STATE), dtype),
                             None, None)
        st_p.append(st)
        past_ckv = cache_ckv[l, page_table].reshape(n_seq, n_pages * PAGE_SIZE, MLA_KV_RANK)
        past_kpe = cache_kpe[l, page_table].reshape(n_seq, n_pages * PAGE_SIZE, MLA_ROPE)
        xs, st = trunk_layer(xs, wl, state_gdn_conv[l], state_gdn_s[l], state_ssm_conv[l], state_ssm_h[l],
                             past_ckv, past_kpe)
        st_s.append(st)

    def stack(states, i):
        return jnp.stack([s[i] for s in states])

    return (xp, xs,
            stack(st_p, 0), stack(st_p, 1), stack(st_p, 2), stack(st_p, 3), stack(st_p, 4), stack(st_p, 5),
            stack(st_s, 0), stack(st_s, 1), stack(st_s, 2), stack(st_s, 3), stack(st_s, 4), stack(st_s, 5),
            stack(st_s, 6))
```

```python
import math
import numpy as np
import ml_dtypes
import concourse.bass as bass
import concourse.mybir as mybir
from concourse.bass_utils import run_bass_kernel_spmd

F32 = mybir.dt.float32
BF16 = mybir.dt.bfloat16
I32 = mybir.dt.int32
AF = mybir.ActivationFunctionType
ALU = mybir.AluOpType
AX = mybir.AxisListType

ENGS = ("tensor", "vector", "scalar", "gpsimd", "sync")
N_DMA_SEMS = 48

D = 1024
DC = 8
FF = 2816
FC = 22
EPS = 1e-6
NEG = -30000.0
C_QKV, C_ZA, C_GA, C_U, C_V, C_ZC, C_XBC, C_DT, C_CQ, C_CKV, C_KPE = 0, 768, 1024, 1032, 1288, 1544, 1800, 2568, 2572, 2828, 2956
IN_COLS = 2988
MLA_SCALE = 96 ** -0.5


class Tl:
    def __init__(self, t, k):
        self.t = t
        self.k = k


class Prog:
    def __init__(self, nc, arena_f=0, arena_b=0):
        self.nc = nc
        self.ops = []
        self.n_alloc = 0
        self.af = nc.alloc_sbuf_tensor("arena_f", [128, arena_f], F32) if arena_f else None
        self.ab = nc.alloc_sbuf_tensor("arena_b", [128, arena_b], BF16) if arena_b else None
        self.af_n, self.ab_n = arena_f, arena_b
        self.af_off = 0
        self.ab_off = 0
        self.af_max = 0
        self.ab_max = 0
        self.dbg_outs = {}
        self.psum_tokens = set()

    def sb(self, shape, dtype=F32, name=None):
        self.n_alloc += 1
        nm = name or f"sb{self.n_alloc}"
        return Tl(self.nc.alloc_sbuf_tensor(nm, list(shape), dtype)[:], nm)

    def ps(self, shape, dtype=F32, name=None):
        self.n_alloc += 1
        nm = name or f"ps{self.n_alloc}"
        self.psum_tokens.add(nm)
        return Tl(self.nc.alloc_psum_tensor(nm, list(shape), dtype)[:], nm)

    def mark(self):
        return (self.af_off, self.ab_off)

    def release(self, m):
        self.af_off, self.ab_off = m
        self.barrier()

    def tmp(self, shape, dtype=F32):
        n = 1
        for s in shape[1:]:
            n *= s
        n = (n + 7) // 8 * 8
        self.n_alloc += 1
        if dtype == F32:
            base, off = self.af, self.af_off
            self.af_off += n
            self.af_max = max(self.af_max, self.af_off)
            assert self.af_off <= self.af_n, ("arena_f overflow", self.af_off, self.af_n)
        else:
            base, off = self.ab, self.ab_off
            self.ab_off += n
            self.ab_max = max(self.ab_max, self.ab_off)
            assert self.ab_off <= self.ab_n, ("arena_b overflow", self.ab_off, self.ab_n)
        n0 = 1
        for s in shape[1:]:
            n0 *= s
        v = base[0:shape[0], off:off + n0]
        if len(shape) == 3:
            v = v.rearrange("p (a b) -> p a b", a=shape[1])
        elif len(shape) == 4:
            v = v.rearrange("p (a b c) -> p a b c", a=shape[1], b=shape[2])
        return Tl(v, f"tmp{self.n_alloc}")

    def op(self, eng, fn, r=(), w=(), dma=False, persist=False):
        pr = [t for t in r if t in self.psum_tokens]
        if pr:
            w = tuple(w) + tuple(pr)
        self.ops.append(dict(eng=eng, fn=fn, r=tuple(r), w=tuple(w), dma=dma, persist=persist))

    def barrier(self):
        self.ops.append(dict(barrier=True))

    def dma(self, out, in_, r=(), w=(), eng=None, persist=False, **kw):
        if eng is None:
            eng = "sync" if out.dtype == in_.dtype else "gpsimd"
        self.op(eng, lambda e: e.dma_start(out=out, in_=in_, **kw), r, w, dma=True, persist=persist)

    def mm(self, out, lhsT, rhs, start=True, stop=True, r=(), w=(), **kw):
        self.op("tensor", lambda e: e.matmul(out, lhsT, rhs, start=start, stop=stop, **kw), r, w)

    def tr(self, out, in_, ident, r=(), w=()):
        self.op("tensor", lambda e: e.transpose(out, in_, ident), r, w)

    def act(self, out, in_, func, r=(), w=(), **kw):
        self.op("scalar", lambda e: e.activation(out, in_, func, **kw), r, w)

    def tt(self, out, in0, in1, op, r=(), w=(), eng="vector"):
        self.op(eng, lambda e: e.tensor_tensor(out, in0, in1, op), r, w)

    def ts(self, out, in0, s1, s2, op0, op1=None, r=(), w=(), eng="vector"):
        if op1 is None:
            self.op(eng, lambda e: e.tensor_scalar(out, in0, s1, s2, op0), r, w)
        else:
            self.op(eng, lambda e: e.tensor_scalar(out, in0, s1, s2, op0, op1), r, w)

    def stt(self, out, in0, scalar, in1, op0, op1, r=(), w=(), eng="vector"):
        self.op(eng, lambda e: e.scalar_tensor_tensor(out, in0, scalar, in1, op0, op1), r, w)

    def copy(self, out, in_, r=(), w=(), eng="vector"):
        if eng == "scalar":
            self.op(eng, lambda e: e.copy(out, in_), r, w)
        else:
            self.op(eng, lambda e: e.tensor_copy(out, in_), r, w)

    def memset(self, ap, val, w=(), eng="vector"):
        self.op(eng, lambda e: e.memset(ap, val), (), w)

    def recip(self, out, in_, r=(), w=()):
        self.op("vector", lambda e: e.reciprocal(out, in_), r, w)

    def red(self, out, in_, r=(), w=(), op=None):
        self.op("vector", lambda e: e.tensor_reduce(out, in_, AX.X, op or ALU.add), r, w)

    def dbg(self, name, ap, r=()):
        shp = list(ap.shape)
        d = self.nc.dram_tensor("dbg_" + name, shp, F32, kind="ExternalOutput").ap()
        self.dbg_outs[name] = shp
        self.dma(d, ap, r=r)

    def build(self):
        nc = self.nc
        ops = self.ops
        last_w = {}
        readers = {}
        last_on_eng = {}
        pend_dma = []
        pend_prev = []
        barrier_deps = None
        first_after = {}
        for i, o in enumerate(ops):
            if o.get("barrier"):
                barrier_deps = set(last_on_eng.values()) | set(pend_dma) | set(pend_prev)
                first_after = {e: True for e in ENGS}
                pend_prev = pend_dma
                pend_dma = []
                continue
            deps = set()
            for t in o["r"]:
                if t in last_w:
                    deps.add(last_w[t])
            for t in o["w"]:
                if t in last_w:
                    deps.add(last_w[t])
                for j in readers.get(t, {}).values():
                    deps.add(j)
            if barrier_deps is not None and first_after.get(o["eng"]):
                deps |= barrier_deps
                first_after[o["eng"]] = False
            deps.discard(i)
            if o["eng"] == "tensor" and not o["dma"]:
                deps = {j for j in deps if not (ops[j]["eng"] == "tensor" and not ops[j]["dma"])}
            o["deps"] = deps
            for t in o["w"]:
                last_w[t] = i
                readers[t] = {}
            for t in o["r"]:
                key = ("dma", i) if o["dma"] else o["eng"]
                readers.setdefault(t, {})[key] = i
            if o["dma"]:
                if not o["persist"]:
                    pend_dma.append(i)
            else:
                last_on_eng[o["eng"]] = i
        ops = [o for o in ops if not o.get("barrier")]
        idx_map = {}
        k = 0
        for i, o in enumerate(self.ops):
            if not o.get("barrier"):
                idx_map[i] = k
                k += 1
        for o in ops:
            o["deps"] = {idx_map[j] for j in o["deps"]}
            o["sig"] = False
        for o in ops:
            for j in o["deps"]:
                if not ops[j]["dma"]:
                    ops[j]["sig"] = True
        esem = {e: nc.alloc_semaphore(f"e_{e}") for e in ENGS}
        dsem = [nc.alloc_semaphore(f"d_{k}") for k in range(N_DMA_SEMS)]
        ecount = {e: 0 for e in ENGS}
        dcount = [0] * N_DMA_SEMS
        nd = 0
        for o in ops:
            if o["dma"]:
                k = nd % N_DMA_SEMS
                nd += 1
                o["dsem"] = k
                o["dprev"] = dcount[k]
                dcount[k] += 16
                o["dval"] = dcount[k]
            elif o["sig"]:
                ecount[o["eng"]] += 1
                o["eval"] = ecount[o["eng"]]
        ewaited = {e: {f: 0 for f in ENGS} for e in ENGS}
        dwaited = {e: [0] * N_DMA_SEMS for e in ENGS}
        for o in ops:
            e = o["eng"]
            waits = []
            need_e = {}
            need_d = {}
            for j in o["deps"]:
                dd = ops[j]
                if dd["dma"]:
                    need_d[dd["dsem"]] = max(need_d.get(dd["dsem"], 0), dd["dval"])
                else:
                    need_e[dd["eng"]] = max(need_e.get(dd["eng"], 0), dd["eval"])
            if o["dma"] and o["dprev"] > 0:
                need_d[o["dsem"]] = max(need_d.get(o["dsem"], 0), o["dprev"])
            for f, v in need_e.items():
                if ewaited[e][f] < v:
                    ewaited[e][f] = v
                    waits.append(("e", f, v))
            for k, v in need_d.items():
                if dwaited[e][k] < v:
                    dwaited[e][k] = v
                    waits.append(("d", k, v))
            o["waits"] = waits
        final_d = list(dcount)
        self.stats = dict(n_ops=len(ops), ecount=dict(ecount), n_dma=nd,
                          per_eng={e: sum(1 for o in ops if o["eng"] == e) for e in ENGS},
                          af_max=self.af_max, ab_max=self.ab_max)

        def run_engine(ename):
            def body(eng):
                for o in ops:
                    if o["eng"] != ename:
                        continue
                    for kind, a, v in o["waits"]:
                        eng.wait_ge(esem[a] if kind == "e" else dsem[a], v)
                    ins = o["fn"](eng)
                    if o["dma"]:
                        ins.then_inc(dsem[o["dsem"]], 16)
                    elif o["sig"]:
                        ins.then_inc(esem[ename], 1)
                if ename == "sync":
                    for k in range(N_DMA_SEMS):
                        if final_d[k] > 0:
                            eng.wait_ge(dsem[k], final_d[k])
            return body

        with nc.Block() as block:
            block.tensor(run_engine("tensor"))
            block.vector(run_engine("vector"))
            block.scalar(run_engine("scalar"))
            block.gpsimd(run_engine("gpsimd"))
            block.sync(run_engine("sync"))


def _seg_consts(n, seg):
    idx = np.arange(n)
    same = (idx[:, None] // seg) == (idx[None, :] // seg)
    tri = (same & (idx[:, None] <= idx[None, :])).astype(np.float32)
    segones = same.astype(np.float32)
    nm_strict = np.where(same & (idx[None, :] > idx[:, None]), 0.0, NEG).astype(np.float32)
    nm_nonstrict = np.where(same & (idx[None, :] >= idx[:, None]), 0.0, NEG).astype(np.float32)
    pm_strict = np.where(same & (idx[:, None] > idx[None, :]), 0.0, -NEG).astype(np.float32)
    return tri, segones, nm_strict, nm_nonstrict, pm_strict


def make_consts(cfg):
    c = {}
    c["ident"] = np.eye(128, dtype=np.float32)
    tri, so, nms, nmn, pms = _seg_consts(64, 64)
    c["gp"] = np.stack([tri, so, nms, nmn, pms])
    tri, so, nms, nmn, pms = _seg_consts(64, 4)
    c["gs"] = np.stack([tri, so, nms, nmn, pms])
    tri, so, nms, nmn, pms = _seg_consts(128, 128)
    c["cp"] = np.stack([tri, so, nmn])
    c["segmask"] = (np.arange(64)[:, None] // 4 == np.arange(16)[None, :]).astype(np.float32)
    half = 16
    inv = (10000.0 ** (-np.arange(half, dtype=np.float32) / half)).astype(np.float32)

    def rt(pos):
        ang = pos.astype(np.float32)[:, None] * inv[None, :]
        cs, sn = np.cos(ang).astype(np.float32), np.sin(ang).astype(np.float32)
        return np.stack([np.concatenate([cs, cs], 1), np.concatenate([-sn, sn], 1)], 1).astype(np.float32)

    c["rope_p"] = rt(np.arange(cfg["SEQ"]))
    c["rope_s"] = rt(cfg["PAST"] + (np.arange(64) % 4))
    k = np.arange(128)
    m = np.where(k[:, None] > k[None, :], NEG, 0.0).astype(np.float32)
    c["mla_diag"] = np.tile(m, (1, 4))
    j = np.arange(64)
    mm = ((j[:, None] // 4 == j[None, :] // 4) & (j[:, None] % 4 <= j[None, :] % 4)).astype(np.float32)
    c["mla_new"] = np.tile(mm, (1, 4))
    c["ws_mask"] = np.triu(np.ones((128, 128), np.float32))
    jj = np.arange(64)
    c["wsbd_mask"] = ((jj[:, None] // 4 == jj[None, :] // 4) & (jj[:, None] % 4 <= jj[None, :] % 4)).astype(np.float32)
    rep = np.zeros((4, 64), np.float32)
    rep[np.arange(64) % 4, np.arange(64)] = 1.0
    c["rep4"] = rep
    return c


CONST_SHAPES = None


def build_program(cfg, dbg=(), upto=99):
    NPS, SEQ, NSS, PAST, NPHYS, TT = cfg["NPS"], cfg["SEQ"], cfg["NSS"], cfg["PAST"], cfg["NPHYS"], cfg["TT"]
    NPG = PAST // 128
    NTS = NSS * 4
    assert NTS == 64 and SEQ % TT == 0 and TT % 128 == 0
    nc = bass.Bass("TRN2", target_bir_lowering=False)
    P = Prog(nc, arena_f=16384, arena_b=20480)
    DI = lambda name, shape, dt=F32: nc.dram_tensor(name, list(shape), dt, kind="ExternalInput").ap()
    DO = lambda name, shape, dt=F32: nc.dram_tensor(name, list(shape), dt, kind="ExternalOutput").ap()
    DS = lambda name, shape, dt=BF16: nc.dram_tensor(name, list(shape), dt, kind="Internal").ap()

    xp_d = DI("xp", [NPS * SEQ, D])
    xs_d = DI("xs", [NTS, D])
    cckv_d = DI("cache_ckv", [2, NPHYS, 128, 128])
    ckpe_d = DI("cache_kpe", [2, NPHYS, 128, 32])
    ptab_d = DI("ptab", [1, NSS * NPG], I32)
    sgs_d = DI("state_gdn_s", [2, NSS, 4, 64, 64])
    sgc_d = DI("state_gdn_conv", [2, NSS * 3, 768])
    ssh_d = DI("state_ssm_h", [2, NSS * 4 * 64, 128])
    ssc_d = DI("state_ssm_conv", [2, NSS * 3, 768])
    W = {}
    for nm, shp in [("norm_g", [2, 6, D]), ("ffn_w_in", [2, 2, D, 2 * FF]), ("ffn_w_out", [2, 2, FF, D]),
                    ("w_in", [2, D, IN_COLS]), ("w_out", [2, D, D]), ("gdn_conv_w", [2, 4, 768]),
                    ("gdn_a_log", [2, 4]), ("gdn_dt_bias", [2, 4]), ("gdn_norm_g", [2, 64]),
                    ("mlp_ln_g", [2, 256]), ("mlp_ln_b", [2, 256]), ("mlp_ws", [2, 4, 128, 128]),
                    ("mlp_bs", [2, 4, 128]), ("ssm_conv_w", [2, 4, 768]), ("ssm_conv_b", [2, 768]),
                    ("ssm_a_log", [2, 4]), ("ssm_dt_bias", [2, 4]), ("ssm_d", [2, 4]), ("ssm_norm_g", [2, 256]),
                    ("mla_q_norm_g", [2, 256]), ("mla_w_uq", [2, 256, 384]), ("mla_kv_norm_g", [2, 128]),
                    ("mla_w_uk", [2, 4, 128, 64]), ("mla_w_uv", [2, 4, 128, 64])]:
        W[nm] = DI(nm, shp)
    cshapes = {k: v.shape for k, v in make_consts(cfg).items()}
    C = {k: DI("c_" + k, list(s)) for k, s in cshapes.items()}

    yp_d = DO("yp", [NPS * SEQ, D])
    ys_d = DO("ys", [NTS, D])
    ckvp_d = DO("ckv_p", [2, NPS * SEQ, 128])
    kpep_d = DO("kpe_p", [2, NPS * SEQ, 32])
    gsp_d = DO("gs_p", [2, NPS, 4, 64, 64])
    gcp_d = DO("gc_p", [2, NPS, 3, 768])
    shp_d = DO("sh_p", [2, NPS * 4 * 64, 128])
    scp_d = DO("sc_p", [2, NPS, 3, 768])
    ckvs_d = DO("ckv_s", [2, NTS, 128])
    kpes_d = DO("kpe_s", [2, NTS, 32])
    gss_d = DO("gs_s", [2, NSS, 4, 64, 64])
    gcs_d = DO("gc_s", [2, NSS, 3, 768])
    shs_d = DO("sh_s", [2, NSS * 4 * 64, 128])
    scs_d = DO("sc_s", [2, NSS, 3, 768])
    mvs_d = DO("mv_s", [2, NTS, 256])

    s_fin = DS("s_fin", [2, 2, D, 2 * FF])
    s_fout = DS("s_fout", [2, 2, FF, D])
    s_win = DS("s_win", [2, D, IN_COLS])
    s_wout = DS("s_wout", [2, D, D])
    for l in range(2):
        for i in range(2):
            for hh in range(2):
                P.dma(s_fin[l, i, hh * 512:(hh + 1) * 512, :], W["ffn_w_in"][l, i, hh * 512:(hh + 1) * 512, :], w=[("s_fin", l, i)])
            P.dma(s_fout[l, i], W["ffn_w_out"][l, i], w=[("s_fout", l, i)])
        P.dma(s_win[l], W["w_in"][l], w=[("s_win", l)])
        P.dma(s_wout[l], W["w_out"][l], w=[("s_wout", l)])

    ident = P.sb([128, 128], F32, "ident")
    identb = P.sb([128, 128], BF16, "identb")
    ones_bf = P.sb([128, 128], BF16, "ones_bf")
    eps_t = P.sb([128, 1], F32, "eps_t")
    gp = P.sb([64, 5, 64], F32, "gp")
    gs = P.sb([64, 5, 64], F32, "gsc")
    cp = P.sb([128, 3, 128], F32, "cp")
    segmask = P.sb([64, 16], F32, "segmask")
    rope_p = P.sb([128, SEQ // 128, 64], F32, "rope_p")
    rope_s = P.sb([64, 64], F32, "rope_s")
    mla_diag = P.sb([128, 512], BF16, "mla_diag")
    mla_new = P.sb([64, 256], F32, "mla_new")
    P.dma(ident.t, C["ident"], w=[ident.k])
    P.copy(identb.t, ident.t, r=[ident.k], w=[identb.k])
    P.memset(ones_bf.t, 1.0, w=[ones_bf.k])
    P.memset(eps_t.t, EPS, w=[eps_t.k])
    P.dma(gp.t, C["gp"].rearrange("a p q -> p a q"), w=[gp.k])
    P.dma(gs.t, C["gs"].rearrange("a p q -> p a q"), w=[gs.k])
    P.dma(cp.t, C["cp"].rearrange("a p q -> p a q"), w=[cp.k])
    P.dma(segmask.t, C["segmask"], w=[segmask.k])
    P.dma(rope_p.t, C["rope_p"].rearrange("(b p) a e -> p b (a e)", p=128), w=[rope_p.k])
    P.dma(rope_s.t, C["rope_s"].rearrange("p a e -> p (a e)"), w=[rope_s.k])
    P.dma(mla_diag.t, C["mla_diag"], w=[mla_diag.k])
    P.dma(mla_new.t, C["mla_new"], w=[mla_new.k])

    NC_ = dict(allow_slow_non_contiguous=True)
    ng = P.sb([128, 2, 6, 8], F32, "ng")
    P.dma(ng.t, W["norm_g"].rearrange("l i (c p) -> p l i c", p=128), w=[ng.k], **NC_)
    for l in range(2):
        for i in (1, 5):
            P.ts(ng.t[:, l, i, :], ng.t[:, l, i, :], 0.5, None, ALU.mult, r=[ng.k], w=[ng.k])
    wg = P.sb([128, 2, 8, 12], BF16, "wg")
    for l in range(2):
        P.dma(wg.t[:, l, :, 0:8], W["w_in"][l, :, C_GA:C_GA + 8].rearrange("(c p) n -> p c n", p=128), w=[wg.k])
        P.dma(wg.t[:, l, :, 8:12], W["w_in"][l, :, C_DT:C_DT + 4].rearrange("(c p) n -> p c n", p=128), w=[wg.k])
    cwA = P.sb([64, 2, 12, 4], F32, "cwA")
    cwC = P.sb([128, 2, 6, 4], F32, "cwC")
    for l in range(2):
        for k in range(4):
            P.dma(cwA.t[:, l, :, k], W["gdn_conv_w"][l, k].rearrange("(c p) -> p c", p=64), w=[cwA.k], **NC_)
            P.dma(cwC.t[:, l, :, k], W["ssm_conv_w"][l, k].rearrange("(c p) -> p c", p=128), w=[cwC.k], **NC_)
    cbC = P.sb([128, 2, 6], F32, "cbC")
    P.dma(cbC.t, W["ssm_conv_b"].rearrange("l (c p) -> p l c", p=128), w=[cbC.k], **NC_)
    gnA = P.sb([64, 2], F32, "gnA")
    P.dma(gnA.t, W["gdn_norm_g"].rearrange("l p -> p l"), w=[gnA.k], **NC_)
    gnC = P.sb([128, 2, 2], F32, "gnC")
    P.dma(gnC.t, W["ssm_norm_g"].rearrange("l (c p) -> p l c", p=128), w=[gnC.k], **NC_)
    dC = P.sb([128, 2, 2], F32, "dC")
    for l in range(2):
        for h in range(4):
            P.dma(dC.t[(h % 2) * 64:(h % 2) * 64 + 64, l, h // 2:h // 2 + 1], W["ssm_d"][l:l + 1, h:h + 1].to_broadcast([64, 1]), w=[dC.k])
    rows = P.sb([128, 2, 4, 4], F32, "rows")
    for l in range(2):
        for j, nm in enumerate(["gdn_dt_bias", "gdn_a_log", "ssm_dt_bias", "ssm_a_log"]):
            P.dma(rows.t[:, l, j, :], W[nm][l:l + 1, :].to_broadcast([128, 4]), w=[rows.k])
    for j in (1, 3):
        P.act(rows.t[:, :, j, :], rows.t[:, :, j, :], AF.Exp, r=[rows.k], w=[rows.k])
        P.ts(rows.t[:, :, j, :], rows.t[:, :, j, :], -1.0, None, ALU.mult, r=[rows.k], w=[rows.k])
    lnr = P.sb([128, 2, 2, 256], F32, "lnr")
    gqr = P.sb([128, 2, 256], F32, "gqr")
    gkr = P.sb([128, 2, 128], F32, "gkr")
    for l in range(2):
        P.dma(lnr.t[:, l, 0, :], W["mlp_ln_g"][l:l + 1, :].to_broadcast([128, 256]), w=[lnr.k])
        P.dma(lnr.t[:, l, 1, :], W["mlp_ln_b"][l:l + 1, :].to_broadcast([128, 256]), w=[lnr.k])
        P.dma(gqr.t[:, l, :], W["mla_q_norm_g"][l:l + 1, :].to_broadcast([128, 256]), w=[gqr.k])
        P.dma(gkr.t[:, l, :], W["mla_kv_norm_g"][l:l + 1, :].to_broadcast([128, 128]), w=[gkr.k])
    wuq = P.sb([128, 2, 2, 384], BF16, "wuq")
    wuv = P.sb([128, 2, 4, 64], BF16, "wuv")
    wukT = P.sb([64, 2, 4, 128], BF16, "wukT")
    for l in range(2):
        P.dma(wuq.t[:, l], W["mla_w_uq"][l].rearrange("(c p) n -> p c n", p=128), w=[wuq.k])
        P.dma(wuv.t[:, l], W["mla_w_uv"][l].rearrange("h r d -> r h d"), w=[wuv.k])
    bsP = P.sb([128, 2, 2, 128], F32, "bsP")
    bsS = P.sb([128, 2, 2, 16, 4], F32, "bsS")
    for l in range(2):
        for g in range(4):
            pr = slice((g % 2) * 64, (g % 2) * 64 + 64)
            P.dma(bsP.t[pr, l, g // 2, :], W["mlp_bs"][l, g:g + 1, :].to_broadcast([64, 128]), w=[bsP.k])
            P.dma(bsS.t[pr, l, g // 2, :, :], W["mlp_bs"][l, g:g + 1, 0:4].unsqueeze(1).to_broadcast([64, 16, 4]), w=[bsS.k])
    wsT = P.sb([128, 2, 4, 128], BF16, "wsT")
    wsbd = P.sb([64, 2, 4, 64], BF16, "wsbd")

    ptab_sb = P.sb([NPG, NSS], I32, "ptab_sb")
    P.dma(ptab_sb.t, ptab_d.rearrange("o (s p) -> p (o s)", p=NPG), w=[ptab_sb.k], **NC_)
    ptab4 = P.sb([NPG, NSS], I32, "ptab4")
    P.ts(ptab4.t, ptab_sb.t, 2, None, ALU.logical_shift_left, r=[ptab_sb.k], w=[ptab4.k])
    psb = [P.ps([128, 512], F32, f"psb{i}") for i in range(7)]
    psh = P.ps([128, 1024], BF16, "psh")

    m0 = P.mark()
    wsm = P.tmp([128, 128], F32)
    wbm = P.tmp([64, 64], F32)
    rep4 = P.tmp([4, 64], F32)
    P.dma(wsm.t, C["ws_mask"], w=[wsm.k])
    P.dma(wbm.t, C["wsbd_mask"], w=[wbm.k])
    P.dma(rep4.t, C["rep4"], w=[rep4.k])
    for l in range(2):
        uk = P.tmp([128, 4, 64], F32)
        P.dma(uk.t, W["mla_w_uk"][l].rearrange("h r d -> r h d"), w=[uk.k])
        for h in range(4):
            P.tr(psb[0].t[0:64, h * 128:(h + 1) * 128], uk.t[:, h, :], ident.t, r=[uk.k, ident.k], w=[psb[0].k])
        P.copy(wukT.t[:, l], psb[0].t[0:64, :].rearrange("p (h r) -> p h r", h=4), r=[psb[0].k], w=[wukT.k])
        wsn = P.tmp([128, 4, 128], F32)
        P.dma(wsn.t, W["mlp_ws"][l].rearrange("g p q -> p g q"), w=[wsn.k])
        for g in range(4):
            P.tr(psb[1].t[:, g * 128:(g + 1) * 128], wsn.t[:, g, :], ident.t, r=[wsn.k, ident.k], w=[psb[1].k])
        P.tt(wsT.t[:, l], psb[1].t[:].rearrange("q (g p) -> q g p", g=4), wsm.t.unsqueeze(1).to_broadcast([128, 4, 128]), ALU.mult,
             r=[psb[1].k, wsm.k], w=[wsT.k])
        w4 = P.tmp([4, 4, 4], F32)
        P.dma(w4.t, W["mlp_ws"][l, :, 0:4, 0:4].rearrange("g b a -> b g a"), w=[w4.k])
        y4 = P.tmp([4, 4, 64], F32)
        for g in range(4):
            P.mm(psb[2].t[0:4, g * 64:(g + 1) * 64], w4.t[:, g, :], rep4.t, r=[w4.k, rep4.k], w=[psb[2].k])
        P.copy(y4.t, psb[2].t[0:4, 0:256].rearrange("a (g p) -> a g p", g=4), r=[psb[2].k], w=[y4.k])
        for g in range(4):
            P.mm(psb[3].t[0:64, g * 64:(g + 1) * 64], rep4.t, y4.t[:, g, :], r=[y4.k, rep4.k], w=[psb[3].k])
        P.tt(wsbd.t[:, l], psb[3].t[0:64, 0:256].rearrange("q (g p) -> q g p", g=4), wbm.t.unsqueeze(1).to_broadcast([64, 4, 64]), ALU.mult,
             r=[psb[3].k, wbm.k], w=[wsbd.k])
    P.release(m0)

    xT = P.sb([128, 8, TT], F32, "xT")
    NBUF = 3
    wring = [P.sb([128, 4096], BF16, f"wring{i}") for i in range(NBUF)]
    wctr = [0]

    def wstage(loads):
        b = wring[wctr[0] % NBUF]
        wctr[0] += 1
        for dstf, src, stok in loads:
            P.dma(dstf(b.t), src, r=[stok], w=[b.k], eng="sync", persist=True)
        return b

    Sst = P.sb([64, 2, 4, 64], F32, "Sst")
    hTst = P.sb([128, 2, 4, 64], F32, "hTst")
    tailA = P.sb([64, 2, 12, 3], F32, "tailA")
    tailC = P.sb([128, 2, 6, 3], F32, "tailC")
    KT = P.sb([128, 2, SEQ], BF16, "KT")
    Kt = P.sb([128, 2, SEQ // 128, 128], BF16, "Kt")
    KPT = P.sb([32, 2, SEQ], BF16, "KPT")

    def bank(i):
        return psb[i % 7]

    def rstd_from(ps_ap, pk, shape, scale, nparts=128):
        r_ = P.tmp(shape, F32)
        P.act(r_.t, ps_ap, AF.Sqrt, scale=scale, bias=eps_t.t[0:nparts, 0:1], r=[pk, eps_t.k], w=[r_.k])
        P.recip(r_.t, r_.t, r=[r_.k], w=[r_.k])
        return r_

    def prenorm(l, i, NT):
        xn = P.tmp([128, 8, NT], BF16)
        sq = P.tmp([128, 8, NT], BF16)
        P.act(sq.t, xT.t[:, :, 0:NT], AF.Square, r=[xT.k], w=[sq.k])
        for c in range(8):
            P.mm(psb[0].t[:, 0:NT], ones_bf.t, sq.t[:, c, :], start=(c == 0), stop=(c == 7), r=[sq.k, ones_bf.k], w=[psb[0].k])
        rstd = rstd_from(psb[0].t[:, 0:NT], psb[0].k, [128, NT], 1.0 / D)
        for c in range(8):
            P.stt(xn.t[:, c, :], xT.t[:, c, 0:NT], ng.t[:, l, i, c:c + 1], rstd.t, ALU.mult, ALU.mult,
                  r=[xT.k, rstd.k, ng.k], w=[xn.k])
        return xn, sq

    def postnorm_residual(ysb, sq, l, i, NT):
        for c in range(8):
            P.mm(psb[0].t[:, 0:NT], ones_bf.t, sq.t[:, c, :], start=(c == 0), stop=(c == 7), r=[sq.k, ones_bf.k], w=[psb[0].k])
        rstd = rstd_from(psb[0].t[:, 0:NT], psb[0].k, [128, NT], 1.0 / D)
        for c in range(8):
            P.stt(ysb.t[:, c, :], ysb.t[:, c, :], ng.t[:, l, i, c:c + 1], rstd.t, ALU.mult, ALU.mult,
                  r=[ysb.k, rstd.k, ng.k], w=[ysb.k])
            P.tt(xT.t[:, c, 0:NT], xT.t[:, c, 0:NT], ysb.t[:, c, :], ALU.add, r=[xT.k, ysb.k], w=[xT.k])

    def ffn(l, i, NT, stop_at=None):
        m = P.mark()
        xn, sq = prenorm(l, 0 if i == 0 else 4, NT)
        hT = P.tmp([128, FC, NT], BF16)
        sgs_ = [P.tmp([128, NT], BF16) for _ in range(2)]
        fin = s_fin[l, i].rearrange("(c p) f -> p c f", p=128)
        for j in range(11):
            wb = wstage([
                (lambda b: b.rearrange("p (c f) -> p c f", c=8)[:, :, 0:256], fin[:, :, 256 * j:256 * j + 256], ("s_fin", l, i)),
                (lambda b: b.rearrange("p (c f) -> p c f", c=8)[:, :, 256:512], fin[:, :, FF + 256 * j:FF + 256 * j + 256], ("s_fin", l, i)),
            ])
            wv = wb.t.rearrange("p (c f) -> p c f", c=8)
            for f in range(2):
                fc = 2 * j + f
                pg, pu = psb[1 + 2 * (fc % 3)], psb[2 + 2 * (fc % 3)]
                for c in range(8):
                    P.mm(pg.t[:, 0:NT], wv[:, c, f * 128:(f + 1) * 128], xn.t[:, c, :], start=(c == 0), stop=(c == 7), r=[wb.k, xn.k], w=[pg.k])
                for c in range(8):
                    P.mm(pu.t[:, 0:NT], wv[:, c, 256 + f * 128:256 + (f + 1) * 128], xn.t[:, c, :], start=(c == 0), stop=(c == 7), r=[wb.k, xn.k], w=[pu.k])
                sg = sgs_[fc % 2]
                P.act(sg.t, pg.t[:, 0:NT], AF.Silu, r=[pg.k], w=[sg.k])
                P.tt(hT.t[:, fc, :], sg.t, pu.t[:, 0:NT], ALU.mult, r=[sg.k, pu.k], w=[(hT.k, fc)])
        if stop_at == "win":
            hf = P.tmp([128, 4, NT], F32)
            P.copy(hf.t, hT.t[:, 0:4, :], r=[(hT.k, c) for c in range(4)], w=[hf.k])
            P.dbg("hT", hf.t, r=[hf.k])
            return
        ysb = P.tmp([128, 8, NT], F32)
        fo = s_fout[l, i].rearrange("(c p) d -> p c d", p=128)
        for p_ in range(2):
            for s_ in range(6):
                nf = min(4, FC - 4 * s_)
                wb = wstage([(lambda b, nf=nf: b.rearrange("p (c d) -> p c d", c=8)[:, 0:nf, :],
                              fo[:, 4 * s_:4 * s_ + nf, 512 * p_:512 * p_ + 512], ("s_fout", l, i))])
                wv = wb.t.rearrange("p (c d) -> p c d", c=8)
                for dcl in range(4):
                    for f in range(nf):
                        fc = 4 * s_ + f
                        P.mm(psb[1 + dcl].t[:, 0:NT], wv[:, f, dcl * 128:(dcl + 1) * 128], hT.t[:, fc, :], start=(fc == 0), stop=(fc == FC - 1),
                             r=[wb.k, (hT.k, fc)], w=[psb[1 + dcl].k])
            for dcl in range(4):
                c = 4 * p_ + dcl
                P.copy(ysb.t[:, c, :], psb[1 + dcl].t[:, 0:NT], r=[psb[1 + dcl].k], w=[ysb.k])
                P.act(sq.t[:, c, :], psb[1 + dcl].t[:, 0:NT], AF.Square, r=[psb[1 + dcl].k], w=[sq.k])
        if stop_at == "wout":
            P.dbg("ysb", ysb.t, r=[ysb.k])
            return
        postnorm_residual(ysb, sq, l, 1 if i == 0 else 5, NT)
        P.release(m)

    def load_x(src_rows, NT):
        m = P.mark()
        nb = min(128, NT)
        for b in range(NT // nb):
            xt = P.tmp([nb, D], F32)
            P.dma(xt.t, src_rows[b * nb:(b + 1) * nb, :], w=[xt.k])
            for cg in range(2):
                pb = psb[1 + (2 * b + cg) % 6]
                for c4 in range(4):
                    c = cg * 4 + c4
                    P.tr(pb.t[:, c4 * nb:(c4 + 1) * nb], xt.t[:, c * 128:(c + 1) * 128], ident.t[0:nb, 0:nb], r=[xt.k, ident.k], w=[pb.k])
                P.copy(xT.t[:, cg * 4:(cg + 1) * 4, b * nb:(b + 1) * nb], pb.t[:, 0:4 * nb].rearrange("p (c t) -> p c t", c=4),
                       r=[pb.k], w=[xT.k], eng=("scalar" if cg else "vector"))
        P.release(m)

    def store_x(dst_rows, NT):
        m = P.mark()
        nb = min(128, NT)
        for b in range(NT // nb):
            yt = P.tmp([nb, D], F32)
            for cg in range(2):
                pb = psb[1 + (2 * b + cg) % 6]
                for c4 in range(4):
                    c = cg * 4 + c4
                    P.tr(pb.t[0:nb, c4 * 128:(c4 + 1) * 128], xT.t[:, c, b * nb:(b + 1) * nb], ident.t, r=[xT.k, ident.k], w=[pb.k])
                P.copy(yt.t[:, cg * 512:(cg + 1) * 512], pb.t[0:nb, :], r=[pb.k], w=[yt.k], eng=("scalar" if cg else "vector"))
            P.dma(dst_rows[b * nb:(b + 1) * nb, :], yt.t, r=[yt.k])
        P.release(m)


    def softplus(x, shape, npart):
        a_ = P.tmp(shape, F32)
        P.stt(a_.t, x.t, -1.0, x.t, ALU.mult, ALU.max, r=[x.k], w=[a_.k])
        P.act(a_.t, a_.t, AF.Exp, scale=-1.0, r=[a_.k], w=[a_.k])
        P.act(a_.t, a_.t, AF.Ln, bias=1.0, r=[a_.k], w=[a_.k])
        P.ts(x.t, x.t, 0.0, None, ALU.max, r=[x.k], w=[x.k])
        P.tt(x.t, x.t, a_.t, ALU.add, r=[x.k, a_.k], w=[x.k])

    def proj_fm(dst_fn, wv, wk, col0, nch, m, xn, NT, bstart, evac):
        for j in range(nch):
            pb = bank(1 + (bstart + j) % 6)
            for c in range(8):
                P.mm(pb.t[0:m, 0:NT], wv[:, c, col0 + j * m:col0 + (j + 1) * m], xn.t[:, c, :], start=(c == 0), stop=(c == 7), r=[wk, xn.k], w=[pb.k])
            evac(j, pb.t[0:m, 0:NT], pb)

    def mixer(l, T):
        NT, kind = T["NT"], T["kind"]
        prm = kind == "p"
        m_all = P.mark()
        xn, sqn = prenorm(l, 2, NT)
        win = s_win[l].rearrange("(c p) n -> p c n", p=128)
        sw = ("s_win", l)
        as3 = lambda b_: b_.rearrange("p (c f) -> p c f", c=8)
        mixT = P.tmp([128, 6, NT], BF16)
        outA = P.tmp([64, 4, NT], BF16)
        NB64 = NT // 64
        nb = 128 if prm else 64
        NBk = NT // nb
        NSEG = 1 if prm else 16
        GC = gp if prm else gs
        i64 = ident.t[0:64, 0:64]
        W7 = 3 + NT if prm else 16 * 7

        def xview(xp_, j, i):
            if prm:
                return xp_.t[:, j, i:i + NT]
            return xp_.t[:, j, :].rearrange("p (s t) -> p s t", t=7)[:, :, i:i + 4]

        def oview(ap2):
            return ap2 if prm else ap2.rearrange("p (s t) -> p s t", t=4)

        mA = P.mark()
        qkv = P.tmp([64, 12, NT], F32)
        szA = P.tmp([64, 4, NT], BF16)
        wb1 = wstage([(lambda b_: as3(b_), win[:, :, 0:512], sw)])
        wb2 = wstage([(lambda b_: as3(b_), win[:, :, 512:1024], sw)])
        xp4 = P.tmp([64, 4, W7], F32)
        rrs = [P.tmp([64, NT], F32) for _ in range(2)]
        if not prm:
            rawA = P.tmp([64, 12, 16, 3], F32)
        for grp in range(3):
            wb_, c0 = (wb1, grp * 256) if grp < 2 else (wb2, 0)
            if prm:
                P.copy(xp4.t[:, :, 0:3], tailA.t[:, l, grp * 4:(grp + 1) * 4, :], r=[tailA.k], w=[xp4.k], eng="gpsimd")
            else:
                P.copy(xp4.t.rearrange("p j (s t) -> p j s t", t=7)[:, :, :, 0:3], T["tailA"].t[:, grp * 4:(grp + 1) * 4, :, :], r=[T["tailA"].k], w=[xp4.k], eng="gpsimd")

            def ev(j, ps_ap, pb, xp4=xp4):
                P.copy(xview(xp4, j, 3), oview(ps_ap), r=[pb.k], w=[xp4.k], eng="scalar")
            proj_fm(None, as3(wb_.t), wb_.k, c0, 4, 64, xn, NT, grp * 4, ev)
            for j in range(4):
                hc = grp * 4 + j
                ov = oview(qkv.t[:, hc, :])
                if prm:
                    P.ts(ov, xview(xp4, j, 0), cwA.t[:, l, hc, 0:1], None, ALU.mult, r=[xp4.k, cwA.k], w=[qkv.k])
                else:
                    P.tt(ov, xview(xp4, j, 0), cwA.t[:, l, hc, 0:1].unsqueeze(2).to_broadcast([64, 16, 4]), ALU.mult, r=[xp4.k, cwA.k], w=[qkv.k])
                for i in range(1, 4):
                    P.stt(ov, xview(xp4, j, i), cwA.t[:, l, hc, i:i + 1], ov, ALU.mult, ALU.add, r=[xp4.k, cwA.k, qkv.k], w=[qkv.k])
            if prm:
                P.copy(tailA.t[:, l, grp * 4:(grp + 1) * 4, :], xp4.t[:, :, NT:NT + 3], r=[xp4.k], w=[tailA.k], eng="gpsimd")
            else:
                P.copy(rawA.t[:, grp * 4:(grp + 1) * 4], xp4.t.rearrange("p j (s t) -> p j s t", t=7)[:, :, :, 4:7], r=[xp4.k], w=[rawA.k], eng="gpsimd")
        P.act(qkv.t, qkv.t, AF.Silu, r=[qkv.k], w=[qkv.k])
        if not prm:
            rowA = P.tmp([48, 768], F32)
            for hb in range(2):
                pb = bank(1 + hb)
                for j in range(6):
                    hc = hb * 6 + j
                    P.tr(pb.t[0:48, j * 64:(j + 1) * 64], rawA.t[:, hc].rearrange("p s t -> p (s t)"), i64, r=[rawA.k, ident.k], w=[pb.k])
                P.copy(rowA.t[:, hb * 384:(hb + 1) * 384], pb.t[0:48, 0:384], r=[pb.k], w=[rowA.k])
            P.dma(gcs_d[l].rearrange("s t c -> (s t) c"), rowA.t, r=[rowA.k])

        def evz(j, ps_ap, pb):
            P.act(szA.t[:, j, :], ps_ap, AF.Silu, r=[pb.k], w=[szA.k])
        proj_fm(None, as3(wb2.t), wb2.k, 256, 4, 64, xn, NT, 0, evz)
        sqk = Tl(sqn.t[0:64], sqn.k)
        P.act(sqk.t, qkv.t[:, 0:8, :], AF.Square, r=[qkv.k], w=[sqk.k])
        for hc in range(8):
            pb = bank(1 + hc % 6)
            P.mm(pb.t[0:64, 0:NT], ones_bf.t[0:64, 0:64], sqk.t[:, hc, :], r=[sqk.k, ones_bf.k], w=[pb.k])
            rr = rrs[hc % 2]
            P.act(rr.t, pb.t[0:64, 0:NT], AF.Sqrt, bias=eps_t.t[0:64, 0:1], r=[pb.k, eps_t.k], w=[rr.k])
            P.recip(rr.t, rr.t, r=[rr.k], w=[rr.k])
            P.stt(qkv.t[:, hc, :], qkv.t[:, hc, :], (0.125 if hc < 4 else 1.0), rr.t, ALU.mult, ALU.mult, r=[qkv.k, rr.k], w=[qkv.k])
        pgt = bank(0)
        for b_ in range(NB64):
            for c in range(8):
                P.mm(pgt.t[0:64, b_ * 8:(b_ + 1) * 8], xn.t[:, c, b_ * 64:(b_ + 1) * 64], wg.t[:, l, c, 0:8], start=(c == 0), stop=(c == 7), r=[xn.k, wg.k], w=[pgt.k])
        gx = P.tmp([64, NB64, 8], F32)
        pg3 = pgt.t[0:64, 0:NB64 * 8].rearrange("p (b n) -> p b n", n=8)
        P.tt(gx.t[:, :, 0:4], pg3[:, :, 0:4], rows.t[0:64, l, 0, :].unsqueeze(1).to_broadcast([64, NB64, 4]), ALU.add, r=[pgt.k, rows.k], w=[gx.k])
        P.ts(gx.t[:, :, 4:8], pg3[:, :, 4:8], -1.0, None, ALU.mult, r=[pgt.k], w=[gx.k])
        softplus(gx, [64, NB64, 8], 64)
        la = P.tmp([64, NB64, 4], F32)
        lnb = P.tmp([64, NB64, 4], F32)
        P.tt(la.t, gx.t[:, :, 0:4], rows.t[0:64, l, 1, :].unsqueeze(1).to_broadcast([64, NB64, 4]), ALU.mult, r=[gx.k, rows.k], w=[la.k])
        P.ts(lnb.t, gx.t[:, :, 4:8], -1.0, None, ALU.mult, r=[gx.k], w=[lnb.k])
        pc = bank(0)
        for b_ in range(NB64):
            P.mm(pc.t[0:64, b_ * 8:b_ * 8 + 4], GC.t[:, 0, :], la.t[:, b_, :], r=[GC.k, la.k], w=[pc.k])
            P.mm(pc.t[0:64, b_ * 8 + 4:b_ * 8 + 8], GC.t[:, 1, :], la.t[:, b_, :], r=[GC.k, la.k], w=[pc.k])
        gcm = P.tmp([64, NB64, 8], F32)
        P.copy(gcm.t, pc.t[0:64, 0:NB64 * 8].rearrange("p (b n) -> p b n", n=8), r=[pc.k], w=[gcm.k])
        negg = P.tmp([64, NB64, 4], F32)
        gpl = P.tmp([64, NB64, 4], F32)
        sc3 = P.tmp([64, NB64, 12], F32)
        P.ts(negg.t, gcm.t[:, :, 0:4], -1.0, None, ALU.mult, r=[gcm.k], w=[negg.k])
        P.tt(gpl.t, gcm.t[:, :, 0:4], lnb.t, ALU.add, r=[gcm.k, lnb.k], w=[gpl.k])
        P.copy(sc3.t[:, :, 0:4], gpl.t, r=[gpl.k], w=[sc3.k])
        P.copy(sc3.t[:, :, 4:8], lnb.t, r=[lnb.k], w=[sc3.k])
        P.tt(sc3.t[:, :, 8:12], gcm.t[:, :, 4:8], gcm.t[:, :, 0:4], ALU.subtract, r=[gcm.k], w=[sc3.k])
        P.act(sc3.t, sc3.t, AF.Exp, r=[sc3.k], w=[sc3.k])
        oT = P.tmp([64, 4, NT], F32)
        if prm:
            Sv = lambda sg_, h: Sst.t[:, l, h, :]
            Sk = Sst.k
        else:
            S0 = T["S0"]
            Sv = lambda sg_, h: S0.t[:, sg_, h, :]
            Sk = S0.k
        segw = 64 // NSEG
        n_lev = 5 if prm else 1
        for b_ in range(NB64):
            mB = P.mark()
            cs = slice(b_ * 64, (b_ + 1) * 64)
            bc = lambda t_, j: t_.t[:, b_, j:j + 1].to_broadcast([64, 64])
            pA, pB2, pC, pD = bank(1), bank(2), bank(3), bank(4)
            for h in range(4):
                hs = slice(h * 64, (h + 1) * 64)
                P.mm(pA.t[0:64, hs], bc(la, h), GC.t[:, 0, :], start=True, stop=False, r=[la.k, GC.k], w=[pA.k])
                P.mm(pA.t[0:64, hs], bc(lnb, h), i64, start=False, stop=False, r=[lnb.k, ident.k], w=[pA.k])
                P.mm(pA.t[0:64, hs], i64, GC.t[:, 2, :], start=False, stop=True, r=[GC.k, ident.k], w=[pA.k])
                P.mm(pB2.t[0:64, hs], bc(la, h), GC.t[:, 0, :], start=True, stop=False, r=[la.k, GC.k], w=[pB2.k])
                P.mm(pB2.t[0:64, hs], i64, GC.t[:, 3, :], start=False, stop=True, r=[GC.k, ident.k], w=[pB2.k])
                P.mm(pC.t[0:64, hs], bc(la, h), GC.t[:, 0, :], start=True, stop=False, r=[la.k, GC.k], w=[pC.k])
                P.mm(pC.t[0:64, hs], i64, GC.t[:, 4, :], start=False, stop=True, r=[GC.k, ident.k], w=[pC.k])
                P.mm(pD.t[0:64, hs], bc(la, h), GC.t[:, 0, :], r=[la.k, GC.k], w=[pD.k])
            slot = [P.tmp([64, 4, 64], F32) for _ in range(10)]
            E1, E2, E1T, eG = slot[0:4]
            for h in range(4):
                hs = slice(h * 64, (h + 1) * 64)
                P.act(E1.t[:, h, :], pA.t[0:64, hs], AF.Exp, bias=negg.t[:, b_, h:h + 1], r=[pA.k, negg.k], w=[E1.k])
                P.act(E2.t[:, h, :], pB2.t[0:64, hs], AF.Exp, bias=negg.t[:, b_, h:h + 1], r=[pB2.k, negg.k], w=[E2.k])
                P.act(E1T.t[:, h, :], pC.t[0:64, hs], AF.Exp, scale=-1.0, bias=gpl.t[:, b_, h:h + 1], r=[pC.k, gpl.k], w=[E1T.k])
            P.act(eG.t, pD.t[0:64, 0:256].rearrange("p (h q) -> p h q", h=4), AF.Exp, r=[pD.k], w=[eG.k])
            pG, pQ = bank(5), bank(6)
            for h in range(4):
                hs = slice(h * 64, (h + 1) * 64)
                P.mm(pG.t[0:64, hs], qkv.t[:, 4 + h, cs], qkv.t[:, 4 + h, cs], r=[qkv.k], w=[pG.k])
                P.mm(pQ.t[0:64, hs], qkv.t[:, 4 + h, cs], qkv.t[:, h, cs], r=[qkv.k], w=[pQ.k])
            Pm, Qm, QKd = slot[4:7]
            v4 = lambda pb_: pb_.t[0:64, 0:256].rearrange("p (h q) -> p h q", h=4)
            P.tt(Pm.t, v4(pG), E1.t, ALU.mult, r=[pG.k, E1.k], w=[Pm.k])
            P.tt(Qm.t, v4(pG), E1T.t, ALU.mult, r=[pG.k, E1T.k], w=[Qm.k])
            P.tt(QKd.t, v4(pQ), E2.t, ALU.mult, r=[pQ.k, E2.k], w=[QKd.k])
            Rm = slot[7]
            pq = [(slot[8], slot[9]), (slot[4], slot[5])]
            P.stt(Rm.t, Pm.t, -1.0, i64.unsqueeze(1).to_broadcast([64, 4, 64]), ALU.mult, ALU.add, r=[Pm.k, ident.k], w=[Rm.k])
            for lev in range(n_lev):
                last = lev == n_lev - 1
                pP, pQn, pR = bank(1), bank(2), bank(3)
                Pn, Qn = pq[lev % 2]
                for h in range(4):
                    hs = slice(h * 64, (h + 1) * 64)
                    if not last:
                        P.mm(pP.t[0:64, hs], Qm.t[:, h, :], Pm.t[:, h, :], r=[Qm.k, Pm.k], w=[pP.k])
                    P.mm(pQn.t[0:64, hs], Pm.t[:, h, :], Qm.t[:, h, :], r=[Qm.k, Pm.k], w=[pQn.k])
                if not last:
                    P.copy(Pn.t, v4(pP), r=[pP.k], w=[Pn.k], eng="scalar")
                P.copy(Qn.t, v4(pQn), r=[pQn.k], w=[Qn.k])
                for h in range(4):
                    hs = slice(h * 64, (h + 1) * 64)
                    P.mm(pR.t[0:64, hs], Qn.t[:, h, :], Rm.t[:, h, :], r=[Qn.k, Rm.k], w=[pR.k])
                P.tt(Rm.t, Rm.t, v4(pR), ALU.add, r=[Rm.k, pR.k], w=[Rm.k])
                Pm, Qm = Pn, Qn
            pK, pV = bank(4), bank(5)
            for h in range(4):
                hs = slice(h * 64, (h + 1) * 64)
                P.tr(pK.t[0:64, hs], qkv.t[:, 4 + h, cs], i64, r=[qkv.k, ident.k], w=[pK.k])
                P.tr(pV.t[0:64, hs], qkv.t[:, 8 + h, cs], i64, r=[qkv.k, ident.k], w=[pV.k])
            Yk, Yv, kd = slot[0], slot[1], slot[2]
            scb = lambda j0: sc3.t[:, b_, j0:j0 + 4].unsqueeze(2).to_broadcast([64, 4, 64])
            P.tt(Yk.t, v4(pK), scb(0), ALU.mult, r=[pK.k, sc3.k], w=[Yk.k])
            P.tt(kd.t, v4(pK), scb(8), ALU.mult, r=[pK.k, sc3.k], w=[kd.k])
            P.tt(Yv.t, v4(pV), scb(4), ALU.mult, r=[pV.k, sc3.k], w=[Yv.k])
            pW = bank(6)
            for h in range(4):
                hs = slice(h * 64, (h + 1) * 64)
                P.mm(pW.t[0:64, hs], Yk.t[:, h, :], Rm.t[:, h, :], r=[Yk.k, Rm.k], w=[pW.k])
            nWT, qdT, unT, un = slot[4], slot[5], slot[8], slot[9]
            P.ts(nWT.t, v4(pW), -1.0, None, ALU.mult, r=[pW.k], w=[nWT.k])
            P.tt(qdT.t, qkv.t[:, 0:4, cs], eG.t, ALU.mult, r=[qkv.k, eG.k], w=[qdT.k])
            pU = bank(1)
            for h in range(4):
                hs = slice(h * 64, (h + 1) * 64)
                P.mm(pU.t[0:64, hs], Yv.t[:, h, :], Rm.t[:, h, :], start=True, stop=False, r=[Yv.k, Rm.k], w=[pU.k])
                for sg_ in range(NSEG):
                    P.mm(pU.t[0:64, h * 64 + sg_ * segw:h * 64 + (sg_ + 1) * segw], Sv(sg_, h), nWT.t[:, h, sg_ * segw:(sg_ + 1) * segw],
                         start=False, stop=(sg_ == NSEG - 1), r=[Sk, nWT.k], w=[pU.k])
            P.copy(unT.t, v4(pU), r=[pU.k], w=[unT.k], eng="scalar")
            pUt = bank(2)
            for h in range(4):
                hs = slice(h * 64, (h + 1) * 64)
                P.tr(pUt.t[0:64, hs], unT.t[:, h, :], i64, r=[unT.k, ident.k], w=[pUt.k])
            P.copy(un.t, v4(pUt), r=[pUt.k], w=[un.k])
            pO = bank(3)
            for h in range(4):
                hs = slice(h * 64, (h + 1) * 64)
                P.mm(pO.t[0:64, hs], un.t[:, h, :], QKd.t[:, h, :], start=True, stop=False, r=[un.k, QKd.k], w=[pO.k])
                for sg_ in range(NSEG):
                    P.mm(pO.t[0:64, h * 64 + sg_ * segw:h * 64 + (sg_ + 1) * segw], Sv(sg_, h), qdT.t[:, h, sg_ * segw:(sg_ + 1) * segw],
                         start=False, stop=(sg_ == NSEG - 1), r=[Sk, qdT.k], w=[pO.k])
            P.copy(oT.t[:, :, cs], v4(pO), r=[pO.k], w=[oT.k], eng="scalar")
            if prm:
                pS = bank(4)
                for h in range(4):
                    hs = slice(h * 64, (h + 1) * 64)
                    P.mm(pS.t[0:64, hs], kd.t[:, h, :], un.t[:, h, :], r=[kd.k, un.k], w=[pS.k])
                P.tt(Sst.t[:, l], Sst.t[:, l], eG.t[:, :, 63:64].to_broadcast([64, 4, 64]), ALU.mult, r=[Sst.k, eG.k], w=[Sst.k])
                P.tt(Sst.t[:, l], Sst.t[:, l], v4(pS), ALU.add, r=[Sst.k, pS.k], w=[Sst.k])
            else:
                Sn = T["Sn"]
                kdm = P.tmp([64, 16, 64], F32)
                for h in range(4):
                    P.tt(kdm.t, kd.t[:, h, :].unsqueeze(1).to_broadcast([64, 16, 64]), segmask.t.unsqueeze(2).to_broadcast([64, 16, 64]), ALU.mult,
                         r=[kd.k, segmask.k], w=[kdm.k])
                    for half in range(2):
                        pS = bank(4 + half)
                        for s8 in range(8):
                            sg_ = half * 8 + s8
                            P.mm(pS.t[0:64, s8 * 64:(s8 + 1) * 64], kdm.t[:, sg_, :], un.t[:, h, :], r=[kdm.k, un.k], w=[pS.k])
                        ss_ = slice(half * 8, half * 8 + 8)
                        alb = eG.t[:, h, :].rearrange("p (s t) -> p s t", t=4)[:, ss_, 3:4].to_broadcast([64, 8, 64])
                        P.tt(Sn.t[:, ss_, h, :], S0.t[:, ss_, h, :], alb, ALU.mult, r=[S0.k, eG.k], w=[Sn.k])
                        P.tt(Sn.t[:, ss_, h, :], Sn.t[:, ss_, h, :], pS.t[0:64, :].rearrange("p (s v) -> p s v", v=64), ALU.add, r=[Sn.k, pS.k], w=[Sn.k])
            P.release(mB)
        sqo = Tl(sqn.t[0:64, 0:4], sqn.k)
        P.act(sqo.t, oT.t, AF.Square, r=[oT.k], w=[sqo.k])
        for h in range(4):
            pb = bank(1 + h)
            P.mm(pb.t[0:64, 0:NT], ones_bf.t[0:64, 0:64], sqo.t[:, h, :], r=[sqo.k, ones_bf.k], w=[pb.k])
            rr = rrs[h % 2]
            P.act(rr.t, pb.t[0:64, 0:NT], AF.Sqrt, scale=1.0 / 64, bias=eps_t.t[0:64, 0:1], r=[pb.k, eps_t.k], w=[rr.k])
            P.recip(rr.t, rr.t, r=[rr.k], w=[rr.k])
            P.stt(oT.t[:, h, :], oT.t[:, h, :], gnA.t[:, l:l + 1], rr.t, ALU.mult, ALU.mult, r=[oT.k, rr.k, gnA.k], w=[oT.k])
            P.tt(outA.t[:, h, :], oT.t[:, h, :], szA.t[:, h, :], ALU.mult, r=[oT.k, szA.k], w=[outA.k])
        if "outA" in dbg:
            of = P.tmp([64, 4, NT], F32)
            P.copy(of.t, outA.t, r=[outA.k], w=[of.k])
            P.dbg(f"outA_{kind}{l}", of.t, r=[of.k])
        P.release(mA)
        return dict(l=l, T=T, NT=NT, prm=prm, kind=kind, xn=xn, sqn=sqn, mixT=mixT, outA=outA, m_all=m_all, win=win, sw=sw, as3=as3, nb=nb, NBk=NBk, NSEG=NSEG, W7=W7)


    def mixB(cx):
        l, NT, prm, xn, mixT, nb, NBk, as3, kind = cx["l"], cx["NT"], cx["prm"], cx["xn"], cx["mixT"], cx["nb"], cx["NBk"], cx["as3"], cx["kind"]
        m = P.mark()
        wb = wstage([(lambda b_: as3(b_), cx["win"][:, :, C_U:C_U + 512], cx["sw"])])
        wv = as3(wb.t)
        uT = P.tmp([128, 2, NT], BF16)

        def evu(j, ps_ap, pb):
            P.act(uT.t[:, j, :], ps_ap, AF.Gelu_apprx_tanh, r=[pb.k], w=[uT.k])
        proj_fm(None, wv, wb.k, 0, 2, 128, xn, NT, 0, evu)
        for b_ in range(NBk):
            cs = slice(b_ * nb, (b_ + 1) * nb)
            pv = bank(3 + b_ % 2)
            for c in range(8):
                P.mm(pv.t[0:nb, 0:256], xn.t[:, c, cs], wv[:, c, 256:512], start=(c == 0), stop=(c == 7), r=[xn.k, wb.k], w=[pv.k])
            vg = P.tmp([nb, 256], F32)
            st_ = P.tmp([nb, 4], F32)
            P.memset(st_.t, 0.0, w=[st_.k])
            P.act(vg.t, pv.t[0:nb, 0:256], AF.Gelu_apprx_tanh, accum_out=st_.t[:, 0:1], r=[pv.k, st_.k], w=[vg.k, st_.k])
            P.ts(st_.t[:, 1:2], st_.t[:, 0:1], -1.0 / 256, None, ALU.mult, r=[st_.k], w=[st_.k])
            junk = P.tmp([nb, 256], F32)
            P.act(junk.t, vg.t, AF.Square, bias=st_.t[:, 1:2], accum_out=st_.t[:, 2:3], r=[vg.k, st_.k], w=[junk.k, st_.k])
            P.act(st_.t[:, 3:4], st_.t[:, 2:3], AF.Sqrt, scale=1.0 / 256, bias=eps_t.t[0:nb, 0:1], r=[st_.k, eps_t.k], w=[st_.k])
            P.recip(st_.t[:, 3:4], st_.t[:, 3:4], r=[st_.k], w=[st_.k])
            P.ts(vg.t, vg.t, st_.t[:, 1:2], st_.t[:, 3:4], ALU.add, ALU.mult, r=[vg.k, st_.k], w=[vg.k])
            P.tt(vg.t, vg.t, lnr.t[0:nb, l, 0, :], ALU.mult, r=[vg.k, lnr.k], w=[vg.k])
            P.tt(vg.t, vg.t, lnr.t[0:nb, l, 1, :], ALU.add, r=[vg.k, lnr.k], w=[vg.k])
            if not prm:
                P.dma(mvs_d[l], vg.t, r=[vg.k])
            vb = P.tmp([nb, 256], BF16)
            P.copy(vb.t, vg.t, r=[vg.k], w=[vb.k])
            pm = bank(5 + b_ % 2)
            for g in range(4):
                wmat = wsT.t[:, l, g, :] if prm else wsbd.t[:, l, g, :]
                P.mm(pm.t[(g % 2) * 64:(g % 2) * 64 + 64, (g // 2) * nb:(g // 2 + 1) * nb], vb.t[:, g * 64:(g + 1) * 64], wmat, r=[vb.k, wsT.k, wsbd.k], w=[pm.k])
            for ch in range(2):
                bsv = bsP.t[:, l, ch, :] if prm else bsS.t[:, l, ch, :, :].rearrange("p s t -> p (s t)")
                tmpb = P.tmp([128, nb], F32)
                P.tt(tmpb.t, pm.t[:, ch * nb:(ch + 1) * nb], bsv, ALU.add, r=[pm.k, bsP.k, bsS.k], w=[tmpb.k])
                P.tt(mixT.t[:, ch, cs], tmpb.t, uT.t[:, ch, cs], ALU.mult, r=[tmpb.k, uT.k], w=[mixT.k])
        if "mixB" in dbg:
            of = P.tmp([128, 2, NT], F32)
            P.copy(of.t, mixT.t[:, 0:2, :], r=[mixT.k], w=[of.k])
            P.dbg(f"mixB_{kind}{l}", of.t, r=[of.k])
        P.release(m)

    def mixC(cx):
        l, T, NT, prm, xn, mixT, nb, NBk, as3, kind, NSEG, W7 = (cx[k] for k in ["l", "T", "NT", "prm", "xn", "mixT", "nb", "NBk", "as3", "kind", "NSEG", "W7"])
        m = P.mark()
        wb1 = wstage([(lambda b_: as3(b_), cx["win"][:, :, C_ZC:C_ZC + 512], cx["sw"])])
        wb2 = wstage([(lambda b_: as3(b_), cx["win"][:, :, C_ZC + 512:C_ZC + 1024], cx["sw"])])
        szC = P.tmp([128, 2, NT], BF16)
        xbc = P.tmp([128, 6, NT], F32)
        xpc = P.tmp([128, 6, W7], F32)

        def xv(j, i):
            if prm:
                return xpc.t[:, j, i:i + NT]
            return xpc.t[:, j, :].rearrange("p (s t) -> p s t", t=7)[:, :, i:i + 4]
        ov_ = lambda ap2: ap2 if prm else ap2.rearrange("p (s t) -> p s t", t=4)

        def evz(j, ps_ap, pb):
            P.act(szC.t[:, j, :], ps_ap, AF.Silu, r=[pb.k], w=[szC.k])
        proj_fm(None, as3(wb1.t), wb1.k, 0, 2, 128, xn, NT, 0, evz)
        if prm:
            P.copy(xpc.t[:, :, 0:3], tailC.t[:, l], r=[tailC.k], w=[xpc.k], eng="gpsimd")
        else:
            P.copy(xpc.t.rearrange("p j (s t) -> p j s t", t=7)[:, :, :, 0:3], T["tailC"].t, r=[T["tailC"].k], w=[xpc.k], eng="gpsimd")

        def evx(j0):
            def f(j, ps_ap, pb):
                P.copy(xv(j0 + j, 3), ov_(ps_ap), r=[pb.k], w=[xpc.k], eng="scalar")
            return f
        proj_fm(None, as3(wb1.t), wb1.k, 256, 2, 128, xn, NT, 2, evx(0))
        proj_fm(None, as3(wb2.t), wb2.k, 0, 4, 128, xn, NT, 4, evx(2))
        for j in range(6):
            ov = ov_(xbc.t[:, j, :])
            if prm:
                P.ts(ov, xv(j, 0), cwC.t[:, l, j, 0:1], None, ALU.mult, r=[xpc.k, cwC.k], w=[xbc.k])
            else:
                P.tt(ov, xv(j, 0), cwC.t[:, l, j, 0:1].unsqueeze(2).to_broadcast([128, 16, 4]), ALU.mult, r=[xpc.k, cwC.k], w=[xbc.k])
            for i in range(1, 4):
                P.stt(ov, xv(j, i), cwC.t[:, l, j, i:i + 1], ov, ALU.mult, ALU.add, r=[xpc.k, cwC.k, xbc.k], w=[xbc.k])
            P.act(xbc.t[:, j, :], xbc.t[:, j, :], AF.Silu, bias=cbC.t[:, l, j:j + 1], r=[xbc.k, cbC.k], w=[xbc.k])
        if prm:
            P.copy(tailC.t[:, l], xpc.t[:, :, NT:NT + 3], r=[xpc.k], w=[tailC.k], eng="gpsimd")
        else:
            rawC = P.tmp([128, 6, 16, 3], F32)
            P.copy(rawC.t, xpc.t.rearrange("p j (s t) -> p j s t", t=7)[:, :, :, 4:7], r=[xpc.k], w=[rawC.k], eng="gpsimd")
            rowC = P.tmp([48, 768], F32)
            for hb in range(2):
                pb = bank(1 + hb)
                for j in range(3):
                    P.tr(pb.t[0:48, j * 128:(j + 1) * 128], rawC.t[:, hb * 3 + j].rearrange("p s t -> p (s t)"), ident.t, r=[rawC.k, ident.k], w=[pb.k])
                P.copy(rowC.t[:, hb * 384:(hb + 1) * 384], pb.t[0:48, 0:384], r=[pb.k], w=[rowC.k])
            P.dma(scs_d[l].rearrange("s t c -> (s t) c"), rowC.t, r=[rowC.k])
        bcb = P.tmp([128, 4, NT], BF16)
        P.copy(bcb.t, xbc.t[:, 2:6, :], r=[xbc.k], w=[bcb.k], eng="gpsimd")
        pgt = bank(0)
        for b_ in range(NBk):
            for c in range(8):
                P.mm(pgt.t[0:nb, b_ * 4:(b_ + 1) * 4], xn.t[:, c, b_ * nb:(b_ + 1) * nb], wg.t[:, l, c, 8:12], start=(c == 0), stop=(c == 7), r=[xn.k, wg.k], w=[pgt.k])
        dt = P.tmp([nb, NBk, 4], F32)
        P.tt(dt.t, pgt.t[0:nb, 0:NBk * 4].rearrange("p (b n) -> p b n", n=4), rows.t[0:nb, l, 2, :].unsqueeze(1).to_broadcast([nb, NBk, 4]), ALU.add, r=[pgt.k, rows.k], w=[dt.k])
        softplus(dt, [nb, NBk, 4], nb)
        aa = P.tmp([nb, NBk, 4], F32)
        P.tt(aa.t, dt.t, rows.t[0:nb, l, 3, :].unsqueeze(1).to_broadcast([nb, NBk, 4]), ALU.mult, r=[dt.k, rows.k], w=[aa.k])
        if prm:
            TRI, SEGO, NMN, idn = cp.t[:, 0, :], cp.t[:, 1, :], cp.t[:, 2, :], ident.t
            cK = cp.k
        else:
            TRI, SEGO, NMN, idn = gs.t[:, 0, :], gs.t[:, 1, :], gs.t[:, 3, :], ident.t[0:64, 0:64]
            cK = gs.k
        pc = bank(0)
        for b_ in range(NBk):
            P.mm(pc.t[0:nb, b_ * 8:b_ * 8 + 4], TRI, aa.t[:, b_, :], r=[cK, aa.k], w=[pc.k])
            P.mm(pc.t[0:nb, b_ * 8 + 4:b_ * 8 + 8], SEGO, aa.t[:, b_, :], r=[cK, aa.k], w=[pc.k])
        cm = P.tmp([nb, NBk, 8], F32)
        P.copy(cm.t, pc.t[0:nb, 0:NBk * 8].rearrange("p (b n) -> p b n", n=8), r=[pc.k], w=[cm.k])
        negc = P.tmp([nb, NBk, 4], F32)
        dB = P.tmp([nb, NBk, 4], F32)
        P.ts(negc.t, cm.t[:, :, 0:4], -1.0, None, ALU.mult, r=[cm.k], w=[negc.k])
        P.tt(dB.t, cm.t[:, :, 4:8], cm.t[:, :, 0:4], ALU.subtract, r=[cm.k], w=[dB.k])
        P.act(dB.t, dB.t, AF.Exp, r=[dB.k], w=[dB.k])
        yz = P.tmp([128, 2, NT], F32)
        if prm:
            hv = lambda sg_, h: hTb.t[:, h, :]
        else:
            hT0 = T["hT0"]
            hTn = T["hTn"]
            hv = lambda sg_, h: hTb.t[:, sg_, h, :]
        segw = nb // NSEG
        for b_ in range(NBk):
            mB = P.mark()
            cs = slice(b_ * nb, (b_ + 1) * nb)
            if prm:
                hTb = P.tmp([128, 4, 64], BF16)
                P.copy(hTb.t, hTst.t[:, l], r=[hTst.k], w=[hTb.k])
            else:
                hTb = P.tmp([128, 16, 4, 64], BF16)
                P.copy(hTb.t, hT0.t, r=[hT0.k], w=[hTb.k])
            pE, pGx = bank(1), bank(2)
            for h in range(4):
                P.mm(pE.t[0:nb, h * nb:(h + 1) * nb], aa.t[:, b_, h:h + 1].to_broadcast([nb, nb]), TRI, start=True, stop=False, r=[aa.k, cK], w=[pE.k])
                P.mm(pE.t[0:nb, h * nb:(h + 1) * nb], idn, NMN, start=False, stop=True, r=[ident.k, cK], w=[pE.k])
                P.mm(pGx.t[:, h * nb:(h + 1) * nb], aa.t[:, b_, h:h + 1].to_broadcast([nb, 128]), TRI, r=[aa.k, cK], w=[pGx.k])
            E2 = P.tmp([nb, 4, nb], F32)
            eG = P.tmp([128, 4, nb], F32)
            for h in range(4):
                P.act(E2.t[:, h, :], pE.t[0:nb, h * nb:(h + 1) * nb], AF.Exp, bias=negc.t[:, b_, h:h + 1], r=[pE.k, negc.k], w=[E2.k])
            P.act(eG.t, pGx.t[:, 0:4 * nb].rearrange("p (h q) -> p h q", h=4), AF.Exp, r=[pGx.k], w=[eG.k])
            pM = bank(3)
            for g in range(2):
                P.mm(pM.t[0:nb, g * nb:(g + 1) * nb], bcb.t[:, g, cs], bcb.t[:, 2 + g, cs], r=[bcb.k], w=[pM.k])
            MD = P.tmp([nb, 4, nb], BF16)
            P.tt(MD.t.rearrange("p (g e) q -> p g e q", g=2), pM.t[0:nb, 0:2 * nb].rearrange("p (g q) -> p g q", g=2).unsqueeze(2).to_broadcast([nb, 2, 2, nb]),
                 E2.t.rearrange("p (g e) q -> p g e q", g=2), ALU.mult, r=[pM.k, E2.k], w=[MD.k])
            pX, pBt = bank(4), bank(5)
            for j in range(2):
                P.tr(pX.t[0:nb, j * 128:(j + 1) * 128], xbc.t[:, j, cs], ident.t, r=[xbc.k, ident.k], w=[pX.k])
                P.tr(pBt.t[0:nb, j * 128:(j + 1) * 128], xbc.t[:, 2 + j, cs], ident.t, r=[xbc.k, ident.k], w=[pBt.k])
            xdt = P.tmp([nb, 4, 64], BF16)
            P.tt(xdt.t, pX.t[0:nb, 0:256].rearrange("p (h d) -> p h d", h=4), dt.t[:, b_, :].unsqueeze(2).to_broadcast([nb, 4, 64]), ALU.mult, r=[pX.k, dt.k], w=[xdt.k])
            Bdec = P.tmp([nb, 4, 128], BF16)
            P.tt(Bdec.t.rearrange("p (g e) n -> p g e n", g=2), pBt.t[0:nb, 0:256].rearrange("p (g n) -> p g n", g=2).unsqueeze(2).to_broadcast([nb, 2, 2, 128]),
                 dB.t[:, b_, :].rearrange("p (g e) -> p g e", g=2).unsqueeze(3).to_broadcast([nb, 2, 2, 128]), ALU.mult, r=[pBt.k, dB.k], w=[Bdec.k])
            CdT = P.tmp([128, 4, nb], BF16)
            P.tt(CdT.t.rearrange("p (g e) q -> p g e q", g=2), xbc.t[:, 4:6, cs].unsqueeze(2).to_broadcast([128, 2, 2, nb]),
                 eG.t.rearrange("p (g e) q -> p g e q", g=2), ALU.mult, r=[xbc.k, eG.k], w=[CdT.k])
            pY = bank(6)
            for h in range(4):
                po = pY.t[(h % 2) * 64:(h % 2) * 64 + 64, (h // 2) * nb:(h // 2 + 1) * nb]
                P.mm(po, xdt.t[:, h, :], MD.t[:, h, :], start=True, stop=False, r=[xdt.k, MD.k], w=[pY.k])
                for sg_ in range(NSEG):
                    P.mm(pY.t[(h % 2) * 64:(h % 2) * 64 + 64, (h // 2) * nb + sg_ * segw:(h // 2) * nb + (sg_ + 1) * segw], hv(sg_, h), CdT.t[:, h, sg_ * segw:(sg_ + 1) * segw],
                         start=False, stop=(sg_ == NSEG - 1), r=[hTb.k, CdT.k], w=[pY.k])
            for ch in range(2):
                yc = P.tmp([128, nb], F32)
                P.stt(yc.t, xbc.t[:, ch, cs], dC.t[:, l, ch:ch + 1], pY.t[:, ch * nb:(ch + 1) * nb], ALU.mult, ALU.add, r=[xbc.k, dC.k, pY.k], w=[yc.k])
                P.tt(yz.t[:, ch, cs], yc.t, szC.t[:, ch, cs], ALU.mult, r=[yc.k, szC.k], w=[yz.k])
            if prm:
                pS = bank(1)
                for h in range(4):
                    P.mm(pS.t[:, h * 64:(h + 1) * 64], Bdec.t[:, h, :], xdt.t[:, h, :], r=[Bdec.k, xdt.k], w=[pS.k])
                P.tt(hTst.t[:, l], hTst.t[:, l], eG.t[:, :, nb - 1:nb].to_broadcast([128, 4, 64]), ALU.mult, r=[hTst.k, eG.k], w=[hTst.k])
                P.tt(hTst.t[:, l], hTst.t[:, l], pS.t[:, 0:256].rearrange("p (h d) -> p h d", h=4), ALU.add, r=[hTst.k, pS.k], w=[hTst.k])
            else:
                Bm = P.tmp([64, 16, 128], BF16)
                for h in range(4):
                    P.tt(Bm.t, Bdec.t[:, h, :].unsqueeze(1).to_broadcast([64, 16, 128]), segmask.t.unsqueeze(2).to_broadcast([64, 16, 128]), ALU.mult,
                         r=[Bdec.k, segmask.k], w=[Bm.k])
                    for half in range(2):
                        pS = bank(1 + half)
                        for s8 in range(8):
                            sg_ = half * 8 + s8
                            P.mm(pS.t[:, s8 * 64:(s8 + 1) * 64], Bm.t[:, sg_, :], xdt.t[:, h, :], r=[Bm.k, xdt.k], w=[pS.k])
                        ss_ = slice(half * 8, half * 8 + 8)
                        alb = eG.t[:, h, :].rearrange("p (s t) -> p s t", t=4)[:, ss_, 3:4].to_broadcast([128, 8, 64])
                        P.tt(hTn.t[:, ss_, h, :], hT0.t[:, ss_, h, :], alb, ALU.mult, r=[hT0.k, eG.k], w=[hTn.k])
                        P.tt(hTn.t[:, ss_, h, :], hTn.t[:, ss_, h, :], pS.t[:, :].rearrange("p (s v) -> p s v", v=64), ALU.add, r=[hTn.k, pS.k], w=[hTn.k])
            P.release(mB)
        sqz = P.tmp([128, 2, NT], BF16)
        P.act(sqz.t, yz.t, AF.Square, r=[yz.k], w=[sqz.k])
        for ch in range(2):
            P.mm(psb[0].t[:, 0:NT], ones_bf.t, sqz.t[:, ch, :], start=(ch == 0), stop=(ch == 1), r=[sqz.k, ones_bf.k], w=[psb[0].k])
        rstd = rstd_from(psb[0].t[:, 0:NT], psb[0].k, [128, NT], 1.0 / 256)
        for ch in range(2):
            P.stt(mixT.t[:, 2 + ch, :], yz.t[:, ch, :], gnC.t[:, l, ch:ch + 1], rstd.t, ALU.mult, ALU.mult, r=[yz.k, gnC.k, rstd.k], w=[mixT.k])
        if "mixC" in dbg:
            of = P.tmp([128, 2, NT], F32)
            P.copy(of.t, mixT.t[:, 2:4, :], r=[mixT.k], w=[of.k])
            P.dbg(f"mixC_{kind}{l}", of.t, r=[of.k])
        P.release(m)


    def rope_rows(dst, src, tab, n, H, rk):
        t1 = P.tmp([n, H, 32], F32)
        P.tt(dst.t, src.t, tab[:, 0:32].unsqueeze(1).to_broadcast([n, H, 32]), ALU.mult, r=[src.k] + rk, w=[dst.k])
        P.tt(t1.t[:, :, 0:16], src.t[:, :, 16:32], tab[:, 32:48].unsqueeze(1).to_broadcast([n, H, 16]), ALU.mult, r=[src.k] + rk, w=[t1.k])
        P.tt(t1.t[:, :, 16:32], src.t[:, :, 0:16], tab[:, 48:64].unsqueeze(1).to_broadcast([n, H, 16]), ALU.mult, r=[src.k] + rk, w=[t1.k])
        P.tt(dst.t, dst.t, t1.t, ALU.add, r=[t1.k, dst.k], w=[dst.k])

    def mixD(cx):
        l, T, NT, prm, xn, mixT, nb, NBk, as3, kind = (cx[k] for k in ["l", "T", "NT", "prm", "xn", "mixT", "nb", "NBk", "as3", "kind"])
        m = P.mark()
        wb = wstage([(lambda b_: as3(b_)[:, :, 0:416], cx["win"][:, :, C_CQ:C_CQ + 416], cx["sw"])])
        wv = as3(wb.t)
        sqn_ = cx["sqn"]
        cqnT = Tl(sqn_.t[:, 0:2, :], ("sqn_sub", 0))
        olat = P.tmp([128, 4, NT], BF16)
        qnT = Tl(olat.t[0:64], olat.k)
        qlT = P.tmp([128, 4, NT], BF16)
        qpT = P.tmp([32, 4, NT], BF16)
        if prm:
            b0 = T["t0"] // 128
            rowbase = T["b"] * SEQ + T["t0"]
            KTv, Ktv, KPTv = KT.t[:, l], Kt.t[:, l], KPT.t[:, l]
        else:
            KTn = P.tmp([128, 64], BF16)
            Ktn = P.tmp([64, 128], BF16)
            KPTn = P.tmp([32, 64], BF16)
        for b_ in range(NBk):
            cs = slice(b_ * nb, (b_ + 1) * nb)
            pd = bank(1 + b_ % 2)
            for c in range(8):
                P.mm(pd.t[0:nb, 0:416], xn.t[:, c, cs], wv[:, c, 0:416], start=(c == 0), stop=(c == 7), r=[xn.k, wb.k], w=[pd.k])
            st_ = P.tmp([nb, 4], F32)
            junk = P.tmp([nb, 256], F32)
            P.memset(st_.t, 0.0, w=[st_.k])
            P.act(junk.t, pd.t[0:nb, 0:256], AF.Square, accum_out=st_.t[:, 0:1], r=[pd.k, st_.k], w=[junk.k, st_.k])
            P.act(junk.t[:, 0:128], pd.t[0:nb, 256:384], AF.Square, accum_out=st_.t[:, 1:2], r=[pd.k, st_.k], w=[junk.k, st_.k])
            P.act(st_.t[:, 2:3], st_.t[:, 0:1], AF.Sqrt, scale=1.0 / 256, bias=eps_t.t[0:nb, 0:1], r=[st_.k, eps_t.k], w=[st_.k])
            P.act(st_.t[:, 3:4], st_.t[:, 1:2], AF.Sqrt, scale=1.0 / 128, bias=eps_t.t[0:nb, 0:1], r=[st_.k, eps_t.k], w=[st_.k])
            P.recip(st_.t[:, 2:4], st_.t[:, 2:4], r=[st_.k], w=[st_.k])
            cqn = P.tmp([nb, 256], F32)
            ckvn = P.tmp([nb, 128], F32)
            P.stt(cqn.t, pd.t[0:nb, 0:256], st_.t[:, 2:3], gqr.t[0:nb, l, :], ALU.mult, ALU.mult, r=[pd.k, st_.k, gqr.k], w=[cqn.k])
            P.stt(ckvn.t, pd.t[0:nb, 256:384], st_.t[:, 3:4], gkr.t[0:nb, l, :], ALU.mult, ALU.mult, r=[pd.k, st_.k, gkr.k], w=[ckvn.k])
            tab = rope_p.t[:, b0 + b_, :] if prm else rope_s.t
            kraw = P.tmp([nb, 1, 32], F32)
            P.copy(kraw.t[:, 0, :], pd.t[0:nb, 384:416], r=[pd.k], w=[kraw.k])
            kpe = P.tmp([nb, 1, 32], F32)
            rope_rows(kpe, kraw, tab, nb, 1, [rope_p.k, rope_s.k])
            if prm:
                P.dma(ckvp_d[l, rowbase + b_ * 128:rowbase + (b_ + 1) * 128, :], ckvn.t, r=[ckvn.k])
                P.dma(kpep_d[l, rowbase + b_ * 128:rowbase + (b_ + 1) * 128, :], kpe.t[:, 0, :], r=[kpe.k])
                P.copy(Ktv[:, b0 + b_, :], ckvn.t, r=[ckvn.k], w=[(Kt.k, l)], eng="gpsimd")
            else:
                P.dma(ckvs_d[l], ckvn.t, r=[ckvn.k])
                P.dma(kpes_d[l], kpe.t[:, 0, :], r=[kpe.k])
                P.copy(Ktn.t, ckvn.t, r=[ckvn.k], w=[Ktn.k], eng="gpsimd")
            pt = bank(3 + b_ % 2)
            P.tr(pt.t[:, 0:nb], ckvn.t, ident.t[0:nb, 0:nb], r=[ckvn.k, ident.k], w=[pt.k])
            P.tr(pt.t[0:32, 128:128 + nb], kpe.t[:, 0, :], ident.t[0:nb, 0:nb], r=[kpe.k, ident.k], w=[pt.k])
            for c2 in range(2):
                P.tr(pt.t[:, 256 + c2 * 128:256 + c2 * 128 + nb], cqn.t[:, c2 * 128:(c2 + 1) * 128], ident.t[0:nb, 0:nb], r=[cqn.k, ident.k], w=[pt.k])
            if prm:
                kc = slice((b0 + b_) * 128, (b0 + b_ + 1) * 128)
                P.copy(KTv[:, kc], pt.t[:, 0:128], r=[pt.k], w=[(KT.k, l)])
                P.copy(KPTv[:, kc], pt.t[0:32, 128:256], r=[pt.k], w=[(KPT.k, l)])
            else:
                P.copy(KTn.t, pt.t[:, 0:64], r=[pt.k], w=[KTn.k])
                P.copy(KPTn.t, pt.t[0:32, 128:192], r=[pt.k], w=[KPTn.k])
            for c2 in range(2):
                P.copy(cqnT.t[:, c2, cs], pt.t[:, 256 + c2 * 128:256 + c2 * 128 + nb], r=[pt.k], w=[cqnT.k], eng="scalar")
        for h in range(4):
            pq_ = bank(1 + h % 2)
            for c2 in range(2):
                P.mm(pq_.t[0:64, 0:NT], wuq.t[:, l, c2, h * 96:h * 96 + 64], cqnT.t[:, c2, :], start=(c2 == 0), stop=(c2 == 1), r=[wuq.k, cqnT.k], w=[pq_.k])
            P.copy(qnT.t[:, h, :], pq_.t[0:64, 0:NT], r=[pq_.k], w=[qnT.k], eng="scalar")
            pl_ = bank(3 + h % 2)
            P.mm(pl_.t[:, 0:NT], wukT.t[:, l, h, :], qnT.t[:, h, :], r=[wukT.k, qnT.k], w=[pl_.k])
            P.copy(qlT.t[:, h, :], pl_.t[:, 0:NT], r=[pl_.k], w=[qlT.k])
        for b_ in range(NBk):
            cs = slice(b_ * nb, (b_ + 1) * nb)
            pp = bank(5 + b_ % 2)
            wr = wuq.t[:, l, :, :].rearrange("p c (h e) -> p c h e", e=96)
            for c2 in range(2):
                P.mm(pp.t[0:nb, 0:128].rearrange("p (h e) -> p h e", h=4), cqnT.t[:, c2, cs], wr[:, c2, :, 64:96], start=(c2 == 0), stop=(c2 == 1), r=[cqnT.k, wuq.k], w=[pp.k])
            qraw = P.tmp([nb, 4, 32], F32)
            P.copy(qraw.t, pp.t[0:nb, 0:128].rearrange("p (h e) -> p h e", h=4), r=[pp.k], w=[qraw.k], eng="scalar")
            qpe = P.tmp([nb, 4, 32], F32)
            tab = rope_p.t[:, b0 + b_, :] if prm else rope_s.t
            rope_rows(qpe, qraw, tab, nb, 4, [rope_p.k, rope_s.k])
            pt = bank(1 + b_ % 2)
            for h in range(4):
                P.tr(pt.t[0:32, h * nb:(h + 1) * nb], qpe.t[:, h, :], ident.t[0:nb, 0:nb], r=[qpe.k, ident.k], w=[pt.k])
            P.copy(qpT.t[:, :, cs], pt.t[0:32, 0:4 * nb].rearrange("p (h q) -> p h q", h=4), r=[pt.k], w=[qpT.k])
        if prm:
            pTs = [Tl(sqn_.t[:, 2 + i_, :], ("sqn_sub", 1 + i_)) for i_ in range(2)]
            rs = P.tmp([128, 512], F32)
            v3 = lambda ap_: ap_.rearrange("p (h q) -> p h q", h=4)
            nkb = 0
            for qb in range(NBk):
                gq = b0 + qb
                qs = slice(qb * 128, (qb + 1) * 128)
                pO, pSm = bank(5), bank(6)
                for kb in range(gq + 1):
                    ks = slice(kb * 128, (kb + 1) * 128)
                    pS_ = bank(1 + kb % 3)
                    dg = kb == gq
                    P.mm(v3(pS_.t), KTv[:, ks], qlT.t[:, :, qs], start=True, stop=False, r=[(KT.k, l), qlT.k], w=[pS_.k])
                    P.mm(v3(pS_.t), KPTv[:, ks], qpT.t[:, :, qs], start=False, stop=not dg, r=[(KPT.k, l), qpT.k], w=[pS_.k])
                    if dg:
                        P.mm(pS_.t, identb.t, mla_diag.t, start=False, stop=True, r=[identb.k, mla_diag.k], w=[pS_.k])
                    pT_ = pTs[nkb % 2]
                    nkb += 1
                    P.act(pT_.t, pS_.t, AF.Exp, scale=MLA_SCALE, r=[pS_.k], w=[pT_.k])
                    P.mm(pO.t, Ktv[:, kb, :], pT_.t, start=(kb == 0), stop=dg, r=[(Kt.k, l), pT_.k], w=[pO.k])
                    P.mm(pSm.t, ones_bf.t, pT_.t, start=(kb == 0), stop=dg, r=[ones_bf.k, pT_.k], w=[pSm.k])
                P.recip(rs.t, pSm.t, r=[pSm.k], w=[rs.k])
                P.tt(olat.t[:, :, qs], pO.t.rearrange("p (h q) -> p h q", h=4), rs.t.rearrange("p (h q) -> p h q", h=4), ALU.mult, r=[pO.k, rs.k], w=[olat.k])
        else:
            mla_sample(cx, qlT, qpT, KTn, Ktn, KPTn, olat)
        for ch in range(2):
            po = bank(1 + ch)
            for hl in range(2):
                h = ch * 2 + hl
                P.mm(po.t[hl * 64:(hl + 1) * 64, 0:NT], wuv.t[:, l, h, :], olat.t[:, h, :], r=[wuv.k, olat.k], w=[po.k])
            P.copy(mixT.t[:, 4 + ch, :], po.t[:, 0:NT], r=[po.k], w=[mixT.k])
        if "mixD" in dbg:
            of = P.tmp([128, 2, NT], F32)
            P.copy(of.t, mixT.t[:, 4:6, :], r=[mixT.k], w=[of.k])
            P.dbg(f"mixD_{kind}{l}", of.t, r=[of.k])
        P.release(m)


    def mla_sample(cx, qlT, qpT, KTn, Ktn, KPTn, olat):
        l = cx["l"]
        KQ = 32
        NQ = 128 // KQ
        J = 8
        v3 = lambda ap_, h_: ap_.rearrange("p (h q) -> p h q", h=h_)
        ones_f = P.tmp([128, 128], F32)
        P.memset(ones_f.t, 1.0, w=[ones_f.k])
        pn = bank(1)
        P.mm(v3(pn.t[0:64, 0:256], 4), KTn.t, qlT.t, start=True, stop=False, r=[KTn.k, qlT.k], w=[pn.k])
        P.mm(v3(pn.t[0:64, 0:256], 4), KPTn.t, qpT.t, start=False, stop=True, r=[KPTn.k, qpT.k], w=[pn.k])
        pnf = P.tmp([64, 256], F32)
        P.act(pnf.t, pn.t[0:64, 0:256], AF.Exp, scale=MLA_SCALE, r=[pn.k], w=[pnf.k])
        pnb = P.tmp([64, 4, 64], BF16)
        P.tt(pnf.t, pnf.t, mla_new.t, ALU.mult, r=[pnf.k, mla_new.k], w=[pnf.k])
        P.copy(pnb.t, v3(pnf.t, 4), r=[pnf.k], w=[pnb.k])
        cbs = [P.tmp([NPG, KQ, 128], BF16) for _ in range(2)]
        kbs = [P.tmp([NPG, KQ, 32], BF16) for _ in range(2)]
        pgT = [P.tmp([128, J * NPG], BF16) for _ in range(2)]
        kpT = [P.tmp([32, J * NPG], BF16) for _ in range(2)]
        pTb = [P.tmp([NPG, J * 16], BF16) for _ in range(2)]
        pacc = [P.tmp([NPG, 16], F32) for _ in range(2)]
        pn16 = P.tmp([64, 16], F32)
        red = P.tmp([NPG, 16], F32)
        rs = P.tmp([128, 16], F32)
        it = 0
        gi = 0
        for s_ in range(NSS):
            pO = bank(5 + s_ % 2)
            pa = pacc[s_ % 2]
            P.memset(pa.t, 0.0, w=[pa.k])
            qs = slice(4 * s_, 4 * s_ + 4)
            first = True
            for q in range(NQ):
                cb, kb = cbs[it % 2], kbs[it % 2]
                it += 1
                P.op("gpsimd", (lambda e, cb=cb, q=q, s_=s_: e.indirect_dma_start(
                    out=cb.t.rearrange("p k r -> p (k r)"), out_offset=None, in_=cckv_d.rearrange("l n (q k) r -> (l n q) (k r)", k=KQ),
                    in_offset=bass.IndirectOffsetOnAxis(ap=ptab4.t[:, s_:s_ + 1], axis=0),
                    element_offset=l * NPHYS * 128 * 128 + q * KQ * 128)), r=[ptab4.k], w=[cb.k], dma=True)
                P.op("gpsimd", (lambda e, kb=kb, q=q, s_=s_: e.indirect_dma_start(
                    out=kb.t.rearrange("p k r -> p (k r)"), out_offset=None, in_=ckpe_d.rearrange("l n (q k) r -> (l n q) (k r)", k=KQ),
                    in_offset=bass.IndirectOffsetOnAxis(ap=ptab4.t[:, s_:s_ + 1], axis=0),
                    element_offset=l * NPHYS * 128 * 32 + q * KQ * 32)), r=[ptab4.k], w=[kb.k], dma=True)
                if "gath" in dbg and s_ == 0 and q == 1 and l == 0:
                    cf = P.tmp([NPG, KQ, 128], F32)
                    P.copy(cf.t, cb.t, r=[cb.k], w=[cf.k])
                    P.dbg("gath_c", cf.t, r=[cf.k])
                    kf = P.tmp([NPG, KQ, 32], F32)
                    P.copy(kf.t, kb.t, r=[kb.k], w=[kf.k])
                    P.dbg("gath_k", kf.t, r=[kf.k])
                for j0 in range(0, KQ, J):
                    tt_, kt_, pt_ = pgT[gi % 2], kpT[gi % 2], pTb[gi % 2]
                    gi += 1
                    for j in range(J):
                        P.tr(psh.t[:, j * NPG:(j + 1) * NPG], cb.t[:, j0 + j, :], identb.t[0:NPG, 0:NPG], r=[cb.k, identb.k], w=[psh.k])
                    P.copy(tt_.t, psh.t[:, 0:J * NPG], r=[psh.k], w=[tt_.k], eng=("scalar" if gi % 2 else "vector"))
                    for j in range(J):
                        P.tr(psh.t[0:32, j * NPG:(j + 1) * NPG], kb.t[:, j0 + j, :], identb.t[0:NPG, 0:NPG], r=[kb.k, identb.k], w=[psh.k])
                    P.copy(kt_.t, psh.t[0:32, 0:J * NPG], r=[psh.k], w=[kt_.k], eng=("vector" if gi % 2 else "scalar"))
                    psc = bank(3 + gi % 2)
                    for j in range(J):
                        o_ = v3(psc.t[0:NPG, j * 16:(j + 1) * 16], 4)
                        P.mm(o_, tt_.t[:, j * NPG:(j + 1) * NPG], qlT.t[:, :, qs], start=True, stop=False, r=[tt_.k, qlT.k], w=[psc.k])
                        P.mm(o_, kt_.t[:, j * NPG:(j + 1) * NPG], qpT.t[:, :, qs], start=False, stop=True, r=[kt_.k, qpT.k], w=[psc.k])
                    P.act(pt_.t, psc.t[0:NPG, 0:J * 16], AF.Exp, scale=MLA_SCALE, r=[psc.k], w=[pt_.k])
                    for j in range(J):
                        P.mm(pO.t[:, 0:16], cb.t[:, j0 + j, :], pt_.t[:, j * 16:(j + 1) * 16], start=first, stop=False, r=[cb.k, pt_.k], w=[pO.k])
                        first = False
                    P.red(red.t, pt_.t.rearrange("p (g c) -> p c g", g=J), r=[pt_.k], w=[red.k])
                    P.tt(pa.t, pa.t, red.t, ALU.add, r=[pa.k, red.k], w=[pa.k])
            P.mm(v3(pO.t[:, 0:16], 4), Ktn.t, pnb.t[:, :, qs], start=False, stop=True, r=[Ktn.k, pnb.k], w=[pO.k])
            P.copy(v3(pn16.t, 4), v3(pnf.t, 4)[:, :, qs], r=[pnf.k], w=[pn16.k])
            psm = bank(2)
            P.mm(psm.t[:, 0:16], ones_f.t[0:NPG, :], pa.t, start=True, stop=False, r=[ones_f.k, pa.k], w=[psm.k])
            P.mm(psm.t[:, 0:16], ones_f.t[0:64, :], pn16.t, start=False, stop=True, r=[ones_f.k, pn16.k], w=[psm.k])
            P.recip(rs.t, psm.t[:, 0:16], r=[psm.k], w=[rs.k])
            P.tt(olat.t[:, :, qs], v3(pO.t[:, 0:16], 4), v3(rs.t, 4), ALU.mult, r=[pO.k, rs.k], w=[olat.k])

    def sample_ctx(l):
        T = dict(NT=64, kind="s")
        tA = P.tmp([64, 12, 16, 3], F32)
        tC = P.tmp([128, 6, 16, 3], F32)
        S0 = P.tmp([64, 16, 4, 64], F32)
        hT0 = P.tmp([128, 16, 4, 64], F32)
        m = P.mark()
        ca = P.tmp([48, 768], F32)
        cc = P.tmp([48, 768], F32)
        P.dma(ca.t, sgc_d[l], w=[ca.k])
        P.dma(cc.t, ssc_d[l], w=[cc.k])
        for g3 in range(2):
            pb = bank(1 + g3)
            for j in range(6):
                hc = g3 * 6 + j
                P.tr(pb.t[0:64, j * 48:(j + 1) * 48], ca.t[:, hc * 64:(hc + 1) * 64], ident.t[0:48, 0:48], r=[ca.k, ident.k], w=[pb.k])
            P.copy(tA.t[:, g3 * 6:(g3 + 1) * 6].rearrange("p j s t -> p j (s t)"), pb.t[0:64, 0:288].rearrange("p (j n) -> p j n", j=6), r=[pb.k], w=[tA.k])
        pb = bank(3)
        for j in range(6):
            P.tr(pb.t[:, j * 48:(j + 1) * 48], cc.t[:, j * 128:(j + 1) * 128], ident.t[0:48, 0:48], r=[cc.k, ident.k], w=[pb.k])
        P.copy(tC.t.rearrange("p j s t -> p j (s t)"), pb.t[:, 0:288].rearrange("p (j n) -> p j n", j=6), r=[pb.k], w=[tC.k])
        for h in range(4):
            P.dma(S0.t[:, :, h, :], sgs_d[l, :, h, :, :].rearrange("s k v -> k s v"), w=[S0.k])
        hsrc = ssh_d[l].rearrange("(q r) n -> r q n", r=128)
        hv_ = hT0.t.rearrange("p s h d -> p (s h d)").rearrange("p (q r) -> p q r", r=128)
        for j in range(8):
            hn = P.tmp([128, 4, 128], F32)
            P.dma(hn.t, hsrc[:, 4 * j:4 * j + 4, :], w=[hn.k])
            pb = bank(4 + j % 2)
            for q in range(4):
                P.tr(pb.t[:, q * 128:(q + 1) * 128], hn.t[:, q, :], ident.t, r=[hn.k, ident.k], w=[pb.k])
            P.copy(hv_[:, 4 * j:4 * j + 4, :], pb.t.rearrange("p (q r) -> p q r", q=4), r=[pb.k], w=[hT0.k], eng=("scalar" if j % 2 else "vector"))
        P.release(m)
        T.update(tailA=tA, tailC=tC, S0=S0, Sn=S0, hT0=hT0, hTn=hT0)
        return T

    def out_states_sample(l, T):
        m = P.mark()
        S0, hT0 = T["S0"], T["hT0"]
        for h in range(4):
            P.dma(gss_d[l, :, h, :, :].rearrange("s k v -> k s v"), S0.t[:, :, h, :], r=[S0.k])
        hdst = shs_d[l].rearrange("(q r) n -> r q n", r=128)
        hv_ = hT0.t.rearrange("p s h d -> p (s h d)").rearrange("p (q r) -> p q r", r=128)
        for j in range(8):
            pb = bank(4 + j % 2)
            for q in range(4):
                P.tr(pb.t[:, q * 128:(q + 1) * 128], hv_[:, 4 * j + q, :], ident.t, r=[hT0.k, ident.k], w=[pb.k])
            ho = P.tmp([128, 4, 128], F32)
            P.copy(ho.t, pb.t.rearrange("p (q r) -> p q r", q=4), r=[pb.k], w=[ho.k], eng=("scalar" if j % 2 else "vector"))
            P.dma(hdst[:, 4 * j:4 * j + 4, :], ho.t, r=[ho.k])
        P.release(m)

    def out_states_prompt(l, b):
        m = P.mark()
        P.dma(gsp_d[l, b].rearrange("h k v -> k h v"), Sst.t[:, l], r=[Sst.k])
        rowA = P.tmp([3, 768], F32)
        rowC = P.tmp([3, 768], F32)
        for hb in range(2):
            pa_, pc_ = bank(1 + hb), bank(5 + hb)
            for j in range(6):
                P.tr(pa_.t[0:3, j * 64:(j + 1) * 64], tailA.t[:, l, hb * 6 + j, :], ident.t[0:64, 0:64], r=[tailA.k, ident.k], w=[pa_.k])
            for j in range(3):
                P.tr(pc_.t[0:3, j * 128:(j + 1) * 128], tailC.t[:, l, hb * 3 + j, :], ident.t, r=[tailC.k, ident.k], w=[pc_.k])
            P.copy(rowA.t[:, hb * 384:(hb + 1) * 384], pa_.t[0:3, 0:384], r=[pa_.k], w=[rowA.k])
            P.copy(rowC.t[:, hb * 384:(hb + 1) * 384], pc_.t[0:3, 0:384], r=[pc_.k], w=[rowC.k])
        P.dma(gcp_d[l, b], rowA.t, r=[rowA.k])
        P.dma(scp_d[l, b], rowC.t, r=[rowC.k])
        pb = bank(4)
        hv_ = hTst.t[:, l].rearrange("p h d -> p (h d)")
        for g in range(2):
            P.tr(pb.t[:, g * 128:(g + 1) * 128], hv_[:, g * 128:(g + 1) * 128], ident.t, r=[hTst.k, ident.k], w=[pb.k])
        ho = P.tmp([128, 2, 128], F32)
        P.copy(ho.t, pb.t[:, 0:256].rearrange("p (g n) -> p g n", g=2), r=[pb.k], w=[ho.k])
        P.dma(shp_d[l, b * 256:(b + 1) * 256, :].rearrange("(g r) n -> r g n", r=128), ho.t, r=[ho.k])
        P.release(m)

    def mix_out(cx):
        l, NT, mixT, outA = cx["l"], cx["NT"], cx["mixT"], cx["outA"]
        wo = s_wout[l]
        tok = ("s_wout", l)
        wbA = wstage([(lambda b_: b_[0:64, :].rearrange("p (h d) -> p h d", h=4), wo[0:256, :].rearrange("(h p) d -> p h d", p=64), tok)])
        wbB = wstage([(lambda b_: b_.rearrange("p (c d) -> p c d", c=4), wo[256:768, :].rearrange("(c p) d -> p c d", p=128), tok)])
        wbC = wstage([(lambda b_: b_[:, 0:2048].rearrange("p (c d) -> p c d", c=2), wo[768:1024, :].rearrange("(c p) d -> p c d", p=128), tok)])
        vA = wbA.t[0:64, :].rearrange("p (h d) -> p h d", h=4)
        vB = wbB.t.rearrange("p (c d) -> p c d", c=4)
        vC = wbC.t[:, 0:2048].rearrange("p (c d) -> p c d", c=2)
        ysb = P.tmp([128, 8, NT], F32)
        sq = cx["sqn"]
        for p_ in range(2):
            for dcl in range(4):
                dc = 4 * p_ + dcl
                pb = psb[1 + dcl]
                ds_ = slice(dc * 128, (dc + 1) * 128)
                for h in range(4):
                    P.mm(pb.t[:, 0:NT], vA[:, h, ds_], outA.t[:, h, :], start=(h == 0), stop=False, r=[wbA.k, outA.k], w=[pb.k])
                for c in range(4):
                    P.mm(pb.t[:, 0:NT], vB[:, c, ds_], mixT.t[:, c, :], start=False, stop=False, r=[wbB.k, mixT.k], w=[pb.k])
                for c in range(2):
                    P.mm(pb.t[:, 0:NT], vC[:, c, ds_], mixT.t[:, 4 + c, :], start=False, stop=(c == 1), r=[wbC.k, mixT.k], w=[pb.k])
                P.copy(ysb.t[:, dc, :], pb.t[:, 0:NT], r=[pb.k], w=[ysb.k])
                P.act(sq.t[:, dc, :], pb.t[:, 0:NT], AF.Square, r=[pb.k], w=[sq.k])
        if "mix" in dbg:
            P.dbg(f"mix_{cx['kind']}{l}", ysb.t, r=[ysb.k])
        postnorm_residual(ysb, sq, l, 3, NT)
        P.release(cx["m_all"])

    def run_layer(l, T):
        NT = T["NT"]
        ffn(l, 0, NT)
        cx = mixer(l, T)
        mixB(cx)
        mixC(cx)
        mixD(cx)
        mix_out(cx)
        ffn(l, 1, NT)

    def run_all(groups="ps", layers=(0, 1)):
        for b in (range(NPS) if "p" in groups else []):
            for t_ in (Sst, hTst, tailA, tailC):
                P.memset(t_.t, 0.0, w=[t_.k])
            for ti in range(SEQ // TT):
                t0 = ti * TT
                load_x(xp_d[b * SEQ + t0:b * SEQ + t0 + TT, :], TT)
                for l in layers:
                    run_layer(l, dict(NT=TT, kind="p", b=b, t0=t0))
                    if ti == SEQ // TT - 1:
                        out_states_prompt(l, b)
                store_x(yp_d[b * SEQ + t0:b * SEQ + t0 + TT, :], TT)
        if "s" not in groups:
            return
        load_x(xs_d, 64)
        for l in layers:
            m = P.mark()
            T = sample_ctx(l)
            run_layer(l, T)
            out_states_sample(l, T)
            P.release(m)
        store_x(ys_d, 64)

    st = dict(locals())
    return st


N_CORES = 8
_WNAMES = ["norm_g", "ffn_w_in", "ffn_w_out", "w_in", "w_out", "gdn_conv_w", "gdn_a_log", "gdn_dt_bias", "gdn_norm_g",
           "mlp_ln_g", "mlp_ln_b", "mlp_ws", "mlp_bs", "ssm_conv_w", "ssm_conv_b", "ssm_a_log", "ssm_dt_bias", "ssm_d",
           "ssm_norm_g", "mla_q_norm_g", "mla_w_uq", "mla_kv_norm_g", "mla_w_uk", "mla_w_uv"]


def core_inputs(inputs, cfg, c, consts):
    NPS, NSS = cfg["NPS"], cfg["NSS"]
    f = lambda a: np.ascontiguousarray(np.asarray(a, dtype=np.float32))
    m = {}
    m["xp"] = f(inputs["x_prompt"][c * NPS:(c + 1) * NPS]).reshape(-1, D)
    m["xs"] = f(inputs["x_sample"][c * NSS:(c + 1) * NSS]).reshape(-1, D)
    m["cache_ckv"] = inputs["cache_ckv"]
    m["cache_kpe"] = inputs["cache_kpe"]
    m["ptab"] = np.ascontiguousarray(np.asarray(inputs["page_table"][c * NSS:(c + 1) * NSS], dtype=np.int32)).reshape(1, -1)
    m["state_gdn_s"] = f(inputs["state_gdn_s"][:, c * NSS:(c + 1) * NSS])
    m["state_gdn_conv"] = f(inputs["state_gdn_conv"][:, c * NSS:(c + 1) * NSS]).reshape(2, -1, 768)
    m["state_ssm_h"] = f(inputs["state_ssm_h"][:, c * NSS:(c + 1) * NSS]).reshape(2, -1, 128)
    m["state_ssm_conv"] = f(inputs["state_ssm_conv"][:, c * NSS:(c + 1) * NSS]).reshape(2, -1, 768)
    for k in _WNAMES:
        m[k] = inputs[k]
    for k, v in consts.items():
        m["c_" + k] = v
    return m


def assemble(results, cfg):
    NPS, SEQ, NSS = cfg["NPS"], cfg["SEQ"], cfg["NSS"]
    cat = lambda xs, ax: np.concatenate(xs, axis=ax)
    R_ = results
    out = [
        cat([r["yp"].reshape(NPS, SEQ, D) for r in R_], 0),
        cat([r["ys"].reshape(NSS, 4, D) for r in R_], 0),
        cat([r["ckv_p"].reshape(2, NPS, SEQ, 128) for r in R_], 1),
        cat([r["kpe_p"].reshape(2, NPS, SEQ, 32) for r in R_], 1),
        cat([r["gs_p"].reshape(2, NPS, 4, 64, 64) for r in R_], 1),
        cat([r["gc_p"].reshape(2, NPS, 3, 768) for r in R_], 1),
        cat([r["sh_p"].reshape(2, NPS, 4, 64, 128) for r in R_], 1),
        cat([r["sc_p"].reshape(2, NPS, 3, 768) for r in R_], 1),
        cat([r["ckv_s"].reshape(2, NSS, 4, 128) for r in R_], 1),
        cat([r["kpe_s"].reshape(2, NSS, 4, 32) for r in R_], 1),
        cat([r["gs_s"].reshape(2, NSS, 4, 64, 64) for r in R_], 1),
        cat([r["gc_s"].reshape(2, NSS, 3, 768) for r in R_], 1),
        cat([r["sh_s"].reshape(2, NSS, 4, 64, 128) for r in R_], 1),
        cat([r["sc_s"].reshape(2, NSS, 3, 768) for r in R_], 1),
        cat([r["mv_s"].reshape(2, NSS, 4, 256) for r in R_], 1),
    ]
    return tuple(np.ascontiguousarray(o.astype(np.float32)) for o in out)


def kernel(**inputs):
    inputs = {k: np.asarray(v) for k, v in inputs.items()}
    B, SEQ = inputs["x_prompt"].shape[0], inputs["x_prompt"].shape[1]
    DB = inputs["x_sample"].shape[0]
    NPG = inputs["page_table"].shape[1]
    NPHYS = inputs["cache_ckv"].shape[1]
    cfg = dict(NPS=B // N_CORES, SEQ=SEQ, NSS=DB // N_CORES, PAST=NPG * 128, NPHYS=NPHYS, TT=512)
    st = build_program(cfg)
    st["run_all"]()
    st["P"].build()
    consts = make_consts(cfg)
    in_maps = [core_inputs(inputs, cfg, c, consts) for c in range(N_CORES)]
    res = run_bass_kernel_spmd(st["nc"], in_maps, core_ids=list(range(N_CORES)))
    return assemble(res.results, cfg)
```

```python
import math
import numpy as np
import ml_dtypes
import concourse.bass as bass
import concourse.mybir as mybir
from concourse.bass_utils import run_bass_kernel_spmd

F32 = mybir.dt.float32
BF16 = mybir.dt.bfloat16
I32 = mybir.dt.int32
AF = mybir.ActivationFunctionType
ALU = mybir.AluOpType
AX = mybir.AxisListType

ENGS = ("tensor", "vector", "scalar", "gpsimd", "sync")
N_DMA_SEMS = 48

D = 1024
DC = 8
FF = 2816
FC = 22
EPS = 1e-6
NEG = -30000.0
C_QKV, C_ZA, C_GA, C_U, C_V, C_ZC, C_XBC, C_DT, C_CQ, C_CKV, C_KPE = 0, 768, 1024, 1032, 1288, 1544, 1800, 2568, 2572, 2828, 2956
IN_COLS = 2988
MLA_SCALE = 96 ** -0.5


class Tl:
    def __init__(self, t, k):
        self.t = t
        self.k = k


class Prog:
    def __init__(self, nc, arena_f=0, arena_b=0):
        self.nc = nc
        self.ops = []
        self.n_alloc = 0
        self.af = nc.alloc_sbuf_tensor("arena_f", [128, arena_f], F32) if arena_f else None
        self.ab = nc.alloc_sbuf_tensor("arena_b", [128, arena_b], BF16) if arena_b else None
        self.af_n, self.ab_n = arena_f, arena_b
        self.af_off = 0
        self.ab_off = 0
        self.af_max = 0
        self.ab_max = 0
        self.dbg_outs = {}
        self.psum_tokens = set()

    def sb(self, shape, dtype=F32, name=None):
        self.n_alloc += 1
        nm = name or f"sb{self.n_alloc}"
        return Tl(self.nc.alloc_sbuf_tensor(nm, list(shape), dtype)[:], nm)

    def ps(self, shape, dtype=F32, name=None):
        self.n_alloc += 1
        nm = name or f"ps{self.n_alloc}"
        self.psum_tokens.add(nm)
        return Tl(self.nc.alloc_psum_tensor(nm, list(shape), dtype)[:], nm)

    def mark(self):
        return (self.af_off, self.ab_off)

    def release(self, m):
        self.af_off, self.ab_off = m
        self.barrier()

    def tmp(self, shape, dtype=F32):
        n = 1
        for s in shape[1:]:
            n *= s
        n = (n + 7) // 8 * 8
        self.n_alloc += 1
        if dtype == F32:
            base, off = self.af, self.af_off
            self.af_off += n
            self.af_max = max(self.af_max, self.af_off)
            assert self.af_off <= self.af_n, ("arena_f overflow", self.af_off, self.af_n)
        else:
            base, off = self.ab, self.ab_off
            self.ab_off += n
            self.ab_max = max(self.ab_max, self.ab_off)
            assert self.ab_off <= self.ab_n, ("arena_b overflow", self.ab_off, self.ab_n)
        n0 = 1
        for s in shape[1:]:
            n0 *= s
        v = base[0:shape[0], off:off + n0]
        if len(shape) == 3:
            v = v.rearrange("p (a b) -> p a b", a=shape[1])
        elif len(shape) == 4:
            v = v.rearrange("p (a b c) -> p a b c", a=shape[1], b=shape[2])
        return Tl(v, f"tmp{self.n_alloc}")

    def op(self, eng, fn, r=(), w=(), dma=False, persist=False):
        pr = [t for t in r if t in self.psum_tokens]
        if pr:
            w = tuple(w) + tuple(pr)
        self.ops.append(dict(eng=eng, fn=fn, r=tuple(r), w=tuple(w), dma=dma, persist=persist))

    def barrier(self):
        self.ops.append(dict(barrier=True))

    def dma(self, out, in_, r=(), w=(), eng=None, persist=False, **kw):
        if eng is None:
            eng = "sync" if out.dtype == in_.dtype else "gpsimd"
        self.op(eng, lambda e: e.dma_start(out=out, in_=in_, **kw), r, w, dma=True, persist=persist)

    def mm(self, out, lhsT, rhs, start=True, stop=True, r=(), w=(), **kw):
        self.op("tensor", lambda e: e.matmul(out, lhsT, rhs, start=start, stop=stop, **kw), r, w)

    def tr(self, out, in_, ident, r=(), w=()):
        self.op("tensor", lambda e: e.transpose(out, in_, ident), r, w)

    def act(self, out, in_, func, r=(), w=(), **kw):
        self.op("scalar", lambda e: e.activation(out, in_, func, **kw), r, w)

    def tt(self, out, in0, in1, op, r=(), w=(), eng="vector"):
        self.op(eng, lambda e: e.tensor_tensor(out, in0, in1, op), r, w)

    def ts(self, out, in0, s1, s2, op0, op1=None, r=(), w=(), eng="vector"):
        if op1 is None:
            self.op(eng, lambda e: e.tensor_scalar(out, in0, s1, s2, op0), r, w)
        else:
            self.op(eng, lambda e: e.tensor_scalar(out, in0, s1, s2, op0, op1), r, w)

    def stt(self, out, in0, scalar, in1, op0, op1, r=(), w=(), eng="vector"):
        self.op(eng, lambda e: e.scalar_tensor_tensor(out, in0, scalar, in1, op0, op1), r, w)

    def copy(self, out, in_, r=(), w=(), eng="vector"):
        if eng == "scalar":
            self.op(eng, lambda e: e.copy(out, in_), r, w)
        else:
            self.op(eng, lambda e: e.tensor_copy(out, in_), r, w)

    def memset(self, ap, val, w=(), eng="vector"):
        self.op(eng, lambda e: e.memset(ap, val), (), w)

    def recip(self, out, in_, r=(), w=()):
        self.op("vector", lambda e: e.reciprocal(out, in_), r, w)

    def red(self, out, in_, r=(), w=(), op=None):
        self.op("vector", lambda e: e.tensor_reduce(out, in_, AX.X, op or ALU.add), r, w)

    def dbg(self, name, ap, r=()):
        shp = list(ap.shape)
        d = self.nc.dram_tensor("dbg_" + name, shp, F32, kind="ExternalOutput").ap()
        self.dbg_outs[name] = shp
        self.dma(d, ap, r=r)

    def hoist(self):
        new = []
        for o in self.ops:
            if o.get("persist") and o.get("dma") and not o.get("barrier"):
                btok = o["w"][0]
                pos = None
                for idx in range(len(new) - 1, max(-1, len(new) - 6000), -1):
                    n = new[idx]
                    if n.get("barrier"):
                        continue
                    if btok in n["r"] or btok in n["w"]:
                        pos = idx
                        break
                if pos is None:
                    new.append(o)
                else:
                    new.insert(pos + 1, o)
            else:
                new.append(o)
        self.ops = new

    def build(self):
        nc = self.nc
        self.hoist()
        ops = self.ops
        last_w = {}
        readers = {}
        last_on_eng = {}
        pend_dma = []
        pend_prev = []
        barrier_deps = None
        first_after = {}
        for i, o in enumerate(ops):
            if o.get("barrier"):
                barrier_deps = set(last_on_eng.values()) | set(pend_dma) | set(pend_prev)
                first_after = {e: True for e in ENGS}
                pend_prev = pend_dma
                pend_dma = []
                continue
            deps = set()
            for t in o["r"]:
                if t in last_w:
                    deps.add(last_w[t])
            for t in o["w"]:
                if t in last_w:
                    deps.add(last_w[t])
                for j in readers.get(t, {}).values():
                    deps.add(j)
            if barrier_deps is not None and first_after.get(o["eng"]):
                deps |= barrier_deps
                first_after[o["eng"]] = False
            deps.discard(i)
            if o["eng"] == "tensor" and not o["dma"]:
                deps = {j for j in deps if not (ops[j]["eng"] == "tensor" and not ops[j]["dma"])}
            o["deps"] = deps
            for t in o["w"]:
                last_w[t] = i
                readers[t] = {}
            for t in o["r"]:
                key = ("dma", i) if o["dma"] else o["eng"]
                readers.setdefault(t, {})[key] = i
            if o["dma"]:
                if not o["persist"]:
                    pend_dma.append(i)
            else:
                last_on_eng[o["eng"]] = i
        ops = [o for o in ops if not o.get("barrier")]
        idx_map = {}
        k = 0
        for i, o in enumerate(self.ops):
            if not o.get("barrier"):
                idx_map[i] = k
                k += 1
        for o in ops:
            o["deps"] = {idx_map[j] for j in o["deps"]}
            o["sig"] = False
        for o in ops:
            for j in o["deps"]:
                if not ops[j]["dma"]:
                    ops[j]["sig"] = True
        esem = {e: nc.alloc_semaphore(f"e_{e}") for e in ENGS}
        dsem = [nc.alloc_semaphore(f"d_{k}") for k in range(N_DMA_SEMS)]
        ecount = {e: 0 for e in ENGS}
        dcount = [0] * N_DMA_SEMS
        nd = 0
        for o in ops:
            if o["dma"]:
                k = nd % N_DMA_SEMS
                nd += 1
                o["dsem"] = k
                o["dprev"] = dcount[k]
                dcount[k] += 16
                o["dval"] = dcount[k]
            elif o["sig"]:
                ecount[o["eng"]] += 1
                o["eval"] = ecount[o["eng"]]
        ewaited = {e: {f: 0 for f in ENGS} for e in ENGS}
        dwaited = {e: [0] * N_DMA_SEMS for e in ENGS}
        for o in ops:
            e = o["eng"]
            waits = []
            need_e = {}
            need_d = {}
            for j in o["deps"]:
                dd = ops[j]
                if dd["dma"]:
                    need_d[dd["dsem"]] = max(need_d.get(dd["dsem"], 0), dd["dval"])
                else:
                    need_e[dd["eng"]] = max(need_e.get(dd["eng"], 0), dd["eval"])
            if o["dma"] and o["dprev"] > 0:
                need_d[o["dsem"]] = max(need_d.get(o["dsem"], 0), o["dprev"])
            for f, v in need_e.items():
                if ewaited[e][f] < v:
                    ewaited[e][f] = v
                    waits.append(("e", f, v))
            for k, v in need_d.items():
                if dwaited[e][k] < v:
                    dwaited[e][k] = v
                    waits.append(("d", k, v))
            o["waits"] = waits
        final_d = list(dcount)
        self.stats = dict(n_ops=len(ops), ecount=dict(ecount), n_dma=nd,
                          per_eng={e: sum(1 for o in ops if o["eng"] == e) for e in ENGS},
                          af_max=self.af_max, ab_max=self.ab_max)

        def run_engine(ename):
            def body(eng):
                for o in ops:
                    if o["eng"] != ename:
                        continue
                    for kind, a, v in o["waits"]:
                        eng.wait_ge(esem[a] if kind == "e" else dsem[a], v)
                    ins = o["fn"](eng)
                    if o["dma"]:
                        ins.then_inc(dsem[o["dsem"]], 16)
                    elif o["sig"]:
                        ins.then_inc(esem[ename], 1)
                if ename == "sync":
                    for k in range(N_DMA_SEMS):
                        if final_d[k] > 0:
                            eng.wait_ge(dsem[k], final_d[k])
            return body

        with nc.Block() as block:
            block.tensor(run_engine("tensor"))
            block.vector(run_engine("vector"))
            block.scalar(run_engine("scalar"))
            block.gpsimd(run_engine("gpsimd"))
            block.sync(run_engine("sync"))


def _seg_consts(n, seg):
    idx = np.arange(n)
    same = (idx[:, None] // seg) == (idx[None, :] // seg)
    tri = (same & (idx[:, None] <= idx[None, :])).astype(np.float32)
    segones = same.astype(np.float32)
    nm_strict = np.where(same & (idx[None, :] > idx[:, None]), 0.0, NEG).astype(np.float32)
    nm_nonstrict = np.where(same & (idx[None, :] >= idx[:, None]), 0.0, NEG).astype(np.float32)
    pm_strict = np.where(same & (idx[:, None] > idx[None, :]), 0.0, -NEG).astype(np.float32)
    return tri, segones, nm_strict, nm_nonstrict, pm_strict


def make_consts(cfg):
    c = {}
    c["ident"] = np.eye(128, dtype=np.float32)
    tri, so, nms, nmn, pms = _seg_consts(64, 64)
    c["gp"] = np.stack([tri, so, nms, nmn, pms])
    tri, so, nms, nmn, pms = _seg_consts(64, 4)
    c["gs"] = np.stack([tri, so, nms, nmn, pms])
    tri, so, nms, nmn, pms = _seg_consts(128, 128)
    c["cp"] = np.stack([tri, so, nmn])
    c["segmask"] = (np.arange(64)[:, None] // 4 == np.arange(16)[None, :]).astype(np.float32)
    half = 16
    inv = (10000.0 ** (-np.arange(half, dtype=np.float32) / half)).astype(np.float32)

    def rt(pos):
        ang = pos.astype(np.float32)[:, None] * inv[None, :]
        cs, sn = np.cos(ang).astype(np.float32), np.sin(ang).astype(np.float32)
        return np.stack([np.concatenate([cs, cs], 1), np.concatenate([-sn, sn], 1)], 1).astype(np.float32)

    c["rope_p"] = rt(np.arange(cfg["SEQ"]))
    c["rope_s"] = rt(cfg["PAST"] + (np.arange(64) % 4))
    k = np.arange(128)
    m = np.where(k[:, None] > k[None, :], NEG, 0.0).astype(np.float32)
    c["mla_diag"] = np.tile(m, (1, 4))
    j = np.arange(64)
    mm = ((j[:, None] // 4 == j[None, :] // 4) & (j[:, None] % 4 <= j[None, :] % 4)).astype(np.float32)
    c["mla_new"] = np.tile(mm, (1, 4))
    c["ws_mask"] = np.triu(np.ones((128, 128), np.float32))
    jj = np.arange(64)
    c["wsbd_mask"] = ((jj[:, None] // 4 == jj[None, :] // 4) & (jj[:, None] % 4 <= jj[None, :] % 4)).astype(np.float32)
    rep = np.zeros((4, 64), np.float32)
    rep[np.arange(64) % 4, np.arange(64)] = 1.0
    c["rep4"] = rep
    return c


CONST_SHAPES = None


def build_program(cfg, dbg=(), upto=99):
    NPS, SEQ, NSS, PAST, NPHYS, TT = cfg["NPS"], cfg["SEQ"], cfg["NSS"], cfg["PAST"], cfg["NPHYS"], cfg["TT"]
    NPG = PAST // 128
    NTS = NSS * 4
    assert NTS == 64 and SEQ % TT == 0 and TT % 128 == 0
    nc = bass.Bass("TRN2", target_bir_lowering=False)
    P = Prog(nc, arena_f=16384, arena_b=20480)
    DI = lambda name, shape, dt=F32: nc.dram_tensor(name, list(shape), dt, kind="ExternalInput").ap()
    DO = lambda name, shape, dt=F32: nc.dram_tensor(name, list(shape), dt, kind="ExternalOutput").ap()
    DS = lambda name, shape, dt=BF16: nc.dram_tensor(name, list(shape), dt, kind="Internal").ap()

    xp_d = DI("xp", [NPS * SEQ, D])
    xs_d = DI("xs", [NTS, D])
    cckv_d = DI("cache_ckv", [2, NPHYS, 128, 128])
    ckpe_d = DI("cache_kpe", [2, NPHYS, 128, 32])
    ptab_d = DI("ptab", [1, NSS * NPG], I32)
    sgs_d = DI("state_gdn_s", [2, NSS, 4, 64, 64])
    sgc_d = DI("state_gdn_conv", [2, NSS * 3, 768])
    ssh_d = DI("state_ssm_h", [2, NSS * 4 * 64, 128])
    ssc_d = DI("state_ssm_conv", [2, NSS * 3, 768])
    W = {}
    for nm, shp in [("norm_g", [2, 6, D]), ("ffn_w_in", [2, 2, D, 2 * FF]), ("ffn_w_out", [2, 2, FF, D]),
                    ("w_in", [2, D, IN_COLS]), ("w_out", [2, D, D]), ("gdn_conv_w", [2, 4, 768]),
                    ("gdn_a_log", [2, 4]), ("gdn_dt_bias", [2, 4]), ("gdn_norm_g", [2, 64]),
                    ("mlp_ln_g", [2, 256]), ("mlp_ln_b", [2, 256]), ("mlp_ws", [2, 4, 128, 128]),
                    ("mlp_bs", [2, 4, 128]), ("ssm_conv_w", [2, 4, 768]), ("ssm_conv_b", [2, 768]),
                    ("ssm_a_log", [2, 4]), ("ssm_dt_bias", [2, 4]), ("ssm_d", [2, 4]), ("ssm_norm_g", [2, 256]),
                    ("mla_q_norm_g", [2, 256]), ("mla_w_uq", [2, 256, 384]), ("mla_kv_norm_g", [2, 128]),
                    ("mla_w_uk", [2, 4, 128, 64]), ("mla_w_uv", [2, 4, 128, 64])]:
        W[nm] = DI(nm, shp)
    cshapes = {k: v.shape for k, v in make_consts(cfg).items()}
    C = {k: DI("c_" + k, list(s)) for k, s in cshapes.items()}

    yp_d = DO("yp", [NPS * SEQ, D])
    ys_d = DO("ys", [NTS, D])
    ckvp_d = DO("ckv_p", [2, NPS * SEQ, 128])
    kpep_d = DO("kpe_p", [2, NPS * SEQ, 32])
    gsp_d = DO("gs_p", [2, NPS, 4, 64, 64])
    gcp_d = DO("gc_p", [2, NPS, 3, 768])
    shp_d = DO("sh_p", [2, NPS * 4 * 64, 128])
    scp_d = DO("sc_p", [2, NPS, 3, 768])
    ckvs_d = DO("ckv_s", [2, NTS, 128])
    kpes_d = DO("kpe_s", [2, NTS, 32])
    gss_d = DO("gs_s", [2, NSS, 4, 64, 64])
    gcs_d = DO("gc_s", [2, NSS, 3, 768])
    shs_d = DO("sh_s", [2, NSS * 4 * 64, 128])
    scs_d = DO("sc_s", [2, NSS, 3, 768])
    mvs_d = DO("mv_s", [2, NTS, 256])

    s_fin = DS("s_fin", [2, 2, D, 2 * FF])
    s_fout = DS("s_fout", [2, 2, FF, D])
    s_win = DS("s_win", [2, D, IN_COLS])
    s_wout = DS("s_wout", [2, D, D])
    for l in range(2):
        for i in range(2):
            for hh in range(2):
                P.dma(s_fin[l, i, hh * 512:(hh + 1) * 512, :], W["ffn_w_in"][l, i, hh * 512:(hh + 1) * 512, :], w=[("s_fin", l, i)])
            P.dma(s_fout[l, i], W["ffn_w_out"][l, i], w=[("s_fout", l, i)])
        P.dma(s_win[l], W["w_in"][l], w=[("s_win", l)])
        P.dma(s_wout[l], W["w_out"][l], w=[("s_wout", l)])

    ident = P.sb([128, 128], F32, "ident")
    identb = P.sb([128, 128], BF16, "identb")
    ones_bf = P.sb([128, 128], BF16, "ones_bf")
    eps_t = P.sb([128, 1], F32, "eps_t")
    gp = P.sb([64, 5, 64], F32, "gp")
    gs = P.sb([64, 5, 64], F32, "gsc")
    cp = P.sb([128, 3, 128], F32, "cp")
    segmask = P.sb([64, 16], F32, "segmask")
    rope_p = P.sb([128, SEQ // 128, 64], F32, "rope_p")
    rope_s = P.sb([64, 64], F32, "rope_s")
    mla_diag = P.sb([128, 512], BF16, "mla_diag")
    mla_new = P.sb([64, 256], F32, "mla_new")
    P.dma(ident.t, C["ident"], w=[ident.k])
    P.copy(identb.t, ident.t, r=[ident.k], w=[identb.k])
    P.memset(ones_bf.t, 1.0, w=[ones_bf.k])
    P.memset(eps_t.t, EPS, w=[eps_t.k])
    P.dma(gp.t, C["gp"].rearrange("a p q -> p a q"), w=[gp.k])
    P.dma(gs.t, C["gs"].rearrange("a p q -> p a q"), w=[gs.k])
    P.dma(cp.t, C["cp"].rearrange("a p q -> p a q"), w=[cp.k])
    P.dma(segmask.t, C["segmask"], w=[segmask.k])
    P.dma(rope_p.t, C["rope_p"].rearrange("(b p) a e -> p b (a e)", p=128), w=[rope_p.k])
    P.dma(rope_s.t, C["rope_s"].rearrange("p a e -> p (a e)"), w=[rope_s.k])
    P.dma(mla_diag.t, C["mla_diag"], w=[mla_diag.k])
    P.dma(mla_new.t, C["mla_new"], w=[mla_new.k])

    NC_ = dict(allow_slow_non_contiguous=True)
    ng = P.sb([128, 2, 6, 8], F32, "ng")
    P.dma(ng.t, W["norm_g"].rearrange("l i (c p) -> p l i c", p=128), w=[ng.k], **NC_)
    for l in range(2):
        for i in (1, 5):
            P.ts(ng.t[:, l, i, :], ng.t[:, l, i, :], 0.5, None, ALU.mult, r=[ng.k], w=[ng.k])
    wg = P.sb([128, 2, 8, 12], BF16, "wg")
    for l in range(2):
        P.dma(wg.t[:, l, :, 0:8], W["w_in"][l, :, C_GA:C_GA + 8].rearrange("(c p) n -> p c n", p=128), w=[wg.k])
        P.dma(wg.t[:, l, :, 8:12], W["w_in"][l, :, C_DT:C_DT + 4].rearrange("(c p) n -> p c n", p=128), w=[wg.k])
    cwA = P.sb([64, 2, 12, 4], F32, "cwA")
    cwC = P.sb([128, 2, 6, 4], F32, "cwC")
    for l in range(2):
        for k in range(4):
            P.dma(cwA.t[:, l, :, k], W["gdn_conv_w"][l, k].rearrange("(c p) -> p c", p=64), w=[cwA.k], **NC_)
            P.dma(cwC.t[:, l, :, k], W["ssm_conv_w"][l, k].rearrange("(c p) -> p c", p=128), w=[cwC.k], **NC_)
    cbC = P.sb([128, 2, 6], F32, "cbC")
    P.dma(cbC.t, W["ssm_conv_b"].rearrange("l (c p) -> p l c", p=128), w=[cbC.k], **NC_)
    gnA = P.sb([64, 2], F32, "gnA")
    P.dma(gnA.t, W["gdn_norm_g"].rearrange("l p -> p l"), w=[gnA.k], **NC_)
    gnC = P.sb([128, 2, 2], F32, "gnC")
    P.dma(gnC.t, W["ssm_norm_g"].rearrange("l (c p) -> p l c", p=128), w=[gnC.k], **NC_)
    dC = P.sb([128, 2, 2], F32, "dC")
    for l in range(2):
        for h in range(4):
            P.dma(dC.t[(h % 2) * 64:(h % 2) * 64 + 64, l, h // 2:h // 2 + 1], W["ssm_d"][l:l + 1, h:h + 1].to_broadcast([64, 1]), w=[dC.k])
    rows = P.sb([128, 2, 4, 4], F32, "rows")
    for l in range(2):
        for j, nm in enumerate(["gdn_dt_bias", "gdn_a_log", "ssm_dt_bias", "ssm_a_log"]):
            P.dma(rows.t[:, l, j, :], W[nm][l:l + 1, :].to_broadcast([128, 4]), w=[rows.k])
    for j in (1, 3):
        P.act(rows.t[:, :, j, :], rows.t[:, :, j, :], AF.Exp, r=[rows.k], w=[rows.k])
        P.ts(rows.t[:, :, j, :], rows.t[:, :, j, :], -1.0, None, ALU.mult, r=[rows.k], w=[rows.k])
    lnr = P.sb([128, 2, 2, 256], F32, "lnr")
    gqr = P.sb([128, 2, 256], F32, "gqr")
    gkr = P.sb([128, 2, 128], F32, "gkr")
    for l in range(2):
        P.dma(lnr.t[:, l, 0, :], W["mlp_ln_g"][l:l + 1, :].to_broadcast([128, 256]), w=[lnr.k])
        P.dma(lnr.t[:, l, 1, :], W["mlp_ln_b"][l:l + 1, :].to_broadcast([128, 256]), w=[lnr.k])
        P.dma(gqr.t[:, l, :], W["mla_q_norm_g"][l:l + 1, :].to_broadcast([128, 256]), w=[gqr.k])
        P.dma(gkr.t[:, l, :], W["mla_kv_norm_g"][l:l + 1, :].to_broadcast([128, 128]), w=[gkr.k])
    wuq = P.sb([128, 2, 2, 384], BF16, "wuq")
    wuv = P.sb([128, 2, 4, 64], BF16, "wuv")
    wukT = P.sb([64, 2, 4, 128], BF16, "wukT")
    for l in range(2):
        P.dma(wuq.t[:, l], W["mla_w_uq"][l].rearrange("(c p) n -> p c n", p=128), w=[wuq.k])
        P.dma(wuv.t[:, l], W["mla_w_uv"][l].rearrange("h r d -> r h d"), w=[wuv.k])
    bsP = P.sb([128, 2, 2, 128], F32, "bsP")
    bsS = P.sb([128, 2, 2, 16, 4], F32, "bsS")
    for l in range(2):
        for g in range(4):
            pr = slice((g % 2) * 64, (g % 2) * 64 + 64)
            P.dma(bsP.t[pr, l, g // 2, :], W["mlp_bs"][l, g:g + 1, :].to_broadcast([64, 128]), w=[bsP.k])
            P.dma(bsS.t[pr, l, g // 2, :, :], W["mlp_bs"][l, g:g + 1, 0:4].unsqueeze(1).to_broadcast([64, 16, 4]), w=[bsS.k])
    wsT = P.sb([128, 2, 4, 128], BF16, "wsT")
    wsbd = P.sb([64, 2, 4, 64], BF16, "wsbd")

    ptab_sb = P.sb([NPG, NSS], I32, "ptab_sb")
    P.dma(ptab_sb.t, ptab_d.rearrange("o (s p) -> p (o s)", p=NPG), w=[ptab_sb.k], **NC_)
    ptab4 = P.sb([NPG, NSS], I32, "ptab4")
    P.ts(ptab4.t, ptab_sb.t, 2, None, ALU.logical_shift_left, r=[ptab_sb.k], w=[ptab4.k])
    psb = [P.ps([128, 512], F32, f"psb{i}") for i in range(7)]
    psh = P.ps([128, 1024], BF16, "psh")

    m0 = P.mark()
    wsm = P.tmp([128, 128], F32)
    wbm = P.tmp([64, 64], F32)
    rep4 = P.tmp([4, 64], F32)
    P.dma(wsm.t, C["ws_mask"], w=[wsm.k])
    P.dma(wbm.t, C["wsbd_mask"], w=[wbm.k])
    P.dma(rep4.t, C["rep4"], w=[rep4.k])
    for l in range(2):
        uk = P.tmp([128, 4, 64], F32)
        P.dma(uk.t, W["mla_w_uk"][l].rearrange("h r d -> r h d"), w=[uk.k])
        for h in range(4):
            P.tr(psb[0].t[0:64, h * 128:(h + 1) * 128], uk.t[:, h, :], ident.t, r=[uk.k, ident.k], w=[psb[0].k])
        P.copy(wukT.t[:, l], psb[0].t[0:64, :].rearrange("p (h r) -> p h r", h=4), r=[psb[0].k], w=[wukT.k])
        wsn = P.tmp([128, 4, 128], F32)
        P.dma(wsn.t, W["mlp_ws"][l].rearrange("g p q -> p g q"), w=[wsn.k])
        for g in range(4):
            P.tr(psb[1].t[:, g * 128:(g + 1) * 128], wsn.t[:, g, :], ident.t, r=[wsn.k, ident.k], w=[psb[1].k])
        P.tt(wsT.t[:, l], psb[1].t[:].rearrange("q (g p) -> q g p", g=4), wsm.t.unsqueeze(1).to_broadcast([128, 4, 128]), ALU.mult,
             r=[psb[1].k, wsm.k], w=[wsT.k])
        w4 = P.tmp([4, 4, 4], F32)
        P.dma(w4.t, W["mlp_ws"][l, :, 0:4, 0:4].rearrange("g b a -> b g a"), w=[w4.k])
        y4 = P.tmp([4, 4, 64], F32)
        for g in range(4):
            P.mm(psb[2].t[0:4, g * 64:(g + 1) * 64], w4.t[:, g, :], rep4.t, r=[w4.k, rep4.k], w=[psb[2].k])
        P.copy(y4.t, psb[2].t[0:4, 0:256].rearrange("a (g p) -> a g p", g=4), r=[psb[2].k], w=[y4.k])
        for g in range(4):
            P.mm(psb[3].t[0:64, g * 64:(g + 1) * 64], rep4.t, y4.t[:, g, :], r=[y4.k, rep4.k], w=[psb[3].k])
        P.tt(wsbd.t[:, l], psb[3].t[0:64, 0:256].rearrange("q (g p) -> q g p", g=4), wbm.t.unsqueeze(1).to_broadcast([64, 4, 64]), ALU.mult,
             r=[psb[3].k, wbm.k], w=[wsbd.k])
    P.release(m0)

    xT = P.sb([128, 8, TT], F32, "xT")
    NBUF = 3
    wring = [P.sb([128, 4096], BF16, f"wring{i}") for i in range(NBUF)]
    wctr = [0]

    def wstage(loads):
        b = wring[wctr[0] % NBUF]
        wctr[0] += 1
        for dstf, src, stok in loads:
            P.dma(dstf(b.t), src, r=[stok], w=[b.k], eng="sync", persist=True)
        return b

    Sst = P.sb([64, 2, 4, 64], F32, "Sst")
    hTst = P.sb([128, 2, 4, 64], F32, "hTst")
    tailA = P.sb([64, 2, 12, 3], F32, "tailA")
    tailC = P.sb([128, 2, 6, 3], F32, "tailC")
    KT = P.sb([128, 2, SEQ], BF16, "KT")
    Kt = P.sb([128, 2, SEQ // 128, 128], BF16, "Kt")
    KPT = P.sb([32, 2, SEQ], BF16, "KPT")

    def bank(i):
        return psb[i % 7]

    def rstd_from(ps_ap, pk, shape, scale, nparts=128):
        r_ = P.tmp(shape, F32)
        P.act(r_.t, ps_ap, AF.Sqrt, scale=scale, bias=eps_t.t[0:nparts, 0:1], r=[pk, eps_t.k], w=[r_.k])
        P.recip(r_.t, r_.t, r=[r_.k], w=[r_.k])
        return r_

    def prenorm(l, i, NT):
        xn = P.tmp([128, 8, NT], BF16)
        sq = P.tmp([128, 8, NT], BF16)
        P.act(sq.t, xT.t[:, :, 0:NT], AF.Square, r=[xT.k], w=[sq.k])
        for c in range(8):
            P.mm(psb[0].t[:, 0:NT], ones_bf.t, sq.t[:, c, :], start=(c == 0), stop=(c == 7), r=[sq.k, ones_bf.k], w=[psb[0].k])
        rstd = rstd_from(psb[0].t[:, 0:NT], psb[0].k, [128, NT], 1.0 / D)
        for c in range(8):
            P.stt(xn.t[:, c, :], xT.t[:, c, 0:NT], ng.t[:, l, i, c:c + 1], rstd.t, ALU.mult, ALU.mult,
                  r=[xT.k, rstd.k, ng.k], w=[xn.k])
        return xn, sq

    def postnorm_residual(ysb, sq, l, i, NT):
        for c in range(8):
            P.mm(psb[0].t[:, 0:NT], ones_bf.t, sq.t[:, c, :], start=(c == 0), stop=(c == 7), r=[sq.k, ones_bf.k], w=[psb[0].k])
        rstd = rstd_from(psb[0].t[:, 0:NT], psb[0].k, [128, NT], 1.0 / D)
        for c in range(8):
            P.stt(ysb.t[:, c, :], ysb.t[:, c, :], ng.t[:, l, i, c:c + 1], rstd.t, ALU.mult, ALU.mult,
                  r=[ysb.k, rstd.k, ng.k], w=[ysb.k])
            P.tt(xT.t[:, c, 0:NT], xT.t[:, c, 0:NT], ysb.t[:, c, :], ALU.add, r=[xT.k, ysb.k], w=[xT.k])

    def ffn(l, i, NT, stop_at=None):
        m = P.mark()
        xn, sq = prenorm(l, 0 if i == 0 else 4, NT)
        hT = P.tmp([128, FC, NT], BF16)
        sgs_ = [P.tmp([128, NT], BF16) for _ in range(2)]
        fin = s_fin[l, i].rearrange("(c p) f -> p c f", p=128)
        for j in range(11):
            wb = wstage([
                (lambda b: b.rearrange("p (c f) -> p c f", c=8)[:, :, 0:256], fin[:, :, 256 * j:256 * j + 256], ("s_fin", l, i)),
                (lambda b: b.rearrange("p (c f) -> p c f", c=8)[:, :, 256:512], fin[:, :, FF + 256 * j:FF + 256 * j + 256], ("s_fin", l, i)),
            ])
            wv = wb.t.rearrange("p (c f) -> p c f", c=8)
            for f in range(2):
                fc = 2 * j + f
                pg, pu = psb[1 + 2 * (fc % 3)], psb[2 + 2 * (fc % 3)]
                for c in range(8):
                    P.mm(pg.t[:, 0:NT], wv[:, c, f * 128:(f + 1) * 128], xn.t[:, c, :], start=(c == 0), stop=(c == 7), r=[wb.k, xn.k], w=[pg.k])
                for c in range(8):
                    P.mm(pu.t[:, 0:NT], wv[:, c, 256 + f * 128:256 + (f + 1) * 128], xn.t[:, c, :], start=(c == 0), stop=(c == 7), r=[wb.k, xn.k], w=[pu.k])
                sg = sgs_[fc % 2]
                P.act(sg.t, pg.t[:, 0:NT], AF.Silu, r=[pg.k], w=[sg.k])
                P.tt(hT.t[:, fc, :], sg.t, pu.t[:, 0:NT], ALU.mult, r=[sg.k, pu.k], w=[(hT.k, fc)])
        if stop_at == "win":
            hf = P.tmp([128, 4, NT], F32)
            P.copy(hf.t, hT.t[:, 0:4, :], r=[(hT.k, c) for c in range(4)], w=[hf.k])
            P.dbg("hT", hf.t, r=[hf.k])
            return
        ysb = P.tmp([128, 8, NT], F32)
        fo = s_fout[l, i].rearrange("(c p) d -> p c d", p=128)
        for p_ in range(2):
            for s_ in range(6):
                nf = min(4, FC - 4 * s_)
                wb = wstage([(lambda b, nf=nf: b.rearrange("p (c d) -> p c d", c=8)[:, 0:nf, :],
                              fo[:, 4 * s_:4 * s_ + nf, 512 * p_:512 * p_ + 512], ("s_fout", l, i))])
                wv = wb.t.rearrange("p (c d) -> p c d", c=8)
                for dcl in range(4):
                    for f in range(nf):
                        fc = 4 * s_ + f
                        P.mm(psb[1 + dcl].t[:, 0:NT], wv[:, f, dcl * 128:(dcl + 1) * 128], hT.t[:, fc, :], start=(fc == 0), stop=(fc == FC - 1),
                             r=[wb.k, (hT.k, fc)], w=[psb[1 + dcl].k])
            for dcl in range(4):
                c = 4 * p_ + dcl
                P.copy(ysb.t[:, c, :], psb[1 + dcl].t[:, 0:NT], r=[psb[1 + dcl].k], w=[ysb.k])
                P.act(sq.t[:, c, :], psb[1 + dcl].t[:, 0:NT], AF.Square, r=[psb[1 + dcl].k], w=[sq.k])
        if stop_at == "wout":
            P.dbg("ysb", ysb.t, r=[ysb.k])
            return
        postnorm_residual(ysb, sq, l, 1 if i == 0 else 5, NT)
        P.release(m)

    def load_x(src_rows, NT):
        m = P.mark()
        nb = min(128, NT)
        for b in range(NT // nb):
            xt = P.tmp([nb, D], F32)
            P.dma(xt.t, src_rows[b * nb:(b + 1) * nb, :], w=[xt.k])
            for cg in range(2):
                pb = psb[1 + (2 * b + cg) % 6]
                for c4 in range(4):
                    c = cg * 4 + c4
                    P.tr(pb.t[:, c4 * nb:(c4 + 1) * nb], xt.t[:, c * 128:(c + 1) * 128], ident.t[0:nb, 0:nb], r=[xt.k, ident.k], w=[pb.k])
                P.copy(xT.t[:, cg * 4:(cg + 1) * 4, b * nb:(b + 1) * nb], pb.t[:, 0:4 * nb].rearrange("p (c t) -> p c t", c=4),
                       r=[pb.k], w=[xT.k], eng=("scalar" if cg else "vector"))
        P.release(m)

    def store_x(dst_rows, NT):
        m = P.mark()
        nb = min(128, NT)
        for b in range(NT // nb):
            yt = P.tmp([nb, D], F32)
            for cg in range(2):
                pb = psb[1 + (2 * b + cg) % 6]
                for c4 in range(4):
                    c = cg * 4 + c4
                    P.tr(pb.t[0:nb, c4 * 128:(c4 + 1) * 128], xT.t[:, c, b * nb:(b + 1) * nb], ident.t, r=[xT.k, ident.k], w=[pb.k])
                P.copy(yt.t[:, cg * 512:(cg + 1) * 512], pb.t[0:nb, :], r=[pb.k], w=[yt.k], eng=("scalar" if cg else "vector"))
            P.dma(dst_rows[b * nb:(b + 1) * nb, :], yt.t, r=[yt.k])
        P.release(m)


    def softplus(x, shape, npart):
        a_ = P.tmp(shape, F32)
        P.stt(a_.t, x.t, -1.0, x.t, ALU.mult, ALU.max, r=[x.k], w=[a_.k])
        P.act(a_.t, a_.t, AF.Exp, scale=-1.0, r=[a_.k], w=[a_.k])
        P.act(a_.t, a_.t, AF.Ln, bias=1.0, r=[a_.k], w=[a_.k])
        P.ts(x.t, x.t, 0.0, None, ALU.max, r=[x.k], w=[x.k])
        P.tt(x.t, x.t, a_.t, ALU.add, r=[x.k, a_.k], w=[x.k])

    def proj_fm(dst_fn, wv, wk, col0, nch, m, xn, NT, bstart, evac):
        for j in range(nch):
            pb = bank(1 + (bstart + j) % 6)
            for c in range(8):
                P.mm(pb.t[0:m, 0:NT], wv[:, c, col0 + j * m:col0 + (j + 1) * m], xn.t[:, c, :], start=(c == 0), stop=(c == 7), r=[wk, xn.k], w=[pb.k])
            evac(j, pb.t[0:m, 0:NT], pb)

    def mixer(l, T):
        NT, kind = T["NT"], T["kind"]
        prm = kind == "p"
        m_all = P.mark()
        xn, sqn = prenorm(l, 2, NT)
        win = s_win[l].rearrange("(c p) n -> p c n", p=128)
        sw = ("s_win", l)
        as3 = lambda b_: b_.rearrange("p (c f) -> p c f", c=8)
        mixT = P.tmp([128, 6, NT], BF16)
        outA = P.tmp([64, 4, NT], BF16)
        NB64 = NT // 64
        nb = 128 if prm else 64
        NBk = NT // nb
        NSEG = 1 if prm else 16
        GC = gp if prm else gs
        i64 = ident.t[0:64, 0:64]
        W7 = 3 + NT if prm else 16 * 7

        def xview(xp_, j, i):
            if prm:
                return xp_.t[:, j, i:i + NT]
            return xp_.t[:, j, :].rearrange("p (s t) -> p s t", t=7)[:, :, i:i + 4]

        def oview(ap2):
            return ap2 if prm else ap2.rearrange("p (s t) -> p s t", t=4)

        mA = P.mark()
        qkv = P.tmp([64, 12, NT], F32)
        szA = P.tmp([64, 4, NT], BF16)
        wb1 = wstage([(lambda b_: as3(b_), win[:, :, 0:512], sw)])
        wb2 = wstage([(lambda b_: as3(b_), win[:, :, 512:1024], sw)])
        xp4 = P.tmp([64, 4, W7], F32)
        rrs = [P.tmp([64, NT], F32) for _ in range(2)]
        if not prm:
            rawA = P.tmp([64, 12, 16, 3], F32)
        for grp in range(3):
            wb_, c0 = (wb1, grp * 256) if grp < 2 else (wb2, 0)
            if prm:
                P.copy(xp4.t[:, :, 0:3], tailA.t[:, l, grp * 4:(grp + 1) * 4, :], r=[tailA.k], w=[xp4.k], eng="gpsimd")
            else:
                P.copy(xp4.t.rearrange("p j (s t) -> p j s t", t=7)[:, :, :, 0:3], T["tailA"].t[:, grp * 4:(grp + 1) * 4, :, :], r=[T["tailA"].k], w=[xp4.k], eng="gpsimd")

            def ev(j, ps_ap, pb, xp4=xp4):
                P.copy(xview(xp4, j, 3), oview(ps_ap), r=[pb.k], w=[xp4.k], eng="scalar")
            proj_fm(None, as3(wb_.t), wb_.k, c0, 4, 64, xn, NT, grp * 4, ev)
            for j in range(4):
                hc = grp * 4 + j
                ov = oview(qkv.t[:, hc, :])
                if prm:
                    P.ts(ov, xview(xp4, j, 0), cwA.t[:, l, hc, 0:1], None, ALU.mult, r=[xp4.k, cwA.k], w=[qkv.k])
                else:
                    P.tt(ov, xview(xp4, j, 0), cwA.t[:, l, hc, 0:1].unsqueeze(2).to_broadcast([64, 16, 4]), ALU.mult, r=[xp4.k, cwA.k], w=[qkv.k])
                for i in range(1, 4):
                    P.stt(ov, xview(xp4, j, i), cwA.t[:, l, hc, i:i + 1], ov, ALU.mult, ALU.add, r=[xp4.k, cwA.k, qkv.k], w=[qkv.k])
            if prm:
                P.copy(tailA.t[:, l, grp * 4:(grp + 1) * 4, :], xp4.t[:, :, NT:NT + 3], r=[xp4.k], w=[tailA.k], eng="gpsimd")
            else:
                P.copy(rawA.t[:, grp * 4:(grp + 1) * 4], xp4.t.rearrange("p j (s t) -> p j s t", t=7)[:, :, :, 4:7], r=[xp4.k], w=[rawA.k], eng="gpsimd")
        P.act(qkv.t, qkv.t, AF.Silu, r=[qkv.k], w=[qkv.k])
        if not prm:
            rowA = P.tmp([48, 768], F32)
            for hb in range(2):
                pb = bank(1 + hb)
                for j in range(6):
                    hc = hb * 6 + j
                    P.tr(pb.t[0:48, j * 64:(j + 1) * 64], rawA.t[:, hc].rearrange("p s t -> p (s t)"), i64, r=[rawA.k, ident.k], w=[pb.k])
                P.copy(rowA.t[:, hb * 384:(hb + 1) * 384], pb.t[0:48, 0:384], r=[pb.k], w=[rowA.k])
            P.dma(gcs_d[l].rearrange("s t c -> (s t) c"), rowA.t, r=[rowA.k])

        def evz(j, ps_ap, pb):
            P.act(szA.t[:, j, :], ps_ap, AF.Silu, r=[pb.k], w=[szA.k])
        proj_fm(None, as3(wb2.t), wb2.k, 256, 4, 64, xn, NT, 0, evz)
        sqk = Tl(sqn.t[0:64], sqn.k)
        P.act(sqk.t, qkv.t[:, 0:8, :], AF.Square, r=[qkv.k], w=[sqk.k])
        for hc in range(8):
            pb = bank(1 + hc % 6)
            P.mm(pb.t[0:64, 0:NT], ones_bf.t[0:64, 0:64], sqk.t[:, hc, :], r=[sqk.k, ones_bf.k], w=[pb.k])
            rr = rrs[hc % 2]
            P.act(rr.t, pb.t[0:64, 0:NT], AF.Sqrt, bias=eps_t.t[0:64, 0:1], r=[pb.k, eps_t.k], w=[rr.k])
            P.recip(rr.t, rr.t, r=[rr.k], w=[rr.k])
            P.stt(qkv.t[:, hc, :], qkv.t[:, hc, :], (0.125 if hc < 4 else 1.0), rr.t, ALU.mult, ALU.mult, r=[qkv.k, rr.k], w=[qkv.k])
        pgt = bank(0)
        for b_ in range(NB64):
            for c in range(8):
                P.mm(pgt.t[0:64, b_ * 8:(b_ + 1) * 8], xn.t[:, c, b_ * 64:(b_ + 1) * 64], wg.t[:, l, c, 0:8], start=(c == 0), stop=(c == 7), r=[xn.k, wg.k], w=[pgt.k])
        gx = P.tmp([64, NB64, 8], F32)
        pg3 = pgt.t[0:64, 0:NB64 * 8].rearrange("p (b n) -> p b n", n=8)
        P.tt(gx.t[:, :, 0:4], pg3[:, :, 0:4], rows.t[0:64, l, 0, :].unsqueeze(1).to_broadcast([64, NB64, 4]), ALU.add, r=[pgt.k, rows.k], w=[gx.k])
        P.ts(gx.t[:, :, 4:8], pg3[:, :, 4:8], -1.0, None, ALU.mult, r=[pgt.k], w=[gx.k])
        softplus(gx, [64, NB64, 8], 64)
        la = P.tmp([64, NB64, 4], F32)
        lnb = P.tmp([64, NB64, 4], F32)
        P.tt(la.t, gx.t[:, :, 0:4], rows.t[0:64, l, 1, :].unsqueeze(1).to_broadcast([64, NB64, 4]), ALU.mult, r=[gx.k, rows.k], w=[la.k])
        P.ts(lnb.t, gx.t[:, :, 4:8], -1.0, None, ALU.mult, r=[gx.k], w=[lnb.k])
        pc = bank(0)
        for b_ in range(NB64):
            P.mm(pc.t[0:64, b_ * 8:b_ * 8 + 4], GC.t[:, 0, :], la.t[:, b_, :], r=[GC.k, la.k], w=[pc.k])
            P.mm(pc.t[0:64, b_ * 8 + 4:b_ * 8 + 8], GC.t[:, 1, :], la.t[:, b_, :], r=[GC.k, la.k], w=[pc.k])
        gcm = P.tmp([64, NB64, 8], F32)
        P.copy(gcm.t, pc.t[0:64, 0:NB64 * 8].rearrange("p (b n) -> p b n", n=8), r=[pc.k], w=[gcm.k])
        negg = P.tmp([64, NB64, 4], F32)
        gpl = P.tmp([64, NB64, 4], F32)
        sc3 = P.tmp([64, NB64, 12], F32)
        P.ts(negg.t, gcm.t[:, :, 0:4], -1.0, None, ALU.mult, r=[gcm.k], w=[negg.k])
        P.tt(gpl.t, gcm.t[:, :, 0:4], lnb.t, ALU.add, r=[gcm.k, lnb.k], w=[gpl.k])
        P.copy(sc3.t[:, :, 0:4], gpl.t, r=[gpl.k], w=[sc3.k])
        P.copy(sc3.t[:, :, 4:8], lnb.t, r=[lnb.k], w=[sc3.k])
        P.tt(sc3.t[:, :, 8:12], gcm.t[:, :, 4:8], gcm.t[:, :, 0:4], ALU.subtract, r=[gcm.k], w=[sc3.k])
        P.act(sc3.t, sc3.t, AF.Exp, r=[sc3.k], w=[sc3.k])
        oT = P.tmp([64, 4, NT], F32)
        if prm:
            Sv = lambda sg_, h: Sst.t[:, l, h, :]
            Sk = Sst.k
        else:
            S0 = T["S0"]
            Sv = lambda sg_, h: S0.t[:, sg_, h, :]
            Sk = S0.k
        segw = 64 // NSEG
        n_lev = 5 if prm else 1
        slot = [P.tmp([64, 4, 64], F32) for _ in range(10)]
        if not prm:
            kdm = P.tmp([64, 16, 64], F32)
        for b_ in range(NB64):
            cs = slice(b_ * 64, (b_ + 1) * 64)
            bc = lambda t_, j: t_.t[:, b_, j:j + 1].to_broadcast([64, 64])
            pA, pB2, pC, pD = bank(1), bank(2), bank(3), bank(4)
            for h in range(4):
                hs = slice(h * 64, (h + 1) * 64)
                P.mm(pA.t[0:64, hs], bc(la, h), GC.t[:, 0, :], start=True, stop=False, r=[la.k, GC.k], w=[pA.k])
                P.mm(pA.t[0:64, hs], bc(lnb, h), i64, start=False, stop=False, r=[lnb.k, ident.k], w=[pA.k])
                P.mm(pA.t[0:64, hs], i64, GC.t[:, 2, :], start=False, stop=True, r=[GC.k, ident.k], w=[pA.k])
                P.mm(pB2.t[0:64, hs], bc(la, h), GC.t[:, 0, :], start=True, stop=False, r=[la.k, GC.k], w=[pB2.k])
                P.mm(pB2.t[0:64, hs], i64, GC.t[:, 3, :], start=False, stop=True, r=[GC.k, ident.k], w=[pB2.k])
                P.mm(pC.t[0:64, hs], bc(la, h), GC.t[:, 0, :], start=True, stop=False, r=[la.k, GC.k], w=[pC.k])
                P.mm(pC.t[0:64, hs], i64, GC.t[:, 4, :], start=False, stop=True, r=[GC.k, ident.k], w=[pC.k])
                P.mm(pD.t[0:64, hs], bc(la, h), GC.t[:, 0, :], r=[la.k, GC.k], w=[pD.k])
            E1, E2, E1T, eG = slot[0:4]
            for h in range(4):
                hs = slice(h * 64, (h + 1) * 64)
                P.act(E1.t[:, h, :], pA.t[0:64, hs], AF.Exp, bias=negg.t[:, b_, h:h + 1], r=[pA.k, negg.k], w=[E1.k])
                P.act(E2.t[:, h, :], pB2.t[0:64, hs], AF.Exp, bias=negg.t[:, b_, h:h + 1], r=[pB2.k, negg.k], w=[E2.k])
                P.act(E1T.t[:, h, :], pC.t[0:64, hs], AF.Exp, scale=-1.0, bias=gpl.t[:, b_, h:h + 1], r=[pC.k, gpl.k], w=[E1T.k])
            P.act(eG.t, pD.t[0:64, 0:256].rearrange("p (h q) -> p h q", h=4), AF.Exp, r=[pD.k], w=[eG.k])
            pG, pQ = bank(5), bank(6)
            for h in range(4):
                hs = slice(h * 64, (h + 1) * 64)
                P.mm(pG.t[0:64, hs], qkv.t[:, 4 + h, cs], qkv.t[:, 4 + h, cs], r=[qkv.k], w=[pG.k])
                P.mm(pQ.t[0:64, hs], qkv.t[:, 4 + h, cs], qkv.t[:, h, cs], r=[qkv.k], w=[pQ.k])
            Pm, Qm, QKd = slot[4:7]
            v4 = lambda pb_: pb_.t[0:64, 0:256].rearrange("p (h q) -> p h q", h=4)
            P.tt(Pm.t, v4(pG), E1.t, ALU.mult, r=[pG.k, E1.k], w=[Pm.k])
            P.tt(Qm.t, v4(pG), E1T.t, ALU.mult, r=[pG.k, E1T.k], w=[Qm.k])
            P.tt(QKd.t, v4(pQ), E2.t, ALU.mult, r=[pQ.k, E2.k], w=[QKd.k])
            Rm = slot[7]
            pq = [(slot[8], slot[9]), (slot[4], slot[5])]
            P.stt(Rm.t, Pm.t, -1.0, i64.unsqueeze(1).to_broadcast([64, 4, 64]), ALU.mult, ALU.add, r=[Pm.k, ident.k], w=[Rm.k])
            for lev in range(n_lev):
                last = lev == n_lev - 1
                pP, pQn, pR = bank(1), bank(2), bank(3)
                Pn, Qn = pq[lev % 2]
                for h in range(4):
                    hs = slice(h * 64, (h + 1) * 64)
                    if not last:
                        P.mm(pP.t[0:64, hs], Qm.t[:, h, :], Pm.t[:, h, :], r=[Qm.k, Pm.k], w=[pP.k])
                    P.mm(pQn.t[0:64, hs], Pm.t[:, h, :], Qm.t[:, h, :], r=[Qm.k, Pm.k], w=[pQn.k])
                if not last:
                    P.copy(Pn.t, v4(pP), r=[pP.k], w=[Pn.k], eng="scalar")
                P.copy(Qn.t, v4(pQn), r=[pQn.k], w=[Qn.k])
                for h in range(4):
                    hs = slice(h * 64, (h + 1) * 64)
                    P.mm(pR.t[0:64, hs], Qn.t[:, h, :], Rm.t[:, h, :], r=[Qn.k, Rm.k], w=[pR.k])
                P.tt(Rm.t, Rm.t, v4(pR), ALU.add, r=[Rm.k, pR.k], w=[Rm.k])
                Pm, Qm = Pn, Qn
            pK, pV = bank(4), bank(5)
            for h in range(4):
                hs = slice(h * 64, (h + 1) * 64)
                P.tr(pK.t[0:64, hs], qkv.t[:, 4 + h, cs], i64, r=[qkv.k, ident.k], w=[pK.k])
                P.tr(pV.t[0:64, hs], qkv.t[:, 8 + h, cs], i64, r=[qkv.k, ident.k], w=[pV.k])
            Yk, Yv, kd = slot[0], slot[1], slot[2]
            scb = lambda j0: sc3.t[:, b_, j0:j0 + 4].unsqueeze(2).to_broadcast([64, 4, 64])
            P.tt(Yk.t, v4(pK), scb(0), ALU.mult, r=[pK.k, sc3.k], w=[Yk.k])
            P.tt(kd.t, v4(pK), scb(8), ALU.mult, r=[pK.k, sc3.k], w=[kd.k])
            P.tt(Yv.t, v4(pV), scb(4), ALU.mult, r=[pV.k, sc3.k], w=[Yv.k])
            pW = bank(6)
            for h in range(4):
                hs = slice(h * 64, (h + 1) * 64)
                P.mm(pW.t[0:64, hs], Yk.t[:, h, :], Rm.t[:, h, :], r=[Yk.k, Rm.k], w=[pW.k])
            nWT, qdT, unT, un = slot[4], slot[5], slot[8], slot[9]
            P.ts(nWT.t, v4(pW), -1.0, None, ALU.mult, r=[pW.k], w=[nWT.k])
            P.tt(qdT.t, qkv.t[:, 0:4, cs], eG.t, ALU.mult, r=[qkv.k, eG.k], w=[qdT.k])
            pU = bank(1)
            for h in range(4):
                hs = slice(h * 64, (h + 1) * 64)
                P.mm(pU.t[0:64, hs], Yv.t[:, h, :], Rm.t[:, h, :], start=True, stop=False, r=[Yv.k, Rm.k], w=[pU.k])
                for sg_ in range(NSEG):
                    P.mm(pU.t[0:64, h * 64 + sg_ * segw:h * 64 + (sg_ + 1) * segw], Sv(sg_, h), nWT.t[:, h, sg_ * segw:(sg_ + 1) * segw],
                         start=False, stop=(sg_ == NSEG - 1), r=[Sk, nWT.k], w=[pU.k])
            P.copy(unT.t, v4(pU), r=[pU.k], w=[unT.k], eng="scalar")
            pUt = bank(2)
            for h in range(4):
                hs = slice(h * 64, (h + 1) * 64)
                P.tr(pUt.t[0:64, hs], unT.t[:, h, :], i64, r=[unT.k, ident.k], w=[pUt.k])
            P.copy(un.t, v4(pUt), r=[pUt.k], w=[un.k])
            pO = bank(3)
            for h in range(4):
                hs = slice(h * 64, (h + 1) * 64)
                P.mm(pO.t[0:64, hs], un.t[:, h, :], QKd.t[:, h, :], start=True, stop=False, r=[un.k, QKd.k], w=[pO.k])
                for sg_ in range(NSEG):
                    P.mm(pO.t[0:64, h * 64 + sg_ * segw:h * 64 + (sg_ + 1) * segw], Sv(sg_, h), qdT.t[:, h, sg_ * segw:(sg_ + 1) * segw],
                         start=False, stop=(sg_ == NSEG - 1), r=[Sk, qdT.k], w=[pO.k])
            P.copy(oT.t[:, :, cs], v4(pO), r=[pO.k], w=[oT.k], eng="scalar")
            if prm:
                pS = bank(4)
                for h in range(4):
                    hs = slice(h * 64, (h + 1) * 64)
                    P.mm(pS.t[0:64, hs], kd.t[:, h, :], un.t[:, h, :], r=[kd.k, un.k], w=[pS.k])
                P.tt(Sst.t[:, l], Sst.t[:, l], eG.t[:, :, 63:64].to_broadcast([64, 4, 64]), ALU.mult, r=[Sst.k, eG.k], w=[Sst.k])
                P.tt(Sst.t[:, l], Sst.t[:, l], v4(pS), ALU.add, r=[Sst.k, pS.k], w=[Sst.k])
            else:
                Sn = T["Sn"]
                for h in range(4):
                    P.tt(kdm.t, kd.t[:, h, :].unsqueeze(1).to_broadcast([64, 16, 64]), segmask.t.unsqueeze(2).to_broadcast([64, 16, 64]), ALU.mult,
                         r=[kd.k, segmask.k], w=[kdm.k])
                    for half in range(2):
                        pS = bank(4 + half)
                        for s8 in range(8):
                            sg_ = half * 8 + s8
                            P.mm(pS.t[0:64, s8 * 64:(s8 + 1) * 64], kdm.t[:, sg_, :], un.t[:, h, :], r=[kdm.k, un.k], w=[pS.k])
                        ss_ = slice(half * 8, half * 8 + 8)
                        alb = eG.t[:, h, :].rearrange("p (s t) -> p s t", t=4)[:, ss_, 3:4].to_broadcast([64, 8, 64])
                        P.tt(Sn.t[:, ss_, h, :], S0.t[:, ss_, h, :], alb, ALU.mult, r=[S0.k, eG.k], w=[Sn.k])
                        P.tt(Sn.t[:, ss_, h, :], Sn.t[:, ss_, h, :], pS.t[0:64, :].rearrange("p (s v) -> p s v", v=64), ALU.add, r=[Sn.k, pS.k], w=[Sn.k])
        sqo = Tl(sqn.t[0:64, 0:4], sqn.k)
        P.act(sqo.t, oT.t, AF.Square, r=[oT.k], w=[sqo.k])
        for h in range(4):
            pb = bank(1 + h)
            P.mm(pb.t[0:64, 0:NT], ones_bf.t[0:64, 0:64], sqo.t[:, h, :], r=[sqo.k, ones_bf.k], w=[pb.k])
            rr = rrs[h % 2]
            P.act(rr.t, pb.t[0:64, 0:NT], AF.Sqrt, scale=1.0 / 64, bias=eps_t.t[0:64, 0:1], r=[pb.k, eps_t.k], w=[rr.k])
            P.recip(rr.t, rr.t, r=[rr.k], w=[rr.k])
            P.stt(oT.t[:, h, :], oT.t[:, h, :], gnA.t[:, l:l + 1], rr.t, ALU.mult, ALU.mult, r=[oT.k, rr.k, gnA.k], w=[oT.k])
            P.tt(outA.t[:, h, :], oT.t[:, h, :], szA.t[:, h, :], ALU.mult, r=[oT.k, szA.k], w=[outA.k])
        if "outA" in dbg:
            of = P.tmp([64, 4, NT], F32)
            P.copy(of.t, outA.t, r=[outA.k], w=[of.k])
            P.dbg(f"outA_{kind}{l}", of.t, r=[of.k])
        P.release(mA)
        return dict(l=l, T=T, NT=NT, prm=prm, kind=kind, xn=xn, sqn=sqn, mixT=mixT, outA=outA, m_all=m_all, win=win, sw=sw, as3=as3, nb=nb, NBk=NBk, NSEG=NSEG, W7=W7)


    def mixB(cx):
        l, NT, prm, xn, mixT, nb, NBk, as3, kind = cx["l"], cx["NT"], cx["prm"], cx["xn"], cx["mixT"], cx["nb"], cx["NBk"], cx["as3"], cx["kind"]
        m = P.mark()
        wb = wstage([(lambda b_: as3(b_), cx["win"][:, :, C_U:C_U + 512], cx["sw"])])
        wv = as3(wb.t)
        uT = P.tmp([128, 2, NT], BF16)

        def evu(j, ps_ap, pb):
            P.act(uT.t[:, j, :], ps_ap, AF.Gelu_apprx_tanh, r=[pb.k], w=[uT.k])
        proj_fm(None, wv, wb.k, 0, 2, 128, xn, NT, 0, evu)
        for b_ in range(NBk):
            cs = slice(b_ * nb, (b_ + 1) * nb)
            pv = bank(3 + b_ % 2)
            for c in range(8):
                P.mm(pv.t[0:nb, 0:256], xn.t[:, c, cs], wv[:, c, 256:512], start=(c == 0), stop=(c == 7), r=[xn.k, wb.k], w=[pv.k])
            vg = P.tmp([nb, 256], F32)
            st_ = P.tmp([nb, 4], F32)
            P.memset(st_.t, 0.0, w=[st_.k])
            P.act(vg.t, pv.t[0:nb, 0:256], AF.Gelu_apprx_tanh, accum_out=st_.t[:, 0:1], r=[pv.k, st_.k], w=[vg.k, st_.k])
            P.ts(st_.t[:, 1:2], st_.t[:, 0:1], -1.0 / 256, None, ALU.mult, r=[st_.k], w=[st_.k])
            junk = P.tmp([nb, 256], F32)
            P.act(junk.t, vg.t, AF.Square, bias=st_.t[:, 1:2], accum_out=st_.t[:, 2:3], r=[vg.k, st_.k], w=[junk.k, st_.k])
            P.act(st_.t[:, 3:4], st_.t[:, 2:3], AF.Sqrt, scale=1.0 / 256, bias=eps_t.t[0:nb, 0:1], r=[st_.k, eps_t.k], w=[st_.k])
            P.recip(st_.t[:, 3:4], st_.t[:, 3:4], r=[st_.k], w=[st_.k])
            P.ts(vg.t, vg.t, st_.t[:, 1:2], st_.t[:, 3:4], ALU.add, ALU.mult, r=[vg.k, st_.k], w=[vg.k])
            P.tt(vg.t, vg.t, lnr.t[0:nb, l, 0, :], ALU.mult, r=[vg.k, lnr.k], w=[vg.k])
            P.tt(vg.t, vg.t, lnr.t[0:nb, l, 1, :], ALU.add, r=[vg.k, lnr.k], w=[vg.k])
            if not prm:
                P.dma(mvs_d[l], vg.t, r=[vg.k])
            vb = P.tmp([nb, 256], BF16)
            P.copy(vb.t, vg.t, r=[vg.k], w=[vb.k])
            pm = bank(5 + b_ % 2)
            for g in range(4):
                wmat = wsT.t[:, l, g, :] if prm else wsbd.t[:, l, g, :]
                P.mm(pm.t[(g % 2) * 64:(g % 2) * 64 + 64, (g // 2) * nb:(g // 2 + 1) * nb], vb.t[:, g * 64:(g + 1) * 64], wmat, r=[vb.k, wsT.k, wsbd.k], w=[pm.k])
            for ch in range(2):
                bsv = bsP.t[:, l, ch, :] if prm else bsS.t[:, l, ch, :, :].rearrange("p s t -> p (s t)")
                tmpb = P.tmp([128, nb], F32)
                P.tt(tmpb.t, pm.t[:, ch * nb:(ch + 1) * nb], bsv, ALU.add, r=[pm.k, bsP.k, bsS.k], w=[tmpb.k])
                P.tt(mixT.t[:, ch, cs], tmpb.t, uT.t[:, ch, cs], ALU.mult, r=[tmpb.k, uT.k], w=[mixT.k])
        if "mixB" in dbg:
            of = P.tmp([128, 2, NT], F32)
            P.copy(of.t, mixT.t[:, 0:2, :], r=[mixT.k], w=[of.k])
            P.dbg(f"mixB_{kind}{l}", of.t, r=[of.k])
        P.release(m)

    def mixC(cx):
        l, T, NT, prm, xn, mixT, nb, NBk, as3, kind, NSEG, W7 = (cx[k] for k in ["l", "T", "NT", "prm", "xn", "mixT", "nb", "NBk", "as3", "kind", "NSEG", "W7"])
        m = P.mark()
        wb1 = wstage([(lambda b_: as3(b_), cx["win"][:, :, C_ZC:C_ZC + 512], cx["sw"])])
        wb2 = wstage([(lambda b_: as3(b_), cx["win"][:, :, C_ZC + 512:C_ZC + 1024], cx["sw"])])
        szC = P.tmp([128, 2, NT], BF16)
        xbc = P.tmp([128, 6, NT], F32)
        xpc = P.tmp([128, 6, W7], F32)

        def xv(j, i):
            if prm:
                return xpc.t[:, j, i:i + NT]
            return xpc.t[:, j, :].rearrange("p (s t) -> p s t", t=7)[:, :, i:i + 4]
        ov_ = lambda ap2: ap2 if prm else ap2.rearrange("p (s t) -> p s t", t=4)

        def evz(j, ps_ap, pb):
            P.act(szC.t[:, j, :], ps_ap, AF.Silu, r=[pb.k], w=[szC.k])
        proj_fm(None, as3(wb1.t), wb1.k, 0, 2, 128, xn, NT, 0, evz)
        if prm:
            P.copy(xpc.t[:, :, 0:3], tailC.t[:, l], r=[tailC.k], w=[xpc.k], eng="gpsimd")
        else:
            P.copy(xpc.t.rearrange("p j (s t) -> p j s t", t=7)[:, :, :, 0:3], T["tailC"].t, r=[T["tailC"].k], w=[xpc.k], eng="gpsimd")

        def evx(j0):
            def f(j, ps_ap, pb):
                P.copy(xv(j0 + j, 3), ov_(ps_ap), r=[pb.k], w=[xpc.k], eng="scalar")
            return f
        proj_fm(None, as3(wb1.t), wb1.k, 256, 2, 128, xn, NT, 2, evx(0))
        proj_fm(None, as3(wb2.t), wb2.k, 0, 4, 128, xn, NT, 4, evx(2))
        for j in range(6):
            ov = ov_(xbc.t[:, j, :])
            if prm:
                P.ts(ov, xv(j, 0), cwC.t[:, l, j, 0:1], None, ALU.mult, r=[xpc.k, cwC.k], w=[xbc.k])
            else:
                P.tt(ov, xv(j, 0), cwC.t[:, l, j, 0:1].unsqueeze(2).to_broadcast([128, 16, 4]), ALU.mult, r=[xpc.k, cwC.k], w=[xbc.k])
            for i in range(1, 4):
                P.stt(ov, xv(j, i), cwC.t[:, l, j, i:i + 1], ov, ALU.mult, ALU.add, r=[xpc.k, cwC.k, xbc.k], w=[xbc.k])
            P.act(xbc.t[:, j, :], xbc.t[:, j, :], AF.Silu, bias=cbC.t[:, l, j:j + 1], r=[xbc.k, cbC.k], w=[xbc.k])
        if prm:
            P.copy(tailC.t[:, l], xpc.t[:, :, NT:NT + 3], r=[xpc.k], w=[tailC.k], eng="gpsimd")
        else:
            rawC = P.tmp([128, 6, 16, 3], F32)
            P.copy(rawC.t, xpc.t.rearrange("p j (s t) -> p j s t", t=7)[:, :, :, 4:7], r=[xpc.k], w=[rawC.k], eng="gpsimd")
            rowC = P.tmp([48, 768], F32)
            for hb in range(2):
                pb = bank(1 + hb)
                for j in range(3):
                    P.tr(pb.t[0:48, j * 128:(j + 1) * 128], rawC.t[:, hb * 3 + j].rearrange("p s t -> p (s t)"), ident.t, r=[rawC.k, ident.k], w=[pb.k])
                P.copy(rowC.t[:, hb * 384:(hb + 1) * 384], pb.t[0:48, 0:384], r=[pb.k], w=[rowC.k])
            P.dma(scs_d[l].rearrange("s t c -> (s t) c"), rowC.t, r=[rowC.k])
        bcb = P.tmp([128, 4, NT], BF16)
        P.copy(bcb.t, xbc.t[:, 2:6, :], r=[xbc.k], w=[bcb.k], eng="gpsimd")
        pgt = bank(0)
        for b_ in range(NBk):
            for c in range(8):
                P.mm(pgt.t[0:nb, b_ * 4:(b_ + 1) * 4], xn.t[:, c, b_ * nb:(b_ + 1) * nb], wg.t[:, l, c, 8:12], start=(c == 0), stop=(c == 7), r=[xn.k, wg.k], w=[pgt.k])
        dt = P.tmp([nb, NBk, 4], F32)
        P.tt(dt.t, pgt.t[0:nb, 0:NBk * 4].rearrange("p (b n) -> p b n", n=4), rows.t[0:nb, l, 2, :].unsqueeze(1).to_broadcast([nb, NBk, 4]), ALU.add, r=[pgt.k, rows.k], w=[dt.k])
        softplus(dt, [nb, NBk, 4], nb)
        aa = P.tmp([nb, NBk, 4], F32)
        P.tt(aa.t, dt.t, rows.t[0:nb, l, 3, :].unsqueeze(1).to_broadcast([nb, NBk, 4]), ALU.mult, r=[dt.k, rows.k], w=[aa.k])
        if prm:
            TRI, SEGO, NMN, idn = cp.t[:, 0, :], cp.t[:, 1, :], cp.t[:, 2, :], ident.t
            cK = cp.k
        else:
            TRI, SEGO, NMN, idn = gs.t[:, 0, :], gs.t[:, 1, :], gs.t[:, 3, :], ident.t[0:64, 0:64]
            cK = gs.k
        pc = bank(0)
        for b_ in range(NBk):
            P.mm(pc.t[0:nb, b_ * 8:b_ * 8 + 4], TRI, aa.t[:, b_, :], r=[cK, aa.k], w=[pc.k])
            P.mm(pc.t[0:nb, b_ * 8 + 4:b_ * 8 + 8], SEGO, aa.t[:, b_, :], r=[cK, aa.k], w=[pc.k])
        cm = P.tmp([nb, NBk, 8], F32)
        P.copy(cm.t, pc.t[0:nb, 0:NBk * 8].rearrange("p (b n) -> p b n", n=8), r=[pc.k], w=[cm.k])
        negc = P.tmp([nb, NBk, 4], F32)
        dB = P.tmp([nb, NBk, 4], F32)
        P.ts(negc.t, cm.t[:, :, 0:4], -1.0, None, ALU.mult, r=[cm.k], w=[negc.k])
        P.tt(dB.t, cm.t[:, :, 4:8], cm.t[:, :, 0:4], ALU.subtract, r=[cm.k], w=[dB.k])
        P.act(dB.t, dB.t, AF.Exp, r=[dB.k], w=[dB.k])
        yz = P.tmp([128, 2, NT], F32)
        if prm:
            hv = lambda sg_, h: hTb.t[:, h, :]
        else:
            hT0 = T["hT0"]
            hTn = T["hTn"]
            hv = lambda sg_, h: hTb.t[:, sg_, h, :]
        segw = nb // NSEG
        hTb = P.tmp([128, 4, 64], BF16) if prm else P.tmp([128, 16, 4, 64], BF16)
        E2 = P.tmp([nb, 4, nb], F32)
        eG = P.tmp([128, 4, nb], F32)
        MD = P.tmp([nb, 4, nb], BF16)
        xdt = P.tmp([nb, 4, 64], BF16)
        Bdec = P.tmp([nb, 4, 128], BF16)
        CdT = P.tmp([128, 4, nb], BF16)
        ycs = [P.tmp([128, nb], F32) for _ in range(2)]
        if not prm:
            Bm = P.tmp([64, 16, 128], BF16)
        for b_ in range(NBk):
            cs = slice(b_ * nb, (b_ + 1) * nb)
            if prm:
                P.copy(hTb.t, hTst.t[:, l], r=[hTst.k], w=[hTb.k])
            else:
                P.copy(hTb.t, hT0.t, r=[hT0.k], w=[hTb.k])
            pE, pGx = bank(1), bank(2)
            for h in range(4):
                P.mm(pE.t[0:nb, h * nb:(h + 1) * nb], aa.t[:, b_, h:h + 1].to_broadcast([nb, nb]), TRI, start=True, stop=False, r=[aa.k, cK], w=[pE.k])
                P.mm(pE.t[0:nb, h * nb:(h + 1) * nb], idn, NMN, start=False, stop=True, r=[ident.k, cK], w=[pE.k])
                P.mm(pGx.t[:, h * nb:(h + 1) * nb], aa.t[:, b_, h:h + 1].to_broadcast([nb, 128]), TRI, r=[aa.k, cK], w=[pGx.k])
            for h in range(4):
                P.act(E2.t[:, h, :], pE.t[0:nb, h * nb:(h + 1) * nb], AF.Exp, bias=negc.t[:, b_, h:h + 1], r=[pE.k, negc.k], w=[E2.k])
            P.act(eG.t, pGx.t[:, 0:4 * nb].rearrange("p (h q) -> p h q", h=4), AF.Exp, r=[pGx.k], w=[eG.k])
            pM = bank(3)
            for g in range(2):
                P.mm(pM.t[0:nb, g * nb:(g + 1) * nb], bcb.t[:, g, cs], bcb.t[:, 2 + g, cs], r=[bcb.k], w=[pM.k])
            P.tt(MD.t.rearrange("p (g e) q -> p g e q", g=2), pM.t[0:nb, 0:2 * nb].rearrange("p (g q) -> p g q", g=2).unsqueeze(2).to_broadcast([nb, 2, 2, nb]),
                 E2.t.rearrange("p (g e) q -> p g e q", g=2), ALU.mult, r=[pM.k, E2.k], w=[MD.k])
            pX, pBt = bank(4), bank(5)
            for j in range(2):
                P.tr(pX.t[0:nb, j * 128:(j + 1) * 128], xbc.t[:, j, cs], ident.t, r=[xbc.k, ident.k], w=[pX.k])
                P.tr(pBt.t[0:nb, j * 128:(j + 1) * 128], xbc.t[:, 2 + j, cs], ident.t, r=[xbc.k, ident.k], w=[pBt.k])
            P.tt(xdt.t, pX.t[0:nb, 0:256].rearrange("p (h d) -> p h d", h=4), dt.t[:, b_, :].unsqueeze(2).to_broadcast([nb, 4, 64]), ALU.mult, r=[pX.k, dt.k], w=[xdt.k])
            P.tt(Bdec.t.rearrange("p (g e) n -> p g e n", g=2), pBt.t[0:nb, 0:256].rearrange("p (g n) -> p g n", g=2).unsqueeze(2).to_broadcast([nb, 2, 2, 128]),
                 dB.t[:, b_, :].rearrange("p (g e) -> p g e", g=2).unsqueeze(3).to_broadcast([nb, 2, 2, 128]), ALU.mult, r=[pBt.k, dB.k], w=[Bdec.k])
            P.tt(CdT.t.rearrange("p (g e) q -> p g e q", g=2), xbc.t[:, 4:6, cs].unsqueeze(2).to_broadcast([128, 2, 2, nb]),
                 eG.t.rearrange("p (g e) q -> p g e q", g=2), ALU.mult, r=[xbc.k, eG.k], w=[CdT.k])
            pY = bank(6)
            for h in range(4):
                po = pY.t[(h % 2) * 64:(h % 2) * 64 + 64, (h // 2) * nb:(h // 2 + 1) * nb]
                P.mm(po, xdt.t[:, h, :], MD.t[:, h, :], start=True, stop=False, r=[xdt.k, MD.k], w=[pY.k])
                for sg_ in range(NSEG):
                    P.mm(pY.t[(h % 2) * 64:(h % 2) * 64 + 64, (h // 2) * nb + sg_ * segw:(h // 2) * nb + (sg_ + 1) * segw], hv(sg_, h), CdT.t[:, h, sg_ * segw:(sg_ + 1) * segw],
                         start=False, stop=(sg_ == NSEG - 1), r=[hTb.k, CdT.k], w=[pY.k])
            for ch in range(2):
                yc = ycs[ch]
                P.stt(yc.t, xbc.t[:, ch, cs], dC.t[:, l, ch:ch + 1], pY.t[:, ch * nb:(ch + 1) * nb], ALU.mult, ALU.add, r=[xbc.k, dC.k, pY.k], w=[yc.k])
                P.tt(yz.t[:, ch, cs], yc.t, szC.t[:, ch, cs], ALU.mult, r=[yc.k, szC.k], w=[yz.k])
            if prm:
                pS = bank(1)
                for h in range(4):
                    P.mm(pS.t[:, h * 64:(h + 1) * 64], Bdec.t[:, h, :], xdt.t[:, h, :], r=[Bdec.k, xdt.k], w=[pS.k])
                P.tt(hTst.t[:, l], hTst.t[:, l], eG.t[:, :, nb - 1:nb].to_broadcast([128, 4, 64]), ALU.mult, r=[hTst.k, eG.k], w=[hTst.k])
                P.tt(hTst.t[:, l], hTst.t[:, l], pS.t[:, 0:256].rearrange("p (h d) -> p h d", h=4), ALU.add, r=[hTst.k, pS.k], w=[hTst.k])
            else:
                for h in range(4):
                    P.tt(Bm.t, Bdec.t[:, h, :].unsqueeze(1).to_broadcast([64, 16, 128]), segmask.t.unsqueeze(2).to_broadcast([64, 16, 128]), ALU.mult,
                         r=[Bdec.k, segmask.k], w=[Bm.k])
                    for half in range(2):
                        pS = bank(1 + half)
                        for s8 in range(8):
                            sg_ = half * 8 + s8
                            P.mm(pS.t[:, s8 * 64:(s8 + 1) * 64], Bm.t[:, sg_, :], xdt.t[:, h, :], r=[Bm.k, xdt.k], w=[pS.k])
                        ss_ = slice(half * 8, half * 8 + 8)
                        alb = eG.t[:, h, :].rearrange("p (s t) -> p s t", t=4)[:, ss_, 3:4].to_broadcast([128, 8, 64])
                        P.tt(hTn.t[:, ss_, h, :], hT0.t[:, ss_, h, :], alb, ALU.mult, r=[hT0.k, eG.k], w=[hTn.k])
                        P.tt(hTn.t[:, ss_, h, :], hTn.t[:, ss_, h, :], pS.t[:, :].rearrange("p (s v) -> p s v", v=64), ALU.add, r=[hTn.k, pS.k], w=[hTn.k])
        sqz = P.tmp([128, 2, NT], BF16)
        P.act(sqz.t, yz.t, AF.Square, r=[yz.k], w=[sqz.k])
        for ch in range(2):
            P.mm(psb[0].t[:, 0:NT], ones_bf.t, sqz.t[:, ch, :], start=(ch == 0), stop=(ch == 1), r=[sqz.k, ones_bf.k], w=[psb[0].k])
        rstd = rstd_from(psb[0].t[:, 0:NT], psb[0].k, [128, NT], 1.0 / 256)
        for ch in range(2):
            P.stt(mixT.t[:, 2 + ch, :], yz.t[:, ch, :], gnC.t[:, l, ch:ch + 1], rstd.t, ALU.mult, ALU.mult, r=[yz.k, gnC.k, rstd.k], w=[mixT.k])
        if "mixC" in dbg:
            of = P.tmp([128, 2, NT], F32)
            P.copy(of.t, mixT.t[:, 2:4, :], r=[mixT.k], w=[of.k])
            P.dbg(f"mixC_{kind}{l}", of.t, r=[of.k])
        P.release(m)


    def rope_rows(dst, src, tab, n, H, rk):
        t1 = P.tmp([n, H, 32], F32)
        P.tt(dst.t, src.t, tab[:, 0:32].unsqueeze(1).to_broadcast([n, H, 32]), ALU.mult, r=[src.k] + rk, w=[dst.k])
        P.tt(t1.t[:, :, 0:16], src.t[:, :, 16:32], tab[:, 32:48].unsqueeze(1).to_broadcast([n, H, 16]), ALU.mult, r=[src.k] + rk, w=[t1.k])
        P.tt(t1.t[:, :, 16:32], src.t[:, :, 0:16], tab[:, 48:64].unsqueeze(1).to_broadcast([n, H, 16]), ALU.mult, r=[src.k] + rk, w=[t1.k])
        P.tt(dst.t, dst.t, t1.t, ALU.add, r=[t1.k, dst.k], w=[dst.k])

    def mixD(cx):
        l, T, NT, prm, xn, mixT, nb, NBk, as3, kind = (cx[k] for k in ["l", "T", "NT", "prm", "xn", "mixT", "nb", "NBk", "as3", "kind"])
        m = P.mark()
        wb = wstage([(lambda b_: as3(b_)[:, :, 0:416], cx["win"][:, :, C_CQ:C_CQ + 416], cx["sw"])])
        wv = as3(wb.t)
        sqn_ = cx["sqn"]
        cqnT = Tl(sqn_.t[:, 0:2, :], ("sqn_sub", 0))
        olat = P.tmp([128, 4, NT], BF16)
        qnT = Tl(olat.t[0:64], olat.k)
        qlT = P.tmp([128, 4, NT], BF16)
        qpT = P.tmp([32, 4, NT], BF16)
        if prm:
            b0 = T["t0"] // 128
            rowbase = T["b"] * SEQ + T["t0"]
            KTv, Ktv, KPTv = KT.t[:, l], Kt.t[:, l], KPT.t[:, l]
        else:
            KTn = P.tmp([128, 64], BF16)
            Ktn = P.tmp([64, 128], BF16)
            KPTn = P.tmp([32, 64], BF16)
        for b_ in range(NBk):
            cs = slice(b_ * nb, (b_ + 1) * nb)
            pd = bank(1 + b_ % 2)
            for c in range(8):
                P.mm(pd.t[0:nb, 0:416], xn.t[:, c, cs], wv[:, c, 0:416], start=(c == 0), stop=(c == 7), r=[xn.k, wb.k], w=[pd.k])
            st_ = P.tmp([nb, 4], F32)
            junk = P.tmp([nb, 256], F32)
            P.memset(st_.t, 0.0, w=[st_.k])
            P.act(junk.t, pd.t[0:nb, 0:256], AF.Square, accum_out=st_.t[:, 0:1], r=[pd.k, st_.k], w=[junk.k, st_.k])
            P.act(junk.t[:, 0:128], pd.t[0:nb, 256:384], AF.Square, accum_out=st_.t[:, 1:2], r=[pd.k, st_.k], w=[junk.k, st_.k])
            P.act(st_.t[:, 2:3], st_.t[:, 0:1], AF.Sqrt, scale=1.0 / 256, bias=eps_t.t[0:nb, 0:1], r=[st_.k, eps_t.k], w=[st_.k])
            P.act(st_.t[:, 3:4], st_.t[:, 1:2], AF.Sqrt, scale=1.0 / 128, bias=eps_t.t[0:nb, 0:1], r=[st_.k, eps_t.k], w=[st_.k])
            P.recip(st_.t[:, 2:4], st_.t[:, 2:4], r=[st_.k], w=[st_.k])
            cqn = P.tmp([nb, 256], F32)
            ckvn = P.tmp([nb, 128], F32)
            P.stt(cqn.t, pd.t[0:nb, 0:256], st_.t[:, 2:3], gqr.t[0:nb, l, :], ALU.mult, ALU.mult, r=[pd.k, st_.k, gqr.k], w=[cqn.k])
            P.stt(ckvn.t, pd.t[0:nb, 256:384], st_.t[:, 3:4], gkr.t[0:nb, l, :], ALU.mult, ALU.mult, r=[pd.k, st_.k, gkr.k], w=[ckvn.k])
            tab = rope_p.t[:, b0 + b_, :] if prm else rope_s.t
            kraw = P.tmp([nb, 1, 32], F32)
            P.copy(kraw.t[:, 0, :], pd.t[0:nb, 384:416], r=[pd.k], w=[kraw.k])
            kpe = P.tmp([nb, 1, 32], F32)
            rope_rows(kpe, kraw, tab, nb, 1, [rope_p.k, rope_s.k])
            if prm:
                P.dma(ckvp_d[l, rowbase + b_ * 128:rowbase + (b_ + 1) * 128, :], ckvn.t, r=[ckvn.k])
                P.dma(kpep_d[l, rowbase + b_ * 128:rowbase + (b_ + 1) * 128, :], kpe.t[:, 0, :], r=[kpe.k])
                P.copy(Ktv[:, b0 + b_, :], ckvn.t, r=[ckvn.k], w=[(Kt.k, l)], eng="gpsimd")
            else:
                P.dma(ckvs_d[l], ckvn.t, r=[ckvn.k])
                P.dma(kpes_d[l], kpe.t[:, 0, :], r=[kpe.k])
                P.copy(Ktn.t, ckvn.t, r=[ckvn.k], w=[Ktn.k], eng="gpsimd")
            pt = bank(3 + b_ % 2)
            P.tr(pt.t[:, 0:nb], ckvn.t, ident.t[0:nb, 0:nb], r=[ckvn.k, ident.k], w=[pt.k])
            P.tr(pt.t[0:32, 128:128 + nb], kpe.t[:, 0, :], ident.t[0:nb, 0:nb], r=[kpe.k, ident.k], w=[pt.k])
            for c2 in range(2):
                P.tr(pt.t[:, 256 + c2 * 128:256 + c2 * 128 + nb], cqn.t[:, c2 * 128:(c2 + 1) * 128], ident.t[0:nb, 0:nb], r=[cqn.k, ident.k], w=[pt.k])
            if prm:
                kc = slice((b0 + b_) * 128, (b0 + b_ + 1) * 128)
                P.copy(KTv[:, kc], pt.t[:, 0:128], r=[pt.k], w=[(KT.k, l)])
                P.copy(KPTv[:, kc], pt.t[0:32, 128:256], r=[pt.k], w=[(KPT.k, l)])
            else:
                P.copy(KTn.t, pt.t[:, 0:64], r=[pt.k], w=[KTn.k])
                P.copy(KPTn.t, pt.t[0:32, 128:192], r=[pt.k], w=[KPTn.k])
            for c2 in range(2):
                P.copy(cqnT.t[:, c2, cs], pt.t[:, 256 + c2 * 128:256 + c2 * 128 + nb], r=[pt.k], w=[cqnT.k], eng="scalar")
        for h in range(4):
            pq_ = bank(1 + h % 2)
            for c2 in range(2):
                P.mm(pq_.t[0:64, 0:NT], wuq.t[:, l, c2, h * 96:h * 96 + 64], cqnT.t[:, c2, :], start=(c2 == 0), stop=(c2 == 1), r=[wuq.k, cqnT.k], w=[pq_.k])
            P.copy(qnT.t[:, h, :], pq_.t[0:64, 0:NT], r=[pq_.k], w=[qnT.k], eng="scalar")
            pl_ = bank(3 + h % 2)
            P.mm(pl_.t[:, 0:NT], wukT.t[:, l, h, :], qnT.t[:, h, :], r=[wukT.k, qnT.k], w=[pl_.k])
            P.copy(qlT.t[:, h, :], pl_.t[:, 0:NT], r=[pl_.k], w=[qlT.k])
        for b_ in range(NBk):
            cs = slice(b_ * nb, (b_ + 1) * nb)
            pp = bank(5 + b_ % 2)
            wr = wuq.t[:, l, :, :].rearrange("p c (h e) -> p c h e", e=96)
            for c2 in range(2):
                P.mm(pp.t[0:nb, 0:128].rearrange("p (h e) -> p h e", h=4), cqnT.t[:, c2, cs], wr[:, c2, :, 64:96], start=(c2 == 0), stop=(c2 == 1), r=[cqnT.k, wuq.k], w=[pp.k])
            qraw = P.tmp([nb, 4, 32], F32)
            P.copy(qraw.t, pp.t[0:nb, 0:128].rearrange("p (h e) -> p h e", h=4), r=[pp.k], w=[qraw.k], eng="scalar")
            qpe = P.tmp([nb, 4, 32], F32)
            tab = rope_p.t[:, b0 + b_, :] if prm else rope_s.t
            rope_rows(qpe, qraw, tab, nb, 4, [rope_p.k, rope_s.k])
            pt = bank(1 + b_ % 2)
            for h in range(4):
                P.tr(pt.t[0:32, h * nb:(h + 1) * nb], qpe.t[:, h, :], ident.t[0:nb, 0:nb], r=[qpe.k, ident.k], w=[pt.k])
            P.copy(qpT.t[:, :, cs], pt.t[0:32, 0:4 * nb].rearrange("p (h q) -> p h q", h=4), r=[pt.k], w=[qpT.k])
        if prm:
            pTs = [Tl(sqn_.t[:, 2 + i_, :], ("sqn_sub", 1 + i_)) for i_ in range(2)]
            rs = P.tmp([128, 512], F32)
            v3 = lambda ap_: ap_.rearrange("p (h q) -> p h q", h=4)
            nkb = 0
            for qb in range(NBk):
                gq = b0 + qb
                qs = slice(qb * 128, (qb + 1) * 128)
                pO, pSm = bank(5), bank(6)
                for kb in range(gq + 1):
                    ks = slice(kb * 128, (kb + 1) * 128)
                    pS_ = bank(1 + kb % 3)
                    dg = kb == gq
                    P.mm(v3(pS_.t), KTv[:, ks], qlT.t[:, :, qs], start=True, stop=False, r=[(KT.k, l), qlT.k], w=[pS_.k])
                    P.mm(v3(pS_.t), KPTv[:, ks], qpT.t[:, :, qs], start=False, stop=not dg, r=[(KPT.k, l), qpT.k], w=[pS_.k])
                    if dg:
                        P.mm(pS_.t, identb.t, mla_diag.t, start=False, stop=True, r=[identb.k, mla_diag.k], w=[pS_.k])
                    pT_ = pTs[nkb % 2]
                    nkb += 1
                    P.act(pT_.t, pS_.t, AF.Exp, scale=MLA_SCALE, r=[pS_.k], w=[pT_.k])
                    P.mm(pO.t, Ktv[:, kb, :], pT_.t, start=(kb == 0), stop=dg, r=[(Kt.k, l), pT_.k], w=[pO.k])
                    P.mm(pSm.t, ones_bf.t, pT_.t, start=(kb == 0), stop=dg, r=[ones_bf.k, pT_.k], w=[pSm.k])
                P.recip(rs.t, pSm.t, r=[pSm.k], w=[rs.k])
                P.tt(olat.t[:, :, qs], pO.t.rearrange("p (h q) -> p h q", h=4), rs.t.rearrange("p (h q) -> p h q", h=4), ALU.mult, r=[pO.k, rs.k], w=[olat.k])
        else:
            mla_sample(cx, qlT, qpT, KTn, Ktn, KPTn, olat)
        for ch in range(2):
            po = bank(1 + ch)
            for hl in range(2):
                h = ch * 2 + hl
                P.mm(po.t[hl * 64:(hl + 1) * 64, 0:NT], wuv.t[:, l, h, :], olat.t[:, h, :], r=[wuv.k, olat.k], w=[po.k])
            P.copy(mixT.t[:, 4 + ch, :], po.t[:, 0:NT], r=[po.k], w=[mixT.k])
        if "mixD" in dbg:
            of = P.tmp([128, 2, NT], F32)
            P.copy(of.t, mixT.t[:, 4:6, :], r=[mixT.k], w=[of.k])
            P.dbg(f"mixD_{kind}{l}", of.t, r=[of.k])
        P.release(m)


    def mla_sample(cx, qlT, qpT, KTn, Ktn, KPTn, olat):
        l = cx["l"]
        KQ = 32
        NQ = 128 // KQ
        J = 8
        v3 = lambda ap_, h_: ap_.rearrange("p (h q) -> p h q", h=h_)
        ones_f = P.tmp([128, 128], F32)
        P.memset(ones_f.t, 1.0, w=[ones_f.k])
        pn = bank(1)
        P.mm(v3(pn.t[0:64, 0:256], 4), KTn.t, qlT.t, start=True, stop=False, r=[KTn.k, qlT.k], w=[pn.k])
        P.mm(v3(pn.t[0:64, 0:256], 4), KPTn.t, qpT.t, start=False, stop=True, r=[KPTn.k, qpT.k], w=[pn.k])
        pnf = P.tmp([64, 256], F32)
        P.act(pnf.t, pn.t[0:64, 0:256], AF.Exp, scale=MLA_SCALE, r=[pn.k], w=[pnf.k])
        pnb = P.tmp([64, 4, 64], BF16)
        P.tt(pnf.t, pnf.t, mla_new.t, ALU.mult, r=[pnf.k, mla_new.k], w=[pnf.k])
        P.copy(pnb.t, v3(pnf.t, 4), r=[pnf.k], w=[pnb.k])
        cbs = [P.tmp([NPG, KQ, 128], BF16) for _ in range(2)]
        kbs = [P.tmp([NPG, KQ, 32], BF16) for _ in range(2)]
        pgT = [P.tmp([128, J * NPG], BF16) for _ in range(2)]
        kpT = [P.tmp([32, J * NPG], BF16) for _ in range(2)]
        pTb = [P.tmp([NPG, J * 16], BF16) for _ in range(2)]
        pacc = [P.tmp([NPG, 16], F32) for _ in range(2)]
        pn16 = P.tmp([64, 16], F32)
        red = P.tmp([NPG, 16], F32)
        rs = P.tmp([128, 16], F32)
        it = 0
        gi = 0
        for s_ in range(NSS):
            pO = bank(5 + s_ % 2)
            pa = pacc[s_ % 2]
            P.memset(pa.t, 0.0, w=[pa.k])
            qs = slice(4 * s_, 4 * s_ + 4)
            first = True
            for q in range(NQ):
                cb, kb = cbs[it % 2], kbs[it % 2]
                it += 1
                P.op("gpsimd", (lambda e, cb=cb, q=q, s_=s_: e.indirect_dma_start(
                    out=cb.t.rearrange("p k r -> p (k r)"), out_offset=None, in_=cckv_d.rearrange("l n (q k) r -> (l n q) (k r)", k=KQ),
                    in_offset=bass.IndirectOffsetOnAxis(ap=ptab4.t[:, s_:s_ + 1], axis=0),
                    element_offset=l * NPHYS * 128 * 128 + q * KQ * 128)), r=[ptab4.k], w=[cb.k], dma=True)
                P.op("gpsimd", (lambda e, kb=kb, q=q, s_=s_: e.indirect_dma_start(
                    out=kb.t.rearrange("p k r -> p (k r)"), out_offset=None, in_=ckpe_d.rearrange("l n (q k) r -> (l n q) (k r)", k=KQ),
                    in_offset=bass.IndirectOffsetOnAxis(ap=ptab4.t[:, s_:s_ + 1], axis=0),
                    element_offset=l * NPHYS * 128 * 32 + q * KQ * 32)), r=[ptab4.k], w=[kb.k], dma=True)
                if "gath" in dbg and s_ == 0 and q == 1 and l == 0:
                    cf = P.tmp([NPG, KQ, 128], F32)
                    P.copy(cf.t, cb.t, r=[cb.k], w=[cf.k])
                    P.dbg("gath_c", cf.t, r=[cf.k])
                    kf = P.tmp([NPG, KQ, 32], F32)
                    P.copy(kf.t, kb.t, r=[kb.k], w=[kf.k])
                    P.dbg("gath_k", kf.t, r=[kf.k])
                for j0 in range(0, KQ, J):
                    tt_, kt_, pt_ = pgT[gi % 2], kpT[gi % 2], pTb[gi % 2]
                    gi += 1
                    for j in range(J):
                        P.tr(psh.t[:, j * NPG:(j + 1) * NPG], cb.t[:, j0 + j, :], identb.t[0:NPG, 0:NPG], r=[cb.k, identb.k], w=[psh.k])
                    P.copy(tt_.t, psh.t[:, 0:J * NPG], r=[psh.k], w=[tt_.k], eng=("scalar" if gi % 2 else "vector"))
                    for j in range(J):
                        P.tr(psh.t[0:32, j * NPG:(j + 1) * NPG], kb.t[:, j0 + j, :], identb.t[0:NPG, 0:NPG], r=[kb.k, identb.k], w=[psh.k])
                    P.copy(kt_.t, psh.t[0:32, 0:J * NPG], r=[psh.k], w=[kt_.k], eng=("vector" if gi % 2 else "scalar"))
                    psc = bank(3 + gi % 2)
                    for j in range(J):
                        o_ = v3(psc.t[0:NPG, j * 16:(j + 1) * 16], 4)
                        P.mm(o_, tt_.t[:, j * NPG:(j + 1) * NPG], qlT.t[:, :, qs], start=True, stop=False, r=[tt_.k, qlT.k], w=[psc.k])
                        P.mm(o_, kt_.t[:, j * NPG:(j + 1) * NPG], qpT.t[:, :, qs], start=False, stop=True, r=[kt_.k, qpT.k], w=[psc.k])
                    P.act(pt_.t, psc.t[0:NPG, 0:J * 16], AF.Exp, scale=MLA_SCALE, r=[psc.k], w=[pt_.k])
                    for j in range(J):
                        P.mm(pO.t[:, 0:16], cb.t[:, j0 + j, :], pt_.t[:, j * 16:(j + 1) * 16], start=first, stop=False, r=[cb.k, pt_.k], w=[pO.k])
                        first = False
                    P.red(red.t, pt_.t.rearrange("p (g c) -> p c g", g=J), r=[pt_.k], w=[red.k])
                    P.tt(pa.t, pa.t, red.t, ALU.add, r=[pa.k, red.k], w=[pa.k])
            P.mm(v3(pO.t[:, 0:16], 4), Ktn.t, pnb.t[:, :, qs], start=False, stop=True, r=[Ktn.k, pnb.k], w=[pO.k])
            P.copy(v3(pn16.t, 4), v3(pnf.t, 4)[:, :, qs], r=[pnf.k], w=[pn16.k])
            psm = bank(2)
            P.mm(psm.t[:, 0:16], ones_f.t[0:NPG, :], pa.t, start=True, stop=False, r=[ones_f.k, pa.k], w=[psm.k])
            P.mm(psm.t[:, 0:16], ones_f.t[0:64, :], pn16.t, start=False, stop=True, r=[ones_f.k, pn16.k], w=[psm.k])
            P.recip(rs.t, psm.t[:, 0:16], r=[psm.k], w=[rs.k])
            P.tt(olat.t[:, :, qs], v3(pO.t[:, 0:16], 4), v3(rs.t, 4), ALU.mult, r=[pO.k, rs.k], w=[olat.k])

    def sample_ctx(l):
        T = dict(NT=64, kind="s")
        tA = P.tmp([64, 12, 16, 3], F32)
        tC = P.tmp([128, 6, 16, 3], F32)
        S0 = P.tmp([64, 16, 4, 64], F32)
        hT0 = P.tmp([128, 16, 4, 64], F32)
        m = P.mark()
        ca = P.tmp([48, 768], F32)
        cc = P.tmp([48, 768], F32)
        P.dma(ca.t, sgc_d[l], w=[ca.k])
        P.dma(cc.t, ssc_d[l], w=[cc.k])
        for g3 in range(2):
            pb = bank(1 + g3)
            for j in range(6):
                hc = g3 * 6 + j
                P.tr(pb.t[0:64, j * 48:(j + 1) * 48], ca.t[:, hc * 64:(hc + 1) * 64], ident.t[0:48, 0:48], r=[ca.k, ident.k], w=[pb.k])
            P.copy(tA.t[:, g3 * 6:(g3 + 1) * 6].rearrange("p j s t -> p j (s t)"), pb.t[0:64, 0:288].rearrange("p (j n) -> p j n", j=6), r=[pb.k], w=[tA.k])
        pb = bank(3)
        for j in range(6):
            P.tr(pb.t[:, j * 48:(j + 1) * 48], cc.t[:, j * 128:(j + 1) * 128], ident.t[0:48, 0:48], r=[cc.k, ident.k], w=[pb.k])
        P.copy(tC.t.rearrange("p j s t -> p j (s t)"), pb.t[:, 0:288].rearrange("p (j n) -> p j n", j=6), r=[pb.k], w=[tC.k])
        for h in range(4):
            P.dma(S0.t[:, :, h, :], sgs_d[l, :, h, :, :].rearrange("s k v -> k s v"), w=[S0.k])
        hsrc = ssh_d[l].rearrange("(q r) n -> r q n", r=128)
        hv_ = hT0.t.rearrange("p s h d -> p (s h d)").rearrange("p (q r) -> p q r", r=128)
        for j in range(8):
            hn = P.tmp([128, 4, 128], F32)
            P.dma(hn.t, hsrc[:, 4 * j:4 * j + 4, :], w=[hn.k])
            pb = bank(4 + j % 2)
            for q in range(4):
                P.tr(pb.t[:, q * 128:(q + 1) * 128], hn.t[:, q, :], ident.t, r=[hn.k, ident.k], w=[pb.k])
            P.copy(hv_[:, 4 * j:4 * j + 4, :], pb.t.rearrange("p (q r) -> p q r", q=4), r=[pb.k], w=[hT0.k], eng=("scalar" if j % 2 else "vector"))
        P.release(m)
        T.update(tailA=tA, tailC=tC, S0=S0, Sn=S0, hT0=hT0, hTn=hT0)
        return T

    def out_states_sample(l, T):
        m = P.mark()
        S0, hT0 = T["S0"], T["hT0"]
        for h in range(4):
            P.dma(gss_d[l, :, h, :, :].rearrange("s k v -> k s v"), S0.t[:, :, h, :], r=[S0.k])
        hdst = shs_d[l].rearrange("(q r) n -> r q n", r=128)
        hv_ = hT0.t.rearrange("p s h d -> p (s h d)").rearrange("p (q r) -> p q r", r=128)
        for j in range(8):
            pb = bank(4 + j % 2)
            for q in range(4):
                P.tr(pb.t[:, q * 128:(q + 1) * 128], hv_[:, 4 * j + q, :], ident.t, r=[hT0.k, ident.k], w=[pb.k])
            ho = P.tmp([128, 4, 128], F32)
            P.copy(ho.t, pb.t.rearrange("p (q r) -> p q r", q=4), r=[pb.k], w=[ho.k], eng=("scalar" if j % 2 else "vector"))
            P.dma(hdst[:, 4 * j:4 * j + 4, :], ho.t, r=[ho.k])
        P.release(m)

    def out_states_prompt(l, b):
        m = P.mark()
        P.dma(gsp_d[l, b].rearrange("h k v -> k h v"), Sst.t[:, l], r=[Sst.k])
        rowA = P.tmp([3, 768], F32)
        rowC = P.tmp([3, 768], F32)
        for hb in range(2):
            pa_, pc_ = bank(1 + hb), bank(5 + hb)
            for j in range(6):
                P.tr(pa_.t[0:3, j * 64:(j + 1) * 64], tailA.t[:, l, hb * 6 + j, :], ident.t[0:64, 0:64], r=[tailA.k, ident.k], w=[pa_.k])
            for j in range(3):
                P.tr(pc_.t[0:3, j * 128:(j + 1) * 128], tailC.t[:, l, hb * 3 + j, :], ident.t, r=[tailC.k, ident.k], w=[pc_.k])
            P.copy(rowA.t[:, hb * 384:(hb + 1) * 384], pa_.t[0:3, 0:384], r=[pa_.k], w=[rowA.k])
            P.copy(rowC.t[:, hb * 384:(hb + 1) * 384], pc_.t[0:3, 0:384], r=[pc_.k], w=[rowC.k])
        P.dma(gcp_d[l, b], rowA.t, r=[rowA.k])
        P.dma(scp_d[l, b], rowC.t, r=[rowC.k])
        pb = bank(4)
        hv_ = hTst.t[:, l].rearrange("p h d -> p (h d)")
        for g in range(2):
            P.tr(pb.t[:, g * 128:(g + 1) * 128], hv_[:, g * 128:(g + 1) * 128], ident.t, r=[hTst.k, ident.k], w=[pb.k])
        ho = P.tmp([128, 2, 128], F32)
        P.copy(ho.t, pb.t[:, 0:256].rearrange("p (g n) -> p g n", g=2), r=[pb.k], w=[ho.k])
        P.dma(shp_d[l, b * 256:(b + 1) * 256, :].rearrange("(g r) n -> r g n", r=128), ho.t, r=[ho.k])
        P.release(m)

    def mix_out(cx):
        l, NT, mixT, outA = cx["l"], cx["NT"], cx["mixT"], cx["outA"]
        wo = s_wout[l]
        tok = ("s_wout", l)
        wbA = wstage([(lambda b_: b_[0:64, :].rearrange("p (h d) -> p h d", h=4), wo[0:256, :].rearrange("(h p) d -> p h d", p=64), tok)])
        wbB = wstage([(lambda b_: b_.rearrange("p (c d) -> p c d", c=4), wo[256:768, :].rearrange("(c p) d -> p c d", p=128), tok)])
        wbC = wstage([(lambda b_: b_[:, 0:2048].rearrange("p (c d) -> p c d", c=2), wo[768:1024, :].rearrange("(c p) d -> p c d", p=128), tok)])
        vA = wbA.t[0:64, :].rearrange("p (h d) -> p h d", h=4)
        vB = wbB.t.rearrange("p (c d) -> p c d", c=4)
        vC = wbC.t[:, 0:2048].rearrange("p (c d) -> p c d", c=2)
        ysb = P.tmp([128, 8, NT], F32)
        sq = cx["sqn"]
        for p_ in range(2):
            for dcl in range(4):
                dc = 4 * p_ + dcl
                pb = psb[1 + dcl]
                ds_ = slice(dc * 128, (dc + 1) * 128)
                for h in range(4):
                    P.mm(pb.t[:, 0:NT], vA[:, h, ds_], outA.t[:, h, :], start=(h == 0), stop=False, r=[wbA.k, outA.k], w=[pb.k])
                for c in range(4):
                    P.mm(pb.t[:, 0:NT], vB[:, c, ds_], mixT.t[:, c, :], start=False, stop=False, r=[wbB.k, mixT.k], w=[pb.k])
                for c in range(2):
                    P.mm(pb.t[:, 0:NT], vC[:, c, ds_], mixT.t[:, 4 + c, :], start=False, stop=(c == 1), r=[wbC.k, mixT.k], w=[pb.k])
                P.copy(ysb.t[:, dc, :], pb.t[:, 0:NT], r=[pb.k], w=[ysb.k])
                P.act(sq.t[:, dc, :], pb.t[:, 0:NT], AF.Square, r=[pb.k], w=[sq.k])
        if "mix" in dbg:
            P.dbg(f"mix_{cx['kind']}{l}", ysb.t, r=[ysb.k])
        postnorm_residual(ysb, sq, l, 3, NT)
        P.release(cx["m_all"])

    def run_layer(l, T):
        NT = T["NT"]
        ffn(l, 0, NT)
        cx = mixer(l, T)
        mixB(cx)
        mixC(cx)
        mixD(cx)
        mix_out(cx)
        ffn(l, 1, NT)

    def run_all(groups="ps", layers=(0, 1)):
        for b in (range(NPS) if "p" in groups else []):
            for t_ in (Sst, hTst, tailA, tailC):
                P.memset(t_.t, 0.0, w=[t_.k])
            for ti in range(SEQ // TT):
                t0 = ti * TT
                load_x(xp_d[b * SEQ + t0:b * SEQ + t0 + TT, :], TT)
                for l in layers:
                    run_layer(l, dict(NT=TT, kind="p", b=b, t0=t0))
                    if ti == SEQ // TT - 1:
                        out_states_prompt(l, b)
                store_x(yp_d[b * SEQ + t0:b * SEQ + t0 + TT, :], TT)
        if "s" not in groups:
            return
        load_x(xs_d, 64)
        for l in layers:
            m = P.mark()
            T = sample_ctx(l)
            run_layer(l, T)
            out_states_sample(l, T)
            P.release(m)
        store_x(ys_d, 64)

    st = dict(locals())
    return st


N_CORES = 8
_WNAMES = ["norm_g", "ffn_w_in", "ffn_w_out", "w_in", "w_out", "gdn_conv_w", "gdn_a_log", "gdn_dt_bias", "gdn_norm_g",
           "mlp_ln_g", "mlp_ln_b", "mlp_ws", "mlp_bs", "ssm_conv_w", "ssm_conv_b", "ssm_a_log", "ssm_dt_bias", "ssm_d",
           "ssm_norm_g", "mla_q_norm_g", "mla_w_uq", "mla_kv_norm_g", "mla_w_uk", "mla_w_uv"]


def core_inputs(inputs, cfg, c, consts):
    NPS, NSS = cfg["NPS"], cfg["NSS"]
    f = lambda a: np.ascontiguousarray(np.asarray(a, dtype=np.float32))
    m = {}
    m["xp"] = f(inputs["x_prompt"][c * NPS:(c + 1) * NPS]).reshape(-1, D)
    m["xs"] = f(inputs["x_sample"][c * NSS:(c + 1) * NSS]).reshape(-1, D)
    m["cache_ckv"] = inputs["cache_ckv"]
    m["cache_kpe"] = inputs["cache_kpe"]
    m["ptab"] = np.ascontiguousarray(np.asarray(inputs["page_table"][c * NSS:(c + 1) * NSS], dtype=np.int32)).reshape(1, -1)
    m["state_gdn_s"] = f(inputs["state_gdn_s"][:, c * NSS:(c + 1) * NSS])
    m["state_gdn_conv"] = f(inputs["state_gdn_conv"][:, c * NSS:(c + 1) * NSS]).reshape(2, -1, 768)
    m["state_ssm_h"] = f(inputs["state_ssm_h"][:, c * NSS:(c + 1) * NSS]).reshape(2, -1, 128)
    m["state_ssm_conv"] = f(inputs["state_ssm_conv"][:, c * NSS:(c + 1) * NSS]).reshape(2, -1, 768)
    for k in _WNAMES:
        m[k] = inputs[k]
    for k, v in consts.items():
        m["c_" + k] = v
    return m


def assemble(results, cfg):
    NPS, SEQ, NSS = cfg["NPS"], cfg["SEQ"], cfg["NSS"]
    cat = lambda xs, ax: np.concatenate(xs, axis=ax)
    R_ = results
    out = [
        cat([r["yp"].reshape(NPS, SEQ, D) for r in R_], 0),
        cat([r["ys"].reshape(NSS, 4, D) for r in R_], 0),
        cat([r["ckv_p"].reshape(2, NPS, SEQ, 128) for r in R_], 1),
        cat([r["kpe_p"].reshape(2, NPS, SEQ, 32) for r in R_], 1),
        cat([r["gs_p"].reshape(2, NPS, 4, 64, 64) for r in R_], 1),
        cat([r["gc_p"].reshape(2, NPS, 3, 768) for r in R_], 1),
        cat([r["sh_p"].reshape(2, NPS, 4, 64, 128) for r in R_], 1),
        cat([r["sc_p"].reshape(2, NPS, 3, 768) for r in R_], 1),
        cat([r["ckv_s"].reshape(2, NSS, 4, 128) for r in R_], 1),
        cat([r["kpe_s"].reshape(2, NSS, 4, 32) for r in R_], 1),
        cat([r["gs_s"].reshape(2, NSS, 4, 64, 64) for r in R_], 1),
        cat([r["gc_s"].reshape(2, NSS, 3, 768) for r in R_], 1),
        cat([r["sh_s"].reshape(2, NSS, 4, 64, 128) for r in R_], 1),
        cat([r["sc_s"].reshape(2, NSS, 3, 768) for r in R_], 1),
        cat([r["mv_s"].reshape(2, NSS, 4, 256) for r in R_], 1),
    ]
    return tuple(np.ascontiguousarray(o.astype(np.float32)) for o in out)


def kernel(**inputs):
    inputs = {k: np.asarray(v) for k, v in inputs.items()}
    B, SEQ = inputs["x_prompt"].shape[0], inputs["x_prompt"].shape[1]
    DB = inputs["x_sample"].shape[0]
    NPG = inputs["page_table"].shape[1]
    NPHYS = inputs["cache_ckv"].shape[1]
    cfg = dict(NPS=B // N_CORES, SEQ=SEQ, NSS=DB // N_CORES, PAST=NPG * 128, NPHYS=NPHYS, TT=512)
    st = build_program(cfg)
    st["run_all"]()
    st["P"].build()
    consts = make_consts(cfg)
    in_maps = [core_inputs(inputs, cfg, c, consts) for c in range(N_CORES)]
    res = run_bass_kernel_spmd(st["nc"], in_maps, core_ids=list(range(N_CORES)))
    return assemble(res.results, cfg)
```

```python
import math
import numpy as np
import ml_dtypes
import concourse.bass as bass
import concourse.mybir as mybir
from concourse.bass_utils import run_bass_kernel_spmd

F32 = mybir.dt.float32
BF16 = mybir.dt.bfloat16
I32 = mybir.dt.int32
AF = mybir.ActivationFunctionType
ALU = mybir.AluOpType
AX = mybir.AxisListType

ENGS = ("tensor", "vector", "scalar", "gpsimd", "sync")
N_DMA_SEMS = 48

D = 1024
DC = 8
FF = 2816
FC = 22
EPS = 1e-6
NEG = -30000.0
C_QKV, C_ZA, C_GA, C_U, C_V, C_ZC, C_XBC, C_DT, C_CQ, C_CKV, C_KPE = 0, 768, 1024, 1032, 1288, 1544, 1800, 2568, 2572, 2828, 2956
IN_COLS = 2988
MLA_SCALE = 96 ** -0.5


class Tl:
    def __init__(self, t, k):
        self.t = t
        self.k = k


class Prog:
    def __init__(self, nc, arena_f=0, arena_b=0):
        self.nc = nc
        self.ops = []
        self.n_alloc = 0
        self.af = nc.alloc_sbuf_tensor("arena_f", [128, arena_f], F32) if arena_f else None
        self.ab = nc.alloc_sbuf_tensor("arena_b", [128, arena_b], BF16) if arena_b else None
        self.af_n, self.ab_n = arena_f, arena_b
        self.af_off = 0
        self.ab_off = 0
        self.af_max = 0
        self.ab_max = 0
        self.dbg_outs = {}
        self.psum_tokens = set()

    def sb(self, shape, dtype=F32, name=None):
        self.n_alloc += 1
        nm = name or f"sb{self.n_alloc}"
        return Tl(self.nc.alloc_sbuf_tensor(nm, list(shape), dtype)[:], nm)

    def ps(self, shape, dtype=F32, name=None):
        self.n_alloc += 1
        nm = name or f"ps{self.n_alloc}"
        self.psum_tokens.add(nm)
        return Tl(self.nc.alloc_psum_tensor(nm, list(shape), dtype)[:], nm)

    def mark(self):
        return (self.af_off, self.ab_off)

    def release(self, m):
        self.af_off, self.ab_off = m
        self.barrier()

    def tmp(self, shape, dtype=F32):
        n = 1
        for s in shape[1:]:
            n *= s
        n = (n + 7) // 8 * 8
        self.n_alloc += 1
        if dtype == F32:
            base, off = self.af, self.af_off
            self.af_off += n
            self.af_max = max(self.af_max, self.af_off)
            assert self.af_off <= self.af_n, ("arena_f overflow", self.af_off, self.af_n)
        else:
            base, off = self.ab, self.ab_off
            self.ab_off += n
            self.ab_max = max(self.ab_max, self.ab_off)
            assert self.ab_off <= self.ab_n, ("arena_b overflow", self.ab_off, self.ab_n)
        n0 = 1
        for s in shape[1:]:
            n0 *= s
        v = base[0:shape[0], off:off + n0]
        if len(shape) == 3:
            v = v.rearrange("p (a b) -> p a b", a=shape[1])
        elif len(shape) == 4:
            v = v.rearrange("p (a b c) -> p a b c", a=shape[1], b=shape[2])
        return Tl(v, f"tmp{self.n_alloc}")

    def op(self, eng, fn, r=(), w=(), dma=False, persist=False):
        pr = [t for t in r if t in self.psum_tokens]
        if pr:
            w = tuple(w) + tuple(pr)
        self.ops.append(dict(eng=eng, fn=fn, r=tuple(r), w=tuple(w), dma=dma, persist=persist))

    def barrier(self):
        self.ops.append(dict(barrier=True))

    def dma(self, out, in_, r=(), w=(), eng=None, persist=False, **kw):
        if eng is None:
            eng = "sync" if out.dtype == in_.dtype else "gpsimd"
        self.op(eng, lambda e: e.dma_start(out=out, in_=in_, **kw), r, w, dma=True, persist=persist)

    def mm(self, out, lhsT, rhs, start=True, stop=True, r=(), w=(), **kw):
        self.op("tensor", lambda e: e.matmul(out, lhsT, rhs, start=start, stop=stop, **kw), r, w)

    def tr(self, out, in_, ident, r=(), w=()):
        self.op("tensor", lambda e: e.transpose(out, in_, ident), r, w)

    def act(self, out, in_, func, r=(), w=(), **kw):
        self.op("scalar", lambda e: e.activation(out, in_, func, **kw), r, w)

    def tt(self, out, in0, in1, op, r=(), w=(), eng="vector"):
        self.op(eng, lambda e: e.tensor_tensor(out, in0, in1, op), r, w)

    def ts(self, out, in0, s1, s2, op0, op1=None, r=(), w=(), eng="vector"):
        if op1 is None:
            self.op(eng, lambda e: e.tensor_scalar(out, in0, s1, s2, op0), r, w)
        else:
            self.op(eng, lambda e: e.tensor_scalar(out, in0, s1, s2, op0, op1), r, w)

    def stt(self, out, in0, scalar, in1, op0, op1, r=(), w=(), eng="vector"):
        self.op(eng, lambda e: e.scalar_tensor_tensor(out, in0, scalar, in1, op0, op1), r, w)

    def copy(self, out, in_, r=(), w=(), eng="vector"):
        if eng == "scalar":
            self.op(eng, lambda e: e.copy(out, in_), r, w)
        else:
            self.op(eng, lambda e: e.tensor_copy(out, in_), r, w)

    def memset(self, ap, val, w=(), eng="vector"):
        self.op(eng, lambda e: e.memset(ap, val), (), w)

    def recip(self, out, in_, r=(), w=()):
        self.op("vector", lambda e: e.reciprocal(out, in_), r, w)

    def red(self, out, in_, r=(), w=(), op=None):
        self.op("vector", lambda e: e.tensor_reduce(out, in_, AX.X, op or ALU.add), r, w)

    def dbg(self, name, ap, r=()):
        shp = list(ap.shape)
        d = self.nc.dram_tensor("dbg_" + name, shp, F32, kind="ExternalOutput").ap()
        self.dbg_outs[name] = shp
        self.dma(d, ap, r=r)

    def hoist(self):
        new = []
        for o in self.ops:
            if o.get("persist") and o.get("dma") and not o.get("barrier"):
                btok = o["w"][0]
                pos = None
                for idx in range(len(new) - 1, max(-1, len(new) - 6000), -1):
                    n = new[idx]
                    if n.get("barrier"):
                        continue
                    if btok in n["r"] or btok in n["w"]:
                        pos = idx
                        break
                if pos is None:
                    new.append(o)
                else:
                    new.insert(pos + 1, o)
            else:
                new.append(o)
        self.ops = new

    def build(self):
        nc = self.nc
        self.hoist()
        ops = self.ops
        last_w = {}
        readers = {}
        last_on_eng = {}
        pend_dma = []
        pend_prev = []
        barrier_deps = None
        first_after = {}
        for i, o in enumerate(ops):
            if o.get("barrier"):
                barrier_deps = set(last_on_eng.values()) | set(pend_dma) | set(pend_prev)
                first_after = {e: True for e in ENGS}
                pend_prev = pend_dma
                pend_dma = []
                continue
            deps = set()
            for t in o["r"]:
                if t in last_w:
                    deps.add(last_w[t])
            for t in o["w"]:
                if t in last_w:
                    deps.add(last_w[t])
                for j in readers.get(t, {}).values():
                    deps.add(j)
            if barrier_deps is not None and first_after.get(o["eng"]):
                deps |= barrier_deps
                first_after[o["eng"]] = False
            deps.discard(i)
            if o["eng"] == "tensor" and not o["dma"]:
                deps = {j for j in deps if not (ops[j]["eng"] == "tensor" and not ops[j]["dma"])}
            o["deps"] = deps
            for t in o["w"]:
                last_w[t] = i
                readers[t] = {}
            for t in o["r"]:
                key = ("dma", i) if o["dma"] else o["eng"]
                readers.setdefault(t, {})[key] = i
            if o["dma"]:
                if not o["persist"]:
                    pend_dma.append(i)
            else:
                last_on_eng[o["eng"]] = i
        ops = [o for o in ops if not o.get("barrier")]
        idx_map = {}
        k = 0
        for i, o in enumerate(self.ops):
            if not o.get("barrier"):
                idx_map[i] = k
                k += 1
        for o in ops:
            o["deps"] = {idx_map[j] for j in o["deps"]}
            o["sig"] = False
        for o in ops:
            for j in o["deps"]:
                if not ops[j]["dma"]:
                    ops[j]["sig"] = True
        esem = {e: nc.alloc_semaphore(f"e_{e}") for e in ENGS}
        dsem = [nc.alloc_semaphore(f"d_{k}") for k in range(N_DMA_SEMS)]
        ecount = {e: 0 for e in ENGS}
        dcount = [0] * N_DMA_SEMS
        nd = 0
        for o in ops:
            if o["dma"]:
                k = nd % N_DMA_SEMS
                nd += 1
                o["dsem"] = k
                o["dprev"] = dcount[k]
                dcount[k] += 16
                o["dval"] = dcount[k]
            elif o["sig"]:
                ecount[o["eng"]] += 1
                o["eval"] = ecount[o["eng"]]
        ewaited = {e: {f: 0 for f in ENGS} for e in ENGS}
        dwaited = {e: [0] * N_DMA_SEMS for e in ENGS}
        for o in ops:
            e = o["eng"]
            waits = []
            need_e = {}
            need_d = {}
            for j in o["deps"]:
                dd = ops[j]
                if dd["dma"]:
                    need_d[dd["dsem"]] = max(need_d.get(dd["dsem"], 0), dd["dval"])
                else:
                    need_e[dd["eng"]] = max(need_e.get(dd["eng"], 0), dd["eval"])
            if o["dma"] and o["dprev"] > 0:
                need_d[o["dsem"]] = max(need_d.get(o["dsem"], 0), o["dprev"])
            for f, v in need_e.items():
                if ewaited[e][f] < v:
                    ewaited[e][f] = v
                    waits.append(("e", f, v))
            for k, v in need_d.items():
                if dwaited[e][k] < v:
                    dwaited[e][k] = v
                    waits.append(("d", k, v))
            o["waits"] = waits
        final_d = list(dcount)
        self.stats = dict(n_ops=len(ops), ecount=dict(ecount), n_dma=nd,
                          per_eng={e: sum(1 for o in ops if o["eng"] == e) for e in ENGS},
                          af_max=self.af_max, ab_max=self.ab_max)

        def run_engine(ename):
            def body(eng):
                for o in ops:
                    if o["eng"] != ename:
                        continue
                    for kind, a, v in o["waits"]:
                        eng.wait_ge(esem[a] if kind == "e" else dsem[a], v)
                    ins = o["fn"](eng)
                    if o["dma"]:
                        ins.then_inc(dsem[o["dsem"]], 16)
                    elif o["sig"]:
                        ins.then_inc(esem[ename], 1)
                if ename == "sync":
                    for k in range(N_DMA_SEMS):
                        if final_d[k] > 0:
                            eng.wait_ge(dsem[k], final_d[k])
            return body

        with nc.Block() as block:
            block.tensor(run_engine("tensor"))
            block.vector(run_engine("vector"))
            block.scalar(run_engine("scalar"))
            block.gpsimd(run_engine("gpsimd"))
            block.sync(run_engine("sync"))


def _seg_consts(n, seg):
    idx = np.arange(n)
    same = (idx[:, None] // seg) == (idx[None, :] // seg)
    tri = (same & (idx[:, None] <= idx[None, :])).astype(np.float32)
    segones = same.astype(np.float32)
    nm_strict = np.where(same & (idx[None, :] > idx[:, None]), 0.0, NEG).astype(np.float32)
    nm_nonstrict = np.where(same & (idx[None, :] >= idx[:, None]), 0.0, NEG).astype(np.float32)
    pm_strict = np.where(same & (idx[:, None] > idx[None, :]), 0.0, -NEG).astype(np.float32)
    return tri, segones, nm_strict, nm_nonstrict, pm_strict


def make_consts(cfg):
    c = {}
    c["ident"] = np.eye(128, dtype=np.float32)
    tri, so, nms, nmn, pms = _seg_consts(64, 64)
    c["gp"] = np.stack([tri, so, nms, nmn, pms])
    tri, so, nms, nmn, pms = _seg_consts(64, 4)
    c["gs"] = np.stack([tri, so, nms, nmn, pms])
    tri, so, nms, nmn, pms = _seg_consts(128, 128)
    c["cp"] = np.stack([tri, so, nmn])
    c["segmask"] = (np.arange(64)[:, None] // 4 == np.arange(16)[None, :]).astype(np.float32)
    half = 16
    inv = (10000.0 ** (-np.arange(half, dtype=np.float32) / half)).astype(np.float32)

    def rt(pos):
        ang = pos.astype(np.float32)[:, None] * inv[None, :]
        cs, sn = np.cos(ang).astype(np.float32), np.sin(ang).astype(np.float32)
        return np.stack([np.concatenate([cs, cs], 1), np.concatenate([-sn, sn], 1)], 1).astype(np.float32)

    c["rope_p"] = rt(np.arange(cfg["SEQ"]))
    c["rope_s"] = rt(cfg["PAST"] + (np.arange(64) % 4))
    k = np.arange(128)
    m = np.where(k[:, None] > k[None, :], NEG, 0.0).astype(np.float32)
    c["mla_diag"] = np.tile(m, (1, 4))
    j = np.arange(64)
    mm = ((j[:, None] // 4 == j[None, :] // 4) & (j[:, None] % 4 <= j[None, :] % 4)).astype(np.float32)
    c["mla_new"] = np.tile(mm, (1, 4))
    c["ws_mask"] = np.triu(np.ones((128, 128), np.float32))
    jj = np.arange(64)
    c["wsbd_mask"] = ((jj[:, None] // 4 == jj[None, :] // 4) & (jj[:, None] % 4 <= jj[None, :] % 4)).astype(np.float32)
    rep = np.zeros((4, 64), np.float32)
    rep[np.arange(64) % 4, np.arange(64)] = 1.0
    c["rep4"] = rep
    return c


CONST_SHAPES = None


def build_program(cfg, dbg=(), upto=99):
    NPS, SEQ, NSS, PAST, NPHYS, TT = cfg["NPS"], cfg["SEQ"], cfg["NSS"], cfg["PAST"], cfg["NPHYS"], cfg["TT"]
    NPG = PAST // 128
    NTS = NSS * 4
    assert NTS == 64 and SEQ % TT == 0 and TT % 128 == 0
    nc = bass.Bass("TRN2", target_bir_lowering=False)
    P = Prog(nc, arena_f=16384, arena_b=20480)
    DI = lambda name, shape, dt=F32: nc.dram_tensor(name, list(shape), dt, kind="ExternalInput").ap()
    DO = lambda name, shape, dt=F32: nc.dram_tensor(name, list(shape), dt, kind="ExternalOutput").ap()
    DS = lambda name, shape, dt=BF16: nc.dram_tensor(name, list(shape), dt, kind="Internal").ap()

    xp_d = DI("xp", [NPS * SEQ, D])
    xs_d = DI("xs", [NTS, D])
    cckv_d = DI("cache_ckv", [2, NPHYS, 128, 128])
    ckpe_d = DI("cache_kpe", [2, NPHYS, 128, 32])
    ptab_d = DI("ptab", [1, NSS * NPG], I32)
    sgs_d = DI("state_gdn_s", [2, NSS, 4, 64, 64])
    sgc_d = DI("state_gdn_conv", [2, NSS * 3, 768])
    ssh_d = DI("state_ssm_h", [2, NSS * 4 * 64, 128])
    ssc_d = DI("state_ssm_conv", [2, NSS * 3, 768])
    W = {}
    for nm, shp in [("norm_g", [2, 6, D]), ("ffn_w_in", [2, 2, D, 2 * FF]), ("ffn_w_out", [2, 2, FF, D]),
                    ("w_in", [2, D, IN_COLS]), ("w_out", [2, D, D]), ("gdn_conv_w", [2, 4, 768]),
                    ("gdn_a_log", [2, 4]), ("gdn_dt_bias", [2, 4]), ("gdn_norm_g", [2, 64]),
                    ("mlp_ln_g", [2, 256]), ("mlp_ln_b", [2, 256]), ("mlp_ws", [2, 4, 128, 128]),
                    ("mlp_bs", [2, 4, 128]), ("ssm_conv_w", [2, 4, 768]), ("ssm_conv_b", [2, 768]),
                    ("ssm_a_log", [2, 4]), ("ssm_dt_bias", [2, 4]), ("ssm_d", [2, 4]), ("ssm_norm_g", [2, 256]),
                    ("mla_q_norm_g", [2, 256]), ("mla_w_uq", [2, 256, 384]), ("mla_kv_norm_g", [2, 128]),
                    ("mla_w_uk", [2, 4, 128, 64]), ("mla_w_uv", [2, 4, 128, 64])]:
        W[nm] = DI(nm, shp)
    cshapes = {k: v.shape for k, v in make_consts(cfg).items()}
    C = {k: DI("c_" + k, list(s)) for k, s in cshapes.items()}

    yp_d = DO("yp", [NPS * SEQ, D])
    ys_d = DO("ys", [NTS, D])
    ckvp_d = DO("ckv_p", [2, NPS * SEQ, 128])
    kpep_d = DO("kpe_p", [2, NPS * SEQ, 32])
    gsp_d = DO("gs_p", [2, NPS, 4, 64, 64])
    gcp_d = DO("gc_p", [2, NPS, 3, 768])
    shp_d = DO("sh_p", [2, NPS * 4 * 64, 128])
    scp_d = DO("sc_p", [2, NPS, 3, 768])
    ckvs_d = DO("ckv_s", [2, NTS, 128])
    kpes_d = DO("kpe_s", [2, NTS, 32])
    gss_d = DO("gs_s", [2, NSS, 4, 64, 64])
    gcs_d = DO("gc_s", [2, NSS, 3, 768])
    shs_d = DO("sh_s", [2, NSS * 4 * 64, 128])
    scs_d = DO("sc_s", [2, NSS, 3, 768])
    mvs_d = DO("mv_s", [2, NTS, 256])

    s_fin = DS("s_fin", [2, 2, D, 2 * FF])
    s_fout = DS("s_fout", [2, 2, FF, D])
    s_win = DS("s_win", [2, D, IN_COLS])
    s_wout = DS("s_wout", [2, D, D])
    for l in range(2):
        for i in range(2):
            for hh in range(2):
                P.dma(s_fin[l, i, hh * 512:(hh + 1) * 512, :], W["ffn_w_in"][l, i, hh * 512:(hh + 1) * 512, :], w=[("s_fin", l, i)])
            P.dma(s_fout[l, i], W["ffn_w_out"][l, i], w=[("s_fout", l, i)])
        P.dma(s_win[l], W["w_in"][l], w=[("s_win", l)])
        P.dma(s_wout[l], W["w_out"][l], w=[("s_wout", l)])

    ident = P.sb([128, 128], F32, "ident")
    identb = P.sb([128, 128], BF16, "identb")
    ones_bf = P.sb([128, 128], BF16, "ones_bf")
    eps_t = P.sb([128, 1], F32, "eps_t")
    gp = P.sb([64, 5, 64], F32, "gp")
    gs = P.sb([64, 5, 64], F32, "gsc")
    cp = P.sb([128, 3, 128], F32, "cp")
    segmask = P.sb([64, 16], F32, "segmask")
    rope_p = P.sb([128, SEQ // 128, 64], F32, "rope_p")
    rope_s = P.sb([64, 64], F32, "rope_s")
    mla_diag = P.sb([128, 512], BF16, "mla_diag")
    mla_new = P.sb([64, 256], F32, "mla_new")
    P.dma(ident.t, C["ident"], w=[ident.k])
    P.copy(identb.t, ident.t, r=[ident.k], w=[identb.k])
    P.memset(ones_bf.t, 1.0, w=[ones_bf.k])
    P.memset(eps_t.t, EPS, w=[eps_t.k])
    P.dma(gp.t, C["gp"].rearrange("a p q -> p a q"), w=[gp.k])
    P.dma(gs.t, C["gs"].rearrange("a p q -> p a q"), w=[gs.k])
    P.dma(cp.t, C["cp"].rearrange("a p q -> p a q"), w=[cp.k])
    P.dma(segmask.t, C["segmask"], w=[segmask.k])
    P.dma(rope_p.t, C["rope_p"].rearrange("(b p) a e -> p b (a e)", p=128), w=[rope_p.k])
    P.dma(rope_s.t, C["rope_s"].rearrange("p a e -> p (a e)"), w=[rope_s.k])
    P.dma(mla_diag.t, C["mla_diag"], w=[mla_diag.k])
    P.dma(mla_new.t, C["mla_new"], w=[mla_new.k])

    NC_ = dict(allow_slow_non_contiguous=True)
    ng = P.sb([128, 2, 6, 8], F32, "ng")
    P.dma(ng.t, W["norm_g"].rearrange("l i (c p) -> p l i c", p=128), w=[ng.k], **NC_)
    for l in range(2):
        for i in (1, 5):
            P.ts(ng.t[:, l, i, :], ng.t[:, l, i, :], 0.5, None, ALU.mult, r=[ng.k], w=[ng.k])
    wg = P.sb([128, 2, 8, 12], BF16, "wg")
    for l in range(2):
        P.dma(wg.t[:, l, :, 0:8], W["w_in"][l, :, C_GA:C_GA + 8].rearrange("(c p) n -> p c n", p=128), w=[wg.k])
        P.dma(wg.t[:, l, :, 8:12], W["w_in"][l, :, C_DT:C_DT + 4].rearrange("(c p) n -> p c n", p=128), w=[wg.k])
    cwA = P.sb([64, 2, 12, 4], F32, "cwA")
    cwC = P.sb([128, 2, 6, 4], F32, "cwC")
    for l in range(2):
        for k in range(4):
            P.dma(cwA.t[:, l, :, k], W["gdn_conv_w"][l, k].rearrange("(c p) -> p c", p=64), w=[cwA.k], **NC_)
            P.dma(cwC.t[:, l, :, k], W["ssm_conv_w"][l, k].rearrange("(c p) -> p c", p=128), w=[cwC.k], **NC_)
    cbC = P.sb([128, 2, 6], F32, "cbC")
    P.dma(cbC.t, W["ssm_conv_b"].rearrange("l (c p) -> p l c", p=128), w=[cbC.k], **NC_)
    gnA = P.sb([64, 2], F32, "gnA")
    P.dma(gnA.t, W["gdn_norm_g"].rearrange("l p -> p l"), w=[gnA.k], **NC_)
    gnC = P.sb([128, 2, 2], F32, "gnC")
    P.dma(gnC.t, W["ssm_norm_g"].rearrange("l (c p) -> p l c", p=128), w=[gnC.k], **NC_)
    dC = P.sb([128, 2, 2], F32, "dC")
    for l in range(2):
        for h in range(4):
            P.dma(dC.t[(h % 2) * 64:(h % 2) * 64 + 64, l, h // 2:h // 2 + 1], W["ssm_d"][l:l + 1, h:h + 1].to_broadcast([64, 1]), w=[dC.k])
    rows = P.sb([128, 2, 4, 4], F32, "rows")
    for l in range(2):
        for j, nm in enumerate(["gdn_dt_bias", "gdn_a_log", "ssm_dt_bias", "ssm_a_log"]):
            P.dma(rows.t[:, l, j, :], W[nm][l:l + 1, :].to_broadcast([128, 4]), w=[rows.k])
    for j in (1, 3):
        P.act(rows.t[:, :, j, :], rows.t[:, :, j, :], AF.Exp, r=[rows.k], w=[rows.k])
        P.ts(rows.t[:, :, j, :], rows.t[:, :, j, :], -1.0, None, ALU.mult, r=[rows.k], w=[rows.k])
    lnr = P.sb([128, 2, 2, 256], F32, "lnr")
    gqr = P.sb([128, 2, 256], F32, "gqr")
    gkr = P.sb([128, 2, 128], F32, "gkr")
    for l in range(2):
        P.dma(lnr.t[:, l, 0, :], W["mlp_ln_g"][l:l + 1, :].to_broadcast([128, 256]), w=[lnr.k])
        P.dma(lnr.t[:, l, 1, :], W["mlp_ln_b"][l:l + 1, :].to_broadcast([128, 256]), w=[lnr.k])
        P.dma(gqr.t[:, l, :], W["mla_q_norm_g"][l:l + 1, :].to_broadcast([128, 256]), w=[gqr.k])
        P.dma(gkr.t[:, l, :], W["mla_kv_norm_g"][l:l + 1, :].to_broadcast([128, 128]), w=[gkr.k])
    wuq = P.sb([128, 2, 2, 384], BF16, "wuq")
    wuv = P.sb([128, 2, 4, 64], BF16, "wuv")
    wukT = P.sb([64, 2, 4, 128], BF16, "wukT")
    for l in range(2):
        P.dma(wuq.t[:, l], W["mla_w_uq"][l].rearrange("(c p) n -> p c n", p=128), w=[wuq.k])
        P.dma(wuv.t[:, l], W["mla_w_uv"][l].rearrange("h r d -> r h d"), w=[wuv.k])
    bsP = P.sb([128, 2, 2, 128], F32, "bsP")
    bsS = P.sb([128, 2, 2, 16, 4], F32, "bsS")
    for l in range(2):
        for g in range(4):
            pr = slice((g % 2) * 64, (g % 2) * 64 + 64)
            P.dma(bsP.t[pr, l, g // 2, :], W["mlp_bs"][l, g:g + 1, :].to_broadcast([64, 128]), w=[bsP.k])
            P.dma(bsS.t[pr, l, g // 2, :, :], W["mlp_bs"][l, g:g + 1, 0:4].unsqueeze(1).to_broadcast([64, 16, 4]), w=[bsS.k])
    wsT = P.sb([128, 2, 4, 128], BF16, "wsT")
    wsbd = P.sb([64, 2, 4, 64], BF16, "wsbd")

    ptab_sb = P.sb([NPG, NSS], I32, "ptab_sb")
    P.dma(ptab_sb.t, ptab_d.rearrange("o (s p) -> p (o s)", p=NPG), w=[ptab_sb.k], **NC_)
    ptab4 = P.sb([NPG, NSS], I32, "ptab4")
    P.ts(ptab4.t, ptab_sb.t, 2, None, ALU.logical_shift_left, r=[ptab_sb.k], w=[ptab4.k])
    psb = [P.ps([128, 512], F32, f"psb{i}") for i in range(7)]
    psh = P.ps([128, 1024], BF16, "psh")

    m0 = P.mark()
    wsm = P.tmp([128, 128], F32)
    wbm = P.tmp([64, 64], F32)
    rep4 = P.tmp([4, 64], F32)
    P.dma(wsm.t, C["ws_mask"], w=[wsm.k])
    P.dma(wbm.t, C["wsbd_mask"], w=[wbm.k])
    P.dma(rep4.t, C["rep4"], w=[rep4.k])
    for l in range(2):
        uk = P.tmp([128, 4, 64], F32)
        P.dma(uk.t, W["mla_w_uk"][l].rearrange("h r d -> r h d"), w=[uk.k])
        for h in range(4):
            P.tr(psb[0].t[0:64, h * 128:(h + 1) * 128], uk.t[:, h, :], ident.t, r=[uk.k, ident.k], w=[psb[0].k])
        P.copy(wukT.t[:, l], psb[0].t[0:64, :].rearrange("p (h r) -> p h r", h=4), r=[psb[0].k], w=[wukT.k])
        wsn = P.tmp([128, 4, 128], F32)
        P.dma(wsn.t, W["mlp_ws"][l].rearrange("g p q -> p g q"), w=[wsn.k])
        for g in range(4):
            P.tr(psb[1].t[:, g * 128:(g + 1) * 128], wsn.t[:, g, :], ident.t, r=[wsn.k, ident.k], w=[psb[1].k])
        P.tt(wsT.t[:, l], psb[1].t[:].rearrange("q (g p) -> q g p", g=4), wsm.t.unsqueeze(1).to_broadcast([128, 4, 128]), ALU.mult,
             r=[psb[1].k, wsm.k], w=[wsT.k])
        w4 = P.tmp([4, 4, 4], F32)
        P.dma(w4.t, W["mlp_ws"][l, :, 0:4, 0:4].rearrange("g b a -> b g a"), w=[w4.k])
        y4 = P.tmp([4, 4, 64], F32)
        for g in range(4):
            P.mm(psb[2].t[0:4, g * 64:(g + 1) * 64], w4.t[:, g, :], rep4.t, r=[w4.k, rep4.k], w=[psb[2].k])
        P.copy(y4.t, psb[2].t[0:4, 0:256].rearrange("a (g p) -> a g p", g=4), r=[psb[2].k], w=[y4.k])
        for g in range(4):
            P.mm(psb[3].t[0:64, g * 64:(g + 1) * 64], rep4.t, y4.t[:, g, :], r=[y4.k, rep4.k], w=[psb[3].k])
        P.tt(wsbd.t[:, l], psb[3].t[0:64, 0:256].rearrange("q (g p) -> q g p", g=4), wbm.t.unsqueeze(1).to_broadcast([64, 4, 64]), ALU.mult,
             r=[psb[3].k, wbm.k], w=[wsbd.k])
    P.release(m0)

    xT = P.sb([128, 8, TT], F32, "xT")
    NBUF = 3
    wring = [P.sb([128, 4096], BF16, f"wring{i}") for i in range(NBUF)]
    wctr = [0]

    def wstage(loads):
        b = wring[wctr[0] % NBUF]
        wctr[0] += 1
        for dstf, src, stok in loads:
            P.dma(dstf(b.t), src, r=[stok], w=[b.k], eng="sync", persist=True)
        return b

    Sst = P.sb([64, 2, 4, 64], F32, "Sst")
    hTst = P.sb([128, 2, 4, 64], F32, "hTst")
    tailA = P.sb([64, 2, 12, 3], F32, "tailA")
    tailC = P.sb([128, 2, 6, 3], F32, "tailC")
    KT = P.sb([128, 2, SEQ], BF16, "KT")
    Kt = P.sb([128, 2, SEQ // 128, 128], BF16, "Kt")
    KPT = P.sb([32, 2, SEQ], BF16, "KPT")

    def bank(i):
        return psb[i % 7]

    def rstd_from(ps_ap, pk, shape, scale, nparts=128):
        r_ = P.tmp(shape, F32)
        P.act(r_.t, ps_ap, AF.Sqrt, scale=scale, bias=eps_t.t[0:nparts, 0:1], r=[pk, eps_t.k], w=[r_.k])
        P.recip(r_.t, r_.t, r=[r_.k], w=[r_.k])
        return r_

    def prenorm(l, i, NT):
        xn = P.tmp([128, 8, NT], BF16)
        sq = P.tmp([128, 8, NT], BF16)
        P.act(sq.t, xT.t[:, :, 0:NT], AF.Square, r=[xT.k], w=[sq.k])
        for c in range(8):
            P.mm(psb[0].t[:, 0:NT], ones_bf.t, sq.t[:, c, :], start=(c == 0), stop=(c == 7), r=[sq.k, ones_bf.k], w=[psb[0].k])
        rstd = rstd_from(psb[0].t[:, 0:NT], psb[0].k, [128, NT], 1.0 / D)
        for c in range(8):
            P.stt(xn.t[:, c, :], xT.t[:, c, 0:NT], ng.t[:, l, i, c:c + 1], rstd.t, ALU.mult, ALU.mult,
                  r=[xT.k, rstd.k, ng.k], w=[xn.k])
        return xn, sq

    def postnorm_residual(ysb, sq, l, i, NT):
        for c in range(8):
            P.mm(psb[0].t[:, 0:NT], ones_bf.t, sq.t[:, c, :], start=(c == 0), stop=(c == 7), r=[sq.k, ones_bf.k], w=[psb[0].k])
        rstd = rstd_from(psb[0].t[:, 0:NT], psb[0].k, [128, NT], 1.0 / D)
        for c in range(8):
            P.stt(ysb.t[:, c, :], ysb.t[:, c, :], ng.t[:, l, i, c:c + 1], rstd.t, ALU.mult, ALU.mult,
                  r=[ysb.k, rstd.k, ng.k], w=[ysb.k])
            P.tt(xT.t[:, c, 0:NT], xT.t[:, c, 0:NT], ysb.t[:, c, :], ALU.add, r=[xT.k, ysb.k], w=[xT.k])

    def ffn(l, i, NT, stop_at=None):
        m = P.mark()
        xn, sq = prenorm(l, 0 if i == 0 else 4, NT)
        hT = P.tmp([128, FC, NT], BF16)
        sgs_ = [P.tmp([128, NT], BF16) for _ in range(2)]
        fin = s_fin[l, i].rearrange("(c p) f -> p c f", p=128)
        for j in range(11):
            wb = wstage([
                (lambda b: b.rearrange("p (c f) -> p c f", c=8)[:, :, 0:256], fin[:, :, 256 * j:256 * j + 256], ("s_fin", l, i)),
                (lambda b: b.rearrange("p (c f) -> p c f", c=8)[:, :, 256:512], fin[:, :, FF + 256 * j:FF + 256 * j + 256], ("s_fin", l, i)),
            ])
            wv = wb.t.rearrange("p (c f) -> p c f", c=8)
            for f in range(2):
                fc = 2 * j + f
                pg, pu = psb[1 + 2 * (fc % 3)], psb[2 + 2 * (fc % 3)]
                for c in range(8):
                    P.mm(pg.t[:, 0:NT], wv[:, c, f * 128:(f + 1) * 128], xn.t[:, c, :], start=(c == 0), stop=(c == 7), r=[wb.k, xn.k], w=[pg.k])
                for c in range(8):
                    P.mm(pu.t[:, 0:NT], wv[:, c, 256 + f * 128:256 + (f + 1) * 128], xn.t[:, c, :], start=(c == 0), stop=(c == 7), r=[wb.k, xn.k], w=[pu.k])
                sg = sgs_[fc % 2]
                P.act(sg.t, pg.t[:, 0:NT], AF.Silu, r=[pg.k], w=[sg.k])
                P.tt(hT.t[:, fc, :], sg.t, pu.t[:, 0:NT], ALU.mult, r=[sg.k, pu.k], w=[(hT.k, fc)])
        if stop_at == "win":
            hf = P.tmp([128, 4, NT], F32)
            P.copy(hf.t, hT.t[:, 0:4, :], r=[(hT.k, c) for c in range(4)], w=[hf.k])
            P.dbg("hT", hf.t, r=[hf.k])
            return
        ysb = P.tmp([128, 8, NT], F32)
        fo = s_fout[l, i].rearrange("(c p) d -> p c d", p=128)
        for p_ in range(2):
            for s_ in range(6):
                nf = min(4, FC - 4 * s_)
                wb = wstage([(lambda b, nf=nf: b.rearrange("p (c d) -> p c d", c=8)[:, 0:nf, :],
                              fo[:, 4 * s_:4 * s_ + nf, 512 * p_:512 * p_ + 512], ("s_fout", l, i))])
                wv = wb.t.rearrange("p (c d) -> p c d", c=8)
                for dcl in range(4):
                    for f in range(nf):
                        fc = 4 * s_ + f
                        P.mm(psb[1 + dcl].t[:, 0:NT], wv[:, f, dcl * 128:(dcl + 1) * 128], hT.t[:, fc, :], start=(fc == 0), stop=(fc == FC - 1),
                             r=[wb.k, (hT.k, fc)], w=[psb[1 + dcl].k])
            for dcl in range(4):
                c = 4 * p_ + dcl
                P.copy(ysb.t[:, c, :], psb[1 + dcl].t[:, 0:NT], r=[psb[1 + dcl].k], w=[ysb.k])
                P.act(sq.t[:, c, :], psb[1 + dcl].t[:, 0:NT], AF.Square, r=[psb[1 + dcl].k], w=[sq.k])
        if stop_at == "wout":
            P.dbg("ysb", ysb.t, r=[ysb.k])
            return
        postnorm_residual(ysb, sq, l, 1 if i == 0 else 5, NT)
        P.release(m)

    def load_x(src_rows, NT):
        m = P.mark()
        nb = min(128, NT)
        for b in range(NT // nb):
            xt = P.tmp([nb, D], F32)
            P.dma(xt.t, src_rows[b * nb:(b + 1) * nb, :], w=[xt.k])
            for cg in range(2):
                pb = psb[1 + (2 * b + cg) % 6]
                for c4 in range(4):
                    c = cg * 4 + c4
                    P.tr(pb.t[:, c4 * nb:(c4 + 1) * nb], xt.t[:, c * 128:(c + 1) * 128], ident.t[0:nb, 0:nb], r=[xt.k, ident.k], w=[pb.k])
                P.copy(xT.t[:, cg * 4:(cg + 1) * 4, b * nb:(b + 1) * nb], pb.t[:, 0:4 * nb].rearrange("p (c t) -> p c t", c=4),
                       r=[pb.k], w=[xT.k], eng=("scalar" if cg else "vector"))
        P.release(m)

    def store_x(dst_rows, NT):
        m = P.mark()
        nb = min(128, NT)
        for b in range(NT // nb):
            yt = P.tmp([nb, D], F32)
            for cg in range(2):
                pb = psb[1 + (2 * b + cg) % 6]
                for c4 in range(4):
                    c = cg * 4 + c4
                    P.tr(pb.t[0:nb, c4 * 128:(c4 + 1) * 128], xT.t[:, c, b * nb:(b + 1) * nb], ident.t, r=[xT.k, ident.k], w=[pb.k])
                P.copy(yt.t[:, cg * 512:(cg + 1) * 512], pb.t[0:nb, :], r=[pb.k], w=[yt.k], eng=("scalar" if cg else "vector"))
            P.dma(dst_rows[b * nb:(b + 1) * nb, :], yt.t, r=[yt.k])
        P.release(m)


    def softplus(x, shape, npart):
        a_ = P.tmp(shape, F32)
        P.stt(a_.t, x.t, -1.0, x.t, ALU.mult, ALU.max, r=[x.k], w=[a_.k])
        P.act(a_.t, a_.t, AF.Exp, scale=-1.0, r=[a_.k], w=[a_.k])
        P.act(a_.t, a_.t, AF.Ln, bias=1.0, r=[a_.k], w=[a_.k])
        P.ts(x.t, x.t, 0.0, None, ALU.max, r=[x.k], w=[x.k])
        P.tt(x.t, x.t, a_.t, ALU.add, r=[x.k, a_.k], w=[x.k])

    def proj_fm(dst_fn, wv, wk, col0, nch, m, xn, NT, bstart, evac):
        for j in range(nch):
            pb = bank(1 + (bstart + j) % 6)
            for c in range(8):
                P.mm(pb.t[0:m, 0:NT], wv[:, c, col0 + j * m:col0 + (j + 1) * m], xn.t[:, c, :], start=(c == 0), stop=(c == 7), r=[wk, xn.k], w=[pb.k])
            evac(j, pb.t[0:m, 0:NT], pb)

    def mixer(l, T):
        NT, kind = T["NT"], T["kind"]
        prm = kind == "p"
        m_all = P.mark()
        xn, sqn = prenorm(l, 2, NT)
        win = s_win[l].rearrange("(c p) n -> p c n", p=128)
        sw = ("s_win", l)
        as3 = lambda b_: b_.rearrange("p (c f) -> p c f", c=8)
        mixT = P.tmp([128, 6, NT], BF16)
        outA = P.tmp([64, 4, NT], BF16)
        NB64 = NT // 64
        nb = 128 if prm else 64
        NBk = NT // nb
        NSEG = 1 if prm else 16
        GC = gp if prm else gs
        i64 = ident.t[0:64, 0:64]
        W7 = 3 + NT if prm else 16 * 7

        def xview(xp_, j, i):
            if prm:
                return xp_.t[:, j, i:i + NT]
            return xp_.t[:, j, :].rearrange("p (s t) -> p s t", t=7)[:, :, i:i + 4]

        def oview(ap2):
            return ap2 if prm else ap2.rearrange("p (s t) -> p s t", t=4)

        mA = P.mark()
        qkv = P.tmp([64, 12, NT], F32)
        szA = P.tmp([64, 4, NT], BF16)
        wb1 = wstage([(lambda b_: as3(b_), win[:, :, 0:512], sw)])
        wb2 = wstage([(lambda b_: as3(b_), win[:, :, 512:1024], sw)])
        xp4 = P.tmp([64, 4, W7], F32)
        rrs = [P.tmp([64, NT], F32) for _ in range(2)]
        if not prm:
            rawA = P.tmp([64, 12, 16, 3], F32)
        for grp in range(3):
            wb_, c0 = (wb1, grp * 256) if grp < 2 else (wb2, 0)
            if prm:
                P.copy(xp4.t[:, :, 0:3], tailA.t[:, l, grp * 4:(grp + 1) * 4, :], r=[tailA.k], w=[xp4.k], eng="gpsimd")
            else:
                P.copy(xp4.t.rearrange("p j (s t) -> p j s t", t=7)[:, :, :, 0:3], T["tailA"].t[:, grp * 4:(grp + 1) * 4, :, :], r=[T["tailA"].k], w=[xp4.k], eng="gpsimd")

            def ev(j, ps_ap, pb, xp4=xp4):
                P.copy(xview(xp4, j, 3), oview(ps_ap), r=[pb.k], w=[xp4.k], eng="scalar")
            proj_fm(None, as3(wb_.t), wb_.k, c0, 4, 64, xn, NT, grp * 4, ev)
            for j in range(4):
                hc = grp * 4 + j
                ov = oview(qkv.t[:, hc, :])
                if prm:
                    P.ts(ov, xview(xp4, j, 0), cwA.t[:, l, hc, 0:1], None, ALU.mult, r=[xp4.k, cwA.k], w=[qkv.k])
                else:
                    P.tt(ov, xview(xp4, j, 0), cwA.t[:, l, hc, 0:1].unsqueeze(2).to_broadcast([64, 16, 4]), ALU.mult, r=[xp4.k, cwA.k], w=[qkv.k])
                for i in range(1, 4):
                    P.stt(ov, xview(xp4, j, i), cwA.t[:, l, hc, i:i + 1], ov, ALU.mult, ALU.add, r=[xp4.k, cwA.k, qkv.k], w=[qkv.k])
            if prm:
                P.copy(tailA.t[:, l, grp * 4:(grp + 1) * 4, :], xp4.t[:, :, NT:NT + 3], r=[xp4.k], w=[tailA.k], eng="gpsimd")
            else:
                P.copy(rawA.t[:, grp * 4:(grp + 1) * 4], xp4.t.rearrange("p j (s t) -> p j s t", t=7)[:, :, :, 4:7], r=[xp4.k], w=[rawA.k], eng="gpsimd")
        P.act(qkv.t, qkv.t, AF.Silu, r=[qkv.k], w=[qkv.k])
        if not prm:
            rowA = P.tmp([48, 768], F32)
            for hb in range(2):
                pb = bank(1 + hb)
                for j in range(6):
                    hc = hb * 6 + j
                    P.tr(pb.t[0:48, j * 64:(j + 1) * 64], rawA.t[:, hc].rearrange("p s t -> p (s t)"), i64, r=[rawA.k, ident.k], w=[pb.k])
                P.copy(rowA.t[:, hb * 384:(hb + 1) * 384], pb.t[0:48, 0:384], r=[pb.k], w=[rowA.k])
            P.dma(gcs_d[l].rearrange("s t c -> (s t) c"), rowA.t, r=[rowA.k])

        def evz(j, ps_ap, pb):
            P.act(szA.t[:, j, :], ps_ap, AF.Silu, r=[pb.k], w=[szA.k])
        proj_fm(None, as3(wb2.t), wb2.k, 256, 4, 64, xn, NT, 0, evz)
        sqk = Tl(sqn.t[0:64], sqn.k)
        P.act(sqk.t, qkv.t[:, 0:8, :], AF.Square, r=[qkv.k], w=[sqk.k])
        for hc in range(8):
            pb = bank(1 + hc % 6)
            P.mm(pb.t[0:64, 0:NT], ones_bf.t[0:64, 0:64], sqk.t[:, hc, :], r=[sqk.k, ones_bf.k], w=[pb.k])
            rr = rrs[hc % 2]
            P.act(rr.t, pb.t[0:64, 0:NT], AF.Sqrt, bias=eps_t.t[0:64, 0:1], r=[pb.k, eps_t.k], w=[rr.k])
            P.recip(rr.t, rr.t, r=[rr.k], w=[rr.k])
            P.stt(qkv.t[:, hc, :], qkv.t[:, hc, :], (0.125 if hc < 4 else 1.0), rr.t, ALU.mult, ALU.mult, r=[qkv.k, rr.k], w=[qkv.k])
        pgt = bank(0)
        for b_ in range(NB64):
            for c in range(8):
                P.mm(pgt.t[0:64, b_ * 8:(b_ + 1) * 8], xn.t[:, c, b_ * 64:(b_ + 1) * 64], wg.t[:, l, c, 0:8], start=(c == 0), stop=(c == 7), r=[xn.k, wg.k], w=[pgt.k])
        gx = P.tmp([64, NB64, 8], F32)
        pg3 = pgt.t[0:64, 0:NB64 * 8].rearrange("p (b n) -> p b n", n=8)
        P.tt(gx.t[:, :, 0:4], pg3[:, :, 0:4], rows.t[0:64, l, 0, :].unsqueeze(1).to_broadcast([64, NB64, 4]), ALU.add, r=[pgt.k, rows.k], w=[gx.k])
        P.ts(gx.t[:, :, 4:8], pg3[:, :, 4:8], -1.0, None, ALU.mult, r=[pgt.k], w=[gx.k])
        softplus(gx, [64, NB64, 8], 64)
        la = P.tmp([64, NB64, 4], F32)
        lnb = P.tmp([64, NB64, 4], F32)
        P.tt(la.t, gx.t[:, :, 0:4], rows.t[0:64, l, 1, :].unsqueeze(1).to_broadcast([64, NB64, 4]), ALU.mult, r=[gx.k, rows.k], w=[la.k])
        P.ts(lnb.t, gx.t[:, :, 4:8], -1.0, None, ALU.mult, r=[gx.k], w=[lnb.k])
        pc = bank(0)
        for b_ in range(NB64):
            P.mm(pc.t[0:64, b_ * 8:b_ * 8 + 4], GC.t[:, 0, :], la.t[:, b_, :], r=[GC.k, la.k], w=[pc.k])
            P.mm(pc.t[0:64, b_ * 8 + 4:b_ * 8 + 8], GC.t[:, 1, :], la.t[:, b_, :], r=[GC.k, la.k], w=[pc.k])
        gcm = P.tmp([64, NB64, 8], F32)
        P.copy(gcm.t, pc.t[0:64, 0:NB64 * 8].rearrange("p (b n) -> p b n", n=8), r=[pc.k], w=[gcm.k])
        negg = P.tmp([64, NB64, 4], F32)
        gpl = P.tmp([64, NB64, 4], F32)
        sc3 = P.tmp([64, NB64, 12], F32)
        P.ts(negg.t, gcm.t[:, :, 0:4], -1.0, None, ALU.mult, r=[gcm.k], w=[negg.k])
        P.tt(gpl.t, gcm.t[:, :, 0:4], lnb.t, ALU.add, r=[gcm.k, lnb.k], w=[gpl.k])
        P.copy(sc3.t[:, :, 0:4], gpl.t, r=[gpl.k], w=[sc3.k])
        P.copy(sc3.t[:, :, 4:8], lnb.t, r=[lnb.k], w=[sc3.k])
        P.tt(sc3.t[:, :, 8:12], gcm.t[:, :, 4:8], gcm.t[:, :, 0:4], ALU.subtract, r=[gcm.k], w=[sc3.k])
        P.act(sc3.t, sc3.t, AF.Exp, r=[sc3.k], w=[sc3.k])
        oT = P.tmp([64, 4, NT], F32)
        if prm:
            Sv = lambda sg_, h: Sst.t[:, l, h, :]
            Sk = Sst.k
        else:
            S0 = T["S0"]
            Sv = lambda sg_, h: S0.t[:, sg_, h, :]
            Sk = S0.k
        segw = 64 // NSEG
        n_lev = 5 if prm else 1
        slot = [P.tmp([64, 4, 64], F32) for _ in range(10)]
        if not prm:
            kdm = P.tmp([64, 16, 64], F32)
        for b_ in range(NB64):
            cs = slice(b_ * 64, (b_ + 1) * 64)
            bc = lambda t_, j: t_.t[:, b_, j:j + 1].to_broadcast([64, 64])
            pA, pB2, pC, pD = bank(1), bank(2), bank(3), bank(4)
            for h in range(4):
                hs = slice(h * 64, (h + 1) * 64)
                P.mm(pA.t[0:64, hs], bc(la, h), GC.t[:, 0, :], start=True, stop=False, r=[la.k, GC.k], w=[pA.k])
                P.mm(pA.t[0:64, hs], bc(lnb, h), i64, start=False, stop=False, r=[lnb.k, ident.k], w=[pA.k])
                P.mm(pA.t[0:64, hs], i64, GC.t[:, 2, :], start=False, stop=True, r=[GC.k, ident.k], w=[pA.k])
                P.mm(pB2.t[0:64, hs], bc(la, h), GC.t[:, 0, :], start=True, stop=False, r=[la.k, GC.k], w=[pB2.k])
                P.mm(pB2.t[0:64, hs], i64, GC.t[:, 3, :], start=False, stop=True, r=[GC.k, ident.k], w=[pB2.k])
                P.mm(pC.t[0:64, hs], bc(la, h), GC.t[:, 0, :], start=True, stop=False, r=[la.k, GC.k], w=[pC.k])
                P.mm(pC.t[0:64, hs], i64, GC.t[:, 4, :], start=False, stop=True, r=[GC.k, ident.k], w=[pC.k])
                P.mm(pD.t[0:64, hs], bc(la, h), GC.t[:, 0, :], r=[la.k, GC.k], w=[pD.k])
            E1, E2, E1T, eG = slot[0:4]
            for h in range(4):
                hs = slice(h * 64, (h + 1) * 64)
                P.act(E1.t[:, h, :], pA.t[0:64, hs], AF.Exp, bias=negg.t[:, b_, h:h + 1], r=[pA.k, negg.k], w=[E1.k])
                P.act(E2.t[:, h, :], pB2.t[0:64, hs], AF.Exp, bias=negg.t[:, b_, h:h + 1], r=[pB2.k, negg.k], w=[E2.k])
                P.act(E1T.t[:, h, :], pC.t[0:64, hs], AF.Exp, scale=-1.0, bias=gpl.t[:, b_, h:h + 1], r=[pC.k, gpl.k], w=[E1T.k])
            P.act(eG.t, pD.t[0:64, 0:256].rearrange("p (h q) -> p h q", h=4), AF.Exp, r=[pD.k], w=[eG.k])
            pG, pQ = bank(5), bank(6)
            for h in range(4):
                hs = slice(h * 64, (h + 1) * 64)
                P.mm(pG.t[0:64, hs], qkv.t[:, 4 + h, cs], qkv.t[:, 4 + h, cs], r=[qkv.k], w=[pG.k])
                P.mm(pQ.t[0:64, hs], qkv.t[:, 4 + h, cs], qkv.t[:, h, cs], r=[qkv.k], w=[pQ.k])
            Pm, Qm, QKd = slot[4:7]
            v4 = lambda pb_: pb_.t[0:64, 0:256].rearrange("p (h q) -> p h q", h=4)
            P.tt(Pm.t, v4(pG), E1.t, ALU.mult, r=[pG.k, E1.k], w=[Pm.k])
            P.tt(Qm.t, v4(pG), E1T.t, ALU.mult, r=[pG.k, E1T.k], w=[Qm.k])
            P.tt(QKd.t, v4(pQ), E2.t, ALU.mult, r=[pQ.k, E2.k], w=[QKd.k])
            Rm = slot[7]
            pq = [(slot[8], slot[9]), (slot[4], slot[5])]
            P.stt(Rm.t, Pm.t, -1.0, i64.unsqueeze(1).to_broadcast([64, 4, 64]), ALU.mult, ALU.add, r=[Pm.k, ident.k], w=[Rm.k])
            for lev in range(n_lev):
                last = lev == n_lev - 1
                pP, pQn, pR = bank(1), bank(2), bank(3)
                Pn, Qn = pq[lev % 2]
                for h in range(4):
                    hs = slice(h * 64, (h + 1) * 64)
                    if not last:
                        P.mm(pP.t[0:64, hs], Qm.t[:, h, :], Pm.t[:, h, :], r=[Qm.k, Pm.k], w=[pP.k])
                    P.mm(pQn.t[0:64, hs], Pm.t[:, h, :], Qm.t[:, h, :], r=[Qm.k, Pm.k], w=[pQn.k])
                if not last:
                    P.copy(Pn.t, v4(pP), r=[pP.k], w=[Pn.k], eng="scalar")
                P.copy(Qn.t, v4(pQn), r=[pQn.k], w=[Qn.k])
                for h in range(4):
                    hs = slice(h * 64, (h + 1) * 64)
                    P.mm(pR.t[0:64, hs], Qn.t[:, h, :], Rm.t[:, h, :], r=[Qn.k, Rm.k], w=[pR.k])
                P.tt(Rm.t, Rm.t, v4(pR), ALU.add, r=[Rm.k, pR.k], w=[Rm.k])
                Pm, Qm = Pn, Qn
            pK, pV = bank(4), bank(5)
            for h in range(4):
                hs = slice(h * 64, (h + 1) * 64)
                P.tr(pK.t[0:64, hs], qkv.t[:, 4 + h, cs], i64, r=[qkv.k, ident.k], w=[pK.k])
                P.tr(pV.t[0:64, hs], qkv.t[:, 8 + h, cs], i64, r=[qkv.k, ident.k], w=[pV.k])
            Yk, Yv, kd = slot[0], slot[1], slot[2]
            scb = lambda j0: sc3.t[:, b_, j0:j0 + 4].unsqueeze(2).to_broadcast([64, 4, 64])
            P.tt(Yk.t, v4(pK), scb(0), ALU.mult, r=[pK.k, sc3.k], w=[Yk.k])
            P.tt(kd.t, v4(pK), scb(8), ALU.mult, r=[pK.k, sc3.k], w=[kd.k])
            P.tt(Yv.t, v4(pV), scb(4), ALU.mult, r=[pV.k, sc3.k], w=[Yv.k])
            pW = bank(6)
            for h in range(4):
                hs = slice(h * 64, (h + 1) * 64)
                P.mm(pW.t[0:64, hs], Yk.t[:, h, :], Rm.t[:, h, :], r=[Yk.k, Rm.k], w=[pW.k])
            nWT, qdT, unT, un = slot[4], slot[5], slot[8], slot[9]
            P.ts(nWT.t, v4(pW), -1.0, None, ALU.mult, r=[pW.k], w=[nWT.k])
            P.tt(qdT.t, qkv.t[:, 0:4, cs], eG.t, ALU.mult, r=[qkv.k, eG.k], w=[qdT.k])
            pU = bank(1)
            for h in range(4):
                hs = slice(h * 64, (h + 1) * 64)
                P.mm(pU.t[0:64, hs], Yv.t[:, h, :], Rm.t[:, h, :], start=True, stop=False, r=[Yv.k, Rm.k], w=[pU.k])
                for sg_ in range(NSEG):
                    P.mm(pU.t[0:64, h * 64 + sg_ * segw:h * 64 + (sg_ + 1) * segw], Sv(sg_, h), nWT.t[:, h, sg_ * segw:(sg_ + 1) * segw],
                         start=False, stop=(sg_ == NSEG - 1), r=[Sk, nWT.k], w=[pU.k])
            P.copy(unT.t, v4(pU), r=[pU.k], w=[unT.k], eng="scalar")
            pUt = bank(2)
            for h in range(4):
                hs = slice(h * 64, (h + 1) * 64)
                P.tr(pUt.t[0:64, hs], unT.t[:, h, :], i64, r=[unT.k, ident.k], w=[pUt.k])
            P.copy(un.t, v4(pUt), r=[pUt.k], w=[un.k])
            pO = bank(3)
            for h in range(4):
                hs = slice(h * 64, (h + 1) * 64)
                P.mm(pO.t[0:64, hs], un.t[:, h, :], QKd.t[:, h, :], start=True, stop=False, r=[un.k, QKd.k], w=[pO.k])
                for sg_ in range(NSEG):
                    P.mm(pO.t[0:64, h * 64 + sg_ * segw:h * 64 + (sg_ + 1) * segw], Sv(sg_, h), qdT.t[:, h, sg_ * segw:(sg_ + 1) * segw],
                         start=False, stop=(sg_ == NSEG - 1), r=[Sk, qdT.k], w=[pO.k])
            P.copy(oT.t[:, :, cs], v4(pO), r=[pO.k], w=[oT.k], eng="scalar")
            if prm:
                pS = bank(4)
                for h in range(4):
                    hs = slice(h * 64, (h + 1) * 64)
                    P.mm(pS.t[0:64, hs], kd.t[:, h, :], un.t[:, h, :], r=[kd.k, un.k], w=[pS.k])
                P.tt(Sst.t[:, l], Sst.t[:, l], eG.t[:, :, 63:64].to_broadcast([64, 4, 64]), ALU.mult, r=[Sst.k, eG.k], w=[Sst.k])
                P.tt(Sst.t[:, l], Sst.t[:, l], v4(pS), ALU.add, r=[Sst.k, pS.k], w=[Sst.k])
            else:
                Sn = T["Sn"]
                for h in range(4):
                    P.tt(kdm.t, kd.t[:, h, :].unsqueeze(1).to_broadcast([64, 16, 64]), segmask.t.unsqueeze(2).to_broadcast([64, 16, 64]), ALU.mult,
                         r=[kd.k, segmask.k], w=[kdm.k])
                    for half in range(2):
                        pS = bank(4 + half)
                        for s8 in range(8):
                            sg_ = half * 8 + s8
                            P.mm(pS.t[0:64, s8 * 64:(s8 + 1) * 64], kdm.t[:, sg_, :], un.t[:, h, :], r=[kdm.k, un.k], w=[pS.k])
                        ss_ = slice(half * 8, half * 8 + 8)
                        alb = eG.t[:, h, :].rearrange("p (s t) -> p s t", t=4)[:, ss_, 3:4].to_broadcast([64, 8, 64])
                        P.tt(Sn.t[:, ss_, h, :], S0.t[:, ss_, h, :], alb, ALU.mult, r=[S0.k, eG.k], w=[Sn.k])
                        P.tt(Sn.t[:, ss_, h, :], Sn.t[:, ss_, h, :], pS.t[0:64, :].rearrange("p (s v) -> p s v", v=64), ALU.add, r=[Sn.k, pS.k], w=[Sn.k])
        sqo = Tl(sqn.t[0:64, 0:4], sqn.k)
        P.act(sqo.t, oT.t, AF.Square, r=[oT.k], w=[sqo.k])
        for h in range(4):
            pb = bank(1 + h)
            P.mm(pb.t[0:64, 0:NT], ones_bf.t[0:64, 0:64], sqo.t[:, h, :], r=[sqo.k, ones_bf.k], w=[pb.k])
            rr = rrs[h % 2]
            P.act(rr.t, pb.t[0:64, 0:NT], AF.Sqrt, scale=1.0 / 64, bias=eps_t.t[0:64, 0:1], r=[pb.k, eps_t.k], w=[rr.k])
            P.recip(rr.t, rr.t, r=[rr.k], w=[rr.k])
            P.stt(oT.t[:, h, :], oT.t[:, h, :], gnA.t[:, l:l + 1], rr.t, ALU.mult, ALU.mult, r=[oT.k, rr.k, gnA.k], w=[oT.k])
            P.tt(outA.t[:, h, :], oT.t[:, h, :], szA.t[:, h, :], ALU.mult, r=[oT.k, szA.k], w=[outA.k])
        if "outA" in dbg:
            of = P.tmp([64, 4, NT], F32)
            P.copy(of.t, outA.t, r=[outA.k], w=[of.k])
            P.dbg(f"outA_{kind}{l}", of.t, r=[of.k])
        P.release(mA)
        return dict(l=l, T=T, NT=NT, prm=prm, kind=kind, xn=xn, sqn=sqn, mixT=mixT, outA=outA, m_all=m_all, win=win, sw=sw, as3=as3, nb=nb, NBk=NBk, NSEG=NSEG, W7=W7)


    def mixB(cx):
        l, NT, prm, xn, mixT, nb, NBk, as3, kind = cx["l"], cx["NT"], cx["prm"], cx["xn"], cx["mixT"], cx["nb"], cx["NBk"], cx["as3"], cx["kind"]
        m = P.mark()
        wb = wstage([(lambda b_: as3(b_), cx["win"][:, :, C_U:C_U + 512], cx["sw"])])
        wv = as3(wb.t)
        uT = P.tmp([128, 2, NT], BF16)

        def evu(j, ps_ap, pb):
            P.act(uT.t[:, j, :], ps_ap, AF.Gelu_apprx_tanh, r=[pb.k], w=[uT.k])
        proj_fm(None, wv, wb.k, 0, 2, 128, xn, NT, 0, evu)
        for b_ in range(NBk):
            cs = slice(b_ * nb, (b_ + 1) * nb)
            pv = bank(3 + b_ % 2)
            for c in range(8):
                P.mm(pv.t[0:nb, 0:256], xn.t[:, c, cs], wv[:, c, 256:512], start=(c == 0), stop=(c == 7), r=[xn.k, wb.k], w=[pv.k])
            vg = P.tmp([nb, 256], F32)
            st_ = P.tmp([nb, 4], F32)
            P.memset(st_.t, 0.0, w=[st_.k])
            P.act(vg.t, pv.t[0:nb, 0:256], AF.Gelu_apprx_tanh, accum_out=st_.t[:, 0:1], r=[pv.k, st_.k], w=[vg.k, st_.k])
            P.ts(st_.t[:, 1:2], st_.t[:, 0:1], -1.0 / 256, None, ALU.mult, r=[st_.k], w=[st_.k])
            junk = P.tmp([nb, 256], F32)
            P.act(junk.t, vg.t, AF.Square, bias=st_.t[:, 1:2], accum_out=st_.t[:, 2:3], r=[vg.k, st_.k], w=[junk.k, st_.k])
            P.act(st_.t[:, 3:4], st_.t[:, 2:3], AF.Sqrt, scale=1.0 / 256, bias=eps_t.t[0:nb, 0:1], r=[st_.k, eps_t.k], w=[st_.k])
            P.recip(st_.t[:, 3:4], st_.t[:, 3:4], r=[st_.k], w=[st_.k])
            P.ts(vg.t, vg.t, st_.t[:, 1:2], st_.t[:, 3:4], ALU.add, ALU.mult, r=[vg.k, st_.k], w=[vg.k])
            P.tt(vg.t, vg.t, lnr.t[0:nb, l, 0, :], ALU.mult, r=[vg.k, lnr.k], w=[vg.k])
            P.tt(vg.t, vg.t, lnr.t[0:nb, l, 1, :], ALU.add, r=[vg.k, lnr.k], w=[vg.k])
            if not prm:
                P.dma(mvs_d[l], vg.t, r=[vg.k])
            vb = P.tmp([nb, 256], BF16)
            P.copy(vb.t, vg.t, r=[vg.k], w=[vb.k])
            pm = bank(5 + b_ % 2)
            for g in range(4):
                wmat = wsT.t[:, l, g, :] if prm else wsbd.t[:, l, g, :]
                P.mm(pm.t[(g % 2) * 64:(g % 2) * 64 + 64, (g // 2) * nb:(g // 2 + 1) * nb], vb.t[:, g * 64:(g + 1) * 64], wmat, r=[vb.k, wsT.k, wsbd.k], w=[pm.k])
            for ch in range(2):
                bsv = bsP.t[:, l, ch, :] if prm else bsS.t[:, l, ch, :, :].rearrange("p s t -> p (s t)")
                tmpb = P.tmp([128, nb], F32)
                P.tt(tmpb.t, pm.t[:, ch * nb:(ch + 1) * nb], bsv, ALU.add, r=[pm.k, bsP.k, bsS.k], w=[tmpb.k])
                P.tt(mixT.t[:, ch, cs], tmpb.t, uT.t[:, ch, cs], ALU.mult, r=[tmpb.k, uT.k], w=[mixT.k])
        if "mixB" in dbg:
            of = P.tmp([128, 2, NT], F32)
            P.copy(of.t, mixT.t[:, 0:2, :], r=[mixT.k], w=[of.k])
            P.dbg(f"mixB_{kind}{l}", of.t, r=[of.k])
        P.release(m)

    def mixC(cx):
        l, T, NT, prm, xn, mixT, nb, NBk, as3, kind, NSEG, W7 = (cx[k] for k in ["l", "T", "NT", "prm", "xn", "mixT", "nb", "NBk", "as3", "kind", "NSEG", "W7"])
        m = P.mark()
        wb1 = wstage([(lambda b_: as3(b_), cx["win"][:, :, C_ZC:C_ZC + 512], cx["sw"])])
        wb2 = wstage([(lambda b_: as3(b_), cx["win"][:, :, C_ZC + 512:C_ZC + 1024], cx["sw"])])
        szC = P.tmp([128, 2, NT], BF16)
        xbc = P.tmp([128, 6, NT], F32)
        xpc = P.tmp([128, 6, W7], F32)

        def xv(j, i):
            if prm:
                return xpc.t[:, j, i:i + NT]
            return xpc.t[:, j, :].rearrange("p (s t) -> p s t", t=7)[:, :, i:i + 4]
        ov_ = lambda ap2: ap2 if prm else ap2.rearrange("p (s t) -> p s t", t=4)

        def evz(j, ps_ap, pb):
            P.act(szC.t[:, j, :], ps_ap, AF.Silu, r=[pb.k], w=[szC.k])
        proj_fm(None, as3(wb1.t), wb1.k, 0, 2, 128, xn, NT, 0, evz)
        if prm:
            P.copy(xpc.t[:, :, 0:3], tailC.t[:, l], r=[tailC.k], w=[xpc.k], eng="gpsimd")
        else:
            P.copy(xpc.t.rearrange("p j (s t) -> p j s t", t=7)[:, :, :, 0:3], T["tailC"].t, r=[T["tailC"].k], w=[xpc.k], eng="gpsimd")

        def evx(j0):
            def f(j, ps_ap, pb):
                P.copy(xv(j0 + j, 3), ov_(ps_ap), r=[pb.k], w=[xpc.k], eng="scalar")
            return f
        proj_fm(None, as3(wb1.t), wb1.k, 256, 2, 128, xn, NT, 2, evx(0))
        proj_fm(None, as3(wb2.t), wb2.k, 0, 4, 128, xn, NT, 4, evx(2))
        for j in range(6):
            ov = ov_(xbc.t[:, j, :])
            if prm:
                P.ts(ov, xv(j, 0), cwC.t[:, l, j, 0:1], None, ALU.mult, r=[xpc.k, cwC.k], w=[xbc.k])
            else:
                P.tt(ov, xv(j, 0), cwC.t[:, l, j, 0:1].unsqueeze(2).to_broadcast([128, 16, 4]), ALU.mult, r=[xpc.k, cwC.k], w=[xbc.k])
            for i in range(1, 4):
                P.stt(ov, xv(j, i), cwC.t[:, l, j, i:i + 1], ov, ALU.mult, ALU.add, r=[xpc.k, cwC.k, xbc.k], w=[xbc.k])
            P.act(xbc.t[:, j, :], xbc.t[:, j, :], AF.Silu, bias=cbC.t[:, l, j:j + 1], r=[xbc.k, cbC.k], w=[xbc.k])
        if prm:
            P.copy(tailC.t[:, l], xpc.t[:, :, NT:NT + 3], r=[xpc.k], w=[tailC.k], eng="gpsimd")
        else:
            rawC = P.tmp([128, 6, 16, 3], F32)
            P.copy(rawC.t, xpc.t.rearrange("p j (s t) -> p j s t", t=7)[:, :, :, 4:7], r=[xpc.k], w=[rawC.k], eng="gpsimd")
            rowC = P.tmp([48, 768], F32)
            for hb in range(2):
                pb = bank(1 + hb)
                for j in range(3):
                    P.tr(pb.t[0:48, j * 128:(j + 1) * 128], rawC.t[:, hb * 3 + j].rearrange("p s t -> p (s t)"), ident.t, r=[rawC.k, ident.k], w=[pb.k])
                P.copy(rowC.t[:, hb * 384:(hb + 1) * 384], pb.t[0:48, 0:384], r=[pb.k], w=[rowC.k])
            P.dma(scs_d[l].rearrange("s t c -> (s t) c"), rowC.t, r=[rowC.k])
        bcb = P.tmp([128, 4, NT], BF16)
        P.copy(bcb.t, xbc.t[:, 2:6, :], r=[xbc.k], w=[bcb.k], eng="gpsimd")
        pgt = bank(0)
        for b_ in range(NBk):
            for c in range(8):
                P.mm(pgt.t[0:nb, b_ * 4:(b_ + 1) * 4], xn.t[:, c, b_ * nb:(b_ + 1) * nb], wg.t[:, l, c, 8:12], start=(c == 0), stop=(c == 7), r=[xn.k, wg.k], w=[pgt.k])
        dt = P.tmp([nb, NBk, 4], F32)
        P.tt(dt.t, pgt.t[0:nb, 0:NBk * 4].rearrange("p (b n) -> p b n", n=4), rows.t[0:nb, l, 2, :].unsqueeze(1).to_broadcast([nb, NBk, 4]), ALU.add, r=[pgt.k, rows.k], w=[dt.k])
        softplus(dt, [nb, NBk, 4], nb)
        aa = P.tmp([nb, NBk, 4], F32)
        P.tt(aa.t, dt.t, rows.t[0:nb, l, 3, :].unsqueeze(1).to_broadcast([nb, NBk, 4]), ALU.mult, r=[dt.k, rows.k], w=[aa.k])
        if prm:
            TRI, SEGO, NMN, idn = cp.t[:, 0, :], cp.t[:, 1, :], cp.t[:, 2, :], ident.t
            cK = cp.k
        else:
            TRI, SEGO, NMN, idn = gs.t[:, 0, :], gs.t[:, 1, :], gs.t[:, 3, :], ident.t[0:64, 0:64]
            cK = gs.k
        pc = bank(0)
        for b_ in range(NBk):
            P.mm(pc.t[0:nb, b_ * 8:b_ * 8 + 4], TRI, aa.t[:, b_, :], r=[cK, aa.k], w=[pc.k])
            P.mm(pc.t[0:nb, b_ * 8 + 4:b_ * 8 + 8], SEGO, aa.t[:, b_, :], r=[cK, aa.k], w=[pc.k])
        cm = P.tmp([nb, NBk, 8], F32)
        P.copy(cm.t, pc.t[0:nb, 0:NBk * 8].rearrange("p (b n) -> p b n", n=8), r=[pc.k], w=[cm.k])
        negc = P.tmp([nb, NBk, 4], F32)
        dB = P.tmp([nb, NBk, 4], F32)
        P.ts(negc.t, cm.t[:, :, 0:4], -1.0, None, ALU.mult, r=[cm.k], w=[negc.k])
        P.tt(dB.t, cm.t[:, :, 4:8], cm.t[:, :, 0:4], ALU.subtract, r=[cm.k], w=[dB.k])
        P.act(dB.t, dB.t, AF.Exp, r=[dB.k], w=[dB.k])
        yz = P.tmp([128, 2, NT], F32)
        if prm:
            hv = lambda sg_, h: hTb.t[:, h, :]
        else:
            hT0 = T["hT0"]
            hTn = T["hTn"]
            hv = lambda sg_, h: hTb.t[:, sg_, h, :]
        segw = nb // NSEG
        hTb = P.tmp([128, 4, 64], BF16) if prm else P.tmp([128, 16, 4, 64], BF16)
        E2 = P.tmp([nb, 4, nb], F32)
        eG = P.tmp([128, 4, nb], F32)
        MD = P.tmp([nb, 4, nb], BF16)
        xdt = P.tmp([nb, 4, 64], BF16)
        Bdec = P.tmp([nb, 4, 128], BF16)
        CdT = P.tmp([128, 4, nb], BF16)
        ycs = [P.tmp([128, nb], F32) for _ in range(2)]
        if not prm:
            Bm = P.tmp([64, 16, 128], BF16)
        for b_ in range(NBk):
            cs = slice(b_ * nb, (b_ + 1) * nb)
            if prm:
                P.copy(hTb.t, hTst.t[:, l], r=[hTst.k], w=[hTb.k])
            else:
                P.copy(hTb.t, hT0.t, r=[hT0.k], w=[hTb.k])
            pE, pGx = bank(1), bank(2)
            for h in range(4):
                P.mm(pE.t[0:nb, h * nb:(h + 1) * nb], aa.t[:, b_, h:h + 1].to_broadcast([nb, nb]), TRI, start=True, stop=False, r=[aa.k, cK], w=[pE.k])
                P.mm(pE.t[0:nb, h * nb:(h + 1) * nb], idn, NMN, start=False, stop=True, r=[ident.k, cK], w=[pE.k])
                P.mm(pGx.t[:, h * nb:(h + 1) * nb], aa.t[:, b_, h:h + 1].to_broadcast([nb, 128]), TRI, r=[aa.k, cK], w=[pGx.k])
            for h in range(4):
                P.act(E2.t[:, h, :], pE.t[0:nb, h * nb:(h + 1) * nb], AF.Exp, bias=negc.t[:, b_, h:h + 1], r=[pE.k, negc.k], w=[E2.k])
            P.act(eG.t, pGx.t[:, 0:4 * nb].rearrange("p (h q) -> p h q", h=4), AF.Exp, r=[pGx.k], w=[eG.k])
            pM = bank(3)
            for g in range(2):
                P.mm(pM.t[0:nb, g * nb:(g + 1) * nb], bcb.t[:, g, cs], bcb.t[:, 2 + g, cs], r=[bcb.k], w=[pM.k])
            P.tt(MD.t.rearrange("p (g e) q -> p g e q", g=2), pM.t[0:nb, 0:2 * nb].rearrange("p (g q) -> p g q", g=2).unsqueeze(2).to_broadcast([nb, 2, 2, nb]),
                 E2.t.rearrange("p (g e) q -> p g e q", g=2), ALU.mult, r=[pM.k, E2.k], w=[MD.k])
            pX, pBt = bank(4), bank(5)
            for j in range(2):
                P.tr(pX.t[0:nb, j * 128:(j + 1) * 128], xbc.t[:, j, cs], ident.t, r=[xbc.k, ident.k], w=[pX.k])
                P.tr(pBt.t[0:nb, j * 128:(j + 1) * 128], xbc.t[:, 2 + j, cs], ident.t, r=[xbc.k, ident.k], w=[pBt.k])
            P.tt(xdt.t, pX.t[0:nb, 0:256].rearrange("p (h d) -> p h d", h=4), dt.t[:, b_, :].unsqueeze(2).to_broadcast([nb, 4, 64]), ALU.mult, r=[pX.k, dt.k], w=[xdt.k])
            P.tt(Bdec.t.rearrange("p (g e) n -> p g e n", g=2), pBt.t[0:nb, 0:256].rearrange("p (g n) -> p g n", g=2).unsqueeze(2).to_broadcast([nb, 2, 2, 128]),
                 dB.t[:, b_, :].rearrange("p (g e) -> p g e", g=2).unsqueeze(3).to_broadcast([nb, 2, 2, 128]), ALU.mult, r=[pBt.k, dB.k], w=[Bdec.k])
            P.tt(CdT.t.rearrange("p (g e) q -> p g e q", g=2), xbc.t[:, 4:6, cs].unsqueeze(2).to_broadcast([128, 2, 2, nb]),
                 eG.t.rearrange("p (g e) q -> p g e q", g=2), ALU.mult, r=[xbc.k, eG.k], w=[CdT.k])
            pY = bank(6)
            for h in range(4):
                po = pY.t[(h % 2) * 64:(h % 2) * 64 + 64, (h // 2) * nb:(h // 2 + 1) * nb]
                P.mm(po, xdt.t[:, h, :], MD.t[:, h, :], start=True, stop=False, r=[xdt.k, MD.k], w=[pY.k])
                for sg_ in range(NSEG):
                    P.mm(pY.t[(h % 2) * 64:(h % 2) * 64 + 64, (h // 2) * nb + sg_ * segw:(h // 2) * nb + (sg_ + 1) * segw], hv(sg_, h), CdT.t[:, h, sg_ * segw:(sg_ + 1) * segw],
                         start=False, stop=(sg_ == NSEG - 1), r=[hTb.k, CdT.k], w=[pY.k])
            for ch in range(2):
                yc = ycs[ch]
                P.stt(yc.t, xbc.t[:, ch, cs], dC.t[:, l, ch:ch + 1], pY.t[:, ch * nb:(ch + 1) * nb], ALU.mult, ALU.add, r=[xbc.k, dC.k, pY.k], w=[yc.k])
                P.tt(yz.t[:, ch, cs], yc.t, szC.t[:, ch, cs], ALU.mult, r=[yc.k, szC.k], w=[yz.k])
            if prm:
                pS = bank(1)
                for h in range(4):
                    P.mm(pS.t[:, h * 64:(h + 1) * 64], Bdec.t[:, h, :], xdt.t[:, h, :], r=[Bdec.k, xdt.k], w=[pS.k])
                P.tt(hTst.t[:, l], hTst.t[:, l], eG.t[:, :, nb - 1:nb].to_broadcast([128, 4, 64]), ALU.mult, r=[hTst.k, eG.k], w=[hTst.k])
                P.tt(hTst.t[:, l], hTst.t[:, l], pS.t[:, 0:256].rearrange("p (h d) -> p h d", h=4), ALU.add, r=[hTst.k, pS.k], w=[hTst.k])
            else:
                for h in range(4):
                    P.tt(Bm.t, Bdec.t[:, h, :].unsqueeze(1).to_broadcast([64, 16, 128]), segmask.t.unsqueeze(2).to_broadcast([64, 16, 128]), ALU.mult,
                         r=[Bdec.k, segmask.k], w=[Bm.k])
                    for half in range(2):
                        pS = bank(1 + half)
                        for s8 in range(8):
                            sg_ = half * 8 + s8
                            P.mm(pS.t[:, s8 * 64:(s8 + 1) * 64], Bm.t[:, sg_, :], xdt.t[:, h, :], r=[Bm.k, xdt.k], w=[pS.k])
                        ss_ = slice(half * 8, half * 8 + 8)
                        alb = eG.t[:, h, :].rearrange("p (s t) -> p s t", t=4)[:, ss_, 3:4].to_broadcast([128, 8, 64])
                        P.tt(hTn.t[:, ss_, h, :], hT0.t[:, ss_, h, :], alb, ALU.mult, r=[hT0.k, eG.k], w=[hTn.k])
                        P.tt(hTn.t[:, ss_, h, :], hTn.t[:, ss_, h, :], pS.t[:, :].rearrange("p (s v) -> p s v", v=64), ALU.add, r=[hTn.k, pS.k], w=[hTn.k])
        sqz = P.tmp([128, 2, NT], BF16)
        P.act(sqz.t, yz.t, AF.Square, r=[yz.k], w=[sqz.k])
        for ch in range(2):
            P.mm(psb[0].t[:, 0:NT], ones_bf.t, sqz.t[:, ch, :], start=(ch == 0), stop=(ch == 1), r=[sqz.k, ones_bf.k], w=[psb[0].k])
        rstd = rstd_from(psb[0].t[:, 0:NT], psb[0].k, [128, NT], 1.0 / 256)
        for ch in range(2):
            P.stt(mixT.t[:, 2 + ch, :], yz.t[:, ch, :], gnC.t[:, l, ch:ch + 1], rstd.t, ALU.mult, ALU.mult, r=[yz.k, gnC.k, rstd.k], w=[mixT.k])
        if "mixC" in dbg:
            of = P.tmp([128, 2, NT], F32)
            P.copy(of.t, mixT.t[:, 2:4, :], r=[mixT.k], w=[of.k])
            P.dbg(f"mixC_{kind}{l}", of.t, r=[of.k])
        P.release(m)


    def rope_rows(dst, src, tab, n, H, rk):
        t1 = P.tmp([n, H, 32], F32)
        P.tt(dst.t, src.t, tab[:, 0:32].unsqueeze(1).to_broadcast([n, H, 32]), ALU.mult, r=[src.k] + rk, w=[dst.k])
        P.tt(t1.t[:, :, 0:16], src.t[:, :, 16:32], tab[:, 32:48].unsqueeze(1).to_broadcast([n, H, 16]), ALU.mult, r=[src.k] + rk, w=[t1.k])
        P.tt(t1.t[:, :, 16:32], src.t[:, :, 0:16], tab[:, 48:64].unsqueeze(1).to_broadcast([n, H, 16]), ALU.mult, r=[src.k] + rk, w=[t1.k])
        P.tt(dst.t, dst.t, t1.t, ALU.add, r=[t1.k, dst.k], w=[dst.k])

    def mixD(cx):
        l, T, NT, prm, xn, mixT, nb, NBk, as3, kind = (cx[k] for k in ["l", "T", "NT", "prm", "xn", "mixT", "nb", "NBk", "as3", "kind"])
        m = P.mark()
        wb = wstage([(lambda b_: as3(b_)[:, :, 0:416], cx["win"][:, :, C_CQ:C_CQ + 416], cx["sw"])])
        wv = as3(wb.t)
        sqn_ = cx["sqn"]
        cqnT = Tl(sqn_.t[:, 0:2, :], ("sqn_sub", 0))
        olat = P.tmp([128, 4, NT], BF16)
        qnT = Tl(olat.t[0:64], olat.k)
        qlT = P.tmp([128, 4, NT], BF16)
        qpT = P.tmp([32, 4, NT], BF16)
        if prm:
            b0 = T["t0"] // 128
            rowbase = T["b"] * SEQ + T["t0"]
            KTv, Ktv, KPTv = KT.t[:, l], Kt.t[:, l], KPT.t[:, l]
        else:
            KTn = P.tmp([128, 64], BF16)
            Ktn = P.tmp([64, 128], BF16)
            KPTn = P.tmp([32, 64], BF16)
        for b_ in range(NBk):
            cs = slice(b_ * nb, (b_ + 1) * nb)
            pd = bank(1 + b_ % 2)
            for c in range(8):
                P.mm(pd.t[0:nb, 0:416], xn.t[:, c, cs], wv[:, c, 0:416], start=(c == 0), stop=(c == 7), r=[xn.k, wb.k], w=[pd.k])
            st_ = P.tmp([nb, 4], F32)
            junk = P.tmp([nb, 256], F32)
            P.memset(st_.t, 0.0, w=[st_.k])
            P.act(junk.t, pd.t[0:nb, 0:256], AF.Square, accum_out=st_.t[:, 0:1], r=[pd.k, st_.k], w=[junk.k, st_.k])
            P.act(junk.t[:, 0:128], pd.t[0:nb, 256:384], AF.Square, accum_out=st_.t[:, 1:2], r=[pd.k, st_.k], w=[junk.k, st_.k])
            P.act(st_.t[:, 2:3], st_.t[:, 0:1], AF.Sqrt, scale=1.0 / 256, bias=eps_t.t[0:nb, 0:1], r=[st_.k, eps_t.k], w=[st_.k])
            P.act(st_.t[:, 3:4], st_.t[:, 1:2], AF.Sqrt, scale=1.0 / 128, bias=eps_t.t[0:nb, 0:1], r=[st_.k, eps_t.k], w=[st_.k])
            P.recip(st_.t[:, 2:4], st_.t[:, 2:4], r=[st_.k], w=[st_.k])
            cqn = P.tmp([nb, 256], F32)
            ckvn = P.tmp([nb, 128], F32)
            P.stt(cqn.t, pd.t[0:nb, 0:256], st_.t[:, 2:3], gqr.t[0:nb, l, :], ALU.mult, ALU.mult, r=[pd.k, st_.k, gqr.k], w=[cqn.k])
            P.stt(ckvn.t, pd.t[0:nb, 256:384], st_.t[:, 3:4], gkr.t[0:nb, l, :], ALU.mult, ALU.mult, r=[pd.k, st_.k, gkr.k], w=[ckvn.k])
            tab = rope_p.t[:, b0 + b_, :] if prm else rope_s.t
            kraw = P.tmp([nb, 1, 32], F32)
            P.copy(kraw.t[:, 0, :], pd.t[0:nb, 384:416], r=[pd.k], w=[kraw.k])
            kpe = P.tmp([nb, 1, 32], F32)
            rope_rows(kpe, kraw, tab, nb, 1, [rope_p.k, rope_s.k])
            if prm:
                P.dma(ckvp_d[l, rowbase + b_ * 128:rowbase + (b_ + 1) * 128, :], ckvn.t, r=[ckvn.k])
                P.dma(kpep_d[l, rowbase + b_ * 128:rowbase + (b_ + 1) * 128, :], kpe.t[:, 0, :], r=[kpe.k])
                P.copy(Ktv[:, b0 + b_, :], ckvn.t, r=[ckvn.k], w=[(Kt.k, l)], eng="gpsimd")
            else:
                P.dma(ckvs_d[l], ckvn.t, r=[ckvn.k])
                P.dma(kpes_d[l], kpe.t[:, 0, :], r=[kpe.k])
                P.copy(Ktn.t, ckvn.t, r=[ckvn.k], w=[Ktn.k], eng="gpsimd")
            pt = bank(3 + b_ % 2)
            P.tr(pt.t[:, 0:nb], ckvn.t, ident.t[0:nb, 0:nb], r=[ckvn.k, ident.k], w=[pt.k])
            P.tr(pt.t[0:32, 128:128 + nb], kpe.t[:, 0, :], ident.t[0:nb, 0:nb], r=[kpe.k, ident.k], w=[pt.k])
            for c2 in range(2):
                P.tr(pt.t[:, 256 + c2 * 128:256 + c2 * 128 + nb], cqn.t[:, c2 * 128:(c2 + 1) * 128], ident.t[0:nb, 0:nb], r=[cqn.k, ident.k], w=[pt.k])
            if prm:
                kc = slice((b0 + b_) * 128, (b0 + b_ + 1) * 128)
                P.copy(KTv[:, kc], pt.t[:, 0:128], r=[pt.k], w=[(KT.k, l)])
                P.copy(KPTv[:, kc], pt.t[0:32, 128:256], r=[pt.k], w=[(KPT.k, l)])
            else:
                P.copy(KTn.t, pt.t[:, 0:64], r=[pt.k], w=[KTn.k])
                P.copy(KPTn.t, pt.t[0:32, 128:192], r=[pt.k], w=[KPTn.k])
            for c2 in range(2):
                P.copy(cqnT.t[:, c2, cs], pt.t[:, 256 + c2 * 128:256 + c2 * 128 + nb], r=[pt.k], w=[cqnT.k], eng="scalar")
        for h in range(4):
            pq_ = bank(1 + h % 2)
            for c2 in range(2):
                P.mm(pq_.t[0:64, 0:NT], wuq.t[:, l, c2, h * 96:h * 96 + 64], cqnT.t[:, c2, :], start=(c2 == 0), stop=(c2 == 1), r=[wuq.k, cqnT.k], w=[pq_.k])
            P.copy(qnT.t[:, h, :], pq_.t[0:64, 0:NT], r=[pq_.k], w=[qnT.k], eng="scalar")
            pl_ = bank(3 + h % 2)
            P.mm(pl_.t[:, 0:NT], wukT.t[:, l, h, :], qnT.t[:, h, :], r=[wukT.k, qnT.k], w=[pl_.k])
            P.copy(qlT.t[:, h, :], pl_.t[:, 0:NT], r=[pl_.k], w=[qlT.k])
        for b_ in range(NBk):
            cs = slice(b_ * nb, (b_ + 1) * nb)
            pp = bank(5 + b_ % 2)
            wr = wuq.t[:, l, :, :].rearrange("p c (h e) -> p c h e", e=96)
            for c2 in range(2):
                P.mm(pp.t[0:nb, 0:128].rearrange("p (h e) -> p h e", h=4), cqnT.t[:, c2, cs], wr[:, c2, :, 64:96], start=(c2 == 0), stop=(c2 == 1), r=[cqnT.k, wuq.k], w=[pp.k])
            qraw = P.tmp([nb, 4, 32], F32)
            P.copy(qraw.t, pp.t[0:nb, 0:128].rearrange("p (h e) -> p h e", h=4), r=[pp.k], w=[qraw.k], eng="scalar")
            qpe = P.tmp([nb, 4, 32], F32)
            tab = rope_p.t[:, b0 + b_, :] if prm else rope_s.t
            rope_rows(qpe, qraw, tab, nb, 4, [rope_p.k, rope_s.k])
            pt = bank(1 + b_ % 2)
            for h in range(4):
                P.tr(pt.t[0:32, h * nb:(h + 1) * nb], qpe.t[:, h, :], ident.t[0:nb, 0:nb], r=[qpe.k, ident.k], w=[pt.k])
            P.copy(qpT.t[:, :, cs], pt.t[0:32, 0:4 * nb].rearrange("p (h q) -> p h q", h=4), r=[pt.k], w=[qpT.k])
        if prm:
            pTs = [Tl(sqn_.t[:, 2 + i_, :], ("sqn_sub", 1 + i_)) for i_ in range(2)]
            rs = P.tmp([128, 512], F32)
            v3 = lambda ap_: ap_.rearrange("p (h q) -> p h q", h=4)
            nkb = 0
            for qb in range(NBk):
                gq = b0 + qb
                qs = slice(qb * 128, (qb + 1) * 128)
                pO, pSm = bank(5), bank(6)
                for kb in range(gq + 1):
                    ks = slice(kb * 128, (kb + 1) * 128)
                    pS_ = bank(1 + kb % 3)
                    dg = kb == gq
                    P.mm(v3(pS_.t), KTv[:, ks], qlT.t[:, :, qs], start=True, stop=False, r=[(KT.k, l), qlT.k], w=[pS_.k])
                    P.mm(v3(pS_.t), KPTv[:, ks], qpT.t[:, :, qs], start=False, stop=not dg, r=[(KPT.k, l), qpT.k], w=[pS_.k])
                    if dg:
                        P.mm(pS_.t, identb.t, mla_diag.t, start=False, stop=True, r=[identb.k, mla_diag.k], w=[pS_.k])
                    pT_ = pTs[nkb % 2]
                    nkb += 1
                    P.act(pT_.t, pS_.t, AF.Exp, scale=MLA_SCALE, r=[pS_.k], w=[pT_.k])
                    P.mm(pO.t, Ktv[:, kb, :], pT_.t, start=(kb == 0), stop=dg, r=[(Kt.k, l), pT_.k], w=[pO.k])
                    P.mm(pSm.t, ones_bf.t, pT_.t, start=(kb == 0), stop=dg, r=[ones_bf.k, pT_.k], w=[pSm.k])
                P.recip(rs.t, pSm.t, r=[pSm.k], w=[rs.k])
                P.tt(olat.t[:, :, qs], pO.t.rearrange("p (h q) -> p h q", h=4), rs.t.rearrange("p (h q) -> p h q", h=4), ALU.mult, r=[pO.k, rs.k], w=[olat.k])
        else:
            mla_sample(cx, qlT, qpT, KTn, Ktn, KPTn, olat)
        for ch in range(2):
            po = bank(1 + ch)
            for hl in range(2):
                h = ch * 2 + hl
                P.mm(po.t[hl * 64:(hl + 1) * 64, 0:NT], wuv.t[:, l, h, :], olat.t[:, h, :], r=[wuv.k, olat.k], w=[po.k])
            P.copy(mixT.t[:, 4 + ch, :], po.t[:, 0:NT], r=[po.k], w=[mixT.k])
        if "mixD" in dbg:
            of = P.tmp([128, 2, NT], F32)
            P.copy(of.t, mixT.t[:, 4:6, :], r=[mixT.k], w=[of.k])
            P.dbg(f"mixD_{kind}{l}", of.t, r=[of.k])
        P.release(m)


    def mla_sample(cx, qlT, qpT, KTn, Ktn, KPTn, olat):
        l = cx["l"]
        KQ = 32
        NQ = 128 // KQ
        J = 8
        v3 = lambda ap_, h_: ap_.rearrange("p (h q) -> p h q", h=h_)
        ones_f = P.tmp([128, 128], F32)
        P.memset(ones_f.t, 1.0, w=[ones_f.k])
        pn = bank(1)
        P.mm(v3(pn.t[0:64, 0:256], 4), KTn.t, qlT.t, start=True, stop=False, r=[KTn.k, qlT.k], w=[pn.k])
        P.mm(v3(pn.t[0:64, 0:256], 4), KPTn.t, qpT.t, start=False, stop=True, r=[KPTn.k, qpT.k], w=[pn.k])
        pnf = P.tmp([64, 256], F32)
        P.act(pnf.t, pn.t[0:64, 0:256], AF.Exp, scale=MLA_SCALE, r=[pn.k], w=[pnf.k])
        pnb = P.tmp([64, 4, 64], BF16)
        P.tt(pnf.t, pnf.t, mla_new.t, ALU.mult, r=[pnf.k, mla_new.k], w=[pnf.k])
        P.copy(pnb.t, v3(pnf.t, 4), r=[pnf.k], w=[pnb.k])
        cbs = [P.tmp([NPG, KQ, 128], BF16) for _ in range(2)]
        kbs = [P.tmp([NPG, KQ, 32], BF16) for _ in range(2)]
        pgT = [P.tmp([128, J * NPG], BF16) for _ in range(2)]
        kpT = [P.tmp([32, J * NPG], BF16) for _ in range(2)]
        pTb = [P.tmp([NPG, J * 16], BF16) for _ in range(2)]
        pacc = [P.tmp([NPG, 16], F32) for _ in range(2)]
        pn16 = P.tmp([64, 16], F32)
        red = P.tmp([NPG, 16], F32)
        rs = P.tmp([128, 16], F32)
        groups = [(s_, q, j0) for s_ in range(NSS) for q in range(NQ) for j0 in range(0, KQ, J)]
        n = len(groups)
        gq_per_seq = NQ * (KQ // J)

        def bufs(i):
            s_, q, j0 = groups[i]
            return cbs[(s_ * NQ + q) % 2], kbs[(s_ * NQ + q) % 2], pgT[i % 2], kpT[i % 2], pTb[i % 2], bank(3 + i % 2), bank(5 + s_ % 2), pacc[s_ % 2]

        def A1(i):
            s_, q, j0 = groups[i]
            cb, kb, tt_, kt_, pt_, psc, pO, pa = bufs(i)
            if q == 0 and j0 == 0:
                P.memset(pa.t, 0.0, w=[pa.k])
            if j0 == 0:
                P.op("gpsimd", (lambda e, cb=cb, q=q, s_=s_: e.indirect_dma_start(
                    out=cb.t.rearrange("p k r -> p (k r)"), out_offset=None, in_=cckv_d.rearrange("l n (q k) r -> (l n q) (k r)", k=KQ),
                    in_offset=bass.IndirectOffsetOnAxis(ap=ptab4.t[:, s_:s_ + 1], axis=0),
                    element_offset=l * NPHYS * 128 * 128 + q * KQ * 128)), r=[ptab4.k], w=[cb.k], dma=True)
                P.op("gpsimd", (lambda e, kb=kb, q=q, s_=s_: e.indirect_dma_start(
                    out=kb.t.rearrange("p k r -> p (k r)"), out_offset=None, in_=ckpe_d.rearrange("l n (q k) r -> (l n q) (k r)", k=KQ),
                    in_offset=bass.IndirectOffsetOnAxis(ap=ptab4.t[:, s_:s_ + 1], axis=0),
                    element_offset=l * NPHYS * 128 * 32 + q * KQ * 32)), r=[ptab4.k], w=[kb.k], dma=True)
            for j in range(J):
                P.tr(psh.t[:, j * NPG:(j + 1) * NPG], cb.t[:, j0 + j, :], identb.t[0:NPG, 0:NPG], r=[cb.k, identb.k], w=[psh.k])
            P.copy(tt_.t, psh.t[:, 0:J * NPG], r=[psh.k], w=[tt_.k], eng=("scalar" if i % 2 else "vector"))

        def A2(i):
            s_, q, j0 = groups[i]
            cb, kb, tt_, kt_, pt_, psc, pO, pa = bufs(i)
            for j in range(J):
                P.tr(psh.t[0:32, j * NPG:(j + 1) * NPG], kb.t[:, j0 + j, :], identb.t[0:NPG, 0:NPG], r=[kb.k, identb.k], w=[psh.k])
            P.copy(kt_.t, psh.t[0:32, 0:J * NPG], r=[psh.k], w=[kt_.k], eng=("vector" if i % 2 else "scalar"))

        def B1(i):
            s_, q, j0 = groups[i]
            cb, kb, tt_, kt_, pt_, psc, pO, pa = bufs(i)
            qs = slice(4 * s_, 4 * s_ + 4)
            for j in range(J):
                o_ = v3(psc.t[0:NPG, j * 16:(j + 1) * 16], 4)
                P.mm(o_, tt_.t[:, j * NPG:(j + 1) * NPG], qlT.t[:, :, qs], start=True, stop=False, r=[tt_.k, qlT.k], w=[psc.k])
                P.mm(o_, kt_.t[:, j * NPG:(j + 1) * NPG], qpT.t[:, :, qs], start=False, stop=True, r=[kt_.k, qpT.k], w=[psc.k])
            P.act(pt_.t, psc.t[0:NPG, 0:J * 16], AF.Exp, scale=MLA_SCALE, r=[psc.k], w=[pt_.k])

        def B2(i):
            s_, q, j0 = groups[i]
            cb, kb, tt_, kt_, pt_, psc, pO, pa = bufs(i)
            qs = slice(4 * s_, 4 * s_ + 4)
            for j in range(J):
                P.mm(pO.t[:, 0:16], cb.t[:, j0 + j, :], pt_.t[:, j * 16:(j + 1) * 16], start=(q == 0 and j0 == 0 and j == 0), stop=False, r=[cb.k, pt_.k], w=[pO.k])
            P.red(red.t, pt_.t.rearrange("p (g c) -> p c g", g=J), r=[pt_.k], w=[red.k])
            P.tt(pa.t, pa.t, red.t, ALU.add, r=[pa.k, red.k], w=[pa.k])
            if (i + 1) % gq_per_seq == 0:
                P.mm(v3(pO.t[:, 0:16], 4), Ktn.t, pnb.t[:, :, qs], start=False, stop=True, r=[Ktn.k, pnb.k], w=[pO.k])
                P.copy(v3(pn16.t, 4), v3(pnf.t, 4)[:, :, qs], r=[pnf.k], w=[pn16.k])
                psm = bank(2)
                P.mm(psm.t[:, 0:16], ones_f.t[0:NPG, :], pa.t, start=True, stop=False, r=[ones_f.k, pa.k], w=[psm.k])
                P.mm(psm.t[:, 0:16], ones_f.t[0:64, :], pn16.t, start=False, stop=True, r=[ones_f.k, pn16.k], w=[psm.k])
                P.recip(rs.t, psm.t[:, 0:16], r=[psm.k], w=[rs.k])
                P.tt(olat.t[:, :, qs], v3(pO.t[:, 0:16], 4), v3(rs.t, 4), ALU.mult, r=[pO.k, rs.k], w=[olat.k])

        for i in range(n + 1):
            if i < n:
                A1(i)
            if i > 0:
                B1(i - 1)
            if i < n:
                A2(i)
            if i > 0:
                B2(i - 1)

    def sample_ctx(l):
        T = dict(NT=64, kind="s")
        tA = P.tmp([64, 12, 16, 3], F32)
        tC = P.tmp([128, 6, 16, 3], F32)
        S0 = P.tmp([64, 16, 4, 64], F32)
        hT0 = P.tmp([128, 16, 4, 64], F32)
        m = P.mark()
        ca = P.tmp([48, 768], F32)
        cc = P.tmp([48, 768], F32)
        P.dma(ca.t, sgc_d[l], w=[ca.k])
        P.dma(cc.t, ssc_d[l], w=[cc.k])
        for g3 in range(2):
            pb = bank(1 + g3)
            for j in range(6):
                hc = g3 * 6 + j
                P.tr(pb.t[0:64, j * 48:(j + 1) * 48], ca.t[:, hc * 64:(hc + 1) * 64], ident.t[0:48, 0:48], r=[ca.k, ident.k], w=[pb.k])
            P.copy(tA.t[:, g3 * 6:(g3 + 1) * 6].rearrange("p j s t -> p j (s t)"), pb.t[0:64, 0:288].rearrange("p (j n) -> p j n", j=6), r=[pb.k], w=[tA.k])
        pb = bank(3)
        for j in range(6):
            P.tr(pb.t[:, j * 48:(j + 1) * 48], cc.t[:, j * 128:(j + 1) * 128], ident.t[0:48, 0:48], r=[cc.k, ident.k], w=[pb.k])
        P.copy(tC.t.rearrange("p j s t -> p j (s t)"), pb.t[:, 0:288].rearrange("p (j n) -> p j n", j=6), r=[pb.k], w=[tC.k])
        for h in range(4):
            P.dma(S0.t[:, :, h, :], sgs_d[l, :, h, :, :].rearrange("s k v -> k s v"), w=[S0.k])
        hsrc = ssh_d[l].rearrange("(q r) n -> r q n", r=128)
        hv_ = hT0.t.rearrange("p s h d -> p (s h d)").rearrange("p (q r) -> p q r", r=128)
        for j in range(8):
            hn = P.tmp([128, 4, 128], F32)
            P.dma(hn.t, hsrc[:, 4 * j:4 * j + 4, :], w=[hn.k])
            pb = bank(4 + j % 2)
            for q in range(4):
                P.tr(pb.t[:, q * 128:(q + 1) * 128], hn.t[:, q, :], ident.t, r=[hn.k, ident.k], w=[pb.k])
            P.copy(hv_[:, 4 * j:4 * j + 4, :], pb.t.rearrange("p (q r) -> p q r", q=4), r=[pb.k], w=[hT0.k], eng=("scalar" if j % 2 else "vector"))
        P.release(m)
        T.update(tailA=tA, tailC=tC, S0=S0, Sn=S0, hT0=hT0, hTn=hT0)
        return T

    def out_states_sample(l, T):
        m = P.mark()
        S0, hT0 = T["S0"], T["hT0"]
        for h in range(4):
            P.dma(gss_d[l, :, h, :, :].rearrange("s k v -> k s v"), S0.t[:, :, h, :], r=[S0.k])
        hdst = shs_d[l].rearrange("(q r) n -> r q n", r=128)
        hv_ = hT0.t.rearrange("p s h d -> p (s h d)").rearrange("p (q r) -> p q r", r=128)
        for j in range(8):
            pb = bank(4 + j % 2)
            for q in range(4):
                P.tr(pb.t[:, q * 128:(q + 1) * 128], hv_[:, 4 * j + q, :], ident.t, r=[hT0.k, ident.k], w=[pb.k])
            ho = P.tmp([128, 4, 128], F32)
            P.copy(ho.t, pb.t.rearrange("p (q r) -> p q r", q=4), r=[pb.k], w=[ho.k], eng=("scalar" if j % 2 else "vector"))
            P.dma(hdst[:, 4 * j:4 * j + 4, :], ho.t, r=[ho.k])
        P.release(m)

    def out_states_prompt(l, b):
        m = P.mark()
        P.dma(gsp_d[l, b].rearrange("h k v -> k h v"), Sst.t[:, l], r=[Sst.k])
        rowA = P.tmp([3, 768], F32)
        rowC = P.tmp([3, 768], F32)
        for hb in range(2):
            pa_, pc_ = bank(1 + hb), bank(5 + hb)
            for j in range(6):
                P.tr(pa_.t[0:3, j * 64:(j + 1) * 64], tailA.t[:, l, hb * 6 + j, :], ident.t[0:64, 0:64], r=[tailA.k, ident.k], w=[pa_.k])
            for j in range(3):
                P.tr(pc_.t[0:3, j * 128:(j + 1) * 128], tailC.t[:, l, hb * 3 + j, :], ident.t, r=[tailC.k, ident.k], w=[pc_.k])
            P.copy(rowA.t[:, hb * 384:(hb + 1) * 384], pa_.t[0:3, 0:384], r=[pa_.k], w=[rowA.k])
            P.copy(rowC.t[:, hb * 384:(hb + 1) * 384], pc_.t[0:3, 0:384], r=[pc_.k], w=[rowC.k])
        P.dma(gcp_d[l, b], rowA.t, r=[rowA.k])
        P.dma(scp_d[l, b], rowC.t, r=[rowC.k])
        pb = bank(4)
        hv_ = hTst.t[:, l].rearrange("p h d -> p (h d)")
        for g in range(2):
            P.tr(pb.t[:, g * 128:(g + 1) * 128], hv_[:, g * 128:(g + 1) * 128], ident.t, r=[hTst.k, ident.k], w=[pb.k])
        ho = P.tmp([128, 2, 128], F32)
        P.copy(ho.t, pb.t[:, 0:256].rearrange("p (g n) -> p g n", g=2), r=[pb.k], w=[ho.k])
        P.dma(shp_d[l, b * 256:(b + 1) * 256, :].rearrange("(g r) n -> r g n", r=128), ho.t, r=[ho.k])
        P.release(m)

    def mix_out(cx):
        l, NT, mixT, outA = cx["l"], cx["NT"], cx["mixT"], cx["outA"]
        wo = s_wout[l]
        tok = ("s_wout", l)
        wbA = wstage([(lambda b_: b_[0:64, :].rearrange("p (h d) -> p h d", h=4), wo[0:256, :].rearrange("(h p) d -> p h d", p=64), tok)])
        wbB = wstage([(lambda b_: b_.rearrange("p (c d) -> p c d", c=4), wo[256:768, :].rearrange("(c p) d -> p c d", p=128), tok)])
        wbC = wstage([(lambda b_: b_[:, 0:2048].rearrange("p (c d) -> p c d", c=2), wo[768:1024, :].rearrange("(c p) d -> p c d", p=128), tok)])
        vA = wbA.t[0:64, :].rearrange("p (h d) -> p h d", h=4)
        vB = wbB.t.rearrange("p (c d) -> p c d", c=4)
        vC = wbC.t[:, 0:2048].rearrange("p (c d) -> p c d", c=2)
        ysb = P.tmp([128, 8, NT], F32)
        sq = cx["sqn"]
        for p_ in range(2):
            for dcl in range(4):
                dc = 4 * p_ + dcl
                pb = psb[1 + dcl]
                ds_ = slice(dc * 128, (dc + 1) * 128)
                for h in range(4):
                    P.mm(pb.t[:, 0:NT], vA[:, h, ds_], outA.t[:, h, :], start=(h == 0), stop=False, r=[wbA.k, outA.k], w=[pb.k])
                for c in range(4):
                    P.mm(pb.t[:, 0:NT], vB[:, c, ds_], mixT.t[:, c, :], start=False, stop=False, r=[wbB.k, mixT.k], w=[pb.k])
                for c in range(2):
                    P.mm(pb.t[:, 0:NT], vC[:, c, ds_], mixT.t[:, 4 + c, :], start=False, stop=(c == 1), r=[wbC.k, mixT.k], w=[pb.k])
                P.copy(ysb.t[:, dc, :], pb.t[:, 0:NT], r=[pb.k], w=[ysb.k])
                P.act(sq.t[:, dc, :], pb.t[:, 0:NT], AF.Square, r=[pb.k], w=[sq.k])
        if "mix" in dbg:
            P.dbg(f"mix_{cx['kind']}{l}", ysb.t, r=[ysb.k])
        postnorm_residual(ysb, sq, l, 3, NT)
        P.release(cx["m_all"])

    def run_layer(l, T):
        NT = T["NT"]
        ffn(l, 0, NT)
        cx = mixer(l, T)
        mixB(cx)
        mixC(cx)
        mixD(cx)
        mix_out(cx)
        ffn(l, 1, NT)

    def run_all(groups="ps", layers=(0, 1)):
        for b in (range(NPS) if "p" in groups else []):
            for t_ in (Sst, hTst, tailA, tailC):
                P.memset(t_.t, 0.0, w=[t_.k])
            for ti in range(SEQ // TT):
                t0 = ti * TT
                load_x(xp_d[b * SEQ + t0:b * SEQ + t0 + TT, :], TT)
                for l in layers:
                    run_layer(l, dict(NT=TT, kind="p", b=b, t0=t0))
                    if ti == SEQ // TT - 1:
                        out_states_prompt(l, b)
                store_x(yp_d[b * SEQ + t0:b * SEQ + t0 + TT, :], TT)
        if "s" not in groups:
            return
        load_x(xs_d, 64)
        for l in layers:
            m = P.mark()
            T = sample_ctx(l)
            run_layer(l, T)
            out_states_sample(l, T)
            P.release(m)
        store_x(ys_d, 64)

    st = dict(locals())
    return st


N_CORES = 8
_WNAMES = ["norm_g", "ffn_w_in", "ffn_w_out", "w_in", "w_out", "gdn_conv_w", "gdn_a_log", "gdn_dt_bias", "gdn_norm_g",
           "mlp_ln_g", "mlp_ln_b", "mlp_ws", "mlp_bs", "ssm_conv_w", "ssm_conv_b", "ssm_a_log", "ssm_dt_bias", "ssm_d",
           "ssm_norm_g", "mla_q_norm_g", "mla_w_uq", "mla_kv_norm_g", "mla_w_uk", "mla_w_uv"]


def core_inputs(inputs, cfg, c, consts):
    NPS, NSS = cfg["NPS"], cfg["NSS"]
    f = lambda a: np.ascontiguousarray(np.asarray(a, dtype=np.float32))
    m = {}
    m["xp"] = f(inputs["x_prompt"][c * NPS:(c + 1) * NPS]).reshape(-1, D)
    m["xs"] = f(inputs["x_sample"][c * NSS:(c + 1) * NSS]).reshape(-1, D)
    m["cache_ckv"] = inputs["cache_ckv"]
    m["cache_kpe"] = inputs["cache_kpe"]
    m["ptab"] = np.ascontiguousarray(np.asarray(inputs["page_table"][c * NSS:(c + 1) * NSS], dtype=np.int32)).reshape(1, -1)
    m["state_gdn_s"] = f(inputs["state_gdn_s"][:, c * NSS:(c + 1) * NSS])
    m["state_gdn_conv"] = f(inputs["state_gdn_conv"][:, c * NSS:(c + 1) * NSS]).reshape(2, -1, 768)
    m["state_ssm_h"] = f(inputs["state_ssm_h"][:, c * NSS:(c + 1) * NSS]).reshape(2, -1, 128)
    m["state_ssm_conv"] = f(inputs["state_ssm_conv"][:, c * NSS:(c + 1) * NSS]).reshape(2, -1, 768)
    for k in _WNAMES:
        m[k] = inputs[k]
    for k, v in consts.items():
        m["c_" + k] = v
    return m


def assemble(results, cfg):
    NPS, SEQ, NSS = cfg["NPS"], cfg["SEQ"], cfg["NSS"]
    cat = lambda xs, ax: np.concatenate(xs, axis=ax)
    R_ = results
    out = [
        cat([r["yp"].reshape(NPS, SEQ, D) for r in R_], 0),
        cat([r["ys"].reshape(NSS, 4, D) for r in R_], 0),
        cat([r["ckv_p"].reshape(2, NPS, SEQ, 128) for r in R_], 1),
        cat([r["kpe_p"].reshape(2, NPS, SEQ, 32) for r in R_], 1),
        cat([r["gs_p"].reshape(2, NPS, 4, 64, 64) for r in R_], 1),
        cat([r["gc_p"].reshape(2, NPS, 3, 768) for r in R_], 1),
        cat([r["sh_p"].reshape(2, NPS, 4, 64, 128) for r in R_], 1),
        cat([r["sc_p"].reshape(2, NPS, 3, 768) for r in R_], 1),
        cat([r["ckv_s"].reshape(2, NSS, 4, 128) for r in R_], 1),
        cat([r["kpe_s"].reshape(2, NSS, 4, 32) for r in R_], 1),
        cat([r["gs_s"].reshape(2, NSS, 4, 64, 64) for r in R_], 1),
        cat([r["gc_s"].reshape(2, NSS, 3, 768) for r in R_], 1),
        cat([r["sh_s"].reshape(2, NSS, 4, 64, 128) for r in R_], 1),
        cat([r["sc_s"].reshape(2, NSS, 3, 768) for r in R_], 1),
        cat([r["mv_s"].reshape(2, NSS, 4, 256) for r in R_], 1),
    ]
    return tuple(np.ascontiguousarray(o.astype(np.float32)) for o in out)


def kernel(**inputs):
    inputs = {k: np.asarray(v) for k, v in inputs.items()}
    B, SEQ = inputs["x_prompt"].shape[0], inputs["x_prompt"].shape[1]
    DB = inputs["x_sample"].shape[0]
    NPG = inputs["page_table"].shape[1]
    NPHYS = inputs["cache_ckv"].shape[1]
    cfg = dict(NPS=B // N_CORES, SEQ=SEQ, NSS=DB // N_CORES, PAST=NPG * 128, NPHYS=NPHYS, TT=512)
    st = build_program(cfg)
    st["run_all"]()
    st["P"].build()
    consts = make_consts(cfg)
    in_maps = [core_inputs(inputs, cfg, c, consts) for c in range(N_CORES)]
    res = run_bass_kernel_spmd(st["nc"], in_maps, core_ids=list(range(N_CORES)))
    return assemble(res.results, cfg)
```

```python
import math
import numpy as np
import ml_dtypes
import concourse.bass as bass
import concourse.mybir as mybir
from concourse.bass_utils import run_bass_kernel_spmd

F32 = mybir.dt.float32
BF16 = mybir.dt.bfloat16
I32 = mybir.dt.int32
AF = mybir.ActivationFunctionType
ALU = mybir.AluOpType
AX = mybir.AxisListType

ENGS = ("tensor", "vector", "scalar", "gpsimd", "sync")
N_DMA_SEMS = 48

D = 1024
DC = 8
FF = 2816
FC = 22
EPS = 1e-6
NEG = -30000.0
C_QKV, C_ZA, C_GA, C_U, C_V, C_ZC, C_XBC, C_DT, C_CQ, C_CKV, C_KPE = 0, 768, 1024, 1032, 1288, 1544, 1800, 2568, 2572, 2828, 2956
IN_COLS = 2988
MLA_SCALE = 96 ** -0.5


class Tl:
    def __init__(self, t, k):
        self.t = t
        self.k = k


class Prog:
    def __init__(self, nc, arena_f=0, arena_b=0):
        self.nc = nc
        self.ops = []
        self.n_alloc = 0
        self.af = nc.alloc_sbuf_tensor("arena_f", [128, arena_f], F32) if arena_f else None
        self.ab = nc.alloc_sbuf_tensor("arena_b", [128, arena_b], BF16) if arena_b else None
        self.af_n, self.ab_n = arena_f, arena_b
        self.af_off = 0
        self.ab_off = 0
        self.af_max = 0
        self.ab_max = 0
        self.dbg_outs = {}
        self.psum_tokens = set()

    def sb(self, shape, dtype=F32, name=None):
        self.n_alloc += 1
        nm = name or f"sb{self.n_alloc}"
        return Tl(self.nc.alloc_sbuf_tensor(nm, list(shape), dtype)[:], nm)

    def ps(self, shape, dtype=F32, name=None):
        self.n_alloc += 1
        nm = name or f"ps{self.n_alloc}"
        self.psum_tokens.add(nm)
        return Tl(self.nc.alloc_psum_tensor(nm, list(shape), dtype)[:], nm)

    def mark(self):
        return (self.af_off, self.ab_off)

    def release(self, m):
        self.af_off, self.ab_off = m
        self.barrier()

    def tmp(self, shape, dtype=F32):
        n = 1
        for s in shape[1:]:
            n *= s
        n = (n + 7) // 8 * 8
        self.n_alloc += 1
        if dtype == F32:
            base, off = self.af, self.af_off
            self.af_off += n
            self.af_max = max(self.af_max, self.af_off)
            assert self.af_off <= self.af_n, ("arena_f overflow", self.af_off, self.af_n)
        else:
            base, off = self.ab, self.ab_off
            self.ab_off += n
            self.ab_max = max(self.ab_max, self.ab_off)
            assert self.ab_off <= self.ab_n, ("arena_b overflow", self.ab_off, self.ab_n)
        n0 = 1
        for s in shape[1:]:
            n0 *= s
        v = base[0:shape[0], off:off + n0]
        if len(shape) == 3:
            v = v.rearrange("p (a b) -> p a b", a=shape[1])
        elif len(shape) == 4:
            v = v.rearrange("p (a b c) -> p a b c", a=shape[1], b=shape[2])
        return Tl(v, f"tmp{self.n_alloc}")

    def op(self, eng, fn, r=(), w=(), dma=False, persist=False):
        pr = [t for t in r if t in self.psum_tokens]
        if pr:
            w = tuple(w) + tuple(pr)
        self.ops.append(dict(eng=eng, fn=fn, r=tuple(r), w=tuple(w), dma=dma, persist=persist))

    def barrier(self):
        self.ops.append(dict(barrier=True))

    def dma(self, out, in_, r=(), w=(), eng=None, persist=False, **kw):
        if eng is None:
            eng = "sync" if out.dtype == in_.dtype else "gpsimd"
        self.op(eng, lambda e: e.dma_start(out=out, in_=in_, **kw), r, w, dma=True, persist=persist)

    def mm(self, out, lhsT, rhs, start=True, stop=True, r=(), w=(), **kw):
        self.op("tensor", lambda e: e.matmul(out, lhsT, rhs, start=start, stop=stop, **kw), r, w)

    def tr(self, out, in_, ident, r=(), w=()):
        self.op("tensor", lambda e: e.transpose(out, in_, ident), r, w)

    def act(self, out, in_, func, r=(), w=(), **kw):
        self.op("scalar", lambda e: e.activation(out, in_, func, **kw), r, w)

    def tt(self, out, in0, in1, op, r=(), w=(), eng="vector"):
        self.op(eng, lambda e: e.tensor_tensor(out, in0, in1, op), r, w)

    def ts(self, out, in0, s1, s2, op0, op1=None, r=(), w=(), eng="vector"):
        if op1 is None:
            self.op(eng, lambda e: e.tensor_scalar(out, in0, s1, s2, op0), r, w)
        else:
            self.op(eng, lambda e: e.tensor_scalar(out, in0, s1, s2, op0, op1), r, w)

    def stt(self, out, in0, scalar, in1, op0, op1, r=(), w=(), eng="vector"):
        self.op(eng, lambda e: e.scalar_tensor_tensor(out, in0, scalar, in1, op0, op1), r, w)

    def copy(self, out, in_, r=(), w=(), eng="vector"):
        if eng == "scalar":
            self.op(eng, lambda e: e.copy(out, in_), r, w)
        else:
            self.op(eng, lambda e: e.tensor_copy(out, in_), r, w)

    def memset(self, ap, val, w=(), eng="vector"):
        self.op(eng, lambda e: e.memset(ap, val), (), w)

    def recip(self, out, in_, r=(), w=()):
        self.op("vector", lambda e: e.reciprocal(out, in_), r, w)

    def red(self, out, in_, r=(), w=(), op=None):
        self.op("vector", lambda e: e.tensor_reduce(out, in_, AX.X, op or ALU.add), r, w)

    def dbg(self, name, ap, r=()):
        shp = list(ap.shape)
        d = self.nc.dram_tensor("dbg_" + name, shp, F32, kind="ExternalOutput").ap()
        self.dbg_outs[name] = shp
        self.dma(d, ap, r=r)

    def hoist(self):
        new = []
        for o in self.ops:
            if o.get("persist") and o.get("dma") and not o.get("barrier"):
                btok = o["w"][0]
                pos = None
                for idx in range(len(new) - 1, max(-1, len(new) - 6000), -1):
                    n = new[idx]
                    if n.get("barrier"):
                        continue
                    if btok in n["r"] or btok in n["w"]:
                        pos = idx
                        break
                if pos is None:
                    new.append(o)
                else:
                    new.insert(pos + 1, o)
            else:
                new.append(o)
        self.ops = new

    def build(self):
        nc = self.nc
        self.hoist()
        ops = self.ops
        last_w = {}
        readers = {}
        last_on_eng = {}
        pend_dma = []
        pend_prev = []
        barrier_deps = None
        first_after = {}
        for i, o in enumerate(ops):
            if o.get("barrier"):
                barrier_deps = set(last_on_eng.values()) | set(pend_dma) | set(pend_prev)
                first_after = {e: True for e in ENGS}
                pend_prev = pend_dma
                pend_dma = []
                continue
            deps = set()
            for t in o["r"]:
                if t in last_w:
                    deps.add(last_w[t])
            for t in o["w"]:
                if t in last_w:
                    deps.add(last_w[t])
                for j in readers.get(t, {}).values():
                    deps.add(j)
            if barrier_deps is not None and first_after.get(o["eng"]):
                deps |= barrier_deps
                first_after[o["eng"]] = False
            deps.discard(i)
            if o["eng"] == "tensor" and not o["dma"]:
                deps = {j for j in deps if not (ops[j]["eng"] == "tensor" and not ops[j]["dma"])}
            o["deps"] = deps
            for t in o["w"]:
                last_w[t] = i
                readers[t] = {}
            for t in o["r"]:
                key = ("dma", i) if o["dma"] else o["eng"]
                readers.setdefault(t, {})[key] = i
            if o["dma"]:
                if not o["persist"]:
                    pend_dma.append(i)
            else:
                last_on_eng[o["eng"]] = i
        ops = [o for o in ops if not o.get("barrier")]
        idx_map = {}
        k = 0
        for i, o in enumerate(self.ops):
            if not o.get("barrier"):
                idx_map[i] = k
                k += 1
        for o in ops:
            o["deps"] = {idx_map[j] for j in o["deps"]}
            o["sig"] = False
        for o in ops:
            for j in o["deps"]:
                if not ops[j]["dma"]:
                    ops[j]["sig"] = True
        esem = {e: nc.alloc_semaphore(f"e_{e}") for e in ENGS}
        dsem = [nc.alloc_semaphore(f"d_{k}") for k in range(N_DMA_SEMS)]
        ecount = {e: 0 for e in ENGS}
        dcount = [0] * N_DMA_SEMS
        nd = 0
        for o in ops:
            if o["dma"]:
                k = nd % N_DMA_SEMS
                nd += 1
                o["dsem"] = k
                o["dprev"] = dcount[k]
                dcount[k] += 16
                o["dval"] = dcount[k]
            elif o["sig"]:
                ecount[o["eng"]] += 1
                o["eval"] = ecount[o["eng"]]
        ewaited = {e: {f: 0 for f in ENGS} for e in ENGS}
        dwaited = {e: [0] * N_DMA_SEMS for e in ENGS}
        for o in ops:
            e = o["eng"]
            waits = []
            need_e = {}
            need_d = {}
            for j in o["deps"]:
                dd = ops[j]
                if dd["dma"]:
                    need_d[dd["dsem"]] = max(need_d.get(dd["dsem"], 0), dd["dval"])
                else:
                    need_e[dd["eng"]] = max(need_e.get(dd["eng"], 0), dd["eval"])
            if o["dma"] and o["dprev"] > 0:
                need_d[o["dsem"]] = max(need_d.get(o["dsem"], 0), o["dprev"])
            for f, v in need_e.items():
                if ewaited[e][f] < v:
                    ewaited[e][f] = v
                    waits.append(("e", f, v))
            for k, v in need_d.items():
                if dwaited[e][k] < v:
                    dwaited[e][k] = v
                    waits.append(("d", k, v))
            o["waits"] = waits
        final_d = list(dcount)
        self.stats = dict(n_ops=len(ops), ecount=dict(ecount), n_dma=nd,
                          per_eng={e: sum(1 for o in ops if o["eng"] == e) for e in ENGS},
                          af_max=self.af_max, ab_max=self.ab_max)

        def run_engine(ename):
            def body(eng):
                for o in ops:
                    if o["eng"] != ename:
                        continue
                    for kind, a, v in o["waits"]:
                        eng.wait_ge(esem[a] if kind == "e" else dsem[a], v)
                    ins = o["fn"](eng)
                    if o["dma"]:
                        ins.then_inc(dsem[o["dsem"]], 16)
                    elif o["sig"]:
                        ins.then_inc(esem[ename], 1)
                if ename == "sync":
                    for k in range(N_DMA_SEMS):
                        if final_d[k] > 0:
                            eng.wait_ge(dsem[k], final_d[k])
            return body

        with nc.Block() as block:
            block.tensor(run_engine("tensor"))
            block.vector(run_engine("vector"))
            block.scalar(run_engine("scalar"))
            block.gpsimd(run_engine("gpsimd"))
            block.sync(run_engine("sync"))


def _seg_consts(n, seg):
    idx = np.arange(n)
    same = (idx[:, None] // seg) == (idx[None, :] // seg)
    tri = (same & (idx[:, None] <= idx[None, :])).astype(np.float32)
    segones = same.astype(np.float32)
    nm_strict = np.where(same & (idx[None, :] > idx[:, None]), 0.0, NEG).astype(np.float32)
    nm_nonstrict = np.where(same & (idx[None, :] >= idx[:, None]), 0.0, NEG).astype(np.float32)
    pm_strict = np.where(same & (idx[:, None] > idx[None, :]), 0.0, -NEG).astype(np.float32)
    return tri, segones, nm_strict, nm_nonstrict, pm_strict


def make_consts(cfg):
    c = {}
    c["ident"] = np.eye(128, dtype=np.float32)
    tri, so, nms, nmn, pms = _seg_consts(64, 64)
    c["gp"] = np.stack([tri, so, nms, nmn, pms])
    tri, so, nms, nmn, pms = _seg_consts(64, 4)
    c["gs"] = np.stack([tri, so, nms, nmn, pms])
    tri, so, nms, nmn, pms = _seg_consts(128, 128)
    c["cp"] = np.stack([tri, so, nmn])
    c["segmask"] = (np.arange(64)[:, None] // 4 == np.arange(16)[None, :]).astype(np.float32)
    half = 16
    inv = (10000.0 ** (-np.arange(half, dtype=np.float32) / half)).astype(np.float32)

    def rt(pos):
        ang = pos.astype(np.float32)[:, None] * inv[None, :]
        cs, sn = np.cos(ang).astype(np.float32), np.sin(ang).astype(np.float32)
        return np.stack([np.concatenate([cs, cs], 1), np.concatenate([-sn, sn], 1)], 1).astype(np.float32)

    c["rope_p"] = rt(np.arange(cfg["SEQ"]))
    c["rope_s"] = rt(cfg["PAST"] + (np.arange(64) % 4))
    k = np.arange(128)
    m = np.where(k[:, None] > k[None, :], NEG, 0.0).astype(np.float32)
    c["mla_diag"] = np.tile(m, (1, 4))
    j = np.arange(64)
    mm = ((j[:, None] // 4 == j[None, :] // 4) & (j[:, None] % 4 <= j[None, :] % 4)).astype(np.float32)
    c["mla_new"] = np.tile(mm, (1, 4))
    c["ws_mask"] = np.triu(np.ones((128, 128), np.float32))
    jj = np.arange(64)
    c["wsbd_mask"] = ((jj[:, None] // 4 == jj[None, :] // 4) & (jj[:, None] % 4 <= jj[None, :] % 4)).astype(np.float32)
    rep = np.zeros((4, 64), np.float32)
    rep[np.arange(64) % 4, np.arange(64)] = 1.0
    c["rep4"] = rep
    return c


CONST_SHAPES = None


def build_program(cfg, dbg=(), upto=99):
    NPS, SEQ, NSS, PAST, NPHYS, TT = cfg["NPS"], cfg["SEQ"], cfg["NSS"], cfg["PAST"], cfg["NPHYS"], cfg["TT"]
    NPG = PAST // 128
    NTS = NSS * 4
    assert NTS == 64 and SEQ % TT == 0 and TT % 128 == 0
    nc = bass.Bass("TRN2", target_bir_lowering=False)
    P = Prog(nc, arena_f=16384, arena_b=20480)
    DI = lambda name, shape, dt=F32: nc.dram_tensor(name, list(shape), dt, kind="ExternalInput").ap()
    DO = lambda name, shape, dt=F32: nc.dram_tensor(name, list(shape), dt, kind="ExternalOutput").ap()
    DS = lambda name, shape, dt=BF16: nc.dram_tensor(name, list(shape), dt, kind="Internal").ap()

    xp_d = DI("xp", [NPS * SEQ, D])
    xs_d = DI("xs", [NTS, D])
    cckv_d = DI("cache_ckv", [2, NPHYS, 128, 128])
    ckpe_d = DI("cache_kpe", [2, NPHYS, 128, 32])
    ptab_d = DI("ptab", [1, NSS * NPG], I32)
    sgs_d = DI("state_gdn_s", [2, NSS, 4, 64, 64])
    sgc_d = DI("state_gdn_conv", [2, NSS * 3, 768])
    ssh_d = DI("state_ssm_h", [2, NSS * 4 * 64, 128])
    ssc_d = DI("state_ssm_conv", [2, NSS * 3, 768])
    W = {}
    for nm, shp in [("norm_g", [2, 6, D]), ("ffn_w_in", [2, 2, D, 2 * FF]), ("ffn_w_out", [2, 2, FF, D]),
                    ("w_in", [2, D, IN_COLS]), ("w_out", [2, D, D]), ("gdn_conv_w", [2, 4, 768]),
                    ("gdn_a_log", [2, 4]), ("gdn_dt_bias", [2, 4]), ("gdn_norm_g", [2, 64]),
                    ("mlp_ln_g", [2, 256]), ("mlp_ln_b", [2, 256]), ("mlp_ws", [2, 4, 128, 128]),
                    ("mlp_bs", [2, 4, 128]), ("ssm_conv_w", [2, 4, 768]), ("ssm_conv_b", [2, 768]),
                    ("ssm_a_log", [2, 4]), ("ssm_dt_bias", [2, 4]), ("ssm_d", [2, 4]), ("ssm_norm_g", [2, 256]),
                    ("mla_q_norm_g", [2, 256]), ("mla_w_uq", [2, 256, 384]), ("mla_kv_norm_g", [2, 128]),
                    ("mla_w_uk", [2, 4, 128, 64]), ("mla_w_uv", [2, 4, 128, 64])]:
        W[nm] = DI(nm, shp)
    cshapes = {k: v.shape for k, v in make_consts(cfg).items()}
    C = {k: DI("c_" + k, list(s)) for k, s in cshapes.items()}

    yp_d = DO("yp", [NPS * SEQ, D])
    ys_d = DO("ys", [NTS, D])
    ckvp_d = DO("ckv_p", [2, NPS * SEQ, 128])
    kpep_d = DO("kpe_p", [2, NPS * SEQ, 32])
    gsp_d = DO("gs_p", [2, NPS, 4, 64, 64])
    gcp_d = DO("gc_p", [2, NPS, 3, 768])
    shp_d = DO("sh_p", [2, NPS * 4 * 64, 128])
    scp_d = DO("sc_p", [2, NPS, 3, 768])
    ckvs_d = DO("ckv_s", [2, NTS, 128])
    kpes_d = DO("kpe_s", [2, NTS, 32])
    gss_d = DO("gs_s", [2, NSS, 4, 64, 64])
    gcs_d = DO("gc_s", [2, NSS, 3, 768])
    shs_d = DO("sh_s", [2, NSS * 4 * 64, 128])
    scs_d = DO("sc_s", [2, NSS, 3, 768])
    mvs_d = DO("mv_s", [2, NTS, 256])

    s_fin = DS("s_fin", [2, 2, D, 2 * FF])
    s_fout = DS("s_fout", [2, 2, FF, D])
    s_win = DS("s_win", [2, D, IN_COLS])
    s_wout = DS("s_wout", [2, D, D])
    for l in range(2):
        for i in range(2):
            for hh in range(2):
                P.dma(s_fin[l, i, hh * 512:(hh + 1) * 512, :], W["ffn_w_in"][l, i, hh * 512:(hh + 1) * 512, :], w=[("s_fin", l, i)])
            P.dma(s_fout[l, i], W["ffn_w_out"][l, i], w=[("s_fout", l, i)])
        P.dma(s_win[l], W["w_in"][l], w=[("s_win", l)])
        P.dma(s_wout[l], W["w_out"][l], w=[("s_wout", l)])

    ident = P.sb([128, 128], F32, "ident")
    identb = P.sb([128, 128], BF16, "identb")
    ones_bf = P.sb([128, 128], BF16, "ones_bf")
    eps_t = P.sb([128, 1], F32, "eps_t")
    gp = P.sb([64, 5, 64], F32, "gp")
    gs = P.sb([64, 5, 64], F32, "gsc")
    cp = P.sb([128, 3, 128], F32, "cp")
    segmask = P.sb([64, 16], F32, "segmask")
    rope_p = P.sb([128, SEQ // 128, 64], F32, "rope_p")
    rope_s = P.sb([64, 64], F32, "rope_s")
    mla_diag = P.sb([128, 512], BF16, "mla_diag")
    mla_new = P.sb([64, 256], F32, "mla_new")
    P.dma(ident.t, C["ident"], w=[ident.k])
    P.copy(identb.t, ident.t, r=[ident.k], w=[identb.k])
    P.memset(ones_bf.t, 1.0, w=[ones_bf.k])
    P.memset(eps_t.t, EPS, w=[eps_t.k])
    P.dma(gp.t, C["gp"].rearrange("a p q -> p a q"), w=[gp.k])
    P.dma(gs.t, C["gs"].rearrange("a p q -> p a q"), w=[gs.k])
    P.dma(cp.t, C["cp"].rearrange("a p q -> p a q"), w=[cp.k])
    P.dma(segmask.t, C["segmask"], w=[segmask.k])
    P.dma(rope_p.t, C["rope_p"].rearrange("(b p) a e -> p b (a e)", p=128), w=[rope_p.k])
    P.dma(rope_s.t, C["rope_s"].rearrange("p a e -> p (a e)"), w=[rope_s.k])
    P.dma(mla_diag.t, C["mla_diag"], w=[mla_diag.k])
    P.dma(mla_new.t, C["mla_new"], w=[mla_new.k])

    NC_ = dict(allow_slow_non_contiguous=True)
    ng = P.sb([128, 2, 6, 8], F32, "ng")
    P.dma(ng.t, W["norm_g"].rearrange("l i (c p) -> p l i c", p=128), w=[ng.k], **NC_)
    for l in range(2):
        for i in (1, 5):
            P.ts(ng.t[:, l, i, :], ng.t[:, l, i, :], 0.5, None, ALU.mult, r=[ng.k], w=[ng.k])
    wg = P.sb([128, 2, 8, 12], BF16, "wg")
    for l in range(2):
        P.dma(wg.t[:, l, :, 0:8], W["w_in"][l, :, C_GA:C_GA + 8].rearrange("(c p) n -> p c n", p=128), w=[wg.k])
        P.dma(wg.t[:, l, :, 8:12], W["w_in"][l, :, C_DT:C_DT + 4].rearrange("(c p) n -> p c n", p=128), w=[wg.k])
    cwA = P.sb([64, 2, 12, 4], F32, "cwA")
    cwC = P.sb([128, 2, 6, 4], F32, "cwC")
    for l in range(2):
        for k in range(4):
            P.dma(cwA.t[:, l, :, k], W["gdn_conv_w"][l, k].rearrange("(c p) -> p c", p=64), w=[cwA.k], **NC_)
            P.dma(cwC.t[:, l, :, k], W["ssm_conv_w"][l, k].rearrange("(c p) -> p c", p=128), w=[cwC.k], **NC_)
    cbC = P.sb([128, 2, 6], F32, "cbC")
    P.dma(cbC.t, W["ssm_conv_b"].rearrange("l (c p) -> p l c", p=128), w=[cbC.k], **NC_)
    gnA = P.sb([64, 2], F32, "gnA")
    P.dma(gnA.t, W["gdn_norm_g"].rearrange("l p -> p l"), w=[gnA.k], **NC_)
    gnC = P.sb([128, 2, 2], F32, "gnC")
    P.dma(gnC.t, W["ssm_norm_g"].rearrange("l (c p) -> p l c", p=128), w=[gnC.k], **NC_)
    dC = P.sb([128, 2, 2], F32, "dC")
    for l in range(2):
        for h in range(4):
            P.dma(dC.t[(h % 2) * 64:(h % 2) * 64 + 64, l, h // 2:h // 2 + 1], W["ssm_d"][l:l + 1, h:h + 1].to_broadcast([64, 1]), w=[dC.k])
    rows = P.sb([128, 2, 4, 4], F32, "rows")
    for l in range(2):
        for j, nm in enumerate(["gdn_dt_bias", "gdn_a_log", "ssm_dt_bias", "ssm_a_log"]):
            P.dma(rows.t[:, l, j, :], W[nm][l:l + 1, :].to_broadcast([128, 4]), w=[rows.k])
    for j in (1, 3):
        P.act(rows.t[:, :, j, :], rows.t[:, :, j, :], AF.Exp, r=[rows.k], w=[rows.k])
        P.ts(rows.t[:, :, j, :], rows.t[:, :, j, :], -1.0, None, ALU.mult, r=[rows.k], w=[rows.k])
    lnr = P.sb([128, 2, 2, 256], F32, "lnr")
    gqr = P.sb([128, 2, 256], F32, "gqr")
    gkr = P.sb([128, 2, 128], F32, "gkr")
    for l in range(2):
        P.dma(lnr.t[:, l, 0, :], W["mlp_ln_g"][l:l + 1, :].to_broadcast([128, 256]), w=[lnr.k])
        P.dma(lnr.t[:, l, 1, :], W["mlp_ln_b"][l:l + 1, :].to_broadcast([128, 256]), w=[lnr.k])
        P.dma(gqr.t[:, l, :], W["mla_q_norm_g"][l:l + 1, :].to_broadcast([128, 256]), w=[gqr.k])
        P.dma(gkr.t[:, l, :], W["mla_kv_norm_g"][l:l + 1, :].to_broadcast([128, 128]), w=[gkr.k])
    wuq = P.sb([128, 2, 2, 384], BF16, "wuq")
    wuv = P.sb([128, 2, 4, 64], BF16, "wuv")
    wukT = P.sb([64, 2, 4, 128], BF16, "wukT")
    for l in range(2):
        P.dma(wuq.t[:, l], W["mla_w_uq"][l].rearrange("(c p) n -> p c n", p=128), w=[wuq.k])
        P.dma(wuv.t[:, l], W["mla_w_uv"][l].rearrange("h r d -> r h d"), w=[wuv.k])
    bsP = P.sb([128, 2, 2, 128], F32, "bsP")
    bsS = P.sb([128, 2, 2, 16, 4], F32, "bsS")
    for l in range(2):
        for g in range(4):
            pr = slice((g % 2) * 64, (g % 2) * 64 + 64)
            P.dma(bsP.t[pr, l, g // 2, :], W["mlp_bs"][l, g:g + 1, :].to_broadcast([64, 128]), w=[bsP.k])
            P.dma(bsS.t[pr, l, g // 2, :, :], W["mlp_bs"][l, g:g + 1, 0:4].unsqueeze(1).to_broadcast([64, 16, 4]), w=[bsS.k])
    wsT = P.sb([128, 2, 4, 128], BF16, "wsT")
    wsbd = P.sb([64, 2, 4, 64], BF16, "wsbd")

    ptab_sb = P.sb([NPG, NSS], I32, "ptab_sb")
    P.dma(ptab_sb.t, ptab_d.rearrange("o (s p) -> p (o s)", p=NPG), w=[ptab_sb.k], **NC_)
    ptab4 = P.sb([NPG, NSS], I32, "ptab4")
    P.ts(ptab4.t, ptab_sb.t, 2, None, ALU.logical_shift_left, r=[ptab_sb.k], w=[ptab4.k])
    psb = [P.ps([128, 512], F32, f"psb{i}") for i in range(7)]
    psh = P.ps([128, 1024], BF16, "psh")

    m0 = P.mark()
    wsm = P.tmp([128, 128], F32)
    wbm = P.tmp([64, 64], F32)
    rep4 = P.tmp([4, 64], F32)
    P.dma(wsm.t, C["ws_mask"], w=[wsm.k])
    P.dma(wbm.t, C["wsbd_mask"], w=[wbm.k])
    P.dma(rep4.t, C["rep4"], w=[rep4.k])
    for l in range(2):
        uk = P.tmp([128, 4, 64], F32)
        P.dma(uk.t, W["mla_w_uk"][l].rearrange("h r d -> r h d"), w=[uk.k])
        for h in range(4):
            P.tr(psb[0].t[0:64, h * 128:(h + 1) * 128], uk.t[:, h, :], ident.t, r=[uk.k, ident.k], w=[psb[0].k])
        P.copy(wukT.t[:, l], psb[0].t[0:64, :].rearrange("p (h r) -> p h r", h=4), r=[psb[0].k], w=[wukT.k])
        wsn = P.tmp([128, 4, 128], F32)
        P.dma(wsn.t, W["mlp_ws"][l].rearrange("g p q -> p g q"), w=[wsn.k])
        for g in range(4):
            P.tr(psb[1].t[:, g * 128:(g + 1) * 128], wsn.t[:, g, :], ident.t, r=[wsn.k, ident.k], w=[psb[1].k])
        P.tt(wsT.t[:, l], psb[1].t[:].rearrange("q (g p) -> q g p", g=4), wsm.t.unsqueeze(1).to_broadcast([128, 4, 128]), ALU.mult,
             r=[psb[1].k, wsm.k], w=[wsT.k])
        w4 = P.tmp([4, 4, 4], F32)
        P.dma(w4.t, W["mlp_ws"][l, :, 0:4, 0:4].rearrange("g b a -> b g a"), w=[w4.k])
        y4 = P.tmp([4, 4, 64], F32)
        for g in range(4):
            P.mm(psb[2].t[0:4, g * 64:(g + 1) * 64], w4.t[:, g, :], rep4.t, r=[w4.k, rep4.k], w=[psb[2].k])
        P.copy(y4.t, psb[2].t[0:4, 0:256].rearrange("a (g p) -> a g p", g=4), r=[psb[2].k], w=[y4.k])
        for g in range(4):
            P.mm(psb[3].t[0:64, g * 64:(g + 1) * 64], rep4.t, y4.t[:, g, :], r=[y4.k, rep4.k], w=[psb[3].k])
        P.tt(wsbd.t[:, l], psb[3].t[0:64, 0:256].rearrange("q (g p) -> q g p", g=4), wbm.t.unsqueeze(1).to_broadcast([64, 4, 64]), ALU.mult,
             r=[psb[3].k, wbm.k], w=[wsbd.k])
    P.release(m0)

    xT = P.sb([128, 8, TT], F32, "xT")
    NBUF = 3
    wring = [P.sb([128, 4096], BF16, f"wring{i}") for i in range(NBUF)]
    wctr = [0]

    def wstage(loads):
        b = wring[wctr[0] % NBUF]
        wctr[0] += 1
        for dstf, src, stok in loads:
            P.dma(dstf(b.t), src, r=[stok], w=[b.k], eng="sync", persist=True)
        return b

    Sst = P.sb([64, 2, 4, 64], F32, "Sst")
    hTst = P.sb([128, 2, 4, 64], F32, "hTst")
    tailA = P.sb([64, 2, 12, 3], F32, "tailA")
    tailC = P.sb([128, 2, 6, 3], F32, "tailC")
    KT = P.sb([128, 2, SEQ], BF16, "KT")
    Kt = P.sb([128, 2, SEQ // 128, 128], BF16, "Kt")
    KPT = P.sb([32, 2, SEQ], BF16, "KPT")

    def bank(i):
        return psb[i % 7]

    def rstd_from(ps_ap, pk, shape, scale, nparts=128):
        r_ = P.tmp(shape, F32)
        P.act(r_.t, ps_ap, AF.Sqrt, scale=scale, bias=eps_t.t[0:nparts, 0:1], r=[pk, eps_t.k], w=[r_.k])
        P.recip(r_.t, r_.t, r=[r_.k], w=[r_.k])
        return r_

    def prenorm(l, i, NT):
        xn = P.tmp([128, 8, NT], BF16)
        sq = P.tmp([128, 8, NT], BF16)
        P.act(sq.t, xT.t[:, :, 0:NT], AF.Square, r=[xT.k], w=[sq.k])
        for c in range(8):
            P.mm(psb[0].t[:, 0:NT], ones_bf.t, sq.t[:, c, :], start=(c == 0), stop=(c == 7), r=[sq.k, ones_bf.k], w=[psb[0].k])
        rstd = rstd_from(psb[0].t[:, 0:NT], psb[0].k, [128, NT], 1.0 / D)
        for c in range(8):
            P.stt(xn.t[:, c, :], xT.t[:, c, 0:NT], ng.t[:, l, i, c:c + 1], rstd.t, ALU.mult, ALU.mult,
                  r=[xT.k, rstd.k, ng.k], w=[xn.k])
        return xn, sq

    def postnorm_residual(ysb, sq, l, i, NT):
        for c in range(8):
            P.mm(psb[0].t[:, 0:NT], ones_bf.t, sq.t[:, c, :], start=(c == 0), stop=(c == 7), r=[sq.k, ones_bf.k], w=[psb[0].k])
        rstd = rstd_from(psb[0].t[:, 0:NT], psb[0].k, [128, NT], 1.0 / D)
        for c in range(8):
            P.stt(ysb.t[:, c, :], ysb.t[:, c, :], ng.t[:, l, i, c:c + 1], rstd.t, ALU.mult, ALU.mult,
                  r=[ysb.k, rstd.k, ng.k], w=[ysb.k])
            P.tt(xT.t[:, c, 0:NT], xT.t[:, c, 0:NT], ysb.t[:, c, :], ALU.add, r=[xT.k, ysb.k], w=[xT.k])

    def ffn(l, i, NT, stop_at=None):
        m = P.mark()
        xn, sq = prenorm(l, 0 if i == 0 else 4, NT)
        hT = P.tmp([128, FC, NT], BF16)
        sgs_ = [P.tmp([128, NT], BF16) for _ in range(2)]
        fin = s_fin[l, i].rearrange("(c p) f -> p c f", p=128)
        for j in range(11):
            wb = wstage([
                (lambda b: b.rearrange("p (c f) -> p c f", c=8)[:, :, 0:256], fin[:, :, 256 * j:256 * j + 256], ("s_fin", l, i)),
                (lambda b: b.rearrange("p (c f) -> p c f", c=8)[:, :, 256:512], fin[:, :, FF + 256 * j:FF + 256 * j + 256], ("s_fin", l, i)),
            ])
            wv = wb.t.rearrange("p (c f) -> p c f", c=8)
            for f in range(2):
                fc = 2 * j + f
                pg, pu = psb[1 + 2 * (fc % 3)], psb[2 + 2 * (fc % 3)]
                for c in range(8):
                    P.mm(pg.t[:, 0:NT], wv[:, c, f * 128:(f + 1) * 128], xn.t[:, c, :], start=(c == 0), stop=(c == 7), r=[wb.k, xn.k], w=[pg.k])
                for c in range(8):
                    P.mm(pu.t[:, 0:NT], wv[:, c, 256 + f * 128:256 + (f + 1) * 128], xn.t[:, c, :], start=(c == 0), stop=(c == 7), r=[wb.k, xn.k], w=[pu.k])
                sg = sgs_[fc % 2]
                P.act(sg.t, pg.t[:, 0:NT], AF.Silu, r=[pg.k], w=[sg.k])
                P.tt(hT.t[:, fc, :], sg.t, pu.t[:, 0:NT], ALU.mult, r=[sg.k, pu.k], w=[(hT.k, fc)])
        if stop_at == "win":
            hf = P.tmp([128, 4, NT], F32)
            P.copy(hf.t, hT.t[:, 0:4, :], r=[(hT.k, c) for c in range(4)], w=[hf.k])
            P.dbg("hT", hf.t, r=[hf.k])
            return
        ysb = P.tmp([128, 8, NT], F32)
        fo = s_fout[l, i].rearrange("(c p) d -> p c d", p=128)
        for p_ in range(2):
            for s_ in range(6):
                nf = min(4, FC - 4 * s_)
                wb = wstage([(lambda b, nf=nf: b.rearrange("p (c d) -> p c d", c=8)[:, 0:nf, :],
                              fo[:, 4 * s_:4 * s_ + nf, 512 * p_:512 * p_ + 512], ("s_fout", l, i))])
                wv = wb.t.rearrange("p (c d) -> p c d", c=8)
                for dcl in range(4):
                    for f in range(nf):
                        fc = 4 * s_ + f
                        P.mm(psb[1 + dcl].t[:, 0:NT], wv[:, f, dcl * 128:(dcl + 1) * 128], hT.t[:, fc, :], start=(fc == 0), stop=(fc == FC - 1),
                             r=[wb.k, (hT.k, fc)], w=[psb[1 + dcl].k])
            for dcl in range(4):
                c = 4 * p_ + dcl
                P.copy(ysb.t[:, c, :], psb[1 + dcl].t[:, 0:NT], r=[psb[1 + dcl].k], w=[ysb.k])
                P.act(sq.t[:, c, :], psb[1 + dcl].t[:, 0:NT], AF.Square, r=[psb[1 + dcl].k], w=[sq.k])
        if stop_at == "wout":
            P.dbg("ysb", ysb.t, r=[ysb.k])
            return
        postnorm_residual(ysb, sq, l, 1 if i == 0 else 5, NT)
        P.release(m)

    def load_x(src_rows, NT):
        m = P.mark()
        nb = min(128, NT)
        for b in range(NT // nb):
            xt = P.tmp([nb, D], F32)
            P.dma(xt.t, src_rows[b * nb:(b + 1) * nb, :], w=[xt.k])
            for cg in range(2):
                pb = psb[1 + (2 * b + cg) % 6]
                for c4 in range(4):
                    c = cg * 4 + c4
                    P.tr(pb.t[:, c4 * nb:(c4 + 1) * nb], xt.t[:, c * 128:(c + 1) * 128], ident.t[0:nb, 0:nb], r=[xt.k, ident.k], w=[pb.k])
                P.copy(xT.t[:, cg * 4:(cg + 1) * 4, b * nb:(b + 1) * nb], pb.t[:, 0:4 * nb].rearrange("p (c t) -> p c t", c=4),
                       r=[pb.k], w=[xT.k], eng=("scalar" if cg else "vector"))
        P.release(m)

    def store_x(dst_rows, NT):
        m = P.mark()
        nb = min(128, NT)
        for b in range(NT // nb):
            yt = P.tmp([nb, D], F32)
            for cg in range(2):
                pb = psb[1 + (2 * b + cg) % 6]
                for c4 in range(4):
                    c = cg * 4 + c4
                    P.tr(pb.t[0:nb, c4 * 128:(c4 + 1) * 128], xT.t[:, c, b * nb:(b + 1) * nb], ident.t, r=[xT.k, ident.k], w=[pb.k])
                P.copy(yt.t[:, cg * 512:(cg + 1) * 512], pb.t[0:nb, :], r=[pb.k], w=[yt.k], eng=("scalar" if cg else "vector"))
            P.dma(dst_rows[b * nb:(b + 1) * nb, :], yt.t, r=[yt.k])
        P.release(m)


    def softplus(x, shape, npart):
        a_ = P.tmp(shape, F32)
        P.stt(a_.t, x.t, -1.0, x.t, ALU.mult, ALU.max, r=[x.k], w=[a_.k])
        P.act(a_.t, a_.t, AF.Exp, scale=-1.0, r=[a_.k], w=[a_.k])
        P.act(a_.t, a_.t, AF.Ln, bias=1.0, r=[a_.k], w=[a_.k])
        P.ts(x.t, x.t, 0.0, None, ALU.max, r=[x.k], w=[x.k])
        P.tt(x.t, x.t, a_.t, ALU.add, r=[x.k, a_.k], w=[x.k])

    def proj_fm(dst_fn, wv, wk, col0, nch, m, xn, NT, bstart, evac):
        for j in range(nch):
            pb = bank(1 + (bstart + j) % 6)
            for c in range(8):
                P.mm(pb.t[0:m, 0:NT], wv[:, c, col0 + j * m:col0 + (j + 1) * m], xn.t[:, c, :], start=(c == 0), stop=(c == 7), r=[wk, xn.k], w=[pb.k])
            evac(j, pb.t[0:m, 0:NT], pb)

    def mixer(l, T):
        NT, kind = T["NT"], T["kind"]
        prm = kind == "p"
        m_all = P.mark()
        xn, sqn = prenorm(l, 2, NT)
        win = s_win[l].rearrange("(c p) n -> p c n", p=128)
        sw = ("s_win", l)
        as3 = lambda b_: b_.rearrange("p (c f) -> p c f", c=8)
        mixT = P.tmp([128, 6, NT], BF16)
        outA = P.tmp([64, 4, NT], BF16)
        NB64 = NT // 64
        nb = 128 if prm else 64
        NBk = NT // nb
        NSEG = 1 if prm else 16
        GC = gp if prm else gs
        i64 = ident.t[0:64, 0:64]
        W7 = 3 + NT if prm else 16 * 7

        def xview(xp_, j, i):
            if prm:
                return xp_.t[:, j, i:i + NT]
            return xp_.t[:, j, :].rearrange("p (s t) -> p s t", t=7)[:, :, i:i + 4]

        def oview(ap2):
            return ap2 if prm else ap2.rearrange("p (s t) -> p s t", t=4)

        mA = P.mark()
        qkv = P.tmp([64, 12, NT], F32)
        szA = P.tmp([64, 4, NT], BF16)
        wb1 = wstage([(lambda b_: as3(b_), win[:, :, 0:512], sw)])
        wb2 = wstage([(lambda b_: as3(b_), win[:, :, 512:1024], sw)])
        xp4 = P.tmp([64, 4, W7], F32)
        rrs = [P.tmp([64, NT], F32) for _ in range(2)]
        if not prm:
            rawA = P.tmp([64, 12, 16, 3], F32)
        for grp in range(3):
            wb_, c0 = (wb1, grp * 256) if grp < 2 else (wb2, 0)
            if prm:
                P.copy(xp4.t[:, :, 0:3], tailA.t[:, l, grp * 4:(grp + 1) * 4, :], r=[tailA.k], w=[xp4.k], eng="gpsimd")
            else:
                P.copy(xp4.t.rearrange("p j (s t) -> p j s t", t=7)[:, :, :, 0:3], T["tailA"].t[:, grp * 4:(grp + 1) * 4, :, :], r=[T["tailA"].k], w=[xp4.k], eng="gpsimd")

            def ev(j, ps_ap, pb, xp4=xp4):
                P.copy(xview(xp4, j, 3), oview(ps_ap), r=[pb.k], w=[xp4.k], eng="scalar")
            proj_fm(None, as3(wb_.t), wb_.k, c0, 4, 64, xn, NT, grp * 4, ev)
            for j in range(4):
                hc = grp * 4 + j
                ov = oview(qkv.t[:, hc, :])
                if prm:
                    P.ts(ov, xview(xp4, j, 0), cwA.t[:, l, hc, 0:1], None, ALU.mult, r=[xp4.k, cwA.k], w=[qkv.k])
                else:
                    P.tt(ov, xview(xp4, j, 0), cwA.t[:, l, hc, 0:1].unsqueeze(2).to_broadcast([64, 16, 4]), ALU.mult, r=[xp4.k, cwA.k], w=[qkv.k])
                for i in range(1, 4):
                    P.stt(ov, xview(xp4, j, i), cwA.t[:, l, hc, i:i + 1], ov, ALU.mult, ALU.add, r=[xp4.k, cwA.k, qkv.k], w=[qkv.k])
            if prm:
                P.copy(tailA.t[:, l, grp * 4:(grp + 1) * 4, :], xp4.t[:, :, NT:NT + 3], r=[xp4.k], w=[tailA.k], eng="gpsimd")
            else:
                P.copy(rawA.t[:, grp * 4:(grp + 1) * 4], xp4.t.rearrange("p j (s t) -> p j s t", t=7)[:, :, :, 4:7], r=[xp4.k], w=[rawA.k], eng="gpsimd")
        P.act(qkv.t, qkv.t, AF.Silu, r=[qkv.k], w=[qkv.k])
        if not prm:
            rowA = P.tmp([48, 768], F32)
            for hb in range(2):
                pb = bank(1 + hb)
                for j in range(6):
                    hc = hb * 6 + j
                    P.tr(pb.t[0:48, j * 64:(j + 1) * 64], rawA.t[:, hc].rearrange("p s t -> p (s t)"), i64, r=[rawA.k, ident.k], w=[pb.k])
                P.copy(rowA.t[:, hb * 384:(hb + 1) * 384], pb.t[0:48, 0:384], r=[pb.k], w=[rowA.k])
            P.dma(gcs_d[l].rearrange("s t c -> (s t) c"), rowA.t, r=[rowA.k])

        def evz(j, ps_ap, pb):
            P.act(szA.t[:, j, :], ps_ap, AF.Silu, r=[pb.k], w=[szA.k])
        proj_fm(None, as3(wb2.t), wb2.k, 256, 4, 64, xn, NT, 0, evz)
        sqk = Tl(sqn.t[0:64], sqn.k)
        P.act(sqk.t, qkv.t[:, 0:8, :], AF.Square, r=[qkv.k], w=[sqk.k])
        for hc in range(8):
            pb = bank(1 + hc % 6)
            P.mm(pb.t[0:64, 0:NT], ones_bf.t[0:64, 0:64], sqk.t[:, hc, :], r=[sqk.k, ones_bf.k], w=[pb.k])
            rr = rrs[hc % 2]
            P.act(rr.t, pb.t[0:64, 0:NT], AF.Sqrt, bias=eps_t.t[0:64, 0:1], r=[pb.k, eps_t.k], w=[rr.k])
            P.recip(rr.t, rr.t, r=[rr.k], w=[rr.k])
            P.stt(qkv.t[:, hc, :], qkv.t[:, hc, :], (0.125 if hc < 4 else 1.0), rr.t, ALU.mult, ALU.mult, r=[qkv.k, rr.k], w=[qkv.k])
        pgt = bank(0)
        for b_ in range(NB64):
            for c in range(8):
                P.mm(pgt.t[0:64, b_ * 8:(b_ + 1) * 8], xn.t[:, c, b_ * 64:(b_ + 1) * 64], wg.t[:, l, c, 0:8], start=(c == 0), stop=(c == 7), r=[xn.k, wg.k], w=[pgt.k])
        gx = P.tmp([64, NB64, 8], F32)
        pg3 = pgt.t[0:64, 0:NB64 * 8].rearrange("p (b n) -> p b n", n=8)
        P.tt(gx.t[:, :, 0:4], pg3[:, :, 0:4], rows.t[0:64, l, 0, :].unsqueeze(1).to_broadcast([64, NB64, 4]), ALU.add, r=[pgt.k, rows.k], w=[gx.k])
        P.ts(gx.t[:, :, 4:8], pg3[:, :, 4:8], -1.0, None, ALU.mult, r=[pgt.k], w=[gx.k])
        softplus(gx, [64, NB64, 8], 64)
        la = P.tmp([64, NB64, 4], F32)
        lnb = P.tmp([64, NB64, 4], F32)
        P.tt(la.t, gx.t[:, :, 0:4], rows.t[0:64, l, 1, :].unsqueeze(1).to_broadcast([64, NB64, 4]), ALU.mult, r=[gx.k, rows.k], w=[la.k])
        P.ts(lnb.t, gx.t[:, :, 4:8], -1.0, None, ALU.mult, r=[gx.k], w=[lnb.k])
        pc = bank(0)
        for b_ in range(NB64):
            P.mm(pc.t[0:64, b_ * 8:b_ * 8 + 4], GC.t[:, 0, :], la.t[:, b_, :], r=[GC.k, la.k], w=[pc.k])
            P.mm(pc.t[0:64, b_ * 8 + 4:b_ * 8 + 8], GC.t[:, 1, :], la.t[:, b_, :], r=[GC.k, la.k], w=[pc.k])
        gcm = P.tmp([64, NB64, 8], F32)
        P.copy(gcm.t, pc.t[0:64, 0:NB64 * 8].rearrange("p (b n) -> p b n", n=8), r=[pc.k], w=[gcm.k])
        negg = P.tmp([64, NB64, 4], F32)
        gpl = P.tmp([64, NB64, 4], F32)
        sc3 = P.tmp([64, NB64, 12], F32)
        P.ts(negg.t, gcm.t[:, :, 0:4], -1.0, None, ALU.mult, r=[gcm.k], w=[negg.k])
        P.tt(gpl.t, gcm.t[:, :, 0:4], lnb.t, ALU.add, r=[gcm.k, lnb.k], w=[gpl.k])
        P.copy(sc3.t[:, :, 0:4], gpl.t, r=[gpl.k], w=[sc3.k])
        P.copy(sc3.t[:, :, 4:8], lnb.t, r=[lnb.k], w=[sc3.k])
        P.tt(sc3.t[:, :, 8:12], gcm.t[:, :, 4:8], gcm.t[:, :, 0:4], ALU.subtract, r=[gcm.k], w=[sc3.k])
        P.act(sc3.t, sc3.t, AF.Exp, r=[sc3.k], w=[sc3.k])
        oT = P.tmp([64, 4, NT], F32)
        if prm:
            Sv = lambda sg_, h: Sst.t[:, l, h, :]
            Sk = Sst.k
        else:
            S0 = T["S0"]
            Sv = lambda sg_, h: S0.t[:, sg_, h, :]
            Sk = S0.k
        segw = 64 // NSEG
        n_lev = 5 if prm else 1
        slot = [P.tmp([64, 4, 64], F32) for _ in range(10)]
        if not prm:
            kdm = P.tmp([64, 16, 64], F32)
        for b_ in range(NB64):
            cs = slice(b_ * 64, (b_ + 1) * 64)
            bc = lambda t_, j: t_.t[:, b_, j:j + 1].to_broadcast([64, 64])
            pA, pB2, pC, pD = bank(1), bank(2), bank(3), bank(4)
            for h in range(4):
                hs = slice(h * 64, (h + 1) * 64)
                P.mm(pA.t[0:64, hs], bc(la, h), GC.t[:, 0, :], start=True, stop=False, r=[la.k, GC.k], w=[pA.k])
                P.mm(pA.t[0:64, hs], bc(lnb, h), i64, start=False, stop=False, r=[lnb.k, ident.k], w=[pA.k])
                P.mm(pA.t[0:64, hs], i64, GC.t[:, 2, :], start=False, stop=True, r=[GC.k, ident.k], w=[pA.k])
                P.mm(pB2.t[0:64, hs], bc(la, h), GC.t[:, 0, :], start=True, stop=False, r=[la.k, GC.k], w=[pB2.k])
                P.mm(pB2.t[0:64, hs], i64, GC.t[:, 3, :], start=False, stop=True, r=[GC.k, ident.k], w=[pB2.k])
                P.mm(pC.t[0:64, hs], bc(la, h), GC.t[:, 0, :], start=True, stop=False, r=[la.k, GC.k], w=[pC.k])
                P.mm(pC.t[0:64, hs], i64, GC.t[:, 4, :], start=False, stop=True, r=[GC.k, ident.k], w=[pC.k])
                P.mm(pD.t[0:64, hs], bc(la, h), GC.t[:, 0, :], r=[la.k, GC.k], w=[pD.k])
            E1, E2, E1T, eG = slot[0:4]
            for h in range(4):
                hs = slice(h * 64, (h + 1) * 64)
                P.act(E1.t[:, h, :], pA.t[0:64, hs], AF.Exp, bias=negg.t[:, b_, h:h + 1], r=[pA.k, negg.k], w=[E1.k])
                P.act(E2.t[:, h, :], pB2.t[0:64, hs], AF.Exp, bias=negg.t[:, b_, h:h + 1], r=[pB2.k, negg.k], w=[E2.k])
                P.act(E1T.t[:, h, :], pC.t[0:64, hs], AF.Exp, scale=-1.0, bias=gpl.t[:, b_, h:h + 1], r=[pC.k, gpl.k], w=[E1T.k])
            P.act(eG.t, pD.t[0:64, 0:256].rearrange("p (h q) -> p h q", h=4), AF.Exp, r=[pD.k], w=[eG.k])
            pG, pQ = bank(5), bank(6)
            for h in range(4):
                hs = slice(h * 64, (h + 1) * 64)
                P.mm(pG.t[0:64, hs], qkv.t[:, 4 + h, cs], qkv.t[:, 4 + h, cs], r=[qkv.k], w=[pG.k])
                P.mm(pQ.t[0:64, hs], qkv.t[:, 4 + h, cs], qkv.t[:, h, cs], r=[qkv.k], w=[pQ.k])
            Pm, Qm, QKd = slot[4:7]
            v4 = lambda pb_: pb_.t[0:64, 0:256].rearrange("p (h q) -> p h q", h=4)
            P.tt(Pm.t, v4(pG), E1.t, ALU.mult, r=[pG.k, E1.k], w=[Pm.k])
            P.tt(Qm.t, v4(pG), E1T.t, ALU.mult, r=[pG.k, E1T.k], w=[Qm.k])
            P.tt(QKd.t, v4(pQ), E2.t, ALU.mult, r=[pQ.k, E2.k], w=[QKd.k])
            Rm = slot[7]
            pq = [(slot[8], slot[9]), (slot[4], slot[5])]
            P.stt(Rm.t, Pm.t, -1.0, i64.unsqueeze(1).to_broadcast([64, 4, 64]), ALU.mult, ALU.add, r=[Pm.k, ident.k], w=[Rm.k])
            for lev in range(n_lev):
                last = lev == n_lev - 1
                pP, pQn, pR = bank(1), bank(2), bank(3)
                Pn, Qn = pq[lev % 2]
                for h in range(4):
                    hs = slice(h * 64, (h + 1) * 64)
                    if not last:
                        P.mm(pP.t[0:64, hs], Qm.t[:, h, :], Pm.t[:, h, :], r=[Qm.k, Pm.k], w=[pP.k])
                    P.mm(pQn.t[0:64, hs], Pm.t[:, h, :], Qm.t[:, h, :], r=[Qm.k, Pm.k], w=[pQn.k])
                if not last:
                    P.copy(Pn.t, v4(pP), r=[pP.k], w=[Pn.k], eng="scalar")
                P.copy(Qn.t, v4(pQn), r=[pQn.k], w=[Qn.k])
                for h in range(4):
                    hs = slice(h * 64, (h + 1) * 64)
                    P.mm(pR.t[0:64, hs], Qn.t[:, h, :], Rm.t[:, h, :], r=[Qn.k, Rm.k], w=[pR.k])
                P.tt(Rm.t, Rm.t, v4(pR), ALU.add, r=[Rm.k, pR.k], w=[Rm.k])
                Pm, Qm = Pn, Qn
            pK, pV = bank(4), bank(5)
            for h in range(4):
                hs = slice(h * 64, (h + 1) * 64)
                P.tr(pK.t[0:64, hs], qkv.t[:, 4 + h, cs], i64, r=[qkv.k, ident.k], w=[pK.k])
                P.tr(pV.t[0:64, hs], qkv.t[:, 8 + h, cs], i64, r=[qkv.k, ident.k], w=[pV.k])
            Yk, Yv, kd = slot[0], slot[1], slot[2]
            scb = lambda j0: sc3.t[:, b_, j0:j0 + 4].unsqueeze(2).to_broadcast([64, 4, 64])
            P.tt(Yk.t, v4(pK), scb(0), ALU.mult, r=[pK.k, sc3.k], w=[Yk.k])
            P.tt(kd.t, v4(pK), scb(8), ALU.mult, r=[pK.k, sc3.k], w=[kd.k])
            P.tt(Yv.t, v4(pV), scb(4), ALU.mult, r=[pV.k, sc3.k], w=[Yv.k])
            pW = bank(6)
            for h in range(4):
                hs = slice(h * 64, (h + 1) * 64)
                P.mm(pW.t[0:64, hs], Yk.t[:, h, :], Rm.t[:, h, :], r=[Yk.k, Rm.k], w=[pW.k])
            nWT, qdT, unT, un = slot[4], slot[5], slot[8], slot[9]
            P.ts(nWT.t, v4(pW), -1.0, None, ALU.mult, r=[pW.k], w=[nWT.k])
            P.tt(qdT.t, qkv.t[:, 0:4, cs], eG.t, ALU.mult, r=[qkv.k, eG.k], w=[qdT.k])
            pU = bank(1)
            for h in range(4):
                hs = slice(h * 64, (h + 1) * 64)
                P.mm(pU.t[0:64, hs], Yv.t[:, h, :], Rm.t[:, h, :], start=True, stop=False, r=[Yv.k, Rm.k], w=[pU.k])
                for sg_ in range(NSEG):
                    P.mm(pU.t[0:64, h * 64 + sg_ * segw:h * 64 + (sg_ + 1) * segw], Sv(sg_, h), nWT.t[:, h, sg_ * segw:(sg_ + 1) * segw],
                         start=False, stop=(sg_ == NSEG - 1), r=[Sk, nWT.k], w=[pU.k])
            P.copy(unT.t, v4(pU), r=[pU.k], w=[unT.k], eng="scalar")
            pUt = bank(2)
            for h in range(4):
                hs = slice(h * 64, (h + 1) * 64)
                P.tr(pUt.t[0:64, hs], unT.t[:, h, :], i64, r=[unT.k, ident.k], w=[pUt.k])
            P.copy(un.t, v4(pUt), r=[pUt.k], w=[un.k])
            pO = bank(3)
            for h in range(4):
                hs = slice(h * 64, (h + 1) * 64)
                P.mm(pO.t[0:64, hs], un.t[:, h, :], QKd.t[:, h, :], start=True, stop=False, r=[un.k, QKd.k], w=[pO.k])
                for sg_ in range(NSEG):
                    P.mm(pO.t[0:64, h * 64 + sg_ * segw:h * 64 + (sg_ + 1) * segw], Sv(sg_, h), qdT.t[:, h, sg_ * segw:(sg_ + 1) * segw],
                         start=False, stop=(sg_ == NSEG - 1), r=[Sk, qdT.k], w=[pO.k])
            P.copy(oT.t[:, :, cs], v4(pO), r=[pO.k], w=[oT.k], eng="scalar")
            if prm:
                pS = bank(4)
                for h in range(4):
                    hs = slice(h * 64, (h + 1) * 64)
                    P.mm(pS.t[0:64, hs], kd.t[:, h, :], un.t[:, h, :], r=[kd.k, un.k], w=[pS.k])
                P.tt(Sst.t[:, l], Sst.t[:, l], eG.t[:, :, 63:64].to_broadcast([64, 4, 64]), ALU.mult, r=[Sst.k, eG.k], w=[Sst.k])
                P.tt(Sst.t[:, l], Sst.t[:, l], v4(pS), ALU.add, r=[Sst.k, pS.k], w=[Sst.k])
            else:
                Sn = T["Sn"]
                for h in range(4):
                    P.tt(kdm.t, kd.t[:, h, :].unsqueeze(1).to_broadcast([64, 16, 64]), segmask.t.unsqueeze(2).to_broadcast([64, 16, 64]), ALU.mult,
                         r=[kd.k, segmask.k], w=[kdm.k])
                    for half in range(2):
                        pS = bank(4 + half)
                        for s8 in range(8):
                            sg_ = half * 8 + s8
                            P.mm(pS.t[0:64, s8 * 64:(s8 + 1) * 64], kdm.t[:, sg_, :], un.t[:, h, :], r=[kdm.k, un.k], w=[pS.k])
                        ss_ = slice(half * 8, half * 8 + 8)
                        alb = eG.t[:, h, :].rearrange("p (s t) -> p s t", t=4)[:, ss_, 3:4].to_broadcast([64, 8, 64])
                        P.tt(Sn.t[:, ss_, h, :], S0.t[:, ss_, h, :], alb, ALU.mult, r=[S0.k, eG.k], w=[Sn.k])
                        P.tt(Sn.t[:, ss_, h, :], Sn.t[:, ss_, h, :], pS.t[0:64, :].rearrange("p (s v) -> p s v", v=64), ALU.add, r=[Sn.k, pS.k], w=[Sn.k])
        sqo = Tl(sqn.t[0:64, 0:4], sqn.k)
        P.act(sqo.t, oT.t, AF.Square, r=[oT.k], w=[sqo.k])
        for h in range(4):
            pb = bank(1 + h)
            P.mm(pb.t[0:64, 0:NT], ones_bf.t[0:64, 0:64], sqo.t[:, h, :], r=[sqo.k, ones_bf.k], w=[pb.k])
            rr = rrs[h % 2]
            P.act(rr.t, pb.t[0:64, 0:NT], AF.Sqrt, scale=1.0 / 64, bias=eps_t.t[0:64, 0:1], r=[pb.k, eps_t.k], w=[rr.k])
            P.recip(rr.t, rr.t, r=[rr.k], w=[rr.k])
            P.stt(oT.t[:, h, :], oT.t[:, h, :], gnA.t[:, l:l + 1], rr.t, ALU.mult, ALU.mult, r=[oT.k, rr.k, gnA.k], w=[oT.k])
            P.tt(outA.t[:, h, :], oT.t[:, h, :], szA.t[:, h, :], ALU.mult, r=[oT.k, szA.k], w=[outA.k])
        if "outA" in dbg:
            of = P.tmp([64, 4, NT], F32)
            P.copy(of.t, outA.t, r=[outA.k], w=[of.k])
            P.dbg(f"outA_{kind}{l}", of.t, r=[of.k])
        P.release(mA)
        return dict(l=l, T=T, NT=NT, prm=prm, kind=kind, xn=xn, sqn=sqn, mixT=mixT, outA=outA, m_all=m_all, win=win, sw=sw, as3=as3, nb=nb, NBk=NBk, NSEG=NSEG, W7=W7)


    def mixB(cx):
        l, NT, prm, xn, mixT, nb, NBk, as3, kind = cx["l"], cx["NT"], cx["prm"], cx["xn"], cx["mixT"], cx["nb"], cx["NBk"], cx["as3"], cx["kind"]
        m = P.mark()
        wb = wstage([(lambda b_: as3(b_), cx["win"][:, :, C_U:C_U + 512], cx["sw"])])
        wv = as3(wb.t)
        uT = P.tmp([128, 2, NT], BF16)

        def evu(j, ps_ap, pb):
            P.act(uT.t[:, j, :], ps_ap, AF.Gelu_apprx_tanh, r=[pb.k], w=[uT.k])
        proj_fm(None, wv, wb.k, 0, 2, 128, xn, NT, 0, evu)
        for b_ in range(NBk):
            cs = slice(b_ * nb, (b_ + 1) * nb)
            pv = bank(3 + b_ % 2)
            for c in range(8):
                P.mm(pv.t[0:nb, 0:256], xn.t[:, c, cs], wv[:, c, 256:512], start=(c == 0), stop=(c == 7), r=[xn.k, wb.k], w=[pv.k])
            vg = P.tmp([nb, 256], F32)
            st_ = P.tmp([nb, 4], F32)
            P.memset(st_.t, 0.0, w=[st_.k])
            P.act(vg.t, pv.t[0:nb, 0:256], AF.Gelu_apprx_tanh, accum_out=st_.t[:, 0:1], r=[pv.k, st_.k], w=[vg.k, st_.k])
            P.ts(st_.t[:, 1:2], st_.t[:, 0:1], -1.0 / 256, None, ALU.mult, r=[st_.k], w=[st_.k])
            junk = P.tmp([nb, 256], F32)
            P.act(junk.t, vg.t, AF.Square, bias=st_.t[:, 1:2], accum_out=st_.t[:, 2:3], r=[vg.k, st_.k], w=[junk.k, st_.k])
            P.act(st_.t[:, 3:4], st_.t[:, 2:3], AF.Sqrt, scale=1.0 / 256, bias=eps_t.t[0:nb, 0:1], r=[st_.k, eps_t.k], w=[st_.k])
            P.recip(st_.t[:, 3:4], st_.t[:, 3:4], r=[st_.k], w=[st_.k])
            P.ts(vg.t, vg.t, st_.t[:, 1:2], st_.t[:, 3:4], ALU.add, ALU.mult, r=[vg.k, st_.k], w=[vg.k])
            P.tt(vg.t, vg.t, lnr.t[0:nb, l, 0, :], ALU.mult, r=[vg.k, lnr.k], w=[vg.k])
            P.tt(vg.t, vg.t, lnr.t[0:nb, l, 1, :], ALU.add, r=[vg.k, lnr.k], w=[vg.k])
            if not prm:
                P.dma(mvs_d[l], vg.t, r=[vg.k])
            vb = P.tmp([nb, 256], BF16)
            P.copy(vb.t, vg.t, r=[vg.k], w=[vb.k])
            pm = bank(5 + b_ % 2)
            for g in range(4):
                wmat = wsT.t[:, l, g, :] if prm else wsbd.t[:, l, g, :]
                P.mm(pm.t[(g % 2) * 64:(g % 2) * 64 + 64, (g // 2) * nb:(g // 2 + 1) * nb], vb.t[:, g * 64:(g + 1) * 64], wmat, r=[vb.k, wsT.k, wsbd.k], w=[pm.k])
            for ch in range(2):
                bsv = bsP.t[:, l, ch, :] if prm else bsS.t[:, l, ch, :, :].rearrange("p s t -> p (s t)")
                tmpb = P.tmp([128, nb], F32)
                P.tt(tmpb.t, pm.t[:, ch * nb:(ch + 1) * nb], bsv, ALU.add, r=[pm.k, bsP.k, bsS.k], w=[tmpb.k])
                P.tt(mixT.t[:, ch, cs], tmpb.t, uT.t[:, ch, cs], ALU.mult, r=[tmpb.k, uT.k], w=[mixT.k])
        if "mixB" in dbg:
            of = P.tmp([128, 2, NT], F32)
            P.copy(of.t, mixT.t[:, 0:2, :], r=[mixT.k], w=[of.k])
            P.dbg(f"mixB_{kind}{l}", of.t, r=[of.k])
        P.release(m)

    def mixC(cx):
        l, T, NT, prm, xn, mixT, nb, NBk, as3, kind, NSEG, W7 = (cx[k] for k in ["l", "T", "NT", "prm", "xn", "mixT", "nb", "NBk", "as3", "kind", "NSEG", "W7"])
        m = P.mark()
        wb1 = wstage([(lambda b_: as3(b_), cx["win"][:, :, C_ZC:C_ZC + 512], cx["sw"])])
        wb2 = wstage([(lambda b_: as3(b_), cx["win"][:, :, C_ZC + 512:C_ZC + 1024], cx["sw"])])
        szC = P.tmp([128, 2, NT], BF16)
        xbc = P.tmp([128, 6, NT], F32)
        xpc = P.tmp([128, 6, W7], F32)

        def xv(j, i):
            if prm:
                return xpc.t[:, j, i:i + NT]
            return xpc.t[:, j, :].rearrange("p (s t) -> p s t", t=7)[:, :, i:i + 4]
        ov_ = lambda ap2: ap2 if prm else ap2.rearrange("p (s t) -> p s t", t=4)

        def evz(j, ps_ap, pb):
            P.act(szC.t[:, j, :], ps_ap, AF.Silu, r=[pb.k], w=[szC.k])
        proj_fm(None, as3(wb1.t), wb1.k, 0, 2, 128, xn, NT, 0, evz)
        if prm:
            P.copy(xpc.t[:, :, 0:3], tailC.t[:, l], r=[tailC.k], w=[xpc.k], eng="gpsimd")
        else:
            P.copy(xpc.t.rearrange("p j (s t) -> p j s t", t=7)[:, :, :, 0:3], T["tailC"].t, r=[T["tailC"].k], w=[xpc.k], eng="gpsimd")

        def evx(j0):
            def f(j, ps_ap, pb):
                P.copy(xv(j0 + j, 3), ov_(ps_ap), r=[pb.k], w=[xpc.k], eng="scalar")
            return f
        proj_fm(None, as3(wb1.t), wb1.k, 256, 2, 128, xn, NT, 2, evx(0))
        proj_fm(None, as3(wb2.t), wb2.k, 0, 4, 128, xn, NT, 4, evx(2))
        for j in range(6):
            ov = ov_(xbc.t[:, j, :])
            if prm:
                P.ts(ov, xv(j, 0), cwC.t[:, l, j, 0:1], None, ALU.mult, r=[xpc.k, cwC.k], w=[xbc.k])
            else:
                P.tt(ov, xv(j, 0), cwC.t[:, l, j, 0:1].unsqueeze(2).to_broadcast([128, 16, 4]), ALU.mult, r=[xpc.k, cwC.k], w=[xbc.k])
            for i in range(1, 4):
                P.stt(ov, xv(j, i), cwC.t[:, l, j, i:i + 1], ov, ALU.mult, ALU.add, r=[xpc.k, cwC.k, xbc.k], w=[xbc.k])
            P.act(xbc.t[:, j, :], xbc.t[:, j, :], AF.Silu, bias=cbC.t[:, l, j:j + 1], r=[xbc.k, cbC.k], w=[xbc.k])
        if prm:
            P.copy(tailC.t[:, l], xpc.t[:, :, NT:NT + 3], r=[xpc.k], w=[tailC.k], eng="gpsimd")
        else:
            rawC = P.tmp([128, 6, 16, 3], F32)
            P.copy(rawC.t, xpc.t.rearrange("p j (s t) -> p j s t", t=7)[:, :, :, 4:7], r=[xpc.k], w=[rawC.k], eng="gpsimd")
            rowC = P.tmp([48, 768], F32)
            for hb in range(2):
                pb = bank(1 + hb)
                for j in range(3):
                    P.tr(pb.t[0:48, j * 128:(j + 1) * 128], rawC.t[:, hb * 3 + j].rearrange("p s t -> p (s t)"), ident.t, r=[rawC.k, ident.k], w=[pb.k])
                P.copy(rowC.t[:, hb * 384:(hb + 1) * 384], pb.t[0:48, 0:384], r=[pb.k], w=[rowC.k])
            P.dma(scs_d[l].rearrange("s t c -> (s t) c"), rowC.t, r=[rowC.k])
        bcb = P.tmp([128, 4, NT], BF16)
        P.copy(bcb.t, xbc.t[:, 2:6, :], r=[xbc.k], w=[bcb.k], eng="gpsimd")
        pgt = bank(0)
        for b_ in range(NBk):
            for c in range(8):
                P.mm(pgt.t[0:nb, b_ * 4:(b_ + 1) * 4], xn.t[:, c, b_ * nb:(b_ + 1) * nb], wg.t[:, l, c, 8:12], start=(c == 0), stop=(c == 7), r=[xn.k, wg.k], w=[pgt.k])
        dt = P.tmp([nb, NBk, 4], F32)
        P.tt(dt.t, pgt.t[0:nb, 0:NBk * 4].rearrange("p (b n) -> p b n", n=4), rows.t[0:nb, l, 2, :].unsqueeze(1).to_broadcast([nb, NBk, 4]), ALU.add, r=[pgt.k, rows.k], w=[dt.k])
        softplus(dt, [nb, NBk, 4], nb)
        aa = P.tmp([nb, NBk, 4], F32)
        P.tt(aa.t, dt.t, rows.t[0:nb, l, 3, :].unsqueeze(1).to_broadcast([nb, NBk, 4]), ALU.mult, r=[dt.k, rows.k], w=[aa.k])
        if prm:
            TRI, SEGO, NMN, idn = cp.t[:, 0, :], cp.t[:, 1, :], cp.t[:, 2, :], ident.t
            cK = cp.k
        else:
            TRI, SEGO, NMN, idn = gs.t[:, 0, :], gs.t[:, 1, :], gs.t[:, 3, :], ident.t[0:64, 0:64]
            cK = gs.k
        pc = bank(0)
        for b_ in range(NBk):
            P.mm(pc.t[0:nb, b_ * 8:b_ * 8 + 4], TRI, aa.t[:, b_, :], r=[cK, aa.k], w=[pc.k])
            P.mm(pc.t[0:nb, b_ * 8 + 4:b_ * 8 + 8], SEGO, aa.t[:, b_, :], r=[cK, aa.k], w=[pc.k])
        cm = P.tmp([nb, NBk, 8], F32)
        P.copy(cm.t, pc.t[0:nb, 0:NBk * 8].rearrange("p (b n) -> p b n", n=8), r=[pc.k], w=[cm.k])
        negc = P.tmp([nb, NBk, 4], F32)
        dB = P.tmp([nb, NBk, 4], F32)
        P.ts(negc.t, cm.t[:, :, 0:4], -1.0, None, ALU.mult, r=[cm.k], w=[negc.k])
        P.tt(dB.t, cm.t[:, :, 4:8], cm.t[:, :, 0:4], ALU.subtract, r=[cm.k], w=[dB.k])
        P.act(dB.t, dB.t, AF.Exp, r=[dB.k], w=[dB.k])
        yz = P.tmp([128, 2, NT], F32)
        if prm:
            hv = lambda sg_, h: hTb.t[:, h, :]
        else:
            hT0 = T["hT0"]
            hTn = T["hTn"]
            hv = lambda sg_, h: hTb.t[:, sg_, h, :]
        segw = nb // NSEG
        hTb = P.tmp([128, 4, 64], BF16) if prm else P.tmp([128, 16, 4, 64], BF16)
        E2 = P.tmp([nb, 4, nb], F32)
        eG = P.tmp([128, 4, nb], F32)
        MD = P.tmp([nb, 4, nb], BF16)
        xdt = P.tmp([nb, 4, 64], BF16)
        Bdec = P.tmp([nb, 4, 128], BF16)
        CdT = P.tmp([128, 4, nb], BF16)
        ycs = [P.tmp([128, nb], F32) for _ in range(2)]
        if not prm:
            Bm = P.tmp([64, 16, 128], BF16)
        for b_ in range(NBk):
            cs = slice(b_ * nb, (b_ + 1) * nb)
            if prm:
                P.copy(hTb.t, hTst.t[:, l], r=[hTst.k], w=[hTb.k])
            else:
                P.copy(hTb.t, hT0.t, r=[hT0.k], w=[hTb.k])
            pE, pGx = bank(1), bank(2)
            for h in range(4):
                P.mm(pE.t[0:nb, h * nb:(h + 1) * nb], aa.t[:, b_, h:h + 1].to_broadcast([nb, nb]), TRI, start=True, stop=False, r=[aa.k, cK], w=[pE.k])
                P.mm(pE.t[0:nb, h * nb:(h + 1) * nb], idn, NMN, start=False, stop=True, r=[ident.k, cK], w=[pE.k])
                P.mm(pGx.t[:, h * nb:(h + 1) * nb], aa.t[:, b_, h:h + 1].to_broadcast([nb, 128]), TRI, r=[aa.k, cK], w=[pGx.k])
            for h in range(4):
                P.act(E2.t[:, h, :], pE.t[0:nb, h * nb:(h + 1) * nb], AF.Exp, bias=negc.t[:, b_, h:h + 1], r=[pE.k, negc.k], w=[E2.k])
            P.act(eG.t, pGx.t[:, 0:4 * nb].rearrange("p (h q) -> p h q", h=4), AF.Exp, r=[pGx.k], w=[eG.k])
            pM = bank(3)
            for g in range(2):
                P.mm(pM.t[0:nb, g * nb:(g + 1) * nb], bcb.t[:, g, cs], bcb.t[:, 2 + g, cs], r=[bcb.k], w=[pM.k])
            P.tt(MD.t.rearrange("p (g e) q -> p g e q", g=2), pM.t[0:nb, 0:2 * nb].rearrange("p (g q) -> p g q", g=2).unsqueeze(2).to_broadcast([nb, 2, 2, nb]),
                 E2.t.rearrange("p (g e) q -> p g e q", g=2), ALU.mult, r=[pM.k, E2.k], w=[MD.k])
            pX, pBt = bank(4), bank(5)
            for j in range(2):
                P.tr(pX.t[0:nb, j * 128:(j + 1) * 128], xbc.t[:, j, cs], ident.t, r=[xbc.k, ident.k], w=[pX.k])
                P.tr(pBt.t[0:nb, j * 128:(j + 1) * 128], xbc.t[:, 2 + j, cs], ident.t, r=[xbc.k, ident.k], w=[pBt.k])
            P.tt(xdt.t, pX.t[0:nb, 0:256].rearrange("p (h d) -> p h d", h=4), dt.t[:, b_, :].unsqueeze(2).to_broadcast([nb, 4, 64]), ALU.mult, r=[pX.k, dt.k], w=[xdt.k])
            P.tt(Bdec.t.rearrange("p (g e) n -> p g e n", g=2), pBt.t[0:nb, 0:256].rearrange("p (g n) -> p g n", g=2).unsqueeze(2).to_broadcast([nb, 2, 2, 128]),
                 dB.t[:, b_, :].rearrange("p (g e) -> p g e", g=2).unsqueeze(3).to_broadcast([nb, 2, 2, 128]), ALU.mult, r=[pBt.k, dB.k], w=[Bdec.k])
            P.tt(CdT.t.rearrange("p (g e) q -> p g e q", g=2), xbc.t[:, 4:6, cs].unsqueeze(2).to_broadcast([128, 2, 2, nb]),
                 eG.t.rearrange("p (g e) q -> p g e q", g=2), ALU.mult, r=[xbc.k, eG.k], w=[CdT.k])
            pY = bank(6)
            for h in range(4):
                po = pY.t[(h % 2) * 64:(h % 2) * 64 + 64, (h // 2) * nb:(h // 2 + 1) * nb]
                P.mm(po, xdt.t[:, h, :], MD.t[:, h, :], start=True, stop=False, r=[xdt.k, MD.k], w=[pY.k])
                for sg_ in range(NSEG):
                    P.mm(pY.t[(h % 2) * 64:(h % 2) * 64 + 64, (h // 2) * nb + sg_ * segw:(h // 2) * nb + (sg_ + 1) * segw], hv(sg_, h), CdT.t[:, h, sg_ * segw:(sg_ + 1) * segw],
                         start=False, stop=(sg_ == NSEG - 1), r=[hTb.k, CdT.k], w=[pY.k])
            for ch in range(2):
                yc = ycs[ch]
                P.stt(yc.t, xbc.t[:, ch, cs], dC.t[:, l, ch:ch + 1], pY.t[:, ch * nb:(ch + 1) * nb], ALU.mult, ALU.add, r=[xbc.k, dC.k, pY.k], w=[yc.k])
                P.tt(yz.t[:, ch, cs], yc.t, szC.t[:, ch, cs], ALU.mult, r=[yc.k, szC.k], w=[yz.k])
            if prm:
                pS = bank(1)
                for h in range(4):
                    P.mm(pS.t[:, h * 64:(h + 1) * 64], Bdec.t[:, h, :], xdt.t[:, h, :], r=[Bdec.k, xdt.k], w=[pS.k])
                P.tt(hTst.t[:, l], hTst.t[:, l], eG.t[:, :, nb - 1:nb].to_broadcast([128, 4, 64]), ALU.mult, r=[hTst.k, eG.k], w=[hTst.k])
                P.tt(hTst.t[:, l], hTst.t[:, l], pS.t[:, 0:256].rearrange("p (h d) -> p h d", h=4), ALU.add, r=[hTst.k, pS.k], w=[hTst.k])
            else:
                for h in range(4):
                    P.tt(Bm.t, Bdec.t[:, h, :].unsqueeze(1).to_broadcast([64, 16, 128]), segmask.t.unsqueeze(2).to_broadcast([64, 16, 128]), ALU.mult,
                         r=[Bdec.k, segmask.k], w=[Bm.k])
                    for half in range(2):
                        pS = bank(1 + half)
                        for s8 in range(8):
                            sg_ = half * 8 + s8
                            P.mm(pS.t[:, s8 * 64:(s8 + 1) * 64], Bm.t[:, sg_, :], xdt.t[:, h, :], r=[Bm.k, xdt.k], w=[pS.k])
                        ss_ = slice(half * 8, half * 8 + 8)
                        alb = eG.t[:, h, :].rearrange("p (s t) -> p s t", t=4)[:, ss_, 3:4].to_broadcast([128, 8, 64])
                        P.tt(hTn.t[:, ss_, h, :], hT0.t[:, ss_, h, :], alb, ALU.mult, r=[hT0.k, eG.k], w=[hTn.k])
                        P.tt(hTn.t[:, ss_, h, :], hTn.t[:, ss_, h, :], pS.t[:, :].rearrange("p (s v) -> p s v", v=64), ALU.add, r=[hTn.k, pS.k], w=[hTn.k])
        sqz = P.tmp([128, 2, NT], BF16)
        P.act(sqz.t, yz.t, AF.Square, r=[yz.k], w=[sqz.k])
        for ch in range(2):
            P.mm(psb[0].t[:, 0:NT], ones_bf.t, sqz.t[:, ch, :], start=(ch == 0), stop=(ch == 1), r=[sqz.k, ones_bf.k], w=[psb[0].k])
        rstd = rstd_from(psb[0].t[:, 0:NT], psb[0].k, [128, NT], 1.0 / 256)
        for ch in range(2):
            P.stt(mixT.t[:, 2 + ch, :], yz.t[:, ch, :], gnC.t[:, l, ch:ch + 1], rstd.t, ALU.mult, ALU.mult, r=[yz.k, gnC.k, rstd.k], w=[mixT.k])
        if "mixC" in dbg:
            of = P.tmp([128, 2, NT], F32)
            P.copy(of.t, mixT.t[:, 2:4, :], r=[mixT.k], w=[of.k])
            P.dbg(f"mixC_{kind}{l}", of.t, r=[of.k])
        P.release(m)


    def rope_rows(dst, src, tab, n, H, rk):
        t1 = P.tmp([n, H, 32], F32)
        P.tt(dst.t, src.t, tab[:, 0:32].unsqueeze(1).to_broadcast([n, H, 32]), ALU.mult, r=[src.k] + rk, w=[dst.k])
        P.tt(t1.t[:, :, 0:16], src.t[:, :, 16:32], tab[:, 32:48].unsqueeze(1).to_broadcast([n, H, 16]), ALU.mult, r=[src.k] + rk, w=[t1.k])
        P.tt(t1.t[:, :, 16:32], src.t[:, :, 0:16], tab[:, 48:64].unsqueeze(1).to_broadcast([n, H, 16]), ALU.mult, r=[src.k] + rk, w=[t1.k])
        P.tt(dst.t, dst.t, t1.t, ALU.add, r=[t1.k, dst.k], w=[dst.k])

    def mixD(cx):
        l, T, NT, prm, xn, mixT, nb, NBk, as3, kind = (cx[k] for k in ["l", "T", "NT", "prm", "xn", "mixT", "nb", "NBk", "as3", "kind"])
        m = P.mark()
        wb = wstage([(lambda b_: as3(b_)[:, :, 0:416], cx["win"][:, :, C_CQ:C_CQ + 416], cx["sw"])])
        wv = as3(wb.t)
        sqn_ = cx["sqn"]
        cqnT = Tl(sqn_.t[:, 0:2, :], ("sqn_sub", 0))
        olat = P.tmp([128, 4, NT], BF16)
        qnT = Tl(olat.t[0:64], olat.k)
        qlT = P.tmp([128, 4, NT], BF16)
        qpT = P.tmp([32, 4, NT], BF16)
        if prm:
            b0 = T["t0"] // 128
            rowbase = T["b"] * SEQ + T["t0"]
            KTv, Ktv, KPTv = KT.t[:, l], Kt.t[:, l], KPT.t[:, l]
        else:
            KTn = P.tmp([128, 64], BF16)
            Ktn = P.tmp([64, 128], BF16)
            KPTn = P.tmp([32, 64], BF16)
        for b_ in range(NBk):
            cs = slice(b_ * nb, (b_ + 1) * nb)
            pd = bank(1 + b_ % 2)
            for c in range(8):
                P.mm(pd.t[0:nb, 0:416], xn.t[:, c, cs], wv[:, c, 0:416], start=(c == 0), stop=(c == 7), r=[xn.k, wb.k], w=[pd.k])
            st_ = P.tmp([nb, 4], F32)
            junk = P.tmp([nb, 256], F32)
            P.memset(st_.t, 0.0, w=[st_.k])
            P.act(junk.t, pd.t[0:nb, 0:256], AF.Square, accum_out=st_.t[:, 0:1], r=[pd.k, st_.k], w=[junk.k, st_.k])
            P.act(junk.t[:, 0:128], pd.t[0:nb, 256:384], AF.Square, accum_out=st_.t[:, 1:2], r=[pd.k, st_.k], w=[junk.k, st_.k])
            P.act(st_.t[:, 2:3], st_.t[:, 0:1], AF.Sqrt, scale=1.0 / 256, bias=eps_t.t[0:nb, 0:1], r=[st_.k, eps_t.k], w=[st_.k])
            P.act(st_.t[:, 3:4], st_.t[:, 1:2], AF.Sqrt, scale=1.0 / 128, bias=eps_t.t[0:nb, 0:1], r=[st_.k, eps_t.k], w=[st_.k])
            P.recip(st_.t[:, 2:4], st_.t[:, 2:4], r=[st_.k], w=[st_.k])
            cqn = P.tmp([nb, 256], F32)
            ckvn = P.tmp([nb, 128], F32)
            P.stt(cqn.t, pd.t[0:nb, 0:256], st_.t[:, 2:3], gqr.t[0:nb, l, :], ALU.mult, ALU.mult, r=[pd.k, st_.k, gqr.k], w=[cqn.k])
            P.stt(ckvn.t, pd.t[0:nb, 256:384], st_.t[:, 3:4], gkr.t[0:nb, l, :], ALU.mult, ALU.mult, r=[pd.k, st_.k, gkr.k], w=[ckvn.k])
            tab = rope_p.t[:, b0 + b_, :] if prm else rope_s.t
            kraw = P.tmp([nb, 1, 32], F32)
            P.copy(kraw.t[:, 0, :], pd.t[0:nb, 384:416], r=[pd.k], w=[kraw.k])
            kpe = P.tmp([nb, 1, 32], F32)
            rope_rows(kpe, kraw, tab, nb, 1, [rope_p.k, rope_s.k])
            if prm:
                P.dma(ckvp_d[l, rowbase + b_ * 128:rowbase + (b_ + 1) * 128, :], ckvn.t, r=[ckvn.k])
                P.dma(kpep_d[l, rowbase + b_ * 128:rowbase + (b_ + 1) * 128, :], kpe.t[:, 0, :], r=[kpe.k])
                P.copy(Ktv[:, b0 + b_, :], ckvn.t, r=[ckvn.k], w=[(Kt.k, l)], eng="gpsimd")
            else:
                P.dma(ckvs_d[l], ckvn.t, r=[ckvn.k])
                P.dma(kpes_d[l], kpe.t[:, 0, :], r=[kpe.k])
                P.copy(Ktn.t, ckvn.t, r=[ckvn.k], w=[Ktn.k], eng="gpsimd")
            pt = bank(3 + b_ % 2)
            P.tr(pt.t[:, 0:nb], ckvn.t, ident.t[0:nb, 0:nb], r=[ckvn.k, ident.k], w=[pt.k])
            P.tr(pt.t[0:32, 128:128 + nb], kpe.t[:, 0, :], ident.t[0:nb, 0:nb], r=[kpe.k, ident.k], w=[pt.k])
            for c2 in range(2):
                P.tr(pt.t[:, 256 + c2 * 128:256 + c2 * 128 + nb], cqn.t[:, c2 * 128:(c2 + 1) * 128], ident.t[0:nb, 0:nb], r=[cqn.k, ident.k], w=[pt.k])
            if prm:
                kc = slice((b0 + b_) * 128, (b0 + b_ + 1) * 128)
                P.copy(KTv[:, kc], pt.t[:, 0:128], r=[pt.k], w=[(KT.k, l)])
                P.copy(KPTv[:, kc], pt.t[0:32, 128:256], r=[pt.k], w=[(KPT.k, l)])
            else:
                P.copy(KTn.t, pt.t[:, 0:64], r=[pt.k], w=[KTn.k])
                P.copy(KPTn.t, pt.t[0:32, 128:192], r=[pt.k], w=[KPTn.k])
            for c2 in range(2):
                P.copy(cqnT.t[:, c2, cs], pt.t[:, 256 + c2 * 128:256 + c2 * 128 + nb], r=[pt.k], w=[cqnT.k], eng="scalar")
        for h in range(4):
            pq_ = bank(1 + h % 2)
            for c2 in range(2):
                P.mm(pq_.t[0:64, 0:NT], wuq.t[:, l, c2, h * 96:h * 96 + 64], cqnT.t[:, c2, :], start=(c2 == 0), stop=(c2 == 1), r=[wuq.k, cqnT.k], w=[pq_.k])
            P.copy(qnT.t[:, h, :], pq_.t[0:64, 0:NT], r=[pq_.k], w=[qnT.k], eng="scalar")
            pl_ = bank(3 + h % 2)
            P.mm(pl_.t[:, 0:NT], wukT.t[:, l, h, :], qnT.t[:, h, :], r=[wukT.k, qnT.k], w=[pl_.k])
            P.copy(qlT.t[:, h, :], pl_.t[:, 0:NT], r=[pl_.k], w=[qlT.k])
        for b_ in range(NBk):
            cs = slice(b_ * nb, (b_ + 1) * nb)
            pp = bank(5 + b_ % 2)
            wr = wuq.t[:, l, :, :].rearrange("p c (h e) -> p c h e", e=96)
            for c2 in range(2):
                P.mm(pp.t[0:nb, 0:128].rearrange("p (h e) -> p h e", h=4), cqnT.t[:, c2, cs], wr[:, c2, :, 64:96], start=(c2 == 0), stop=(c2 == 1), r=[cqnT.k, wuq.k], w=[pp.k])
            qraw = P.tmp([nb, 4, 32], F32)
            P.copy(qraw.t, pp.t[0:nb, 0:128].rearrange("p (h e) -> p h e", h=4), r=[pp.k], w=[qraw.k], eng="scalar")
            qpe = P.tmp([nb, 4, 32], F32)
            tab = rope_p.t[:, b0 + b_, :] if prm else rope_s.t
            rope_rows(qpe, qraw, tab, nb, 4, [rope_p.k, rope_s.k])
            pt = bank(1 + b_ % 2)
            for h in range(4):
                P.tr(pt.t[0:32, h * nb:(h + 1) * nb], qpe.t[:, h, :], ident.t[0:nb, 0:nb], r=[qpe.k, ident.k], w=[pt.k])
            P.copy(qpT.t[:, :, cs], pt.t[0:32, 0:4 * nb].rearrange("p (h q) -> p h q", h=4), r=[pt.k], w=[qpT.k])
        if prm:
            pTs = [Tl(sqn_.t[:, 2 + i_, :], ("sqn_sub", 1 + i_)) for i_ in range(2)]
            rs = P.tmp([128, 512], F32)
            v3 = lambda ap_: ap_.rearrange("p (h q) -> p h q", h=4)
            nkb = 0
            for qb in range(NBk):
                gq = b0 + qb
                qs = slice(qb * 128, (qb + 1) * 128)
                pO, pSm = bank(5), bank(6)
                def SC(kb, gq=gq, qs=qs, nkb0=nkb):
                    ks = slice(kb * 128, (kb + 1) * 128)
                    pS_ = bank(1 + kb % 3)
                    dg = kb == gq
                    P.mm(v3(pS_.t), KTv[:, ks], qlT.t[:, :, qs], start=True, stop=False, r=[(KT.k, l), qlT.k], w=[pS_.k])
                    P.mm(v3(pS_.t), KPTv[:, ks], qpT.t[:, :, qs], start=False, stop=not dg, r=[(KPT.k, l), qpT.k], w=[pS_.k])
                    if dg:
                        P.mm(pS_.t, identb.t, mla_diag.t, start=False, stop=True, r=[identb.k, mla_diag.k], w=[pS_.k])
                    pT_ = pTs[(nkb0 + kb) % 2]
                    P.act(pT_.t, pS_.t, AF.Exp, scale=MLA_SCALE, r=[pS_.k], w=[pT_.k])

                def PV(kb, gq=gq, nkb0=nkb):
                    dg = kb == gq
                    pT_ = pTs[(nkb0 + kb) % 2]
                    P.mm(pO.t, Ktv[:, kb, :], pT_.t, start=(kb == 0), stop=dg, r=[(Kt.k, l), pT_.k], w=[pO.k])
                    P.mm(pSm.t, ones_bf.t, pT_.t, start=(kb == 0), stop=dg, r=[ones_bf.k, pT_.k], w=[pSm.k])
                SC(0)
                for kb in range(gq + 1):
                    if kb + 1 <= gq:
                        SC(kb + 1)
                    PV(kb)
                nkb += gq + 1
                P.recip(rs.t, pSm.t, r=[pSm.k], w=[rs.k])
                P.tt(olat.t[:, :, qs], pO.t.rearrange("p (h q) -> p h q", h=4), rs.t.rearrange("p (h q) -> p h q", h=4), ALU.mult, r=[pO.k, rs.k], w=[olat.k])
        else:
            mla_sample(cx, qlT, qpT, KTn, Ktn, KPTn, olat)
        for ch in range(2):
            po = bank(1 + ch)
            for hl in range(2):
                h = ch * 2 + hl
                P.mm(po.t[hl * 64:(hl + 1) * 64, 0:NT], wuv.t[:, l, h, :], olat.t[:, h, :], r=[wuv.k, olat.k], w=[po.k])
            P.copy(mixT.t[:, 4 + ch, :], po.t[:, 0:NT], r=[po.k], w=[mixT.k])
        if "mixD" in dbg:
            of = P.tmp([128, 2, NT], F32)
            P.copy(of.t, mixT.t[:, 4:6, :], r=[mixT.k], w=[of.k])
            P.dbg(f"mixD_{kind}{l}", of.t, r=[of.k])
        P.release(m)


    def mla_sample(cx, qlT, qpT, KTn, Ktn, KPTn, olat):
        l = cx["l"]
        KQ = 32
        NQ = 128 // KQ
        J = 8
        v3 = lambda ap_, h_: ap_.rearrange("p (h q) -> p h q", h=h_)
        ones_f = P.tmp([128, 128], F32)
        P.memset(ones_f.t, 1.0, w=[ones_f.k])
        pn = bank(1)
        P.mm(v3(pn.t[0:64, 0:256], 4), KTn.t, qlT.t, start=True, stop=False, r=[KTn.k, qlT.k], w=[pn.k])
        P.mm(v3(pn.t[0:64, 0:256], 4), KPTn.t, qpT.t, start=False, stop=True, r=[KPTn.k, qpT.k], w=[pn.k])
        pnf = P.tmp([64, 256], F32)
        P.act(pnf.t, pn.t[0:64, 0:256], AF.Exp, scale=MLA_SCALE, r=[pn.k], w=[pnf.k])
        pnb = P.tmp([64, 4, 64], BF16)
        P.tt(pnf.t, pnf.t, mla_new.t, ALU.mult, r=[pnf.k, mla_new.k], w=[pnf.k])
        P.copy(pnb.t, v3(pnf.t, 4), r=[pnf.k], w=[pnb.k])
        cbs = [P.tmp([NPG, KQ, 128], BF16) for _ in range(2)]
        kbs = [P.tmp([NPG, KQ, 32], BF16) for _ in range(2)]
        pgT = [P.tmp([128, J * NPG], BF16) for _ in range(2)]
        kpT = [P.tmp([32, J * NPG], BF16) for _ in range(2)]
        pTb = [P.tmp([NPG, J * 16], BF16) for _ in range(2)]
        pacc = [P.tmp([NPG, 16], F32) for _ in range(2)]
        pn16 = P.tmp([64, 16], F32)
        red = P.tmp([NPG, 16], F32)
        rs = P.tmp([128, 16], F32)
        groups = [(s_, q, j0) for s_ in range(NSS) for q in range(NQ) for j0 in range(0, KQ, J)]
        n = len(groups)
        gq_per_seq = NQ * (KQ // J)

        def bufs(i):
            s_, q, j0 = groups[i]
            return cbs[(s_ * NQ + q) % 2], kbs[(s_ * NQ + q) % 2], pgT[i % 2], kpT[i % 2], pTb[i % 2], bank(3 + i % 2), bank(5 + s_ % 2), pacc[s_ % 2]

        def A1(i):
            s_, q, j0 = groups[i]
            cb, kb, tt_, kt_, pt_, psc, pO, pa = bufs(i)
            if q == 0 and j0 == 0:
                P.memset(pa.t, 0.0, w=[pa.k])
            if j0 == 0:
                P.op("gpsimd", (lambda e, cb=cb, q=q, s_=s_: e.indirect_dma_start(
                    out=cb.t.rearrange("p k r -> p (k r)"), out_offset=None, in_=cckv_d.rearrange("l n (q k) r -> (l n q) (k r)", k=KQ),
                    in_offset=bass.IndirectOffsetOnAxis(ap=ptab4.t[:, s_:s_ + 1], axis=0),
                    element_offset=l * NPHYS * 128 * 128 + q * KQ * 128)), r=[ptab4.k], w=[cb.k], dma=True)
                P.op("gpsimd", (lambda e, kb=kb, q=q, s_=s_: e.indirect_dma_start(
                    out=kb.t.rearrange("p k r -> p (k r)"), out_offset=None, in_=ckpe_d.rearrange("l n (q k) r -> (l n q) (k r)", k=KQ),
                    in_offset=bass.IndirectOffsetOnAxis(ap=ptab4.t[:, s_:s_ + 1], axis=0),
                    element_offset=l * NPHYS * 128 * 32 + q * KQ * 32)), r=[ptab4.k], w=[kb.k], dma=True)
            for j in range(J):
                P.tr(psh.t[:, j * NPG:(j + 1) * NPG], cb.t[:, j0 + j, :], identb.t[0:NPG, 0:NPG], r=[cb.k, identb.k], w=[psh.k])
            P.copy(tt_.t, psh.t[:, 0:J * NPG], r=[psh.k], w=[tt_.k], eng=("scalar" if i % 2 else "vector"))

        def A2(i):
            s_, q, j0 = groups[i]
            cb, kb, tt_, kt_, pt_, psc, pO, pa = bufs(i)
            for j in range(J):
                P.tr(psh.t[0:32, j * NPG:(j + 1) * NPG], kb.t[:, j0 + j, :], identb.t[0:NPG, 0:NPG], r=[kb.k, identb.k], w=[psh.k])
            P.copy(kt_.t, psh.t[0:32, 0:J * NPG], r=[psh.k], w=[kt_.k], eng=("vector" if i % 2 else "scalar"))

        def B1(i):
            s_, q, j0 = groups[i]
            cb, kb, tt_, kt_, pt_, psc, pO, pa = bufs(i)
            qs = slice(4 * s_, 4 * s_ + 4)
            for j in range(J):
                o_ = v3(psc.t[0:NPG, j * 16:(j + 1) * 16], 4)
                P.mm(o_, tt_.t[:, j * NPG:(j + 1) * NPG], qlT.t[:, :, qs], start=True, stop=False, r=[tt_.k, qlT.k], w=[psc.k])
                P.mm(o_, kt_.t[:, j * NPG:(j + 1) * NPG], qpT.t[:, :, qs], start=False, stop=True, r=[kt_.k, qpT.k], w=[psc.k])
            P.act(pt_.t, psc.t[0:NPG, 0:J * 16], AF.Exp, scale=MLA_SCALE, r=[psc.k], w=[pt_.k])

        def B2(i):
            s_, q, j0 = groups[i]
            cb, kb, tt_, kt_, pt_, psc, pO, pa = bufs(i)
            qs = slice(4 * s_, 4 * s_ + 4)
            for j in range(J):
                P.mm(pO.t[:, 0:16], cb.t[:, j0 + j, :], pt_.t[:, j * 16:(j + 1) * 16], start=(q == 0 and j0 == 0 and j == 0), stop=False, r=[cb.k, pt_.k], w=[pO.k])
            P.red(red.t, pt_.t.rearrange("p (g c) -> p c g", g=J), r=[pt_.k], w=[red.k])
            P.tt(pa.t, pa.t, red.t, ALU.add, r=[pa.k, red.k], w=[pa.k])
            if (i + 1) % gq_per_seq == 0:
                P.mm(v3(pO.t[:, 0:16], 4), Ktn.t, pnb.t[:, :, qs], start=False, stop=True, r=[Ktn.k, pnb.k], w=[pO.k])
                P.copy(v3(pn16.t, 4), v3(pnf.t, 4)[:, :, qs], r=[pnf.k], w=[pn16.k])
                psm = bank(2)
                P.mm(psm.t[:, 0:16], ones_f.t[0:NPG, :], pa.t, start=True, stop=False, r=[ones_f.k, pa.k], w=[psm.k])
                P.mm(psm.t[:, 0:16], ones_f.t[0:64, :], pn16.t, start=False, stop=True, r=[ones_f.k, pn16.k], w=[psm.k])
                P.recip(rs.t, psm.t[:, 0:16], r=[psm.k], w=[rs.k])
                P.tt(olat.t[:, :, qs], v3(pO.t[:, 0:16], 4), v3(rs.t, 4), ALU.mult, r=[pO.k, rs.k], w=[olat.k])

        for i in range(n + 1):
            if i < n:
                A1(i)
            if i > 0:
                B1(i - 1)
            if i < n:
                A2(i)
            if i > 0:
                B2(i - 1)

    def sample_ctx(l):
        T = dict(NT=64, kind="s")
        tA = P.tmp([64, 12, 16, 3], F32)
        tC = P.tmp([128, 6, 16, 3], F32)
        S0 = P.tmp([64, 16, 4, 64], F32)
        hT0 = P.tmp([128, 16, 4, 64], F32)
        m = P.mark()
        ca = P.tmp([48, 768], F32)
        cc = P.tmp([48, 768], F32)
        P.dma(ca.t, sgc_d[l], w=[ca.k])
        P.dma(cc.t, ssc_d[l], w=[cc.k])
        for g3 in range(2):
            pb = bank(1 + g3)
            for j in range(6):
                hc = g3 * 6 + j
                P.tr(pb.t[0:64, j * 48:(j + 1) * 48], ca.t[:, hc * 64:(hc + 1) * 64], ident.t[0:48, 0:48], r=[ca.k, ident.k], w=[pb.k])
            P.copy(tA.t[:, g3 * 6:(g3 + 1) * 6].rearrange("p j s t -> p j (s t)"), pb.t[0:64, 0:288].rearrange("p (j n) -> p j n", j=6), r=[pb.k], w=[tA.k])
        pb = bank(3)
        for j in range(6):
            P.tr(pb.t[:, j * 48:(j + 1) * 48], cc.t[:, j * 128:(j + 1) * 128], ident.t[0:48, 0:48], r=[cc.k, ident.k], w=[pb.k])
        P.copy(tC.t.rearrange("p j s t -> p j (s t)"), pb.t[:, 0:288].rearrange("p (j n) -> p j n", j=6), r=[pb.k], w=[tC.k])
        for h in range(4):
            P.dma(S0.t[:, :, h, :], sgs_d[l, :, h, :, :].rearrange("s k v -> k s v"), w=[S0.k])
        hsrc = ssh_d[l].rearrange("(q r) n -> r q n", r=128)
        hv_ = hT0.t.rearrange("p s h d -> p (s h d)").rearrange("p (q r) -> p q r", r=128)
        for j in range(8):
            hn = P.tmp([128, 4, 128], F32)
            P.dma(hn.t, hsrc[:, 4 * j:4 * j + 4, :], w=[hn.k])
            pb = bank(4 + j % 2)
            for q in range(4):
                P.tr(pb.t[:, q * 128:(q + 1) * 128], hn.t[:, q, :], ident.t, r=[hn.k, ident.k], w=[pb.k])
            P.copy(hv_[:, 4 * j:4 * j + 4, :], pb.t.rearrange("p (q r) -> p q r", q=4), r=[pb.k], w=[hT0.k], eng=("scalar" if j % 2 else "vector"))
        P.release(m)
        T.update(tailA=tA, tailC=tC, S0=S0, Sn=S0, hT0=hT0, hTn=hT0)
        return T

    def out_states_sample(l, T):
        m = P.mark()
        S0, hT0 = T["S0"], T["hT0"]
        for h in range(4):
            P.dma(gss_d[l, :, h, :, :].rearrange("s k v -> k s v"), S0.t[:, :, h, :], r=[S0.k])
        hdst = shs_d[l].rearrange("(q r) n -> r q n", r=128)
        hv_ = hT0.t.rearrange("p s h d -> p (s h d)").rearrange("p (q r) -> p q r", r=128)
        for j in range(8):
            pb = bank(4 + j % 2)
            for q in range(4):
                P.tr(pb.t[:, q * 128:(q + 1) * 128], hv_[:, 4 * j + q, :], ident.t, r=[hT0.k, ident.k], w=[pb.k])
            ho = P.tmp([128, 4, 128], F32)
            P.copy(ho.t, pb.t.rearrange("p (q r) -> p q r", q=4), r=[pb.k], w=[ho.k], eng=("scalar" if j % 2 else "vector"))
            P.dma(hdst[:, 4 * j:4 * j + 4, :], ho.t, r=[ho.k])
        P.release(m)

    def out_states_prompt(l, b):
        m = P.mark()
        P.dma(gsp_d[l, b].rearrange("h k v -> k h v"), Sst.t[:, l], r=[Sst.k])
        rowA = P.tmp([3, 768], F32)
        rowC = P.tmp([3, 768], F32)
        for hb in range(2):
            pa_, pc_ = bank(1 + hb), bank(5 + hb)
            for j in range(6):
                P.tr(pa_.t[0:3, j * 64:(j + 1) * 64], tailA.t[:, l, hb * 6 + j, :], ident.t[0:64, 0:64], r=[tailA.k, ident.k], w=[pa_.k])
            for j in range(3):
                P.tr(pc_.t[0:3, j * 128:(j + 1) * 128], tailC.t[:, l, hb * 3 + j, :], ident.t, r=[tailC.k, ident.k], w=[pc_.k])
            P.copy(rowA.t[:, hb * 384:(hb + 1) * 384], pa_.t[0:3, 0:384], r=[pa_.k], w=[rowA.k])
            P.copy(rowC.t[:, hb * 384:(hb + 1) * 384], pc_.t[0:3, 0:384], r=[pc_.k], w=[rowC.k])
        P.dma(gcp_d[l, b], rowA.t, r=[rowA.k])
        P.dma(scp_d[l, b], rowC.t, r=[rowC.k])
        pb = bank(4)
        hv_ = hTst.t[:, l].rearrange("p h d -> p (h d)")
        for g in range(2):
            P.tr(pb.t[:, g * 128:(g + 1) * 128], hv_[:, g * 128:(g + 1) * 128], ident.t, r=[hTst.k, ident.k], w=[pb.k])
        ho = P.tmp([128, 2, 128], F32)
        P.copy(ho.t, pb.t[:, 0:256].rearrange("p (g n) -> p g n", g=2), r=[pb.k], w=[ho.k])
        P.dma(shp_d[l, b * 256:(b + 1) * 256, :].rearrange("(g r) n -> r g n", r=128), ho.t, r=[ho.k])
        P.release(m)

    def mix_out(cx):
        l, NT, mixT, outA = cx["l"], cx["NT"], cx["mixT"], cx["outA"]
        wo = s_wout[l]
        tok = ("s_wout", l)
        wbA = wstage([(lambda b_: b_[0:64, :].rearrange("p (h d) -> p h d", h=4), wo[0:256, :].rearrange("(h p) d -> p h d", p=64), tok)])
        wbB = wstage([(lambda b_: b_.rearrange("p (c d) -> p c d", c=4), wo[256:768, :].rearrange("(c p) d -> p c d", p=128), tok)])
        wbC = wstage([(lambda b_: b_[:, 0:2048].rearrange("p (c d) -> p c d", c=2), wo[768:1024, :].rearrange("(c p) d -> p c d", p=128), tok)])
        vA = wbA.t[0:64, :].rearrange("p (h d) -> p h d", h=4)
        vB = wbB.t.rearrange("p (c d) -> p c d", c=4)
        vC = wbC.t[:, 0:2048].rearrange("p (c d) -> p c d", c=2)
        ysb = P.tmp([128, 8, NT], F32)
        sq = cx["sqn"]
        for p_ in range(2):
            for dcl in range(4):
                dc = 4 * p_ + dcl
                pb = psb[1 + dcl]
                ds_ = slice(dc * 128, (dc + 1) * 128)
                for h in range(4):
                    P.mm(pb.t[:, 0:NT], vA[:, h, ds_], outA.t[:, h, :], start=(h == 0), stop=False, r=[wbA.k, outA.k], w=[pb.k])
                for c in range(4):
                    P.mm(pb.t[:, 0:NT], vB[:, c, ds_], mixT.t[:, c, :], start=False, stop=False, r=[wbB.k, mixT.k], w=[pb.k])
                for c in range(2):
                    P.mm(pb.t[:, 0:NT], vC[:, c, ds_], mixT.t[:, 4 + c, :], start=False, stop=(c == 1), r=[wbC.k, mixT.k], w=[pb.k])
                P.copy(ysb.t[:, dc, :], pb.t[:, 0:NT], r=[pb.k], w=[ysb.k])
                P.act(sq.t[:, dc, :], pb.t[:, 0:NT], AF.Square, r=[pb.k], w=[sq.k])
        if "mix" in dbg:
            P.dbg(f"mix_{cx['kind']}{l}", ysb.t, r=[ysb.k])
        postnorm_residual(ysb, sq, l, 3, NT)
        P.release(cx["m_all"])

    def run_layer(l, T):
        NT = T["NT"]
        ffn(l, 0, NT)
        cx = mixer(l, T)
        mixB(cx)
        mixC(cx)
        mixD(cx)
        mix_out(cx)
        ffn(l, 1, NT)

    def run_all(groups="ps", layers=(0, 1)):
        for b in (range(NPS) if "p" in groups else []):
            for t_ in (Sst, hTst, tailA, tailC):
                P.memset(t_.t, 0.0, w=[t_.k])
            for ti in range(SEQ // TT):
                t0 = ti * TT
                load_x(xp_d[b * SEQ + t0:b * SEQ + t0 + TT, :], TT)
                for l in layers:
                    run_layer(l, dict(NT=TT, kind="p", b=b, t0=t0))
                    if ti == SEQ // TT - 1:
                        out_states_prompt(l, b)
                store_x(yp_d[b * SEQ + t0:b * SEQ + t0 + TT, :], TT)
        if "s" not in groups:
            return
        load_x(xs_d, 64)
        for l in layers:
            m = P.mark()
            T = sample_ctx(l)
            run_layer(l, T)
            out_states_sample(l, T)
            P.release(m)
        store_x(ys_d, 64)

    st = dict(locals())
    return st


N_CORES = 8
_WNAMES = ["norm_g", "ffn_w_in", "ffn_w_out", "w_in", "w_out", "gdn_conv_w", "gdn_a_log", "gdn_dt_bias", "gdn_norm_g",
           "mlp_ln_g", "mlp_ln_b", "mlp_ws", "mlp_bs", "ssm_conv_w", "ssm_conv_b", "ssm_a_log", "ssm_dt_bias", "ssm_d",
           "ssm_norm_g", "mla_q_norm_g", "mla_w_uq", "mla_kv_norm_g", "mla_w_uk", "mla_w_uv"]


def core_inputs(inputs, cfg, c, consts):
    NPS, NSS = cfg["NPS"], cfg["NSS"]
    f = lambda a: np.ascontiguousarray(np.asarray(a, dtype=np.float32))
    m = {}
    m["xp"] = f(inputs["x_prompt"][c * NPS:(c + 1) * NPS]).reshape(-1, D)
    m["xs"] = f(inputs["x_sample"][c * NSS:(c + 1) * NSS]).reshape(-1, D)
    m["cache_ckv"] = inputs["cache_ckv"]
    m["cache_kpe"] = inputs["cache_kpe"]
    m["ptab"] = np.ascontiguousarray(np.asarray(inputs["page_table"][c * NSS:(c + 1) * NSS], dtype=np.int32)).reshape(1, -1)
    m["state_gdn_s"] = f(inputs["state_gdn_s"][:, c * NSS:(c + 1) * NSS])
    m["state_gdn_conv"] = f(inputs["state_gdn_conv"][:, c * NSS:(c + 1) * NSS]).reshape(2, -1, 768)
    m["state_ssm_h"] = f(inputs["state_ssm_h"][:, c * NSS:(c + 1) * NSS]).reshape(2, -1, 128)
    m["state_ssm_conv"] = f(inputs["state_ssm_conv"][:, c * NSS:(c + 1) * NSS]).reshape(2, -1, 768)
    for k in _WNAMES:
        m[k] = inputs[k]
    for k, v in consts.items():
        m["c_" + k] = v
    return m


def assemble(results, cfg):
    NPS, SEQ, NSS = cfg["NPS"], cfg["SEQ"], cfg["NSS"]
    cat = lambda xs, ax: np.concatenate(xs, axis=ax)
    R_ = results
    out = [
        cat([r["yp"].reshape(NPS, SEQ, D) for r in R_], 0),
        cat([r["ys"].reshape(NSS, 4, D) for r in R_], 0),
        cat([r["ckv_p"].reshape(2, NPS, SEQ, 128) for r in R_], 1),
        cat([r["kpe_p"].reshape(2, NPS, SEQ, 32) for r in R_], 1),
        cat([r["gs_p"].reshape(2, NPS, 4, 64, 64) for r in R_], 1),
        cat([r["gc_p"].reshape(2, NPS, 3, 768) for r in R_], 1),
        cat([r["sh_p"].reshape(2, NPS, 4, 64, 128) for r in R_], 1),
        cat([r["sc_p"].reshape(2, NPS, 3, 768) for r in R_], 1),
        cat([r["ckv_s"].reshape(2, NSS, 4, 128) for r in R_], 1),
        cat([r["kpe_s"].reshape(2, NSS, 4, 32) for r in R_], 1),
        cat([r["gs_s"].reshape(2, NSS, 4, 64, 64) for r in R_], 1),
        cat([r["gc_s"].reshape(2, NSS, 3, 768) for r in R_], 1),
        cat([r["sh_s"].reshape(2, NSS, 4, 64, 128) for r in R_], 1),
        cat([r["sc_s"].reshape(2, NSS, 3, 768) for r in R_], 1),
        cat([r["mv_s"].reshape(2, NSS, 4, 256) for r in R_], 1),
    ]
    return tuple(np.ascontiguousarray(o.astype(np.float32)) for o in out)


def kernel(**inputs):
    inputs = {k: np.asarray(v) for k, v in inputs.items()}
    B, SEQ = inputs["x_prompt"].shape[0], inputs["x_prompt"].shape[1]
    DB = inputs["x_sample"].shape[0]
    NPG = inputs["page_table"].shape[1]
    NPHYS = inputs["cache_ckv"].shape[1]
    cfg = dict(NPS=B // N_CORES, SEQ=SEQ, NSS=DB // N_CORES, PAST=NPG * 128, NPHYS=NPHYS, TT=512)
    st = build_program(cfg)
    st["run_all"]()
    st["P"].build()
    consts = make_consts(cfg)
    in_maps = [core_inputs(inputs, cfg, c, consts) for c in range(N_CORES)]
    res = run_bass_kernel_spmd(st["nc"], in_maps, core_ids=list(range(N_CORES)))
    return assemble(res.results, cfg)
```
